# Optimizing a Trainium2 kernel written in Bass

```python
import math
import jax, jax.numpy as jnp
from jax import lax
import numpy as np

D_MODEL = 2048
BATCH = 4
SEQ = 4096
DEPTH = 1

HEAD_DIM = 64
N_Q_HEADS = 16
N_KV_HEADS = 4
Q_PER_KV = N_Q_HEADS // N_KV_HEADS
ATTN_WIDTH = N_Q_HEADS * HEAD_DIM
KV_WIDTH = N_KV_HEADS * HEAD_DIM
WINDOW = 128
BLOCK = 128
SSM_WIDTH = D_MODEL - ATTN_WIDTH
SSM_GROUP = 16
SSM_GROUPS = SSM_WIDTH // SSM_GROUP
SSM_STATE = 64
MIX_WIDTH = ATTN_WIDTH + SSM_WIDTH
IN_WIDTH = ATTN_WIDTH + 2 * KV_WIDTH + SSM_WIDTH
FF_HIDDEN = -(-8 * D_MODEL // (3 * 256)) * 256
REL_BUCKETS = 32
REL_MAX_DISTANCE = 128
EPS = 1e-6

kernel_name = "hymba_s5_swa_sink_hybrid"


def _rmsnorm(x, g):
    xf = x.astype(jnp.float32)
    y = xf * lax.rsqrt(jnp.mean(xf * xf, axis=-1, keepdims=True) + EPS)
    return (y * g.astype(jnp.float32)).astype(x.dtype)


def _t5_bucket(dist):
    n = np.maximum(dist, 0)
    max_exact = REL_BUCKETS // 2
    nf = np.maximum(n, 1).astype(np.float32)
    large = max_exact + (np.log(nf / max_exact) / math.log(REL_MAX_DISTANCE / max_exact)
                         * (REL_BUCKETS - max_exact)).astype(np.int32)
    large = np.minimum(large, REL_BUCKETS - 1)
    return np.where(n < max_exact, n, large).astype(np.int32)


def _sliding_window_attention(q, k, v, sinks, rel_bias):
    bsz, L = q.shape[0], q.shape[1]
    nb = L // BLOCK
    qb = q.reshape(bsz, nb, BLOCK, N_KV_HEADS, Q_PER_KV, HEAD_DIM)
    pad = ((0, 0), (BLOCK, 0), (0, 0), (0, 0))
    kp = jnp.pad(k, pad).reshape(bsz, nb + 1, BLOCK, N_KV_HEADS, HEAD_DIM)
    vp = jnp.pad(v, pad).reshape(bsz, nb + 1, BLOCK, N_KV_HEADS, HEAD_DIM)
    kb = jnp.concatenate([kp[:, :-1], kp[:, 1:]], axis=2)
    vb = jnp.concatenate([vp[:, :-1], vp[:, 1:]], axis=2)
    logits = jnp.einsum('bnqkgd,bnskd->bnkgqs', qb, kb).astype(jnp.float32) * (HEAD_DIM ** -0.5)

    qi = np.arange(BLOCK)[:, None]
    sj = np.arange(2 * BLOCK)[None, :]
    dist = qi + BLOCK - sj
    bucket = _t5_bucket(dist)
    bias = jnp.transpose(rel_bias[bucket].astype(jnp.float32), (2, 0, 1))
    bias = bias.reshape(N_KV_HEADS, Q_PER_KV, BLOCK, 2 * BLOCK)
    in_window = (dist >= 0) & (dist < WINDOW)
    key_pos = np.arange(nb)[:, None] * BLOCK - BLOCK + np.arange(2 * BLOCK)[None, :]
    valid = in_window[None] & (key_pos >= 0)[:, None, :]
    logits = jnp.where(valid[None, :, None, None], logits + bias, -jnp.inf)

    sink = sinks.astype(jnp.float32).reshape(N_KV_HEADS, Q_PER_KV, 1, 1)
    m = jnp.maximum(jnp.max(logits, axis=-1, keepdims=True), sink)
    p = jnp.exp(logits - m)
    w = p / (jnp.sum(p, axis=-1, keepdims=True) + jnp.exp(sink - m))
    out = jnp.einsum('bnkgqs,bnskd->bnqkgd', w.astype(v.dtype), vb)
    return out.reshape(bsz, L, ATTN_WIDTH)


def _scan_combine(a, b):
    a_re, a_im, x_re, x_im = a
    b_re, b_im, y_re, y_im = b
    n_re = b_re * a_re - b_im * a_im
    n_im = b_re * a_im + b_im * a_re
    o_re = b_re * x_re - b_im * x_im + y_re
    o_im = b_re * x_im + b_im * x_re + y_im
    return (n_re, n_im, o_re, o_im)


def _s5_mixer(u, a_re, a_im, log_dt, b_re, b_im, c_re, c_im, d, w_glu):
    bsz, L = u.shape[0], u.shape[1]
    f32 = jnp.float32
    uf = u.reshape(bsz, L, SSM_GROUPS, SSM_GROUP).astype(f32)
    a_re = a_re.astype(f32)
    a_im = a_im.astype(f32)
    dt = jnp.exp(log_dt.astype(f32))[:, None]
    mag = jnp.exp(a_re * dt)
    ang = a_im * dt
    lb_re, lb_im = mag * jnp.cos(ang), mag * jnp.sin(ang)
    nr, ni = lb_re - 1.0, lb_im
    den = a_re * a_re + a_im * a_im
    f_re = (nr * a_re + ni * a_im) / den
    f_im = (ni * a_re - nr * a_im) / den
    b_re = b_re.astype(f32)
    b_im = b_im.astype(f32)
    bb_re = f_re[..., None] * b_re - f_im[..., None] * b_im
    bb_im = f_re[..., None] * b_im + f_im[..., None] * b_re
    bu_re = jnp.einsum('blgp,gnp->blgn', uf, bb_re)
    bu_im = jnp.einsum('blgp,gnp->blgn', uf, bb_im)
    shape_a = (1, L, SSM_GROUPS, SSM_STATE)
    elems = (jnp.broadcast_to(lb_re, shape_a), jnp.broadcast_to(lb_im, shape_a), bu_re, bu_im)
    _, _, h_re, h_im = lax.associative_scan(_scan_combine, elems, axis=1)
    y = (jnp.einsum('blgn,gpn->blgp', h_re, c_re.astype(f32))
         - jnp.einsum('blgn,gpn->blgp', h_im, c_im.astype(f32))
         + d.astype(f32) * uf)
    y = jax.nn.gelu(y.reshape(bsz, L, SSM_WIDTH)).astype(u.dtype)
    return y * jax.nn.sigmoid(y @ w_glu)


def _layer(x, rel_bias, ln1_g, w_in, q_norm_g, k_norm_g, attn_sinks, ssm_a_re, ssm_a_im,
           ssm_log_dt, ssm_b_re, ssm_b_im, ssm_c_re, ssm_c_im, ssm_d, w_glu,
           attn_out_g, ssm_out_g, w_out, ln2_g, w_ff_gate, w_ff_up, w_ff_down):
    bsz, L = x.shape[0], x.shape[1]
    h = _rmsnorm(x, ln1_g)
    proj = h @ w_in
    q, k, v, u = jnp.split(proj, [ATTN_WIDTH, ATTN_WIDTH + KV_WIDTH, ATTN_WIDTH + 2 * KV_WIDTH], axis=-1)
    q = _rmsnorm(q.reshape(bsz, L, N_Q_HEADS, HEAD_DIM), q_norm_g)
    k = _rmsnorm(k.reshape(bsz, L, N_KV_HEADS, HEAD_DIM), k_norm_g)
    v = v.reshape(bsz, L, N_KV_HEADS, HEAD_DIM)
    y_attn = _sliding_window_attention(q, k, v, attn_sinks, rel_bias)
    y_ssm = _s5_mixer(u, ssm_a_re, ssm_a_im, ssm_log_dt, ssm_b_re, ssm_b_im,
                      ssm_c_re, ssm_c_im, ssm_d, w_glu)
    mixed = jnp.concatenate([_rmsnorm(y_attn, attn_out_g), _rmsnorm(y_ssm, ssm_out_g)], axis=-1)
    x = x + mixed @ w_out
    h2 = _rmsnorm(x, ln2_g)
    ff = (jax.nn.silu(h2 @ w_ff_gate) * (h2 @ w_ff_up)) @ w_ff_down
    return x + ff


def setup_inputs(seed: int = 0) -> dict:
    key = jax.random.key(seed)
    ks = jax.random.split(key, 24)
    f32 = jnp.float32
    nrm = lambda k, s, sc: jax.random.normal(k, s, f32) * sc
    Dp = DEPTH
    n_idx = jnp.arange(SSM_STATE, dtype=f32)
    return {
        "x": nrm(ks[0], (BATCH, SEQ, D_MODEL), 1.0),
        "rel_bias": nrm(ks[1], (REL_BUCKETS, N_Q_HEADS), 0.5),
        "ln1_g": 1.0 + nrm(ks[2], (Dp, D_MODEL), 0.02),
        "w_in": nrm(ks[3], (Dp, D_MODEL, IN_WIDTH), D_MODEL ** -0.5),
        "q_norm_g": 1.0 + nrm(ks[4], (Dp, HEAD_DIM), 0.02),
        "k_norm_g": 1.0 + nrm(ks[5], (Dp, HEAD_DIM), 0.02),
        "attn_sinks": nrm(ks[6], (Dp, N_Q_HEADS), 0.5),
        "ssm_a_re": -0.5 + nrm(ks[7], (Dp, SSM_GROUPS, SSM_STATE), 0.01),
        "ssm_a_im": math.pi * n_idx + nrm(ks[8], (Dp, SSM_GROUPS, SSM_STATE), 0.01),
        "ssm_log_dt": jax.random.uniform(ks[9], (Dp, SSM_GROUPS), f32, math.log(1e-3), math.log(1e-1)),
        "ssm_b_re": nrm(ks[10], (Dp, SSM_GROUPS, SSM_STATE, SSM_GROUP), (2 * SSM_GROUP) ** -0.5),
        "ssm_b_im": nrm(ks[11], (Dp, SSM_GROUPS, SSM_STATE, SSM_GROUP), (2 * SSM_GROUP) ** -0.5),
        "ssm_c_re": nrm(ks[12], (Dp, SSM_GROUPS, SSM_GROUP, SSM_STATE), (2 * SSM_STATE) ** -0.5),
        "ssm_c_im": nrm(ks[13], (Dp, SSM_GROUPS, SSM_GROUP, SSM_STATE), (2 * SSM_STATE) ** -0.5),
        "ssm_d": nrm(ks[14], (Dp, SSM_GROUPS, SSM_GROUP), 1.0),
        "w_glu": nrm(ks[15], (Dp, SSM_WIDTH, SSM_WIDTH), SSM_WIDTH ** -0.5),
        "attn_out_g": 1.0 + nrm(ks[16], (Dp, ATTN_WIDTH), 0.02),
        "ssm_out_g": 1.0 + nrm(ks[17], (Dp, SSM_WIDTH), 0.02),
        "w_out": nrm(ks[18], (Dp, MIX_WIDTH, D_MODEL), MIX_WIDTH ** -0.5),
        "ln2_g": 1.0 + nrm(ks[19], (Dp, D_MODEL), 0.02),
        "w_ff_gate": nrm(ks[20], (Dp, D_MODEL, FF_HIDDEN), D_MODEL ** -0.5),
        "w_ff_up": nrm(ks[21], (Dp, D_MODEL, FF_HIDDEN), D_MODEL ** -0.5),
        "w_ff_down": nrm(ks[22], (Dp, FF_HIDDEN, D_MODEL), FF_HIDDEN ** -0.5),
    }


def reference(x, rel_bias, ln1_g, w_in, q_norm_g, k_norm_g, attn_sinks, ssm_a_re, ssm_a_im,
              ssm_log_dt, ssm_b_re, ssm_b_im, ssm_c_re, ssm_c_im, ssm_d, w_glu,
              attn_out_g, ssm_out_g, w_out, ln2_g, w_ff_gate, w_ff_up, w_ff_down):
    for l in range(DEPTH):
        x = _layer(x, rel_bias, ln1_g[l], w_in[l], q_norm_g[l], k_norm_g[l], attn_sinks[l],
                   ssm_a_re[l], ssm_a_im[l], ssm_log_dt[l], ssm_b_re[l], ssm_b_im[l],
                   ssm_c_re[l], ssm_c_im[l], ssm_d[l], w_glu[l], attn_out_g[l], ssm_out_g[l],
                   w_out[l], ln2_g[l], w_ff_gate[l], w_ff_up[l], w_ff_down[l])
    return x
```

```python
import os
import math
import numpy as np
import ml_dtypes
import concourse.bass as bass
import concourse.mybir as mybir
from concourse.bass_utils import run_bass_kernel_spmd

F32 = mybir.dt.float32
BF16 = mybir.dt.bfloat16
I32 = mybir.dt.int32
ALU = mybir.AluOpType
AF = mybir.ActivationFunctionType

D = 2048
NT = 512
FF = 5632
EPS = 1e-6
NEG = -30000.0
SHIFT = 8.0
MS = [-(s + 1) for s in range(8)] + [t + 1 for t in range(8)] + [7 - s for s in range(8)] + [8, 16, 32, 64, 128, 256]
NM = len(MS)
GROUPS_FF = [12, 10, 12, 10]


class Reg:
    __slots__ = ("w", "r")

    def __init__(self):
        self.w = None
        self.r = {}


class Sched:
    def __init__(self, nc, semh):
        self.nc = nc
        self.semh = semh
        self.names = ["pe", "act", "dve", "pool", "sp"]
        self.ops = {k: [] for k in self.names}
        self.cnt = {k: 0 for k in self.names}
        self.waited = {k: {} for k in self.names}
        self.dcnt = {}

    def _deps(self, e, reads, writes):
        deps = {}

        def add(tok):
            if tok is None:
                return
            s, v = tok
            if e == "pe" and s == "pe":
                return
            if deps.get(s, 0) < v:
                deps[s] = v
        for r in reads:
            add(r.w)
        for w in writes:
            add(w.w)
            for s, v in w.r.items():
                add((s, v))
        out = []
        for s, v in deps.items():
            if self.waited[e].get(s, 0) < v:
                self.waited[e][s] = v
                out.append((s, v))
        return out

    def _upd(self, tok, reads, writes):
        s, v = tok
        for r in reads:
            if r.r.get(s, 0) < v:
                r.r[s] = v
        for w in writes:
            w.w = tok
            w.r = {}

    def op(self, e, fn, reads=(), writes=(), sig=True):
        waits = self._deps(e, reads, writes)
        if sig:
            self.cnt[e] += 1
            v = self.cnt[e]
        else:
            v = self.cnt[e] + 1
        self.ops[e].append((waits, fn, (e, 1) if sig else None))
        self._upd((e, v), reads, writes)

    def dma(self, q, fn, sem, reads=(), writes=()):
        waits = self._deps(q, reads, writes)
        self.dcnt[sem] = self.dcnt.get(sem, 0) + 16
        tok = (sem, self.dcnt[sem])
        self.ops[q].append((waits, fn, (sem, 16)))
        self._upd(tok, reads, writes)
        return tok

    def wait_tok(self, e, tok):
        s, v = tok
        if self.waited[e].get(s, 0) < v:
            self.waited[e][s] = v
            self.ops[e].append(([(s, v)], None, None))

    def replay(self, block):
        decs = {"pe": block.tensor, "act": block.scalar, "dve": block.vector, "pool": block.gpsimd, "sp": block.sync}
        for name in self.names:
            lst = self.ops[name]
            if not lst:
                continue

            def body(e, lst=lst):
                for waits, fn, inc in lst:
                    for s, v in waits:
                        e.wait_ge(self.semh[s], v)
                    if fn is None:
                        continue
                    ins = fn(e)
                    if inc is not None:
                        ins.then_inc(self.semh[inc[0]], inc[1])
            decs[name](body)
            self.ops[name] = []


def dap(t, offset, dims):
    return bass.AP(tensor=t.tensor, offset=offset, ap=[[s, c] for s, c in dims])


def build(n_tt=4, do_pred=True, stage=99, do_setup=True):
    nc = bass.Bass("TRN2", target_bir_lowering=False)
    TOK = n_tt * NT

    def din(name, shape, dt=F32):
        return nc.dram_tensor(name, list(shape), dt, kind="ExternalInput").ap()

    x_main = din("x_main", [TOK, D])
    x_pred = din("x_pred", [TOK, D])
    hm8 = din("hm8", [128, 1])
    rel_bias = din("rel_bias", [32, 16])
    ln1_g = din("ln1_g", [D])
    w_in = din("w_in", [D, 2560])
    q_norm_g = din("q_norm_g", [64])
    k_norm_g = din("k_norm_g", [64])
    sinks = din("attn_sinks", [16])
    a_re = din("ssm_a_re", [64, 64])
    a_im = din("ssm_a_im", [64, 64])
    log_dt = din("ssm_log_dt", [64])
    b_re = din("ssm_b_re", [64, 64, 16])
    b_im = din("ssm_b_im", [64, 64, 16])
    c_re = din("ssm_c_re", [64, 16, 64])
    c_im = din("ssm_c_im", [64, 16, 64])
    ssm_d = din("ssm_d", [64 * 16])
    w_glu = din("w_glu", [1024, 1024])
    attn_out_g = din("attn_out_g", [1024])
    ssm_out_g = din("ssm_out_g", [1024])
    w_out = din("w_out", [D, D])
    ln2_g = din("ln2_g", [D])
    w_gate = din("w_ff_gate", [D, FF])
    w_up = din("w_ff_up", [D, FF])
    w_down = din("w_ff_down", [FF, D])
    c_ident = din("c_ident", [128, 128])
    c_mask = din("c_mask", [128, 128])
    c_mv = din("c_mv", [128, NM * 32])
    c_oh = din("c_oh", [33, 512])
    c_bones = din("c_bones", [128, 128])
    c_anti = din("c_anti", [128, 128])
    c_dup = din("c_dup", [128, 256])
    out = nc.dram_tensor("out", [TOK, D], F32, kind="ExternalOutput").ap()
    wt_d = nc.dram_tensor("wt_d", [128, 64 * 128], BF16, kind="Internal").ap()
    kt_d = nc.dram_tensor("kt_d", [128, 64 * 128], BF16, kind="Internal").ap()
    et_d = nc.dram_tensor("et_d", [128, 64 * 256], BF16, kind="Internal").ap()
    ext_d = nc.dram_tensor("ext_d", [16, 512], F32, kind="Internal").ap()

    sem_names = ["pe", "act", "dve", "pool", "sp", "xl0", "xl1", "xl2", "xl3", "wb0", "wb1", "wb2", "wd", "tb0", "tb1",
                 "st", "misc", "ot0", "ot1", "ot2", "ot3"] + ["wd%d" % i for i in range(8)]
    import contextlib
    with contextlib.ExitStack() as es:
        semh = {n: es.enter_context(nc.semaphore(n)) for n in sem_names}
        S = Sched(nc, semh)

        def sb(name, shape, dt):
            return es.enter_context(nc.sbuf_tensor(name, list(shape), dt))

        KSC = sb("KSC", [128, 6, 3, 32], F32)
        BIAS = sb("BIAS", [128, 16, 2, 128], BF16)
        G1 = sb("G1", [128, 16], F32)
        G2 = sb("G2", [128, 16], F32)
        GA = sb("GA", [128, 8], F32)
        GS = sb("GS", [128, 8], F32)
        QG = sb("QG", [128, 1], F32)
        KG = sb("KG", [128, 1], F32)
        ESK = sb("ESK", [128, 16], F32)
        HM8 = sb("HM8", [128, 1], F32)
        DUPB = sb("DUPB", [128, 2, 128], BF16)
        NEG8 = sb("NEG8", [128, 1], F32)
        IDB = sb("IDB", [128, 128], BF16)
        BONES = sb("BONES", [128, 128], BF16)
        ONESB = sb("ONESB", [128, 128], BF16)
        r_const = Reg()
        PS = es.enter_context(nc.psum_tensor("PS", [128, 8, 512], F32))
        r_ps = [Reg() for _ in range(8)]
        bank_i = [0]

        reserved = set()

        def bank(reserve=False):
            while True:
                i = bank_i[0] % 8
                bank_i[0] += 1
                if i not in reserved:
                    break
            if reserve:
                reserved.add(i)
            return r_ps[i], PS[:, i, :]

        with contextlib.ExitStack() as es2:
            def sb2(name, shape, dt=F32):
                return es2.enter_context(nc.sbuf_tensor(name, list(shape), dt))
            ARE = sb2("ARE", [128, 32]); AIM = sb2("AIM", [128, 32]); LDT = sb2("LDT", [128, 32])
            DT = sb2("DT", [128, 32]); ARD = sb2("ARD", [128, 32]); AID = sb2("AID", [128, 32])
            MV = sb2("MV", [128, NM, 32])
            MAG = sb2("MAG", [128, NM, 32])
            ANG = sb2("ANG", [128, 2, NM, 32])
            KI = sb2("KI", [128, 2, NM, 32], I32)
            KF = sb2("KF", [128, 2, NM, 32])
            CM = KF
            SC = sb2("SC", [128, 2, NM, 32])
            PWR = sb2("PWR", [128, NM, 32]); PWI = sb2("PWI", [128, NM, 32])
            SM = [sb2("SM%d" % i, [128, 32]) for i in range(10)]
            BRE = sb2("BRE", [128, 32, 16]); BIM = sb2("BIM", [128, 32, 16])
            BBR = sb2("BBR", [128, 32, 16]); BBI = sb2("BBI", [128, 32, 16])
            CR = sb2("CR", [128, 32, 16]); CI = sb2("CI", [128, 32, 16])
            T1 = sb2("T1", [128, 32, 8, 16]); T2 = T1
            VR = sb2("VR", [128, 32, 8, 16]); VI = sb2("VI", [128, 32, 8, 16])
            WR = VR; WI = VI
            ER = sb2("ER", [128, 32, 8, 16]); EI = sb2("EI", [128, 32, 8, 16])
            IDF = sb2("IDF", [128, 128]); MASK = sb2("MASK", [128, 128]); BONF = sb2("BONF", [128, 128])
            DROW = sb2("DROW", [128, 64, 16])
            KTB = sb2("KTB", [128, 64, 128], BF16)
            WTB = KTB
            ETB = sb2("ETB", [128, 64, 128], BF16)
            TMPK = sb2("TMPK", [128, 4, 8, 16])
            RBA = sb2("RBA", [33, 16]); OH = sb2("OH", [33, 512]); EXT = sb2("EXT", [16, 512])
            SKV = sb2("SKV", [128, 16])
            BREV = sb2("BREV", [128, 16, 2, 128], BF16)
            ANTF = sb2("ANTF", [128, 128]); ANTB = sb2("ANTB", [128, 128], BF16)
            DUPF = sb2("DUPF", [128, 2, 128])
            r = {n: Reg() for n in ["in", "dt", "mag", "ang", "ki", "kf", "cm", "sc", "pw", "sm", "bb", "t1", "t2", "v", "w", "e",
                                    "ktb", "wtb", "etb", "tmpk", "ext", "biasf", "extd", "wtd", "ktd", "etd", "pers"]}

            ld = []

            def L(out_ap, in_ap):
                ld.append(S.dma("sp", lambda e, o=out_ap, i=in_ap: e.dma_start(out=o, in_=i, allow_slow_non_contiguous=True),
                                "misc", writes=[r["in"]]))
            for gi in range(2):
                hs = slice(gi * 64, (gi + 1) * 64)
                L(ARE[hs, :], dap(a_re, gi * 64, [(1, 64), (128, 32)]))
                L(AIM[hs, :], dap(a_im, gi * 64, [(1, 64), (128, 32)]))
                L(LDT[hs, :], dap(log_dt, gi, [(0, 64), (2, 32)]))
                L(BRE[hs, :, :], dap(b_re, gi * 1024, [(16, 64), (2048, 32), (1, 16)]))
                L(BIM[hs, :, :], dap(b_im, gi * 1024, [(16, 64), (2048, 32), (1, 16)]))
                for pr in range(32):
                    L(CR[hs, pr, :], dap(c_re, gi * 1024 + pr * 2048, [(1, 64), (64, 16)]))
                    L(CI[hs, pr, :], dap(c_im, gi * 1024 + pr * 2048, [(1, 64), (64, 16)]))
            L(MV[:], c_mv.rearrange("p (m r) -> p m r", m=NM))
            L(IDF[:], c_ident)
            L(MASK[:], c_mask)
            L(BONF[:], c_bones)
            L(ANTF[:], c_anti)
            L(DUPF[:], c_dup.rearrange("p (c m) -> p c m", c=2))
            L(DROW[:], dap(ssm_d, 0, [(0, 128), (16, 64), (1, 16)]))
            L(RBA[0:32, :], rel_bias)
            L(OH[:], c_oh)
            L(G1[:], dap(ln1_g, 0, [(1, 128), (128, 16)]))
            L(G2[:], dap(ln2_g, 0, [(1, 128), (128, 16)]))
            L(GA[:], dap(attn_out_g, 0, [(1, 128), (128, 8)]))
            L(GS[:], dap(ssm_out_g, 0, [(1, 128), (128, 8)]))
            for h2 in range(2):
                L(QG[h2 * 64:(h2 + 1) * 64, :], dap(q_norm_g, 0, [(1, 64), (1, 1)]))
                L(KG[h2 * 64:(h2 + 1) * 64, :], dap(k_norm_g, 0, [(1, 64), (1, 1)]))
            L(SKV[:], dap(sinks, 0, [(0, 128), (1, 16)]))
            L(HM8[:], hm8)

            V_ = "dve"
            rin = [r["in"]]
            S.op(V_, lambda e: e.memset(NEG8[:], -SHIFT), writes=[r_const])
            S.op(V_, lambda e: e.memset(ONESB[:], 1.0), writes=[r_const])
            S.op(V_, lambda e: e.tensor_copy(out=IDB[:], in_=IDF[:]), reads=rin, writes=[r_const])
            S.op(V_, lambda e: e.tensor_copy(out=BONES[:], in_=BONF[:]), reads=rin, writes=[r_const])
            S.op(V_, lambda e: e.tensor_copy(out=DUPB[:], in_=DUPF[:]), reads=rin, writes=[r_const])
            S.op(V_, lambda e: e.memset(RBA[32:33, :], 1.0), reads=rin, writes=[r["in"]])
            S.op("act", lambda e: e.activation(out=ESK[:], in_=SKV[:], func=AF.Exp, bias=NEG8[:, 0:1], scale=1.0),
                 reads=rin + [r_const], writes=[r["pers"]])
            S.op("act", lambda e: e.activation(out=DT[:], in_=LDT[:], func=AF.Exp), reads=rin, writes=[r["dt"]])
            S.op(V_, lambda e: e.tensor_tensor(out=ARD[:], in0=ARE[:], in1=DT[:], op=ALU.mult), reads=rin + [r["dt"]], writes=[r["sm"]])
            S.op(V_, lambda e: e.tensor_tensor(out=AID[:], in0=AIM[:], in1=DT[:], op=ALU.mult), reads=rin + [r["dt"]], writes=[r["sm"]])

            def bc_m(t):
                return t[:].unsqueeze(1).broadcast_to([128, NM, 32])
            S.op(V_, lambda e: e.tensor_tensor(out=MAG[:], in0=MV[:], in1=bc_m(ARD), op=ALU.mult), reads=rin + [r["sm"]], writes=[r["mag"]])
            S.op("act", lambda e: e.activation(out=MAG[:], in_=MAG[:], func=AF.Exp), reads=[r["mag"]], writes=[r["mag"]])
            S.op(V_, lambda e: e.tensor_tensor(out=ANG[:, 0], in0=MV[:], in1=bc_m(AID), op=ALU.mult), reads=rin + [r["sm"]], writes=[r["ang"]])
            S.op(V_, lambda e: e.tensor_scalar(out=ANG[:, 0], in0=ANG[:, 0], scalar1=1.0 / (2 * math.pi), scalar2=None, op0=ALU.mult),
                 reads=[r["ang"]], writes=[r["ang"]])
            S.op(V_, lambda e: e.tensor_scalar(out=ANG[:, 1], in0=ANG[:, 0], scalar1=0.25, scalar2=None, op0=ALU.add),
                 reads=[r["ang"]], writes=[r["ang"]])
            S.op(V_, lambda e: e.tensor_copy(out=KI[:], in_=ANG[:]), reads=[r["ang"]], writes=[r["ki"]])
            S.op(V_, lambda e: e.tensor_copy(out=KF[:], in_=KI[:]), reads=[r["ki"]], writes=[r["kf"]])
            S.op(V_, lambda e: e.tensor_tensor(out=ANG[:], in0=ANG[:], in1=KF[:], op=ALU.subtract), reads=[r["ang"], r["kf"]], writes=[r["ang"]])
            S.op(V_, lambda e: e.tensor_scalar(out=CM[:], in0=ANG[:], scalar1=0.5, scalar2=None, op0=ALU.is_gt), reads=[r["ang"]], writes=[r["kf"]])
            S.op(V_, lambda e: e.tensor_tensor(out=ANG[:], in0=ANG[:], in1=CM[:], op=ALU.subtract), reads=[r["ang"], r["kf"]], writes=[r["ang"]])
            S.op(V_, lambda e: e.tensor_scalar(out=CM[:], in0=ANG[:], scalar1=-0.5, scalar2=None, op0=ALU.is_lt), reads=[r["ang"]], writes=[r["kf"]])
            S.op(V_, lambda e: e.tensor_tensor(out=ANG[:], in0=ANG[:], in1=CM[:], op=ALU.add), reads=[r["ang"], r["kf"]], writes=[r["ang"]])
            S.op("act", lambda e: e.activation(out=SC[:], in_=ANG[:], func=AF.Sin, scale=6.283185), reads=[r["ang"]], writes=[r["sc"]])
            S.op(V_, lambda e: e.tensor_tensor(out=PWR[:], in0=MAG[:], in1=SC[:, 1], op=ALU.mult), reads=[r["mag"], r["sc"]], writes=[r["pw"]])
            S.op(V_, lambda e: e.tensor_tensor(out=PWI[:], in0=MAG[:], in1=SC[:, 0], op=ALU.mult), reads=[r["mag"], r["sc"]], writes=[r["pw"]])
            for k in range(6):
                S.op(V_, lambda e, k=k: e.tensor_copy(out=KSC[:, k, 0, :], in_=PWR[:, 24 + k, :]), reads=[r["pw"]], writes=[r["pers"]])
                S.op(V_, lambda e, k=k: e.tensor_copy(out=KSC[:, k, 1, :], in_=PWI[:, 24 + k, :]), reads=[r["pw"]], writes=[r["pers"]])
                S.op(V_, lambda e, k=k: e.tensor_scalar(out=KSC[:, k, 2, :], in0=PWI[:, 24 + k, :], scalar1=-1.0, scalar2=None, op0=ALU.mult),
                     reads=[r["pw"]], writes=[r["pers"]])
            nr, ni, den, rden, t0, t1_, fr, fi = SM[0], SM[1], SM[2], SM[3], SM[4], SM[5], SM[6], SM[7]
            rs = [r["sm"]]
            rp = [r["pw"]]

            def tt(o, a, b, op, reads, writes):
                S.op(V_, lambda e: e.tensor_tensor(out=o, in0=a, in1=b, op=op), reads=reads, writes=writes)
            S.op(V_, lambda e: e.tensor_scalar(out=nr[:], in0=PWR[:, 8, :], scalar1=-1.0, scalar2=None, op0=ALU.add), reads=rp, writes=rs)
            tt(den[:], ARE[:], ARE[:], ALU.mult, rin, rs)
            tt(t0[:], AIM[:], AIM[:], ALU.mult, rin, rs)
            tt(den[:], den[:], t0[:], ALU.add, rs, rs)
            S.op(V_, lambda e: e.reciprocal(out=rden[:], in_=den[:]), reads=rs, writes=rs)
            tt(t0[:], nr[:], ARE[:], ALU.mult, rs + rin, rs)
            tt(t1_[:], PWI[:, 8, :], AIM[:], ALU.mult, rp + rin, rs)
            tt(t0[:], t0[:], t1_[:], ALU.add, rs, rs)
            tt(fr[:], t0[:], rden[:], ALU.mult, rs, rs)
            tt(t0[:], PWI[:, 8, :], ARE[:], ALU.mult, rp + rin, rs)
            tt(t1_[:], nr[:], AIM[:], ALU.mult, rs + rin, rs)
            tt(t0[:], t0[:], t1_[:], ALU.subtract, rs, rs)
            tt(fi[:], t0[:], rden[:], ALU.mult, rs, rs)

            def bq(t):
                return t[:].unsqueeze(2).broadcast_to([128, 32, 16])
            rb = [r["bb"]]
            tt(BBR[:], BRE[:], bq(fr), ALU.mult, rin + rs, rb)
            tt(T1[:, :, 0, :], BIM[:], bq(fi), ALU.mult, rin + rs, [r["t1"]])
            tt(BBR[:], BBR[:], T1[:, :, 0, :], ALU.subtract, rb + [r["t1"]], rb)
            tt(BBI[:], BIM[:], bq(fr), ALU.mult, rin + rs, rb)
            tt(T1[:, :, 0, :], BRE[:], bq(fi), ALU.mult, rin + rs, [r["t1"]])
            tt(BBI[:], BBI[:], T1[:, :, 0, :], ALU.add, rb + [r["t1"]], rb)

            def pw8(t, i0):
                return t[:, i0:i0 + 8, :].rearrange("p s r -> p r s").unsqueeze(3).broadcast_to([128, 32, 8, 16])

            def x8(t):
                return t[:].unsqueeze(2).broadcast_to([128, 32, 8, 16])

            def cmul(OR, OI, i0, XR, XI, rx, ro, neg_im=False):
                tt(OR[:], pw8(PWR, i0), x8(XR), ALU.mult, rp + rx, ro)
                tt(T1[:], pw8(PWI, i0), x8(XI), ALU.mult, rp + rx, [r["t1"]])
                tt(OR[:], OR[:], T1[:], ALU.subtract, ro + [r["t1"]], ro)
                tt(OI[:], pw8(PWR, i0), x8(XI), ALU.mult, rp + rx, ro)
                tt(T2[:], pw8(PWI, i0), x8(XR), ALU.mult, rp + rx, [r["t1"]])
                tt(OI[:], OI[:], T2[:], ALU.add, ro + [r["t1"]], ro)
                if neg_im:
                    S.op(V_, lambda e: e.tensor_scalar(out=OI[:], in0=OI[:], scalar1=-1.0, scalar2=None, op0=ALU.mult), reads=ro, writes=ro)
            cmul(VR, VI, 0, BBR, BBI, rb, [r["v"]])
            cmul(ER, EI, 8, CR, CI, rin, [r["e"]], neg_im=True)

            for pq in range(8):
                banks = [bank(), bank()]
                for gi in range(2):
                    hs = slice(gi * 64, (gi + 1) * 64)
                    rb_, pb_ = banks[gi]
                    for p4 in range(4):
                        pr = pq * 4 + p4
                        o = pb_[:, p4 * 128:(p4 + 1) * 128]
                        S.op("pe", lambda e, o=o, hs=hs, pr=pr: e.matmul(o, VR[hs, pr].rearrange("p s q -> p (s q)"),
                                                                         ER[hs, pr].rearrange("p s q -> p (s q)"), start=True, stop=False),
                             reads=[r["v"], r["e"]], writes=[rb_], sig=False)
                        S.op("pe", lambda e, o=o, hs=hs, pr=pr: e.matmul(o, VI[hs, pr].rearrange("p s q -> p (s q)"),
                                                                         EI[hs, pr].rearrange("p s q -> p (s q)"), start=False, stop=True),
                             reads=[r["v"], r["e"]], writes=[rb_], sig=(p4 == 3))
                    gsel = slice(2 * pq * 4 + gi, 2 * (pq * 4 + 4), 2)
                    S.op(V_, lambda e, gsel=gsel: e.tensor_tensor(
                        out=TMPK[:], in0=IDF[:].rearrange("p (t q) -> p t q", t=8).unsqueeze(1).broadcast_to([128, 4, 8, 16]),
                        in1=DROW[:, gsel, :].unsqueeze(2).broadcast_to([128, 4, 8, 16]), op=ALU.mult),
                        reads=rin, writes=[r["tmpk"]])
                    S.op(V_, lambda e, pb_=pb_: e.tensor_tensor(
                        out=pb_.rearrange("p (g c) -> p g c", g=4), in0=pb_.rearrange("p (g c) -> p g c", g=4),
                        in1=MASK[:].unsqueeze(1).broadcast_to([128, 4, 128]), op=ALU.mult), reads=[rb_] + rin, writes=[rb_])
                    S.op(V_, lambda e, pb_=pb_, gsel=gsel: e.tensor_tensor(
                        out=KTB[:, gsel, :], in0=pb_.rearrange("p (g c) -> p g c", g=4),
                        in1=TMPK[:].rearrange("p g t q -> p g (t q)"), op=ALU.add), reads=[rb_, r["tmpk"]], writes=[r["ktb"]])
            tk1 = S.dma("sp", lambda e: e.dma_start(out=kt_d, in_=KTB[:].rearrange("p g c -> p (g c)")), "st", reads=[r["ktb"]], writes=[r["ktd"]])
            cmul(WR, WI, 16, BBR, BBI, rb, [r["v"]])
            for c, WX in enumerate((WR, WI)):
                for pq in range(8):
                    rb_, pb_ = bank()
                    for p4 in range(4):
                        pr = pq * 4 + p4
                        S.op("pe", lambda e, pb_=pb_, p4=p4, pr=pr, WX=WX: e.transpose(
                            pb_[:, p4 * 128:(p4 + 1) * 128], WX[:, pr].rearrange("p s q -> p (s q)"), IDF[:]),
                            reads=[r["v"]] + rin, writes=[rb_], sig=(p4 == 3))
                    S.op("act", lambda e, pb_=pb_, pq=pq, c=c: e.activation(
                        out=WTB[:, pq * 8:(pq + 1) * 8, c * 64:(c + 1) * 64],
                        in_=pb_.rearrange("p (g n) -> p g n", g=8), func=AF.Copy), reads=[rb_], writes=[r["ktb"]])
            tk2 = S.dma("sp", lambda e: e.dma_start(out=wt_d, in_=WTB[:].rearrange("p g c -> p (g c)")), "st", reads=[r["ktb"]], writes=[r["wtd"]])
            tk3s = []
            for c, EX in enumerate((ER, EI)):
                S.op("pool", lambda e: e.memset(ETB[:], 0.0), writes=[r["etb"]])
                for gi in range(2):
                    hs = slice(gi * 64, (gi + 1) * 64)
                    S.op("act", lambda e, hs=hs, gi=gi, EX=EX: e.activation(
                        out=ETB[hs, gi::2, :], in_=EX[hs].rearrange("p r t q -> p r (t q)"), func=AF.Copy),
                        reads=[r["e"]], writes=[r["etb"]])
                tk3s.append(S.dma("sp", lambda e, c=c: e.dma_start(out=et_d.rearrange("p (g c x) -> p g c x", g=64, c=2)[:, :, c, :], in_=ETB[:]),
                                  "st", reads=[r["etb"]], writes=[r["etd"]]))
            rb_, pb_ = bank()
            S.op("pe", lambda e: e.matmul(pb_[0:16, :], RBA[:], OH[:], start=True, stop=True), reads=rin, writes=[rb_])
            S.op("act", lambda e: e.activation(out=EXT[:], in_=pb_[0:16, :], func=AF.Copy), reads=[rb_], writes=[r["ext"]])
            tk4 = S.dma("sp", lambda e: e.dma_start(out=ext_d, in_=EXT[:]), "st", reads=[r["ext"]], writes=[r["extd"]])
            for h in range(16):
                S.dma("pool", lambda e, h=h: e.dma_start(out=BREV[:, h, :, :], in_=dap(ext_d, h * 512, [(1, 128), (256, 2), (1, 128)])),
                      "wd", reads=[r["extd"]], writes=[r["biasf"]])
            S.op(V_, lambda e: e.tensor_copy(out=ANTB[:], in_=ANTF[:]), reads=rin, writes=[r["tmpk"]])
            for h4 in range(8):
                rb_, pb_ = bank()
                S.op("pe", lambda e, pb_=pb_, h4=h4: e.matmul(pb_, ANTB[:], BREV[:, h4 * 2:(h4 + 1) * 2].rearrange("p h a i -> p (h a i)"), start=True, stop=True),
                     reads=[r["biasf"], r["tmpk"]], writes=[rb_])
                S.op("act", lambda e, pb_=pb_, h4=h4: e.activation(out=BIAS[:, h4 * 2:(h4 + 1) * 2].rearrange("p h a i -> p (h a i)"), in_=pb_, func=AF.Copy),
                     reads=[rb_], writes=[r["pers"]])
            for tk in (tk1, tk2, tk4) + tuple(tk3s) + tuple(ld):
                S.wait_tok("sp", tk)
            if not do_setup:
                for k_ in S.ops:
                    S.ops[k_] = []
            else:
                with nc.Block() as block:
                    S.replay(block)

        X = sb("X", [128, 4, D], F32)
        HT = sb("HT", [128, 16, NT], BF16)
        ACTB = sb("ACTB", [128, 12, NT], BF16)
        WB = sb("WB", [128, 3, 16, 256], BF16)
        WD = sb("WD", [128, 8, 512], BF16)
        TB = sb("TB", [128, 2, 8, 4, 128], BF16)
        QT = sb("QT", [128, 8, NT], BF16)
        ST = QT
        KT2 = sb("KT2", [128, 2, 4, NT], BF16)
        KN = sb("KN", [128, 2, NT], BF16)
        VA = sb("VA", [128, 2, 4, 4, 66], BF16)
        YA = sb("YA", [128, 2, 1024], F32)
        HN = sb("HN", [128, 2, D], BF16)
        JUNK = sb("JUNK", [128, D], BF16)
        SS = sb("SS", [128, 8], F32)
        RSTD = sb("RSTD", [128, 8], F32)
        UQ = sb("UQ", [64, 16, 8, 16], BF16)
        UT = sb("UT", [128, 16, 64], BF16)
        SA = sb("SA", [128, 2, 8, 64], F32)
        SB_ = sb("SB", [128, 2, 8, 64], F32)
        HIN = sb("HIN", [128, 2, 32], F32)
        HB = sb("HB", [128, 2, 8, 64], BF16)
        CT = sb("CT", [128, 6, 8], F32)
        YS = sb("YS", [64, 2, 8, 128], BF16)
        GX = sb("GX", [64, 1, 3, 512], F32)
        YT = sb("YT", [128, 8, 8, 64], BF16)
        SQ = sb("SQ", [128, 2, NT], BF16)
        SIG = sb("SIG", [128, 2, NT], BF16)
        RB = sb("RB", [128, 2, NT], F32)
        LG = sb("LG", [128, 2, 256], F32)
        PT = sb("PT", [128, 2, 256], BF16)
        DEN = sb("DEN", [128, 2, 4], F32)

        R = {}

        def rg(name):
            if name not in R:
                R[name] = Reg()
            return R[name]
        rc = [r_const]
        sub_regs = [rg("X%d" % i) for i in range(4)]

        S.op("dve", lambda e: e.memset(VA[:], 1.0), writes=[rg("VA0"), rg("VA1")])
        S.op("dve", lambda e: e.memset(KT2[:], 0.0), writes=[rg("KT0"), rg("KT1")])
        S.op("dve", lambda e: e.memset(HIN[:], 0.0), writes=[rg("HIN")])

        wslot = [0]

        def load_wblock(src_ap_fn_list):
            s_ = wslot[0] % 3
            wslot[0] += 1
            reg = rg("WB%d" % s_)
            for ov, ia in src_ap_fn_list:
                S.dma("pool", lambda e, ov=ov, ia=ia, s_=s_: e.dma_start(out=ov(WB[:, s_]), in_=ia, allow_slow_non_contiguous=True),
                      "wb%d" % s_, writes=[reg])
            return s_, reg

        def wcols(W, ncols_total, c0, ncol, krows=16):
            return dap(W, c0, [(ncols_total, 128), (128 * ncols_total, krows), (1, ncol)])

        def norm_to_HT(G, xsrc_regs):
            for sub in range(4):
                S.op("act", lambda e, sub=sub: e.activation(out=JUNK[:], in_=X[:, sub, :], func=AF.Square, accum_out=SS[:, sub:sub + 1]),
                     reads=[xsrc_regs[sub]], writes=[rg("JUNK"), rg("SS")])
            S.op("dve", lambda e: e.tensor_scalar(out=RSTD[:, 0:4], in0=SS[:, 0:4], scalar1=1.0 / D, scalar2=EPS, op0=ALU.mult, op1=ALU.add),
                 reads=[rg("SS")], writes=[rg("RSTD")])
            S.op("act", lambda e: e.activation(out=RSTD[:, 0:4], in_=RSTD[:, 0:4], func=AF.Sqrt), reads=[rg("RSTD")], writes=[rg("RSTD")])
            S.op("dve", lambda e: e.reciprocal(out=RSTD[:, 0:4], in_=RSTD[:, 0:4]), reads=[rg("RSTD")], writes=[rg("RSTD")])
            for sub in range(4):
                hb = sub % 2
                S.op("act", lambda e, sub=sub, hb=hb: e.activation(out=HN[:, hb, :], in_=X[:, sub, :], func=AF.Copy, scale=RSTD[:, sub:sub + 1]),
                     reads=[xsrc_regs[sub], rg("RSTD")], writes=[rg("HN%d" % hb)])
                for kh in range(2):
                    rb_, pb_ = bank()
                    pbb = pb_.bitcast(BF16)
                    for k8 in range(8):
                        k = kh * 8 + k8
                        S.op("pe", lambda e, pbb=pbb, k8=k8, k=k, hb=hb: e.transpose(pbb[:, k8 * 128:(k8 + 1) * 128],
                                                                                     HN[:, hb, k * 128:(k + 1) * 128], IDB[:]),
                             reads=[rg("HN%d" % hb)] + rc, writes=[rb_], sig=(k8 == 7))
                    S.op("dve", lambda e, pbb=pbb, kh=kh, sub=sub: e.tensor_tensor(
                        out=HT[:, kh * 8:(kh + 1) * 8, sub * 128:(sub + 1) * 128], in0=pbb.rearrange("p (k t) -> p k t", k=8),
                        in1=G[:, kh * 8:(kh + 1) * 8].unsqueeze(2).broadcast_to([128, 8, 128]), op=ALU.mult),
                        reads=[rb_] + rc, writes=[rg("HT")])

        def qk_norm(pb_, rb_, GV, out_ap, out_reg):
            S.op("act", lambda e: e.activation(out=SQ[:, 0, :], in_=pb_, func=AF.Square), reads=[rb_], writes=[rg("SQ0")])
            rb2, pb2 = bank()
            S.op("pe", lambda e: e.matmul(pb2, BONES[:], SQ[:, 0, :], start=True, stop=True), reads=[rg("SQ0")] + rc, writes=[rb2])
            S.op("act", lambda e: e.activation(out=RB[:, 0, :], in_=pb2, func=AF.Sqrt, scale=1.0 / 64, bias=EPS), reads=[rb2], writes=[rg("RB0")])
            S.op("dve", lambda e: e.reciprocal(out=RB[:, 0, :], in_=RB[:, 0, :]), reads=[rg("RB0")], writes=[rg("RB0")])
            S.op("dve", lambda e: e.scalar_tensor_tensor(out=out_ap, in0=pb_, scalar=GV[:, 0:1], in1=RB[:, 0, :], op0=ALU.mult, op1=ALU.mult),
                 reads=[rb_, rg("RB0")] + rc, writes=[out_reg])

        ssm_batch = [0]

        def ssm_ublock(ub, do_y):
            s_, wreg = load_wblock([(lambda sl: sl, wcols(w_in, 2560, 1536 + ub * 256, 256))])
            tsl = []
            for ch in range(2):
                g0 = ub * 16 + ch * 8
                ts_ = ssm_batch[0] % 2
                ssm_batch[0] += 1
                treg = rg("TB%d" % ts_)
                S.dma("sp", lambda e, ts_=ts_, g0=g0: e.dma_start(out=TB[:, ts_, :, 0, :], in_=wt_d[:, g0 * 128:(g0 + 8) * 128].rearrange("p (g c) -> p g c", g=8)),
                      "tb%d" % ts_, writes=[treg])
                S.dma("sp", lambda e, ts_=ts_, g0=g0: e.dma_start(out=TB[:, ts_, :, 1, :], in_=kt_d[:, g0 * 128:(g0 + 8) * 128].rearrange("p (g c) -> p g c", g=8)),
                      "tb%d" % ts_, writes=[treg])
                S.dma("sp", lambda e, ts_=ts_, g0=g0: e.dma_start(out=TB[:, ts_, :, 2:4, :], in_=et_d[:, g0 * 256:(g0 + 8) * 256].rearrange("p (g c x) -> p g c x", g=8, c=2)),
                      "tb%d" % ts_, writes=[treg])
                tsl.append((ts_, treg))
            for sp_ in range(4):
                rb_, pb_ = bank()
                for s2 in range(2):
                    s = sp_ * 2 + s2
                    for k in range(16):
                        S.op("pe", lambda e, pb_=pb_, s2=s2, s=s, k=k, s_=s_: e.matmul(
                            pb_[0:64, s2 * 256:(s2 + 1) * 256], HT[:, k, s:NT:8], WB[:, s_, k, :], start=(k == 0), stop=(k == 15)),
                            reads=[rg("HT"), wreg], writes=[rb_], sig=(k == 15 and s2 == 1))
                S.op("act", lambda e, pb_=pb_, sp_=sp_: e.activation(
                    out=UQ[:, :, sp_ * 2:(sp_ + 1) * 2, :].rearrange("p g s q -> p s g q"),
                    in_=pb_[0:64, :].rearrange("p (s g q) -> p s g q", s=2, g=16), func=AF.Copy),
                    reads=[rb_], writes=[rg("UQ")])
            for gh in range(2):
                rb_, pb_ = bank()
                pbb = pb_.bitcast(BF16)
                for g8 in range(8):
                    g = gh * 8 + g8
                    S.op("pe", lambda e, pbb=pbb, g8=g8, g=g: e.transpose(pbb[:, g8 * 64:(g8 + 1) * 64],
                                                                          UQ[:, g].rearrange("p s q -> p (s q)"), IDB[0:64, 0:64]),
                         reads=[rg("UQ")] + rc, writes=[rb_], sig=(g8 == 7))
                S.op("dve", lambda e, pbb=pbb, gh=gh: e.tensor_copy(out=UT[:, gh * 8:(gh + 1) * 8, :],
                                                                    in_=pbb[:, 0:512].rearrange("p (g j) -> p g j", g=8)),
                     reads=[rb_], writes=[rg("UT")])
            rbr, pbr = bank()
            rbi, pbi = bank()
            for pr in range(8):
                for gi in range(2):
                    g = 2 * pr + gi
                    ts_, treg = tsl[g // 8]
                    hs = slice(gi * 64, (gi + 1) * 64)
                    last = (pr == 7 and gi == 1)
                    S.op("pe", lambda e, hs=hs, pr=pr, g=g, ts_=ts_: e.matmul(pbr[hs, pr * 64:(pr + 1) * 64], TB[:, ts_, g % 8, 0, 0:64], UT[:, g, :],
                                                                               start=True, stop=True),
                         reads=[rg("UT"), treg], writes=[rbr], sig=False)
                    S.op("pe", lambda e, hs=hs, pr=pr, g=g, ts_=ts_: e.matmul(pbi[hs, pr * 64:(pr + 1) * 64], TB[:, ts_, g % 8, 0, 64:128], UT[:, g, :],
                                                                               start=True, stop=True),
                         reads=[rg("UT"), treg], writes=[rbi], sig=last)
            S.op("act", lambda e: e.activation(out=SA[:, 0], in_=pbr.rearrange("p (r j) -> p r j", r=8), func=AF.Copy), reads=[rbr], writes=[rg("SA")])
            S.op("act", lambda e: e.activation(out=SA[:, 1], in_=pbi.rearrange("p (r j) -> p r j", r=8), func=AF.Copy), reads=[rbi], writes=[rg("SA")])
            prs = slice(ub * 8, (ub + 1) * 8)
            a_r, a_i = KSC[:, 0, 0, prs], KSC[:, 0, 1, prs]
            h_r, h_i = HIN[:, 0, prs], HIN[:, 1, prs]

            def vtt(o, a, b, op, reads, writes):
                S.op("dve", lambda e: e.tensor_tensor(out=o, in0=a, in1=b, op=op), reads=reads, writes=writes)
            rH, rCT, rSA, rSB = rg("HIN"), rg("CT"), rg("SA"), rg("SB")
            vtt(CT[:, 0], a_r, h_r, ALU.mult, [rH] + rc, [rg("CT0")])
            vtt(CT[:, 1], a_i, h_i, ALU.mult, [rH] + rc, [rg("CT1")])
            vtt(CT[:, 2], a_r, h_i, ALU.mult, [rH] + rc, [rg("CT2")])
            vtt(CT[:, 3], a_i, h_r, ALU.mult, [rH] + rc, [rg("CT3")])
            vtt(CT[:, 4], CT[:, 0], CT[:, 1], ALU.subtract, [rg("CT0"), rg("CT1")], [rg("CT4")])
            vtt(CT[:, 5], CT[:, 2], CT[:, 3], ALU.add, [rg("CT2"), rg("CT3")], [rg("CT5")])
            S.op("dve", lambda e: e.tensor_copy(out=HB[:, :, :, 0], in_=HIN[:, :, prs]), reads=[rH], writes=[rg("HB")])
            vtt(SA[:, 0, :, 0], SA[:, 0, :, 0], CT[:, 4], ALU.add, [rSA, rg("CT4")], [rSA])
            vtt(SA[:, 1, :, 0], SA[:, 1, :, 0], CT[:, 5], ALU.add, [rSA, rg("CT5")], [rSA])
            bufs = [(SA, rSA), (SB_, rSB)]
            for k in range(6):
                d = 1 << k
                (XI_, rI), (XO_, rO) = bufs[k % 2], bufs[(k + 1) % 2]
                S.op("dve", lambda e, XI_=XI_, XO_=XO_, d=d: e.tensor_copy(out=XO_[:, :, :, 0:d], in_=XI_[:, :, :, 0:d]), reads=[rI], writes=[rO])
                for which in range(4):
                    for pr in range(8):
                        gp = ub * 8 + pr
                        sre, sim, snim = KSC[:, k, 0, gp:gp + 1], KSC[:, k, 1, gp:gp + 1], KSC[:, k, 2, gp:gp + 1]
                        if which == 0:
                            f = lambda e, XI_=XI_, XO_=XO_, pr=pr, d=d, sc=snim: e.scalar_tensor_tensor(
                                out=XO_[:, 0, pr, d:64], in0=XI_[:, 1, pr, 0:64 - d], scalar=sc, in1=XI_[:, 0, pr, d:64], op0=ALU.mult, op1=ALU.add)
                        elif which == 1:
                            f = lambda e, XI_=XI_, XO_=XO_, pr=pr, d=d, sc=sim: e.scalar_tensor_tensor(
                                out=XO_[:, 1, pr, d:64], in0=XI_[:, 0, pr, 0:64 - d], scalar=sc, in1=XI_[:, 1, pr, d:64], op0=ALU.mult, op1=ALU.add)
                        elif which == 2:
                            f = lambda e, XI_=XI_, XO_=XO_, pr=pr, d=d, sc=sre: e.scalar_tensor_tensor(
                                out=XO_[:, 0, pr, d:64], in0=XI_[:, 0, pr, 0:64 - d], scalar=sc, in1=XO_[:, 0, pr, d:64], op0=ALU.mult, op1=ALU.add)
                        else:
                            f = lambda e, XI_=XI_, XO_=XO_, pr=pr, d=d, sc=sre: e.scalar_tensor_tensor(
                                out=XO_[:, 1, pr, d:64], in0=XI_[:, 1, pr, 0:64 - d], scalar=sc, in1=XO_[:, 1, pr, d:64], op0=ALU.mult, op1=ALU.add)
                        S.op("dve", f, reads=[rI, rO] + rc, writes=[rO])
            S.op("dve", lambda e: e.tensor_copy(out=HIN[:, :, prs], in_=SA[:, :, :, 63]), reads=[rSA], writes=[rH])
            if not do_y:
                return
            S.op("dve", lambda e: e.tensor_copy(out=HB[:, :, :, 1:64], in_=SA[:, :, :, 0:63]), reads=[rSA], writes=[rg("HB")])
            for gq in range(4):
                rb_, pb_ = bank()
                for g4 in range(4):
                    g = gq * 4 + g4
                    ts_, treg = tsl[g // 8]
                    o = pb_[0:64, g4 * 128:(g4 + 1) * 128]
                    pr = g // 2
                    S.op("pe", lambda e, o=o, g=g, ts_=ts_: e.matmul(o, UT[:, g, :], TB[:, ts_, g % 8, 1, :], start=True, stop=False),
                         reads=[rg("UT"), treg], writes=[rb_], sig=False)
                    S.op("pe", lambda e, o=o, g=g, ts_=ts_, pr=pr: e.matmul(o, HB[:, 0, pr, :], TB[:, ts_, g % 8, 2, :], start=False, stop=False),
                         reads=[rg("HB"), treg], writes=[rb_], sig=False)
                    S.op("pe", lambda e, o=o, g=g, ts_=ts_, pr=pr: e.matmul(o, HB[:, 1, pr, :], TB[:, ts_, g % 8, 3, :], start=False, stop=True),
                         reads=[rg("HB"), treg], writes=[rb_], sig=(g4 == 3))
                hb = (gq // 2) % 2
                gb = gq % 2
                src = pb_[0:64, :]
                gx = [GX[:, 0, i, :] for i in range(3)]
                rgx = rg("GX0")
                S.op("act", lambda e, src=src, gx=gx: e.activation(out=gx[0], in_=src, func=AF.Square), reads=[rb_], writes=[rgx])
                S.op("dve", lambda e, gx=gx: e.tensor_scalar(out=gx[0], in0=gx[0], scalar1=0.044715, scalar2=1.0, op0=ALU.mult, op1=ALU.add), reads=[rgx], writes=[rgx])
                S.op("dve", lambda e, src=src, gx=gx: e.tensor_tensor(out=gx[1], in0=src, in1=gx[0], op=ALU.mult), reads=[rb_, rgx], writes=[rgx])
                S.op("act", lambda e, gx=gx: e.activation(out=gx[2], in_=gx[1], func=AF.Sigmoid, scale=2.0 * math.sqrt(2.0 / math.pi)), reads=[rgx], writes=[rgx])
                S.op("dve", lambda e, src=src, gx=gx, hb=hb, gb=gb: e.tensor_tensor(
                    out=YS[:, hb, :, gb * 64:(gb + 1) * 64].rearrange("p t (g q) -> p g t q", g=4),
                    in0=src.rearrange("p (g t q) -> p g t q", g=4, t=8), in1=gx[2].rearrange("p (g t q) -> p g t q", g=4, t=8), op=ALU.mult),
                    reads=[rb_, rgx], writes=[rg("YS%d" % hb)])
                if gb == 1:
                    ct = ub * 2 + gq // 2
                    rb2, pb2 = bank()
                    pbb = pb2.bitcast(BF16)
                    for t in range(8):
                        S.op("pe", lambda e, pbb=pbb, t=t, hb=hb: e.transpose(pbb[:, t * 64:(t + 1) * 64], YS[:, hb, t, :], IDB[0:64, 0:64]),
                             reads=[rg("YS%d" % hb)] + rc, writes=[rb2], sig=(t == 7))
                    S.op("act", lambda e, pbb=pbb, ct=ct: e.activation(out=YT[:, ct].rearrange("p t j -> p (t j)"), in_=pbb[:, 0:512], func=AF.Copy),
                         reads=[rb2], writes=[rg("YT")])

        def process_tt(xsrc, tt_i, pred, last_pred, ping):
            for sub in range(4):
                S.dma("sp", lambda e, sub=sub: e.dma_start(out=X[:, sub, :], in_=xsrc[tt_i * NT + sub * 128: tt_i * NT + (sub + 1) * 128, :]),
                      "xl%d" % sub, writes=[sub_regs[sub]])
            main = not pred
            if stage >= 1:
                norm_to_HT(G1, sub_regs)
            if main and stage >= 2:
                for qb in range(4):
                    s_, wreg = load_wblock([(lambda sl: sl, wcols(w_in, 2560, qb * 256, 256))])
                    for m in range(2):
                        rb_, pb_ = bank()
                        for k in range(16):
                            S.op("pe", lambda e, pb_=pb_, k=k, m=m, s_=s_: e.matmul(pb_, WB[:, s_, k, m * 128:(m + 1) * 128], HT[:, k, :],
                                                                                   start=(k == 0), stop=(k == 15)),
                                 reads=[rg("HT"), wreg], writes=[rb_], sig=(k == 15))
                        qk_norm(pb_, rb_, QG, QT[:, qb * 2 + m, :], rg("QT"))
            if (main or last_pred) and stage >= 2:
                s_, wreg = load_wblock([(lambda sl: sl, wcols(w_in, 2560, 1024, 256))])
                for kt in range(2):
                    rb_, pb_ = bank()
                    for k in range(16):
                        S.op("pe", lambda e, pb_=pb_, k=k, kt=kt, s_=s_: e.matmul(pb_, WB[:, s_, k, kt * 128:(kt + 1) * 128], HT[:, k, :],
                                                                               start=(k == 0), stop=(k == 15)),
                             reads=[rg("HT"), wreg], writes=[rb_], sig=(k == 15))
                    qk_norm(pb_, rb_, KG, KN[:, kt, :], rg("KN%d" % kt))
                    for c in range(2):
                        kh = kt * 2 + c
                        rb2, pb2 = bank()
                        S.op("pe", lambda e, pb2=pb2, c=c, kt=kt: e.matmul(pb2, DUPB[:, c, :], KN[:, kt, :], start=True, stop=True),
                             reads=[rg("KN%d" % kt)] + rc, writes=[rb2])
                        S.op("act", lambda e, pb2=pb2, kh=kh: e.activation(out=KT2[:, ping, kh, :], in_=pb2, func=AF.Copy),
                             reads=[rb2], writes=[rg("KT%d" % ping)])
                s_, wreg = load_wblock([(lambda sl: sl, wcols(w_in, 2560, 1280, 256))])
                for sub in range(4):
                    rb_, pb_ = bank()
                    for k in range(16):
                        S.op("pe", lambda e, pb_=pb_, k=k, sub=sub, s_=s_: e.matmul(pb_[:, 0:256], HT[:, k, sub * 128:(sub + 1) * 128], WB[:, s_, k, :],
                                                                                   start=(k == 0), stop=(k == 15)),
                             reads=[rg("HT"), wreg], writes=[rb_], sig=(k == 15))
                    S.op("act", lambda e, pb_=pb_, sub=sub: e.activation(out=VA[:, ping, sub, :, 0:64], in_=pb_[:, 0:256].rearrange("p (h d) -> p h d", h=4),
                                                                         func=AF.Copy), reads=[rb_], writes=[rg("VA%d" % ping)])
            for ub in range(4 if stage >= 3 else 0):
                ssm_ublock(ub, do_y=main)
            if pred:
                return
            for b in range(4 if stage >= 4 else 0):
                yb = b % 2
                for kh in range(4):
                    rbo, pbo = bank()
                    for g4 in range(4):
                        h = kh * 4 + g4
                        hp = slice((h % 2) * 64, (h % 2) * 64 + 64)
                        qv = QT[hp, h // 2, b * 128:(b + 1) * 128]
                        if b > 0:
                            kprev = KT2[hp, ping, kh, (b - 1) * 128:b * 128]
                            vprev = VA[:, ping, b - 1, kh, 0:65]
                            rkp, rvp = rg("KT%d" % ping), rg("VA%d" % ping)
                        else:
                            kprev = KT2[hp, 1 - ping, kh, 384:512]
                            vprev = VA[:, 1 - ping, 3, kh, 0:65]
                            rkp, rvp = rg("KT%d" % (1 - ping)), rg("VA%d" % (1 - ping))
                        kcur = KT2[hp, ping, kh, b * 128:(b + 1) * 128]
                        vcur = VA[:, ping, b, kh, 0:65]
                        rbs, pbs = bank()
                        S.op("pe", lambda e, pbs=pbs, kprev=kprev, qv=qv: e.matmul(pbs[:, 0:128], kprev, qv, start=True, stop=True),
                             reads=[rkp, rg("QT")], writes=[rbs], sig=False)
                        S.op("pe", lambda e, pbs=pbs, kcur=kcur, qv=qv: e.matmul(pbs[:, 128:256], kcur, qv, start=True, stop=True),
                             reads=[rg("KT%d" % ping), rg("QT")], writes=[rbs])
                        lb = h % 2
                        S.op("dve", lambda e, pbs=pbs, h=h, lb=lb: e.scalar_tensor_tensor(
                            out=LG[:, lb, :], in0=pbs[:, 0:256], scalar=0.125, in1=BIAS[:, h].rearrange("p a i -> p (a i)"), op0=ALU.mult, op1=ALU.add),
                            reads=[rbs] + rc, writes=[rg("LG%d" % lb)])
                        if b == 0 and tt_i == 0:
                            S.op("act", lambda e, lb=lb: e.activation(out=PT[:, lb, 0:128], in_=LG[:, lb, 0:128], func=AF.Exp, bias=HM8[:, 0:1], scale=1.0),
                                 reads=[rg("LG%d" % lb)] + rc, writes=[rg("PT%d" % lb)])
                            S.op("act", lambda e, lb=lb: e.activation(out=PT[:, lb, 128:256], in_=LG[:, lb, 128:256], func=AF.Exp, bias=NEG8[:, 0:1], scale=1.0),
                                 reads=[rg("LG%d" % lb)] + rc, writes=[rg("PT%d" % lb)])
                        else:
                            S.op("act", lambda e, lb=lb: e.activation(out=PT[:, lb, :], in_=LG[:, lb, :], func=AF.Exp, bias=NEG8[:, 0:1], scale=1.0),
                                 reads=[rg("LG%d" % lb)] + rc, writes=[rg("PT%d" % lb)])
                        o = pbo[:, g4 * 65:(g4 + 1) * 65]
                        S.op("pe", lambda e, o=o, lb=lb, vprev=vprev: e.matmul(o, PT[:, lb, 0:128], vprev, start=True, stop=False),
                             reads=[rg("PT%d" % lb), rvp], writes=[rbo], sig=False)
                        S.op("pe", lambda e, o=o, lb=lb, vcur=vcur: e.matmul(o, PT[:, lb, 128:256], vcur, start=False, stop=True),
                             reads=[rg("PT%d" % lb), rg("VA%d" % ping)], writes=[rbo], sig=(g4 == 3))
                    dn = kh % 2
                    ov = pbo[:, 0:260].rearrange("p (g c) -> p g c", g=4)
                    S.op("dve", lambda e, ov=ov, dn=dn, kh=kh: e.tensor_tensor(out=DEN[:, dn, :], in0=ov[:, :, 64], in1=ESK[:, kh * 4:(kh + 1) * 4], op=ALU.add),
                         reads=[rbo] + rc, writes=[rg("DEN%d" % dn)])
                    S.op("dve", lambda e, dn=dn: e.reciprocal(out=DEN[:, dn, :], in_=DEN[:, dn, :]), reads=[rg("DEN%d" % dn)], writes=[rg("DEN%d" % dn)])
                    S.op("dve", lambda e, ov=ov, dn=dn, kh=kh, yb=yb: e.tensor_tensor(
                        out=YA[:, yb, kh * 256:(kh + 1) * 256].rearrange("p (g d) -> p g d", g=4), in0=ov[:, :, 0:64],
                        in1=DEN[:, dn, :].unsqueeze(2).broadcast_to([128, 4, 64]), op=ALU.mult),
                        reads=[rbo, rg("DEN%d" % dn)], writes=[rg("YA%d" % yb)])
                S.op("act", lambda e, yb=yb, b=b: e.activation(out=JUNK[:, 0:1024], in_=YA[:, yb, :], func=AF.Square, accum_out=SS[:, 4 + b:5 + b]),
                     reads=[rg("YA%d" % yb)], writes=[rg("JUNK"), rg("SSA%d" % b)])
                S.op("dve", lambda e, b=b: e.tensor_scalar(out=RSTD[:, 4 + b:5 + b], in0=SS[:, 4 + b:5 + b], scalar1=1.0 / 1024, scalar2=EPS, op0=ALU.mult, op1=ALU.add),
                     reads=[rg("SSA%d" % b)], writes=[rg("RSA%d" % b)])
                S.op("act", lambda e, b=b: e.activation(out=RSTD[:, 4 + b:5 + b], in_=RSTD[:, 4 + b:5 + b], func=AF.Sqrt), reads=[rg("RSA%d" % b)], writes=[rg("RSA%d" % b)])
                S.op("dve", lambda e, b=b: e.reciprocal(out=RSTD[:, 4 + b:5 + b], in_=RSTD[:, 4 + b:5 + b]), reads=[rg("RSA%d" % b)], writes=[rg("RSA%d" % b)])
                S.op("act", lambda e, yb=yb, b=b: e.activation(out=HN[:, yb, 0:1024], in_=YA[:, yb, :], func=AF.Copy, scale=RSTD[:, 4 + b:5 + b]),
                     reads=[rg("YA%d" % yb), rg("RSA%d" % b)], writes=[rg("HN%d" % yb)])
                rb_, pb_ = bank()
                pbb = pb_.bitcast(BF16)
                for k8 in range(8):
                    S.op("pe", lambda e, pbb=pbb, k8=k8, yb=yb: e.transpose(pbb[:, k8 * 128:(k8 + 1) * 128], HN[:, yb, k8 * 128:(k8 + 1) * 128], IDB[:]),
                         reads=[rg("HN%d" % yb)] + rc, writes=[rb_], sig=(k8 == 7))
                S.op("dve", lambda e, pbb=pbb, b=b: e.tensor_tensor(
                    out=HT[:, 0:8, b * 128:(b + 1) * 128], in0=pbb.rearrange("p (k t) -> p k t", k=8),
                    in1=GA[:].unsqueeze(2).broadcast_to([128, 8, 128]), op=ALU.mult), reads=[rb_] + rc, writes=[rg("HT")])
            rbss, pbss = bank(reserve=True)
            for nb in range(4 if stage >= 5 else 0):
                s_, wreg = load_wblock([(lambda sl: sl[:, 0:8, :], wcols(w_glu, 1024, nb * 256, 256, krows=8))])
                for m in range(2):
                    ct = nb * 2 + m
                    rb_, pb_ = bank()
                    for c in range(8):
                        S.op("pe", lambda e, pb_=pb_, c=c, m=m, s_=s_: e.matmul(pb_, WB[:, s_, c, m * 128:(m + 1) * 128], YT[:, c].rearrange("p t j -> p (t j)"),
                                                                               start=(c == 0), stop=(c == 7)),
                             reads=[rg("YT"), wreg], writes=[rb_], sig=(c == 7))
                    sg = ct % 2
                    S.op("act", lambda e, pb_=pb_, sg=sg: e.activation(out=SIG[:, sg, :], in_=pb_, func=AF.Sigmoid), reads=[rb_], writes=[rg("SIG%d" % sg)])
                    S.op("dve", lambda e, ct=ct, sg=sg: e.tensor_tensor(out=ST[:, ct, :], in0=YT[:, ct].rearrange("p t j -> p (t j)"), in1=SIG[:, sg, :], op=ALU.mult),
                         reads=[rg("YT"), rg("SIG%d" % sg)], writes=[rg("QT")])
                    S.op("act", lambda e, ct=ct, sg=sg: e.activation(out=SQ[:, sg, :], in_=ST[:, ct, :], func=AF.Square), reads=[rg("QT")], writes=[rg("SQ%d" % sg)])
                    S.op("pe", lambda e, ct=ct, sg=sg: e.matmul(pbss, ONESB[:], SQ[:, sg, :], start=(ct == 0), stop=(ct == 7)),
                         reads=[rg("SQ%d" % sg)] + rc, writes=[rbss])
            reserved.clear()
            if stage >= 5:
                S.op("act", lambda e: e.activation(out=RB[:, 1, :], in_=pbss, func=AF.Sqrt, scale=1.0 / 1024, bias=EPS), reads=[rbss], writes=[rg("RB1")])
                S.op("dve", lambda e: e.reciprocal(out=RB[:, 1, :], in_=RB[:, 1, :]), reads=[rg("RB1")], writes=[rg("RB1")])
            for ct in range(8 if stage >= 5 else 0):
                S.op("dve", lambda e, ct=ct: e.scalar_tensor_tensor(
                    out=HT[:, 8 + ct, :].rearrange("p (j t) -> p t j", t=8), in0=ST[:, ct, :].rearrange("p (t j) -> p t j", t=8),
                    scalar=GS[:, ct:ct + 1], in1=RB[:, 1, :].rearrange("p (t j) -> p t j", t=8), op0=ALU.mult, op1=ALU.mult),
                    reads=[rg("QT"), rg("RB1")] + rc, writes=[rg("HT")])
            for fb in range(8 if stage >= 6 else 0):
                s_, wreg = load_wblock([(lambda sl: sl, wcols(w_out, D, fb * 256, 256))])
                for sp2 in range(2):
                    rb_, pb_ = bank()
                    for s2 in range(2):
                        sub = sp2 * 2 + s2
                        for k in range(16):
                            S.op("pe", lambda e, pb_=pb_, s2=s2, sub=sub, k=k, s_=s_: e.matmul(
                                pb_[:, s2 * 256:(s2 + 1) * 256], HT[:, k, sub * 128:(sub + 1) * 128], WB[:, s_, k, :], start=(k == 0), stop=(k == 15)),
                                reads=[rg("HT"), wreg], writes=[rb_], sig=(k == 15))
                        S.op("dve", lambda e, pb_=pb_, s2=s2, sub=sub, fb=fb: e.tensor_tensor(
                            out=X[:, sub, fb * 256:(fb + 1) * 256], in0=pb_[:, s2 * 256:(s2 + 1) * 256], in1=X[:, sub, fb * 256:(fb + 1) * 256], op=ALU.add),
                            reads=[rb_, sub_regs[sub]], writes=[sub_regs[sub]])
            if stage >= 7:
                norm_to_HT(G2, sub_regs)
            c0 = 0
            for grp, nch in enumerate(GROUPS_FF if stage >= 7 else []):
                for bl in range(nch // 2):
                    col = (c0 + bl * 2) * 128
                    sg_, wg = load_wblock([(lambda sl: sl, wcols(w_gate, FF, col, 256))])
                    su_, wu = load_wblock([(lambda sl: sl, wcols(w_up, FF, col, 256))])
                    for m in range(2):
                        j = bl * 2 + m
                        rbg, pbg = bank()
                        for k in range(16):
                            S.op("pe", lambda e, pbg=pbg, k=k, m=m, sg_=sg_: e.matmul(pbg, WB[:, sg_, k, m * 128:(m + 1) * 128], HT[:, k, :], start=(k == 0), stop=(k == 15)),
                                 reads=[rg("HT"), wg], writes=[rbg], sig=(k == 15))
                        rbu, pbu = bank()
                        for k in range(16):
                            S.op("pe", lambda e, pbu=pbu, k=k, m=m, su_=su_: e.matmul(pbu, WB[:, su_, k, m * 128:(m + 1) * 128], HT[:, k, :], start=(k == 0), stop=(k == 15)),
                                 reads=[rg("HT"), wu], writes=[rbu], sig=(k == 15))
                        sg = j % 2
                        S.op("act", lambda e, pbg=pbg, sg=sg: e.activation(out=SIG[:, sg, :], in_=pbg, func=AF.Silu), reads=[rbg], writes=[rg("SIG%d" % sg)])
                        S.op("dve", lambda e, pbu=pbu, sg=sg, j=j: e.tensor_tensor(out=ACTB[:, j, :], in0=pbu, in1=SIG[:, sg, :], op=ALU.mult),
                             reads=[rbu, rg("SIG%d" % sg)], writes=[rg("ACT%d" % j)])
                for f in range(4):
                    bk = [bank() for _ in range(4)]
                    for j in range(nch):
                        ws = (wslot_d[0]) % 8
                        wslot_d[0] += 1
                        wreg = rg("WD%d" % ws)
                        S.dma("pool", lambda e, ws=ws, j=j, f=f, c0=c0: e.dma_start(out=WD[:, ws, :], in_=w_down[(c0 + j) * 128:(c0 + j + 1) * 128, f * 512:(f + 1) * 512]),
                              "wd%d" % ws, writes=[wreg])
                        for sub in range(4):
                            S.op("pe", lambda e, sub=sub, j=j, ws=ws, pb_=bk[sub][1]: e.matmul(pb_, ACTB[:, j, sub * 128:(sub + 1) * 128], WD[:, ws, :],
                                                                                              start=(j == 0), stop=(j == nch - 1)),
                                 reads=[rg("ACT%d" % j), wreg], writes=[bk[sub][0]], sig=(j == nch - 1 or sub == 3))
                    for sub in range(4):
                        S.op("dve", lambda e, sub=sub, f=f, pb_=bk[sub][1]: e.tensor_tensor(
                            out=X[:, sub, f * 512:(f + 1) * 512], in0=pb_, in1=X[:, sub, f * 512:(f + 1) * 512], op=ALU.add),
                            reads=[bk[sub][0], sub_regs[sub]], writes=[sub_regs[sub]])
                c0 += nch
            for sub in range(4):
                out_toks.append(S.dma("sp", lambda e, sub=sub: e.dma_start(out=out[tt_i * NT + sub * 128: tt_i * NT + (sub + 1) * 128, :], in_=X[:, sub, :]),
                                      "ot%d" % sub, reads=[sub_regs[sub]]))

        wslot_d = [0]
        out_toks = []
        ping = 0
        if do_pred:
            for t in range(n_tt):
                process_tt(x_pred, t, True, t == n_tt - 1, ping)
            ping = 1 - ping
        for t in range(n_tt):
            process_tt(x_main, t, False, False, ping)
            ping = 1 - ping
        for tk in out_toks:
            S.wait_tok("sp", tk)
        with nc.Block() as block:
            S.replay(block)
    return nc


def _t5_bucket(dist):
    n = np.maximum(dist, 0)
    max_exact = 16
    nf = np.maximum(n, 1).astype(np.float32)
    large = max_exact + (np.log(nf / max_exact) / math.log(128 / max_exact) * (32 - max_exact)).astype(np.int32)
    large = np.minimum(large, 31)
    return np.where(n < max_exact, n, large).astype(np.int32)


def _consts():
    ident = np.eye(128, dtype=np.float32)
    s_idx = np.arange(128) // 16
    mask = (s_idx[:, None] <= s_idx[None, :]).astype(np.float32)
    mv = np.broadcast_to(np.asarray(MS, np.float32)[None, :, None], (128, NM, 32)).reshape(128, NM * 32).copy()
    bones = (s_idx[:, None] // 4 == s_idx[None, :] // 4).astype(np.float32)
    bucket = _t5_bucket(np.arange(128))
    oh = np.zeros((33, 512), np.float32)
    for e in range(255):
        if e < 127:
            oh[bucket[e + 1], e] = 1.0
            oh[32, 256 + e] = NEG
        else:
            oh[32, e] = NEG
            oh[bucket[e - 127], 256 + e] = 1.0
    oh[32, 255] = NEG
    oh[32, 511] = NEG
    dup = np.zeros((128, 2, 128), np.float32)
    for c in range(2):
        for d in range(64):
            dup[c * 64 + d, c, d] = 1.0
            dup[c * 64 + d, c, 64 + d] = 1.0
    return {"c_dup": dup.reshape(128, 256), "c_ident": ident, "c_mask": mask, "c_mv": mv, "c_oh": oh, "c_bones": bones, "c_anti": np.ascontiguousarray(ident[::-1])}


_NC_CACHE = {}


def kernel(**inputs):
    n_tt = int(os.environ.get("MK_NTT", "4"))
    do_pred = os.environ.get("MK_PRED", "1") == "1"
    stage = int(os.environ.get("MK_STAGE", "99"))
    do_setup = os.environ.get("MK_SETUP", "1") == "1"
    key = (n_tt, do_pred, stage, do_setup)
    if key not in _NC_CACHE:
        _NC_CACHE[key] = build(n_tt, do_pred, stage, do_setup)
    nc = _NC_CACHE[key]
    x = np.asarray(inputs["x"], np.float32)
    TOK = n_tt * NT
    shared = {k: np.ascontiguousarray(np.asarray(inputs[k], np.float32)[0]) for k in
              ["ln1_g", "w_in", "q_norm_g", "k_norm_g", "attn_sinks", "ssm_a_re", "ssm_a_im", "ssm_log_dt", "ssm_b_re", "ssm_b_im",
               "ssm_c_re", "ssm_c_im", "w_glu", "attn_out_g", "ssm_out_g", "w_out", "ln2_g", "w_ff_gate", "w_ff_up", "w_ff_down"]}
    shared["ssm_d"] = np.ascontiguousarray(np.asarray(inputs["ssm_d"], np.float32)[0].reshape(-1))
    shared["rel_bias"] = np.ascontiguousarray(np.asarray(inputs["rel_bias"], np.float32))
    shared.update(_consts())
    in_maps = []
    ncores = int(os.environ.get("MK_CORES", "8"))
    for c in range(ncores):
        b, half = c // 2, c % 2
        m = dict(shared)
        m["x_main"] = np.ascontiguousarray(x[b, half * 2048: half * 2048 + TOK])
        if half == 1:
            m["x_pred"] = np.ascontiguousarray(x[b, 2048 - TOK:2048])
            m["hm8"] = np.full((128, 1), -SHIFT, np.float32)
        else:
            m["x_pred"] = np.zeros((TOK, D), np.float32)
            m["hm8"] = np.full((128, 1), NEG - SHIFT, np.float32)
        in_maps.append(m)
    if os.environ.get("MK_TRACE", "0") == "1":
        res = run_bass_kernel_spmd(nc, in_maps, core_ids=list(range(ncores)), trace=True)
        print("EXEC_NS", res.exec_time_ns)
    else:
        res = run_bass_kernel_spmd(nc, in_maps, core_ids=list(range(ncores)))
    outp = np.zeros((4, 4096, D), np.float32)
    for c in range(ncores):
        b, half = c // 2, c % 2
        outp[b, half * 2048: half * 2048 + TOK] = res.results[c]["out"]
    return outp
```

```python
import os
import math
import numpy as np
import ml_dtypes
import concourse.bass as bass
import concourse.mybir as mybir
from concourse.bass_utils import run_bass_kernel_spmd

F32 = mybir.dt.float32
BF16 = mybir.dt.bfloat16
I32 = mybir.dt.int32
ALU = mybir.AluOpType
AF = mybir.ActivationFunctionType

D = 2048
NT = 512
FF = 5632
EPS = 1e-6
NEG = -30000.0
SHIFT = 8.0
MS = [-(s + 1) for s in range(8)] + [t + 1 for t in range(8)] + [7 - s for s in range(8)] + [8, 16, 32, 64, 128, 256]
NM = len(MS)
GROUPS_FF = [12, 10, 12, 10]


class Reg:
    __slots__ = ("w", "r")

    def __init__(self):
        self.w = None
        self.r = {}


class Sched:
    def __init__(self, nc, semh):
        self.nc = nc
        self.semh = semh
        self.names = ["pe", "act", "dve", "pool", "sp"]
        self.ops = {k: [] for k in self.names}
        self.cnt = {k: 0 for k in self.names}
        self.waited = {k: {} for k in self.names}
        self.dcnt = {}

    def _deps(self, e, reads, writes):
        deps = {}

        def add(tok):
            if tok is None:
                return
            s, v = tok
            if e == "pe" and s == "pe":
                return
            if deps.get(s, 0) < v:
                deps[s] = v
        for r in reads:
            add(r.w)
        for w in writes:
            add(w.w)
            for s, v in w.r.items():
                add((s, v))
        out = []
        for s, v in deps.items():
            if self.waited[e].get(s, 0) < v:
                self.waited[e][s] = v
                out.append((s, v))
        return out

    def _upd(self, tok, reads, writes):
        s, v = tok
        for r in reads:
            if r.r.get(s, 0) < v:
                r.r[s] = v
        for w in writes:
            w.w = tok
            w.r = {}

    def op(self, e, fn, reads=(), writes=(), sig=True):
        waits = self._deps(e, reads, writes)
        if sig:
            self.cnt[e] += 1
            v = self.cnt[e]
        else:
            v = self.cnt[e] + 1
        self.ops[e].append((waits, fn, (e, 1) if sig else None))
        self._upd((e, v), reads, writes)

    def dma(self, q, fn, sem, reads=(), writes=()):
        waits = self._deps(q, reads, writes)
        self.dcnt[sem] = self.dcnt.get(sem, 0) + 16
        tok = (sem, self.dcnt[sem])
        self.ops[q].append((waits, fn, (sem, 16)))
        self._upd(tok, reads, writes)
        return tok

    def wait_tok(self, e, tok):
        s, v = tok
        if self.waited[e].get(s, 0) < v:
            self.waited[e][s] = v
            self.ops[e].append(([(s, v)], None, None))

    def replay(self, block):
        decs = {"pe": block.tensor, "act": block.scalar, "dve": block.vector, "pool": block.gpsimd, "sp": block.sync}
        for name in self.names:
            lst = self.ops[name]
            if not lst:
                continue

            def body(e, lst=lst):
                for waits, fn, inc in lst:
                    for s, v in waits:
                        e.wait_ge(self.semh[s], v)
                    if fn is None:
                        continue
                    ins = fn(e)
                    if inc is not None:
                        ins.then_inc(self.semh[inc[0]], inc[1])
            decs[name](body)
            self.ops[name] = []


def dap(t, offset, dims):
    return bass.AP(tensor=t.tensor, offset=offset, ap=[[s, c] for s, c in dims])


def build(n_tt=4, do_pred=True, stage=99, do_setup=True):
    nc = bass.Bass("TRN2", target_bir_lowering=False)
    TOK = n_tt * NT

    def din(name, shape, dt=F32):
        return nc.dram_tensor(name, list(shape), dt, kind="ExternalInput").ap()

    x_main = din("x_main", [TOK, D])
    x_pred = din("x_pred", [TOK, D])
    hm8 = din("hm8", [128, 1])
    rel_bias = din("rel_bias", [32, 16])
    ln1_g = din("ln1_g", [D])
    w_in = din("w_in", [D, 2560])
    q_norm_g = din("q_norm_g", [64])
    k_norm_g = din("k_norm_g", [64])
    sinks = din("attn_sinks", [16])
    a_re = din("ssm_a_re", [64, 64])
    a_im = din("ssm_a_im", [64, 64])
    log_dt = din("ssm_log_dt", [64])
    b_re = din("ssm_b_re", [64, 64, 16])
    b_im = din("ssm_b_im", [64, 64, 16])
    c_re = din("ssm_c_re", [64, 16, 64])
    c_im = din("ssm_c_im", [64, 16, 64])
    ssm_d = din("ssm_d", [64 * 16])
    w_glu = din("w_glu", [1024, 1024])
    attn_out_g = din("attn_out_g", [1024])
    ssm_out_g = din("ssm_out_g", [1024])
    w_out = din("w_out", [D, D])
    ln2_g = din("ln2_g", [D])
    w_gate = din("w_ff_gate", [D, FF])
    w_up = din("w_ff_up", [D, FF])
    w_down = din("w_ff_down", [FF, D])
    c_ident = din("c_ident", [128, 128])
    c_mask = din("c_mask", [128, 128])
    c_mv = din("c_mv", [128, NM * 32])
    c_mv2 = din("c_mv2", [128, 64 * 32])
    c_oh = din("c_oh", [33, 512])
    c_bones = din("c_bones", [128, 128])
    c_anti = din("c_anti", [128, 128])
    c_dup = din("c_dup", [128, 256])
    out = nc.dram_tensor("out", [TOK, D], F32, kind="ExternalOutput").ap()
    wt_d = nc.dram_tensor("wt_d", [128, 64 * 128], BF16, kind="Internal").ap()
    kt_d = nc.dram_tensor("kt_d", [128, 64 * 128], BF16, kind="Internal").ap()
    et_d = nc.dram_tensor("et_d", [128, 64 * 256], BF16, kind="Internal").ap()
    ext_d = nc.dram_tensor("ext_d", [16, 512], F32, kind="Internal").ap()

    sem_names = ["pe", "act", "dve", "pool", "sp", "xl0", "xl1", "xl2", "xl3", "wb0", "wb1", "wb2", "wd", "tb0", "tb1",
                 "st", "misc", "ot0", "ot1", "ot2", "ot3"] + ["wd%d" % i for i in range(8)]
    import contextlib
    with contextlib.ExitStack() as es:
        semh = {n: es.enter_context(nc.semaphore(n)) for n in sem_names}
        S = Sched(nc, semh)

        def sb(name, shape, dt):
            return es.enter_context(nc.sbuf_tensor(name, list(shape), dt))

        ROTC = sb("ROTC", [128, 32, 64], BF16)
        ROTS = sb("ROTS", [128, 32, 64], BF16)
        RHO = sb("RHO", [128, 32], F32)
        BIAS = sb("BIAS", [128, 16, 2, 128], BF16)
        G1 = sb("G1", [128, 16], F32)
        G2 = sb("G2", [128, 16], F32)
        GA = sb("GA", [128, 8], F32)
        GS = sb("GS", [128, 8], F32)
        QG = sb("QG", [128, 1], F32)
        KG = sb("KG", [128, 1], F32)
        ESK = sb("ESK", [128, 16], F32)
        HM8 = sb("HM8", [128, 1], F32)
        DUPB = sb("DUPB", [128, 2, 128], BF16)
        NEG8 = sb("NEG8", [128, 1], F32)
        IDB = sb("IDB", [128, 128], BF16)
        BONES = sb("BONES", [128, 128], BF16)
        ONESB = sb("ONESB", [128, 128], BF16)
        r_const = Reg()
        PS = es.enter_context(nc.psum_tensor("PS", [128, 8, 512], F32))
        r_ps = [Reg() for _ in range(8)]
        bank_i = [0]

        reserved = set()

        def bank(reserve=False):
            while True:
                i = bank_i[0] % 8
                bank_i[0] += 1
                if i not in reserved:
                    break
            if reserve:
                reserved.add(i)
            return r_ps[i], PS[:, i, :]

        with contextlib.ExitStack() as es2:
            def sb2(name, shape, dt=F32):
                return es2.enter_context(nc.sbuf_tensor(name, list(shape), dt))
            ARE = sb2("ARE", [128, 32]); AIM = sb2("AIM", [128, 32]); LDT = sb2("LDT", [128, 32])
            DT = sb2("DT", [128, 32]); ARD = sb2("ARD", [128, 32]); AID = sb2("AID", [128, 32])
            MV = sb2("MV", [128, NM, 32])
            MAG = sb2("MAG", [128, NM, 32])
            ANG = sb2("ANG", [128, 2, NM, 32])
            SC = sb2("SC", [128, 2, NM, 32])
            PWR = sb2("PWR", [128, NM, 32]); PWI = sb2("PWI", [128, NM, 32])
            SM = [sb2("SM%d" % i, [128, 32]) for i in range(10)]
            BRE = sb2("BRE", [128, 32, 16]); BIM = sb2("BIM", [128, 32, 16])
            BBR = sb2("BBR", [128, 32, 16]); BBI = sb2("BBI", [128, 32, 16])
            CR = sb2("CR", [128, 32, 16]); CI = sb2("CI", [128, 32, 16])
            T1 = sb2("T1", [128, 32, 8, 16]); T2 = T1
            VR = sb2("VR", [128, 32, 8, 16]); VI = sb2("VI", [128, 32, 8, 16])
            WR = VR; WI = VI
            ER = sb2("ER", [128, 32, 8, 16]); EI = sb2("EI", [128, 32, 8, 16])
            KIv = ER[:].bitcast(I32).rearrange("p a b c -> p (a b c)")[:, 0:2 * NM * 32].rearrange("p (s m r) -> p s m r", s=2, m=NM)
            KFv = VR[:].rearrange("p a b c -> p (a b c)")[:, 0:2 * NM * 32].rearrange("p (s m r) -> p s m r", s=2, m=NM)
            IDF = sb2("IDF", [128, 128]); MASK = sb2("MASK", [128, 128]); BONF = sb2("BONF", [128, 128])
            DROW = sb2("DROW", [128, 64, 16])
            KTB = sb2("KTB", [128, 64, 128], BF16)
            MV2 = KTB[:].bitcast(F32).rearrange("p g c -> p (g c)")[:, 0:2048].rearrange("p (m r) -> p m r", m=64)
            WTB = KTB
            ETB = sb2("ETB", [128, 64, 128], BF16)
            TMPK = sb2("TMPK", [128, 4, 8, 16])
            RBA = sb2("RBA", [33, 16]); OH = sb2("OH", [33, 512])
            EXTV = SC[:].rearrange("p a m r -> p (a m r)")[0:16, 0:512]
            SKV = sb2("SKV", [128, 16])
            ANTF = sb2("ANTF", [128, 128]); ANTB = sb2("ANTB", [128, 128], BF16)
            DUPF = sb2("DUPF", [128, 2, 128])
            r = {n: Reg() for n in ["in", "dt", "mag", "ang", "ki", "kf", "cm", "sc", "pw", "sm", "bb", "t1", "t2", "v", "w", "e",
                                    "ktb", "wtb", "etb", "tmpk", "ext", "biasf", "extd", "wtd", "ktd", "etd", "pers"]}

            ld = []

            def L(out_ap, in_ap):
                ld.append(S.dma("sp", lambda e, o=out_ap, i=in_ap: e.dma_start(out=o, in_=i, allow_slow_non_contiguous=True),
                                "misc", writes=[r["in"]]))
            for gi in range(2):
                hs = slice(gi * 64, (gi + 1) * 64)
                L(ARE[hs, :], dap(a_re, gi * 64, [(1, 64), (128, 32)]))
                L(AIM[hs, :], dap(a_im, gi * 64, [(1, 64), (128, 32)]))
                L(LDT[hs, :], dap(log_dt, gi, [(0, 64), (2, 32)]))
                L(BRE[hs, :, :], dap(b_re, gi * 1024, [(16, 64), (2048, 32), (1, 16)]))
                L(BIM[hs, :, :], dap(b_im, gi * 1024, [(16, 64), (2048, 32), (1, 16)]))
                for pr in range(32):
                    L(CR[hs, pr, :], dap(c_re, gi * 1024 + pr * 2048, [(1, 64), (64, 16)]))
                    L(CI[hs, pr, :], dap(c_im, gi * 1024 + pr * 2048, [(1, 64), (64, 16)]))
            L(MV[:], c_mv.rearrange("p (m r) -> p m r", m=NM))
            L(MV2, c_mv2.rearrange("p (m r) -> p m r", m=64))
            L(IDF[:], c_ident)
            L(MASK[:], c_mask)
            L(BONF[:], c_bones)
            L(ANTF[:], c_anti)
            L(DUPF[:], c_dup.rearrange("p (c m) -> p c m", c=2))
            L(DROW[:], dap(ssm_d, 0, [(0, 128), (16, 64), (1, 16)]))
            L(RBA[0:32, :], rel_bias)
            L(OH[:], c_oh)
            L(G1[:], dap(ln1_g, 0, [(1, 128), (128, 16)]))
            L(G2[:], dap(ln2_g, 0, [(1, 128), (128, 16)]))
            L(GA[:], dap(attn_out_g, 0, [(1, 128), (128, 8)]))
            L(GS[:], dap(ssm_out_g, 0, [(1, 128), (128, 8)]))
            for h2 in range(2):
                L(QG[h2 * 64:(h2 + 1) * 64, :], dap(q_norm_g, 0, [(1, 64), (1, 1)]))
                L(KG[h2 * 64:(h2 + 1) * 64, :], dap(k_norm_g, 0, [(1, 64), (1, 1)]))
            L(SKV[:], dap(sinks, 0, [(0, 128), (1, 16)]))
            L(HM8[:], hm8)

            V_ = "dve"
            rin = [r["in"]]
            S.op(V_, lambda e: e.memset(NEG8[:], -SHIFT), writes=[r_const])
            S.op(V_, lambda e: e.memset(ONESB[:], 1.0), writes=[r_const])
            S.op(V_, lambda e: e.tensor_copy(out=IDB[:], in_=IDF[:]), reads=rin, writes=[r_const])
            S.op(V_, lambda e: e.tensor_copy(out=BONES[:], in_=BONF[:]), reads=rin, writes=[r_const])
            S.op(V_, lambda e: e.tensor_copy(out=DUPB[:], in_=DUPF[:]), reads=rin, writes=[r_const])
            S.op(V_, lambda e: e.memset(RBA[32:33, :], 1.0), reads=rin, writes=[r["in"]])
            S.op("act", lambda e: e.activation(out=ESK[:], in_=SKV[:], func=AF.Exp, bias=NEG8[:, 0:1], scale=1.0),
                 reads=rin + [r_const], writes=[r["pers"]])
            S.op("act", lambda e: e.activation(out=DT[:], in_=LDT[:], func=AF.Exp), reads=rin, writes=[r["dt"]])
            S.op(V_, lambda e: e.tensor_tensor(out=ARD[:], in0=ARE[:], in1=DT[:], op=ALU.mult), reads=rin + [r["dt"]], writes=[r["sm"]])
            S.op(V_, lambda e: e.tensor_tensor(out=AID[:], in0=AIM[:], in1=DT[:], op=ALU.mult), reads=rin + [r["dt"]], writes=[r["sm"]])

            def bc_m(t):
                return t[:].unsqueeze(1).broadcast_to([128, NM, 32])
            S.op(V_, lambda e: e.tensor_tensor(out=MAG[:], in0=MV[:], in1=bc_m(ARD), op=ALU.mult), reads=rin + [r["sm"]], writes=[r["mag"]])
            S.op("act", lambda e: e.activation(out=MAG[:], in_=MAG[:], func=AF.Exp), reads=[r["mag"]], writes=[r["mag"]])
            S.op(V_, lambda e: e.tensor_tensor(out=ANG[:, 0], in0=MV[:], in1=bc_m(AID), op=ALU.mult), reads=rin + [r["sm"]], writes=[r["ang"]])
            S.op(V_, lambda e: e.tensor_scalar(out=ANG[:, 0], in0=ANG[:, 0], scalar1=1.0 / (2 * math.pi), scalar2=None, op0=ALU.mult),
                 reads=[r["ang"]], writes=[r["ang"]])
            S.op(V_, lambda e: e.tensor_scalar(out=ANG[:, 1], in0=ANG[:, 0], scalar1=0.25, scalar2=None, op0=ALU.add),
                 reads=[r["ang"]], writes=[r["ang"]])
            S.op(V_, lambda e: e.tensor_copy(out=KIv, in_=ANG[:]), reads=[r["ang"]], writes=[r["e"]])
            S.op(V_, lambda e: e.tensor_copy(out=KFv, in_=KIv), reads=[r["e"]], writes=[r["v"]])
            S.op(V_, lambda e: e.tensor_tensor(out=ANG[:], in0=ANG[:], in1=KFv, op=ALU.subtract), reads=[r["ang"], r["v"]], writes=[r["ang"]])
            S.op(V_, lambda e: e.tensor_scalar(out=KFv, in0=ANG[:], scalar1=0.5, scalar2=None, op0=ALU.is_gt), reads=[r["ang"]], writes=[r["v"]])
            S.op(V_, lambda e: e.tensor_tensor(out=ANG[:], in0=ANG[:], in1=KFv, op=ALU.subtract), reads=[r["ang"], r["v"]], writes=[r["ang"]])
            S.op(V_, lambda e: e.tensor_scalar(out=KFv, in0=ANG[:], scalar1=-0.5, scalar2=None, op0=ALU.is_lt), reads=[r["ang"]], writes=[r["v"]])
            S.op(V_, lambda e: e.tensor_tensor(out=ANG[:], in0=ANG[:], in1=KFv, op=ALU.add), reads=[r["ang"], r["v"]], writes=[r["ang"]])
            S.op("act", lambda e: e.activation(out=SC[:], in_=ANG[:], func=AF.Sin, scale=6.283185), reads=[r["ang"]], writes=[r["sc"]])
            S.op(V_, lambda e: e.tensor_tensor(out=PWR[:], in0=MAG[:], in1=SC[:, 1], op=ALU.mult), reads=[r["mag"], r["sc"]], writes=[r["pw"]])
            S.op(V_, lambda e: e.tensor_tensor(out=PWI[:], in0=MAG[:], in1=SC[:, 0], op=ALU.mult), reads=[r["mag"], r["sc"]], writes=[r["pw"]])
            S.op(V_, lambda e: e.tensor_copy(out=RHO[:], in_=MAG[:, 15, :]), reads=[r["mag"]], writes=[r["pers"]])
            A2 = T1[:].rearrange("p a b c -> p (a b c)").rearrange("p (s j r) -> p s j r", s=2, j=64)
            K2 = ER[:].bitcast(I32).rearrange("p a b c -> p (a b c)").rearrange("p (s j r) -> p s j r", s=2, j=64)
            F2 = VR[:].rearrange("p a b c -> p (a b c)").rearrange("p (s j r) -> p s j r", s=2, j=64)
            S2 = VI[:].rearrange("p a b c -> p (a b c)").rearrange("p (s j r) -> p s j r", s=2, j=64)
            ra, rk, rf = [r["t1"]], [r["e"]], [r["v"]]
            S.op(V_, lambda e: e.tensor_tensor(out=A2[:, 0], in0=MV2, in1=AID[:].unsqueeze(1).broadcast_to([128, 64, 32]), op=ALU.mult),
                 reads=rin + [r["sm"], r["ktb"]], writes=ra)
            S.op(V_, lambda e: e.tensor_scalar(out=A2[:, 0], in0=A2[:, 0], scalar1=1.0 / (2 * math.pi), scalar2=None, op0=ALU.mult), reads=ra, writes=ra)
            S.op(V_, lambda e: e.tensor_scalar(out=A2[:, 1], in0=A2[:, 0], scalar1=0.25, scalar2=None, op0=ALU.add), reads=ra, writes=ra)
            S.op(V_, lambda e: e.tensor_copy(out=K2, in_=A2), reads=ra, writes=rk)
            S.op(V_, lambda e: e.tensor_copy(out=F2, in_=K2), reads=rk, writes=rf)
            S.op(V_, lambda e: e.tensor_tensor(out=A2, in0=A2, in1=F2, op=ALU.subtract), reads=ra + rf, writes=ra)
            S.op(V_, lambda e: e.tensor_scalar(out=F2, in0=A2, scalar1=0.5, scalar2=None, op0=ALU.is_gt), reads=ra, writes=rf)
            S.op(V_, lambda e: e.tensor_tensor(out=A2, in0=A2, in1=F2, op=ALU.subtract), reads=ra + rf, writes=ra)
            S.op(V_, lambda e: e.tensor_scalar(out=F2, in0=A2, scalar1=-0.5, scalar2=None, op0=ALU.is_lt), reads=ra, writes=rf)
            S.op(V_, lambda e: e.tensor_tensor(out=A2, in0=A2, in1=F2, op=ALU.add), reads=ra + rf, writes=ra)
            S.op("act", lambda e: e.activation(out=S2, in_=A2, func=AF.Sin, scale=6.283185), reads=ra, writes=rf)
            S.op(V_, lambda e: e.tensor_copy(out=ROTS[:].rearrange("p r j -> p j r"), in_=S2[:, 0]), reads=rf, writes=[r["pers"]])
            S.op(V_, lambda e: e.tensor_copy(out=ROTC[:].rearrange("p r j -> p j r"), in_=S2[:, 1]), reads=rf, writes=[r["pers"]])
            nr, ni, den, rden, t0, t1_, fr, fi = SM[0], SM[1], SM[2], SM[3], SM[4], SM[5], SM[6], SM[7]
            rs = [r["sm"]]
            rp = [r["pw"]]

            def tt(o, a, b, op, reads, writes):
                S.op(V_, lambda e: e.tensor_tensor(out=o, in0=a, in1=b, op=op), reads=reads, writes=writes)
            S.op(V_, lambda e: e.tensor_scalar(out=nr[:], in0=PWR[:, 8, :], scalar1=-1.0, scalar2=None, op0=ALU.add), reads=rp, writes=rs)
            tt(den[:], ARE[:], ARE[:], ALU.mult, rin, rs)
            tt(t0[:], AIM[:], AIM[:], ALU.mult, rin, rs)
            tt(den[:], den[:], t0[:], ALU.add, rs, rs)
            S.op(V_, lambda e: e.reciprocal(out=rden[:], in_=den[:]), reads=rs, writes=rs)
            tt(t0[:], nr[:], ARE[:], ALU.mult, rs + rin, rs)
            tt(t1_[:], PWI[:, 8, :], AIM[:], ALU.mult, rp + rin, rs)
            tt(t0[:], t0[:], t1_[:], ALU.add, rs, rs)
            tt(fr[:], t0[:], rden[:], ALU.mult, rs, rs)
            tt(t0[:], PWI[:, 8, :], ARE[:], ALU.mult, rp + rin, rs)
            tt(t1_[:], nr[:], AIM[:], ALU.mult, rs + rin, rs)
            tt(t0[:], t0[:], t1_[:], ALU.subtract, rs, rs)
            tt(fi[:], t0[:], rden[:], ALU.mult, rs, rs)

            def bq(t):
                return t[:].unsqueeze(2).broadcast_to([128, 32, 16])
            rb = [r["bb"]]
            tt(BBR[:], BRE[:], bq(fr), ALU.mult, rin + rs, rb)
            tt(T1[:, :, 0, :], BIM[:], bq(fi), ALU.mult, rin + rs, [r["t1"]])
            tt(BBR[:], BBR[:], T1[:, :, 0, :], ALU.subtract, rb + [r["t1"]], rb)
            tt(BBI[:], BIM[:], bq(fr), ALU.mult, rin + rs, rb)
            tt(T1[:, :, 0, :], BRE[:], bq(fi), ALU.mult, rin + rs, [r["t1"]])
            tt(BBI[:], BBI[:], T1[:, :, 0, :], ALU.add, rb + [r["t1"]], rb)

            def pw8(t, i0):
                return t[:, i0:i0 + 8, :].rearrange("p s r -> p r s").unsqueeze(3).broadcast_to([128, 32, 8, 16])

            def x8(t):
                return t[:].unsqueeze(2).broadcast_to([128, 32, 8, 16])

            def cmul(OR, OI, i0, XR, XI, rx, ro, neg_im=False):
                tt(OR[:], pw8(PWR, i0), x8(XR), ALU.mult, rp + rx, ro)
                tt(T1[:], pw8(PWI, i0), x8(XI), ALU.mult, rp + rx, [r["t1"]])
                tt(OR[:], OR[:], T1[:], ALU.subtract, ro + [r["t1"]], ro)
                tt(OI[:], pw8(PWR, i0), x8(XI), ALU.mult, rp + rx, ro)
                tt(T2[:], pw8(PWI, i0), x8(XR), ALU.mult, rp + rx, [r["t1"]])
                tt(OI[:], OI[:], T2[:], ALU.add, ro + [r["t1"]], ro)
                if neg_im:
                    S.op(V_, lambda e: e.tensor_scalar(out=OI[:], in0=OI[:], scalar1=-1.0, scalar2=None, op0=ALU.mult), reads=ro, writes=ro)
            cmul(VR, VI, 0, BBR, BBI, rb, [r["v"]])
            cmul(ER, EI, 8, CR, CI, rin, [r["e"]], neg_im=True)

            for pq in range(8):
                banks = [bank(), bank()]
                for gi in range(2):
                    hs = slice(gi * 64, (gi + 1) * 64)
                    rb_, pb_ = banks[gi]
                    for p4 in range(4):
                        pr = pq * 4 + p4
                        o = pb_[:, p4 * 128:(p4 + 1) * 128]
                        S.op("pe", lambda e, o=o, hs=hs, pr=pr: e.matmul(o, VR[hs, pr].rearrange("p s q -> p (s q)"),
                                                                         ER[hs, pr].rearrange("p s q -> p (s q)"), start=True, stop=False),
                             reads=[r["v"], r["e"]], writes=[rb_], sig=False)
                        S.op("pe", lambda e, o=o, hs=hs, pr=pr: e.matmul(o, VI[hs, pr].rearrange("p s q -> p (s q)"),
                                                                         EI[hs, pr].rearrange("p s q -> p (s q)"), start=False, stop=True),
                             reads=[r["v"], r["e"]], writes=[rb_], sig=(p4 == 3))
                    gsel = slice(2 * pq * 4 + gi, 2 * (pq * 4 + 4), 2)
                    S.op(V_, lambda e, gsel=gsel: e.tensor_tensor(
                        out=TMPK[:], in0=IDF[:].rearrange("p (t q) -> p t q", t=8).unsqueeze(1).broadcast_to([128, 4, 8, 16]),
                        in1=DROW[:, gsel, :].unsqueeze(2).broadcast_to([128, 4, 8, 16]), op=ALU.mult),
                        reads=rin, writes=[r["tmpk"]])
                    S.op(V_, lambda e, pb_=pb_: e.tensor_tensor(
                        out=pb_.rearrange("p (g c) -> p g c", g=4), in0=pb_.rearrange("p (g c) -> p g c", g=4),
                        in1=MASK[:].unsqueeze(1).broadcast_to([128, 4, 128]), op=ALU.mult), reads=[rb_] + rin, writes=[rb_])
                    S.op(V_, lambda e, pb_=pb_, gsel=gsel: e.tensor_tensor(
                        out=KTB[:, gsel, :], in0=pb_.rearrange("p (g c) -> p g c", g=4),
                        in1=TMPK[:].rearrange("p g t q -> p g (t q)"), op=ALU.add), reads=[rb_, r["tmpk"]], writes=[r["ktb"]])
            tk1 = S.dma("sp", lambda e: e.dma_start(out=kt_d, in_=KTB[:].rearrange("p g c -> p (g c)")), "st", reads=[r["ktb"]], writes=[r["ktd"]])
            cmul(WR, WI, 16, BBR, BBI, rb, [r["v"]])
            for c, WX in enumerate((WR, WI)):
                for pq in range(8):
                    rb_, pb_ = bank()
                    for p4 in range(4):
                        pr = pq * 4 + p4
                        S.op("pe", lambda e, pb_=pb_, p4=p4, pr=pr, WX=WX: e.transpose(
                            pb_[:, p4 * 128:(p4 + 1) * 128], WX[:, pr].rearrange("p s q -> p (s q)"), IDF[:]),
                            reads=[r["v"]] + rin, writes=[rb_], sig=(p4 == 3))
                    S.op("act", lambda e, pb_=pb_, pq=pq, c=c: e.activation(
                        out=WTB[:, pq * 8:(pq + 1) * 8, c * 64:(c + 1) * 64],
                        in_=pb_.rearrange("p (g n) -> p g n", g=8), func=AF.Copy), reads=[rb_], writes=[r["ktb"]])
            tk2 = S.dma("sp", lambda e: e.dma_start(out=wt_d, in_=WTB[:].rearrange("p g c -> p (g c)")), "st", reads=[r["ktb"]], writes=[r["wtd"]])
            tk3s = []
            for c, EX in enumerate((ER, EI)):
                S.op("pool", lambda e: e.memset(ETB[:], 0.0), writes=[r["etb"]])
                for gi in range(2):
                    hs = slice(gi * 64, (gi + 1) * 64)
                    S.op("act", lambda e, hs=hs, gi=gi, EX=EX: e.activation(
                        out=ETB[hs, gi::2, :], in_=EX[hs].rearrange("p r t q -> p r (t q)"), func=AF.Copy),
                        reads=[r["e"]], writes=[r["etb"]])
                tk3s.append(S.dma("sp", lambda e, c=c: e.dma_start(out=et_d.rearrange("p (g c x) -> p g c x", g=64, c=2)[:, :, c, :], in_=ETB[:]),
                                  "st", reads=[r["etb"]], writes=[r["etd"]]))
            rb_, pb_ = bank()
            S.op("pe", lambda e: e.matmul(pb_[0:16, :], RBA[:], OH[:], start=True, stop=True), reads=rin, writes=[rb_])
            S.op("act", lambda e: e.activation(out=EXTV, in_=pb_[0:16, :], func=AF.Copy), reads=[rb_, r["sc"]], writes=[r["sc"]])
            tk4 = S.dma("sp", lambda e: e.dma_start(out=ext_d, in_=EXTV), "st", reads=[r["sc"]], writes=[r["extd"]])
            BREV = T1[:].bitcast(BF16).rearrange("p a b c -> p (a b c)")[:, 0:4096].rearrange("p (h a i) -> p h a i", h=16, a=2)
            for h in range(16):
                S.dma("pool", lambda e, h=h: e.dma_start(out=BREV[:, h, :, :], in_=dap(ext_d, h * 512, [(1, 128), (256, 2), (1, 128)])),
                      "wd", reads=[r["extd"]], writes=[r["biasf"], r["t1"]])
            S.op(V_, lambda e: e.tensor_copy(out=ANTB[:], in_=ANTF[:]), reads=rin, writes=[r["tmpk"]])
            for h4 in range(8):
                rb_, pb_ = bank()
                S.op("pe", lambda e, pb_=pb_, h4=h4: e.matmul(pb_, ANTB[:], BREV[:, h4 * 2:(h4 + 1) * 2].rearrange("p h a i -> p (h a i)"), start=True, stop=True),
                     reads=[r["biasf"], r["tmpk"]], writes=[rb_])
                S.op("act", lambda e, pb_=pb_, h4=h4: e.activation(out=BIAS[:, h4 * 2:(h4 + 1) * 2].rearrange("p h a i -> p (h a i)"), in_=pb_, func=AF.Copy),
                     reads=[rb_], writes=[r["pers"]])
            for tk in (tk1, tk2, tk4) + tuple(tk3s) + tuple(ld):
                S.wait_tok("sp", tk)
            if not do_setup:
                for k_ in S.ops:
                    S.ops[k_] = []
            else:
                with nc.Block() as block:
                    S.replay(block)

        X = sb("X", [128, 4, D], F32)
        HT = sb("HT", [128, 16, NT], BF16)
        ACTB = sb("ACTB", [128, 12, NT], BF16)
        WB = sb("WB", [128, 3, 16, 256], BF16)
        WD = sb("WD", [128, 6, 512], BF16)
        TB = sb("TB", [128, 2, 8, 4, 128], BF16)
        QT = sb("QT", [128, 8, NT], BF16)
        ST = QT
        KT2 = sb("KT2", [128, 2, 4, NT], BF16)
        KN = sb("KN", [128, 2, NT], BF16)
        VA = sb("VA", [128, 2, 4, 4, 66], BF16)
        YA = sb("YA", [128, 1, 1024], F32)
        HN = sb("HN", [128, 2, D], BF16)
        SS = sb("SS", [128, 8], F32)
        RSTD = sb("RSTD", [128, 8], F32)
        UQ = sb("UQ", [64, 16, 8, 16], BF16)
        UT = sb("UT", [128, 16, 64], BF16)
        SA = sb("SA", [128, 2, 8, 64], F32)
        SB_ = sb("SB", [128, 2, 8, 64], F32)
        HIN = sb("HIN", [128, 2, 32], F32)
        HB = sb("HB", [128, 2, 8, 64], BF16)
        TA = sb("TA", [128, 2, 8, 64], F32)
        YS = sb("YS", [64, 2, 8, 128], BF16)
        GX = sb("GX", [64, 1, 3, 512], F32)
        YT = sb("YT", [128, 8, 8, 64], BF16)
        SQ = sb("SQ", [128, 2, NT], BF16)
        SIG = sb("SIG", [128, 2, NT], BF16)
        RB = sb("RB", [128, 1, NT], F32)
        LG = sb("LG", [128, 2, 256], F32)
        PT = sb("PT", [128, 2, 256], BF16)
        DEN = sb("DEN", [128, 2, 4], F32)

        R = {}

        def rg(name):
            if name not in R:
                R[name] = Reg()
            return R[name]
        rc = [r_const]
        sub_regs = [rg("X%d" % i) for i in range(4)]

        S.op("dve", lambda e: e.memset(VA[:], 1.0), writes=[rg("VA0"), rg("VA1")])
        S.op("dve", lambda e: e.memset(KT2[:], 0.0), writes=[rg("KT0"), rg("KT1")])
        S.op("dve", lambda e: e.memset(HIN[:], 0.0), writes=[rg("HIN")])

        wslot = [0]

        def load_wblock(src_ap_fn_list):
            s_ = wslot[0] % 3
            wslot[0] += 1
            reg = rg("WB%d" % s_)
            for ov, ia in src_ap_fn_list:
                S.dma("pool", lambda e, ov=ov, ia=ia, s_=s_: e.dma_start(out=ov(WB[:, s_]), in_=ia, allow_slow_non_contiguous=True),
                      "wb%d" % s_, writes=[reg])
            return s_, reg

        def wcols(W, ncols_total, c0, ncol, krows=16):
            return dap(W, c0, [(ncols_total, 128), (128 * ncols_total, krows), (1, ncol)])

        def norm_to_HT(G, xsrc_regs):
            for sub in range(4):
                S.op("act", lambda e, sub=sub: e.activation(out=ACTB[:, 0:4, :].rearrange("p a b -> p (a b)"), in_=X[:, sub, :], func=AF.Square, accum_out=SS[:, sub:sub + 1]),
                     reads=[xsrc_regs[sub]], writes=[rg("ACT0"), rg("ACT1"), rg("ACT2"), rg("ACT3"), rg("SS")])
            S.op("dve", lambda e: e.tensor_scalar(out=RSTD[:, 0:4], in0=SS[:, 0:4], scalar1=1.0 / D, scalar2=EPS, op0=ALU.mult, op1=ALU.add),
                 reads=[rg("SS")], writes=[rg("RSTD")])
            S.op("act", lambda e: e.activation(out=RSTD[:, 0:4], in_=RSTD[:, 0:4], func=AF.Sqrt), reads=[rg("RSTD")], writes=[rg("RSTD")])
            S.op("dve", lambda e: e.reciprocal(out=RSTD[:, 0:4], in_=RSTD[:, 0:4]), reads=[rg("RSTD")], writes=[rg("RSTD")])
            for sub in range(4):
                hb = sub % 2
                S.op("act", lambda e, sub=sub, hb=hb: e.activation(out=HN[:, hb, :], in_=X[:, sub, :], func=AF.Copy, scale=RSTD[:, sub:sub + 1]),
                     reads=[xsrc_regs[sub], rg("RSTD")], writes=[rg("HN%d" % hb)])
                for kh in range(2):
                    rb_, pb_ = bank()
                    pbb = pb_.bitcast(BF16)
                    for k8 in range(8):
                        k = kh * 8 + k8
                        S.op("pe", lambda e, pbb=pbb, k8=k8, k=k, hb=hb: e.transpose(pbb[:, k8 * 128:(k8 + 1) * 128],
                                                                                     HN[:, hb, k * 128:(k + 1) * 128], IDB[:]),
                             reads=[rg("HN%d" % hb)] + rc, writes=[rb_], sig=(k8 == 7))
                    S.op("dve", lambda e, pbb=pbb, kh=kh, sub=sub: e.tensor_tensor(
                        out=HT[:, kh * 8:(kh + 1) * 8, sub * 128:(sub + 1) * 128], in0=pbb.rearrange("p (k t) -> p k t", k=8),
                        in1=G[:, kh * 8:(kh + 1) * 8].unsqueeze(2).broadcast_to([128, 8, 128]), op=ALU.mult),
                        reads=[rb_] + rc, writes=[rg("HT")])

        def qk_norm(pb_, rb_, GV, out_ap, out_reg):
            S.op("act", lambda e: e.activation(out=SQ[:, 0, :], in_=pb_, func=AF.Square), reads=[rb_], writes=[rg("SQ0")])
            rb2, pb2 = bank()
            S.op("pe", lambda e: e.matmul(pb2, BONES[:], SQ[:, 0, :], start=True, stop=True), reads=[rg("SQ0")] + rc, writes=[rb2])
            S.op("act", lambda e: e.activation(out=RB[:, 0, :], in_=pb2, func=AF.Sqrt, scale=1.0 / 64, bias=EPS), reads=[rb2], writes=[rg("RB0")])
            S.op("dve", lambda e: e.reciprocal(out=RB[:, 0, :], in_=RB[:, 0, :]), reads=[rg("RB0")], writes=[rg("RB0")])
            S.op("dve", lambda e: e.scalar_tensor_tensor(out=out_ap, in0=pb_, scalar=GV[:, 0:1], in1=RB[:, 0, :], op0=ALU.mult, op1=ALU.mult),
                 reads=[rb_, rg("RB0")] + rc, writes=[out_reg])

        ssm_batch = [0]

        def ssm_ublock(ub, do_y):
            s_, wreg = load_wblock([(lambda sl: sl, wcols(w_in, 2560, 1536 + ub * 256, 256))])
            tsl = []
            for ch in range(2):
                g0 = ub * 16 + ch * 8
                ts_ = ssm_batch[0] % 2
                ssm_batch[0] += 1
                treg = rg("TB%d" % ts_)
                S.dma("sp", lambda e, ts_=ts_, g0=g0: e.dma_start(out=TB[:, ts_, :, 0, :], in_=wt_d[:, g0 * 128:(g0 + 8) * 128].rearrange("p (g c) -> p g c", g=8)),
                      "tb%d" % ts_, writes=[treg])
                S.dma("sp", lambda e, ts_=ts_, g0=g0: e.dma_start(out=TB[:, ts_, :, 1, :], in_=kt_d[:, g0 * 128:(g0 + 8) * 128].rearrange("p (g c) -> p g c", g=8)),
                      "tb%d" % ts_, writes=[treg])
                S.dma("sp", lambda e, ts_=ts_, g0=g0: e.dma_start(out=TB[:, ts_, :, 2:4, :], in_=et_d[:, g0 * 256:(g0 + 8) * 256].rearrange("p (g c x) -> p g c x", g=8, c=2)),
                      "tb%d" % ts_, writes=[treg])
                tsl.append((ts_, treg))
            for sp_ in range(4):
                rb_, pb_ = bank()
                for s2 in range(2):
                    s = sp_ * 2 + s2
                    for k in range(16):
                        S.op("pe", lambda e, pb_=pb_, s2=s2, s=s, k=k, s_=s_: e.matmul(
                            pb_[0:64, s2 * 256:(s2 + 1) * 256], HT[:, k, s:NT:8], WB[:, s_, k, :], start=(k == 0), stop=(k == 15)),
                            reads=[rg("HT"), wreg], writes=[rb_], sig=(k == 15 and s2 == 1))
                S.op("act", lambda e, pb_=pb_, sp_=sp_: e.activation(
                    out=UQ[:, :, sp_ * 2:(sp_ + 1) * 2, :].rearrange("p g s q -> p s g q"),
                    in_=pb_[0:64, :].rearrange("p (s g q) -> p s g q", s=2, g=16), func=AF.Copy),
                    reads=[rb_], writes=[rg("UQ")])
            for gh in range(2):
                rb_, pb_ = bank()
                pbb = pb_.bitcast(BF16)
                for g8 in range(8):
                    g = gh * 8 + g8
                    S.op("pe", lambda e, pbb=pbb, g8=g8, g=g: e.transpose(pbb[:, g8 * 64:(g8 + 1) * 64],
                                                                          UQ[:, g].rearrange("p s q -> p (s q)"), IDB[0:64, 0:64]),
                         reads=[rg("UQ")] + rc, writes=[rb_], sig=(g8 == 7))
                S.op("dve", lambda e, pbb=pbb, gh=gh: e.tensor_copy(out=UT[:, gh * 8:(gh + 1) * 8, :],
                                                                    in_=pbb[:, 0:512].rearrange("p (g j) -> p g j", g=8)),
                     reads=[rb_], writes=[rg("UT")])
            rbr, pbr = bank()
            rbi, pbi = bank()
            for pr in range(8):
                for gi in range(2):
                    g = 2 * pr + gi
                    ts_, treg = tsl[g // 8]
                    hs = slice(gi * 64, (gi + 1) * 64)
                    last = (pr == 7 and gi == 1)
                    S.op("pe", lambda e, hs=hs, pr=pr, g=g, ts_=ts_: e.matmul(pbr[hs, pr * 64:(pr + 1) * 64], TB[:, ts_, g % 8, 0, 0:64], UT[:, g, :],
                                                                               start=True, stop=True),
                         reads=[rg("UT"), treg], writes=[rbr], sig=False)
                    S.op("pe", lambda e, hs=hs, pr=pr, g=g, ts_=ts_: e.matmul(pbi[hs, pr * 64:(pr + 1) * 64], TB[:, ts_, g % 8, 0, 64:128], UT[:, g, :],
                                                                               start=True, stop=True),
                         reads=[rg("UT"), treg], writes=[rbi], sig=last)
            S.op("act", lambda e: e.activation(out=SA[:, 0], in_=pbr.rearrange("p (r j) -> p r j", r=8), func=AF.Copy), reads=[rbr], writes=[rg("SA"), rg("SC0")])
            S.op("act", lambda e: e.activation(out=SA[:, 1], in_=pbi.rearrange("p (r j) -> p r j", r=8), func=AF.Copy), reads=[rbi], writes=[rg("SA"), rg("SC1")])
            prs = slice(ub * 8, (ub + 1) * 8)
            rH, rSA, rSB = rg("HIN"), rg("SA"), rg("SB")
            Cc, Sn = ROTC[:, prs, :], ROTS[:, prs, :]

            def vtt(o, a_, b_, op, reads, writes, eng="dve"):
                S.op(eng, lambda e: e.tensor_tensor(out=o, in0=a_, in1=b_, op=op), reads=reads, writes=writes)
            vtt(TA[:, 0], Cc, SA[:, 0], ALU.mult, [rSA] + rc, [rg("TA0")])
            vtt(TA[:, 1], Sn, SA[:, 1], ALU.mult, [rSA] + rc, [rg("TA1")])
            vtt(SB_[:, 0], TA[:, 0], TA[:, 1], ALU.add, [rg("TA0"), rg("TA1")], [rg("SB0")])
            vtt(TA[:, 0], Cc, SA[:, 1], ALU.mult, [rSA] + rc, [rg("TA0")])
            vtt(TA[:, 1], Sn, SA[:, 0], ALU.mult, [rSA] + rc, [rg("TA1")])
            vtt(SB_[:, 1], TA[:, 0], TA[:, 1], ALU.subtract, [rg("TA0"), rg("TA1")], [rg("SB1")])
            S.op("dve", lambda e: e.tensor_copy(out=HB[:, :, :, 0], in_=HIN[:, :, prs]), reads=[rH], writes=[rg("HB")])
            for c in range(2):
                for pr in range(8):
                    gp = ub * 8 + pr
                    S.op("dve", lambda e, c=c, pr=pr, gp=gp: e.tensor_tensor_scan(
                        out=SA[:, c, pr, :], data0=RHO[:, gp:gp + 1].broadcast_to([128, 64]), data1=SB_[:, c, pr, :],
                        initial=HIN[:, c, gp:gp + 1], op0=ALU.mult, op1=ALU.add),
                        reads=[rg("SB%d" % c), rH] + rc, writes=[rg("SC%d" % c)])
            vtt(TA[:, 0], Cc, SA[:, 0], ALU.mult, [rg("SC0")] + rc, [rg("TA0")])
            vtt(TA[:, 1], Sn, SA[:, 1], ALU.mult, [rg("SC1")] + rc, [rg("TA1")])
            vtt(SB_[:, 0], TA[:, 0], TA[:, 1], ALU.subtract, [rg("TA0"), rg("TA1")], [rg("SB0")])
            vtt(TA[:, 0], Cc, SA[:, 1], ALU.mult, [rg("SC1")] + rc, [rg("TA0")])
            vtt(TA[:, 1], Sn, SA[:, 0], ALU.mult, [rg("SC0")] + rc, [rg("TA1")])
            vtt(SB_[:, 1], TA[:, 0], TA[:, 1], ALU.add, [rg("TA0"), rg("TA1")], [rg("SB1")])
            S.op("dve", lambda e: e.tensor_copy(out=HIN[:, :, prs], in_=SB_[:, :, :, 63]), reads=[rg("SB0"), rg("SB1")], writes=[rH])
            if not do_y:
                return
            S.op("dve", lambda e: e.tensor_copy(out=HB[:, :, :, 1:64], in_=SB_[:, :, :, 0:63]), reads=[rg("SB0"), rg("SB1")], writes=[rg("HB")])
            for gq in range(4):
                rb_, pb_ = bank()
                for g4 in range(4):
                    g = gq * 4 + g4
                    ts_, treg = tsl[g // 8]
                    o = pb_[0:64, g4 * 128:(g4 + 1) * 128]
                    pr = g // 2
                    S.op("pe", lambda e, o=o, g=g, ts_=ts_: e.matmul(o, UT[:, g, :], TB[:, ts_, g % 8, 1, :], start=True, stop=False),
                         reads=[rg("UT"), treg], writes=[rb_], sig=False)
                    S.op("pe", lambda e, o=o, g=g, ts_=ts_, pr=pr: e.matmul(o, HB[:, 0, pr, :], TB[:, ts_, g % 8, 2, :], start=False, stop=False),
                         reads=[rg("HB"), treg], writes=[rb_], sig=False)
                    S.op("pe", lambda e, o=o, g=g, ts_=ts_, pr=pr: e.matmul(o, HB[:, 1, pr, :], TB[:, ts_, g % 8, 3, :], start=False, stop=True),
                         reads=[rg("HB"), treg], writes=[rb_], sig=(g4 == 3))
                hb = (gq // 2) % 2
                gb = gq % 2
                src = pb_[0:64, :]
                gx = [GX[:, 0, i, :] for i in range(3)]
                rgx = rg("GX0")
                S.op("act", lambda e, src=src, gx=gx: e.activation(out=gx[0], in_=src, func=AF.Square), reads=[rb_], writes=[rgx])
                S.op("dve", lambda e, gx=gx: e.tensor_scalar(out=gx[0], in0=gx[0], scalar1=0.044715, scalar2=1.0, op0=ALU.mult, op1=ALU.add), reads=[rgx], writes=[rgx])
                S.op("dve", lambda e, src=src, gx=gx: e.tensor_tensor(out=gx[1], in0=src, in1=gx[0], op=ALU.mult), reads=[rb_, rgx], writes=[rgx])
                S.op("act", lambda e, gx=gx: e.activation(out=gx[2], in_=gx[1], func=AF.Sigmoid, scale=2.0 * math.sqrt(2.0 / math.pi)), reads=[rgx], writes=[rgx])
                S.op("dve", lambda e, src=src, gx=gx, hb=hb, gb=gb: e.tensor_tensor(
                    out=YS[:, hb, :, gb * 64:(gb + 1) * 64].rearrange("p t (g q) -> p g t q", g=4),
                    in0=src.rearrange("p (g t q) -> p g t q", g=4, t=8), in1=gx[2].rearrange("p (g t q) -> p g t q", g=4, t=8), op=ALU.mult),
                    reads=[rb_, rgx], writes=[rg("YS%d" % hb)])
                if gb == 1:
                    ct = ub * 2 + gq // 2
                    rb2, pb2 = bank()
                    pbb = pb2.bitcast(BF16)
                    for t in range(8):
                        S.op("pe", lambda e, pbb=pbb, t=t, hb=hb: e.transpose(pbb[:, t * 64:(t + 1) * 64], YS[:, hb, t, :], IDB[0:64, 0:64]),
                             reads=[rg("YS%d" % hb)] + rc, writes=[rb2], sig=(t == 7))
                    S.op("act", lambda e, pbb=pbb, ct=ct: e.activation(out=YT[:, ct].rearrange("p t j -> p (t j)"), in_=pbb[:, 0:512], func=AF.Copy),
                         reads=[rb2], writes=[rg("YT")])

        def process_tt(xsrc, tt_i, pred, last_pred, ping):
            for sub in range(4):
                S.dma("sp", lambda e, sub=sub: e.dma_start(out=X[:, sub, :], in_=xsrc[tt_i * NT + sub * 128: tt_i * NT + (sub + 1) * 128, :]),
                      "xl%d" % sub, writes=[sub_regs[sub]])
            main = not pred
            if stage >= 1:
                norm_to_HT(G1, sub_regs)
            if main and stage >= 2:
                for qb in range(4):
                    s_, wreg = load_wblock([(lambda sl: sl, wcols(w_in, 2560, qb * 256, 256))])
                    for m in range(2):
                        rb_, pb_ = bank()
                        for k in range(16):
                            S.op("pe", lambda e, pb_=pb_, k=k, m=m, s_=s_: e.matmul(pb_, WB[:, s_, k, m * 128:(m + 1) * 128], HT[:, k, :],
                                                                                   start=(k == 0), stop=(k == 15)),
                                 reads=[rg("HT"), wreg], writes=[rb_], sig=(k == 15))
                        qk_norm(pb_, rb_, QG, QT[:, qb * 2 + m, :], rg("QT"))
            if (main or last_pred) and stage >= 2:
                s_, wreg = load_wblock([(lambda sl: sl, wcols(w_in, 2560, 1024, 256))])
                for kt in range(2):
                    rb_, pb_ = bank()
                    for k in range(16):
                        S.op("pe", lambda e, pb_=pb_, k=k, kt=kt, s_=s_: e.matmul(pb_, WB[:, s_, k, kt * 128:(kt + 1) * 128], HT[:, k, :],
                                                                               start=(k == 0), stop=(k == 15)),
                             reads=[rg("HT"), wreg], writes=[rb_], sig=(k == 15))
                    qk_norm(pb_, rb_, KG, KN[:, kt, :], rg("KN%d" % kt))
                    for c in range(2):
                        kh = kt * 2 + c
                        rb2, pb2 = bank()
                        S.op("pe", lambda e, pb2=pb2, c=c, kt=kt: e.matmul(pb2, DUPB[:, c, :], KN[:, kt, :], start=True, stop=True),
                             reads=[rg("KN%d" % kt)] + rc, writes=[rb2])
                        S.op("act", lambda e, pb2=pb2, kh=kh: e.activation(out=KT2[:, ping, kh, :], in_=pb2, func=AF.Copy),
                             reads=[rb2], writes=[rg("KT%d" % ping)])
                s_, wreg = load_wblock([(lambda sl: sl, wcols(w_in, 2560, 1280, 256))])
                for sub in range(4):
                    rb_, pb_ = bank()
                    for k in range(16):
                        S.op("pe", lambda e, pb_=pb_, k=k, sub=sub, s_=s_: e.matmul(pb_[:, 0:256], HT[:, k, sub * 128:(sub + 1) * 128], WB[:, s_, k, :],
                                                                                   start=(k == 0), stop=(k == 15)),
                             reads=[rg("HT"), wreg], writes=[rb_], sig=(k == 15))
                    S.op("act", lambda e, pb_=pb_, sub=sub: e.activation(out=VA[:, ping, sub, :, 0:64], in_=pb_[:, 0:256].rearrange("p (h d) -> p h d", h=4),
                                                                         func=AF.Copy), reads=[rb_], writes=[rg("VA%d" % ping)])
            for ub in range(4 if stage >= 3 else 0):
                ssm_ublock(ub, do_y=main)
            if pred:
                return
            for b in range(4 if stage >= 4 else 0):
                yb = 0
                hnb = b % 2
                for kh in range(4):
                    rbo, pbo = bank()
                    for g4 in range(4):
                        h = kh * 4 + g4
                        hp = slice((h % 2) * 64, (h % 2) * 64 + 64)
                        qv = QT[hp, h // 2, b * 128:(b + 1) * 128]
                        if b > 0:
                            kprev = KT2[hp, ping, kh, (b - 1) * 128:b * 128]
                            vprev = VA[:, ping, b - 1, kh, 0:65]
                            rkp, rvp = rg("KT%d" % ping), rg("VA%d" % ping)
                        else:
                            kprev = KT2[hp, 1 - ping, kh, 384:512]
                            vprev = VA[:, 1 - ping, 3, kh, 0:65]
                            rkp, rvp = rg("KT%d" % (1 - ping)), rg("VA%d" % (1 - ping))
                        kcur = KT2[hp, ping, kh, b * 128:(b + 1) * 128]
                        vcur = VA[:, ping, b, kh, 0:65]
                        rbs, pbs = bank()
                        S.op("pe", lambda e, pbs=pbs, kprev=kprev, qv=qv: e.matmul(pbs[:, 0:128], kprev, qv, start=True, stop=True),
                             reads=[rkp, rg("QT")], writes=[rbs], sig=False)
                        S.op("pe", lambda e, pbs=pbs, kcur=kcur, qv=qv: e.matmul(pbs[:, 128:256], kcur, qv, start=True, stop=True),
                             reads=[rg("KT%d" % ping), rg("QT")], writes=[rbs])
                        lb = h % 2
                        S.op("dve", lambda e, pbs=pbs, h=h, lb=lb: e.scalar_tensor_tensor(
                            out=LG[:, lb, :], in0=pbs[:, 0:256], scalar=0.125, in1=BIAS[:, h].rearrange("p a i -> p (a i)"), op0=ALU.mult, op1=ALU.add),
                            reads=[rbs] + rc, writes=[rg("LG%d" % lb)])
                        if b == 0 and tt_i == 0:
                            S.op("act", lambda e, lb=lb: e.activation(out=PT[:, lb, 0:128], in_=LG[:, lb, 0:128], func=AF.Exp, bias=HM8[:, 0:1], scale=1.0),
                                 reads=[rg("LG%d" % lb)] + rc, writes=[rg("PT%d" % lb)])
                            S.op("act", lambda e, lb=lb: e.activation(out=PT[:, lb, 128:256], in_=LG[:, lb, 128:256], func=AF.Exp, bias=NEG8[:, 0:1], scale=1.0),
                                 reads=[rg("LG%d" % lb)] + rc, writes=[rg("PT%d" % lb)])
                        else:
                            S.op("act", lambda e, lb=lb: e.activation(out=PT[:, lb, :], in_=LG[:, lb, :], func=AF.Exp, bias=NEG8[:, 0:1], scale=1.0),
                                 reads=[rg("LG%d" % lb)] + rc, writes=[rg("PT%d" % lb)])
                        o = pbo[:, g4 * 65:(g4 + 1) * 65]
                        S.op("pe", lambda e, o=o, lb=lb, vprev=vprev: e.matmul(o, PT[:, lb, 0:128], vprev, start=True, stop=False),
                             reads=[rg("PT%d" % lb), rvp], writes=[rbo], sig=False)
                        S.op("pe", lambda e, o=o, lb=lb, vcur=vcur: e.matmul(o, PT[:, lb, 128:256], vcur, start=False, stop=True),
                             reads=[rg("PT%d" % lb), rg("VA%d" % ping)], writes=[rbo], sig=(g4 == 3))
                    dn = kh % 2
                    ov = pbo[:, 0:260].rearrange("p (g c) -> p g c", g=4)
                    S.op("dve", lambda e, ov=ov, dn=dn, kh=kh: e.tensor_tensor(out=DEN[:, dn, :], in0=ov[:, :, 64], in1=ESK[:, kh * 4:(kh + 1) * 4], op=ALU.add),
                         reads=[rbo] + rc, writes=[rg("DEN%d" % dn)])
                    S.op("dve", lambda e, dn=dn: e.reciprocal(out=DEN[:, dn, :], in_=DEN[:, dn, :]), reads=[rg("DEN%d" % dn)], writes=[rg("DEN%d" % dn)])
                    S.op("dve", lambda e, ov=ov, dn=dn, kh=kh, yb=yb: e.tensor_tensor(
                        out=YA[:, yb, kh * 256:(kh + 1) * 256].rearrange("p (g d) -> p g d", g=4), in0=ov[:, :, 0:64],
                        in1=DEN[:, dn, :].unsqueeze(2).broadcast_to([128, 4, 64]), op=ALU.mult),
                        reads=[rbo, rg("DEN%d" % dn)], writes=[rg("YA%d" % yb)])
                S.op("act", lambda e, yb=yb, b=b: e.activation(out=ACTB[:, 0:2, :].rearrange("p a b -> p (a b)"), in_=YA[:, yb, :], func=AF.Square, accum_out=SS[:, 4 + b:5 + b]),
                     reads=[rg("YA%d" % yb)], writes=[rg("ACT0"), rg("ACT1"), rg("SSA%d" % b)])
                S.op("dve", lambda e, b=b: e.tensor_scalar(out=RSTD[:, 4 + b:5 + b], in0=SS[:, 4 + b:5 + b], scalar1=1.0 / 1024, scalar2=EPS, op0=ALU.mult, op1=ALU.add),
                     reads=[rg("SSA%d" % b)], writes=[rg("RSA%d" % b)])
                S.op("act", lambda e, b=b: e.activation(out=RSTD[:, 4 + b:5 + b], in_=RSTD[:, 4 + b:5 + b], func=AF.Sqrt), reads=[rg("RSA%d" % b)], writes=[rg("RSA%d" % b)])
                S.op("dve", lambda e, b=b: e.reciprocal(out=RSTD[:, 4 + b:5 + b], in_=RSTD[:, 4 + b:5 + b]), reads=[rg("RSA%d" % b)], writes=[rg("RSA%d" % b)])
                S.op("act", lambda e, yb=yb, b=b, hnb=hnb: e.activation(out=HN[:, hnb, 0:1024], in_=YA[:, yb, :], func=AF.Copy, scale=RSTD[:, 4 + b:5 + b]),
                     reads=[rg("YA%d" % yb), rg("RSA%d" % b)], writes=[rg("HN%d" % hnb)])
                rb_, pb_ = bank()
                pbb = pb_.bitcast(BF16)
                for k8 in range(8):
                    S.op("pe", lambda e, pbb=pbb, k8=k8, hnb=hnb: e.transpose(pbb[:, k8 * 128:(k8 + 1) * 128], HN[:, hnb, k8 * 128:(k8 + 1) * 128], IDB[:]),
                         reads=[rg("HN%d" % hnb)] + rc, writes=[rb_], sig=(k8 == 7))
                S.op("dve", lambda e, pbb=pbb, b=b: e.tensor_tensor(
                    out=HT[:, 0:8, b * 128:(b + 1) * 128], in0=pbb.rearrange("p (k t) -> p k t", k=8),
                    in1=GA[:].unsqueeze(2).broadcast_to([128, 8, 128]), op=ALU.mult), reads=[rb_] + rc, writes=[rg("HT")])
            rbss, pbss = bank(reserve=True)
            for nb in range(4 if stage >= 5 else 0):
                s_, wreg = load_wblock([(lambda sl: sl[:, 0:8, :], wcols(w_glu, 1024, nb * 256, 256, krows=8))])
                for m in range(2):
                    ct = nb * 2 + m
                    rb_, pb_ = bank()
                    for c in range(8):
                        S.op("pe", lambda e, pb_=pb_, c=c, m=m, s_=s_: e.matmul(pb_, WB[:, s_, c, m * 128:(m + 1) * 128], YT[:, c].rearrange("p t j -> p (t j)"),
                                                                               start=(c == 0), stop=(c == 7)),
                             reads=[rg("YT"), wreg], writes=[rb_], sig=(c == 7))
                    sg = ct % 2
                    S.op("act", lambda e, pb_=pb_, sg=sg: e.activation(out=SIG[:, sg, :], in_=pb_, func=AF.Sigmoid), reads=[rb_], writes=[rg("SIG%d" % sg)])
                    S.op("dve", lambda e, ct=ct, sg=sg: e.tensor_tensor(out=ST[:, ct, :], in0=YT[:, ct].rearrange("p t j -> p (t j)"), in1=SIG[:, sg, :], op=ALU.mult),
                         reads=[rg("YT"), rg("SIG%d" % sg)], writes=[rg("QT")])
                    S.op("act", lambda e, ct=ct, sg=sg: e.activation(out=SQ[:, sg, :], in_=ST[:, ct, :], func=AF.Square), reads=[rg("QT")], writes=[rg("SQ%d" % sg)])
                    S.op("pe", lambda e, ct=ct, sg=sg: e.matmul(pbss, ONESB[:], SQ[:, sg, :], start=(ct == 0), stop=(ct == 7)),
                         reads=[rg("SQ%d" % sg)] + rc, writes=[rbss])
            reserved.clear()
            if stage >= 5:
                S.op("act", lambda e: e.activation(out=RB[:, 0, :], in_=pbss, func=AF.Sqrt, scale=1.0 / 1024, bias=EPS), reads=[rbss], writes=[rg("RB0")])
                S.op("dve", lambda e: e.reciprocal(out=RB[:, 0, :], in_=RB[:, 0, :]), reads=[rg("RB0")], writes=[rg("RB0")])
            for ct in range(8 if stage >= 5 else 0):
                S.op("dve", lambda e, ct=ct: e.scalar_tensor_tensor(
                    out=HT[:, 8 + ct, :].rearrange("p (j t) -> p t j", t=8), in0=ST[:, ct, :].rearrange("p (t j) -> p t j", t=8),
                    scalar=GS[:, ct:ct + 1], in1=RB[:, 0, :].rearrange("p (t j) -> p t j", t=8), op0=ALU.mult, op1=ALU.mult),
                    reads=[rg("QT"), rg("RB0")] + rc, writes=[rg("HT")])
            for fb in range(8 if stage >= 6 else 0):
                s_, wreg = load_wblock([(lambda sl: sl, wcols(w_out, D, fb * 256, 256))])
                for sp2 in range(2):
                    rb_, pb_ = bank()
                    for s2 in range(2):
                        sub = sp2 * 2 + s2
                        for k in range(16):
                            S.op("pe", lambda e, pb_=pb_, s2=s2, sub=sub, k=k, s_=s_: e.matmul(
                                pb_[:, s2 * 256:(s2 + 1) * 256], HT[:, k, sub * 128:(sub + 1) * 128], WB[:, s_, k, :], start=(k == 0), stop=(k == 15)),
                                reads=[rg("HT"), wreg], writes=[rb_], sig=(k == 15))
                        S.op("dve", lambda e, pb_=pb_, s2=s2, sub=sub, fb=fb: e.tensor_tensor(
                            out=X[:, sub, fb * 256:(fb + 1) * 256], in0=pb_[:, s2 * 256:(s2 + 1) * 256], in1=X[:, sub, fb * 256:(fb + 1) * 256], op=ALU.add),
                            reads=[rb_, sub_regs[sub]], writes=[sub_regs[sub]])
            if stage >= 7:
                norm_to_HT(G2, sub_regs)
            c0 = 0
            for grp, nch in enumerate(GROUPS_FF if stage >= 7 else []):
                for bl in range(nch // 2):
                    col = (c0 + bl * 2) * 128
                    sg_, wg = load_wblock([(lambda sl: sl, wcols(w_gate, FF, col, 256))])
                    su_, wu = load_wblock([(lambda sl: sl, wcols(w_up, FF, col, 256))])
                    for m in range(2):
                        j = bl * 2 + m
                        rbg, pbg = bank()
                        for k in range(16):
                            S.op("pe", lambda e, pbg=pbg, k=k, m=m, sg_=sg_: e.matmul(pbg, WB[:, sg_, k, m * 128:(m + 1) * 128], HT[:, k, :], start=(k == 0), stop=(k == 15)),
                                 reads=[rg("HT"), wg], writes=[rbg], sig=(k == 15))
                        rbu, pbu = bank()
                        for k in range(16):
                            S.op("pe", lambda e, pbu=pbu, k=k, m=m, su_=su_: e.matmul(pbu, WB[:, su_, k, m * 128:(m + 1) * 128], HT[:, k, :], start=(k == 0), stop=(k == 15)),
                                 reads=[rg("HT"), wu], writes=[rbu], sig=(k == 15))
                        sg = j % 2
                        S.op("act", lambda e, pbg=pbg, sg=sg: e.activation(out=SIG[:, sg, :], in_=pbg, func=AF.Silu), reads=[rbg], writes=[rg("SIG%d" % sg)])
                        S.op("dve", lambda e, pbu=pbu, sg=sg, j=j: e.tensor_tensor(out=ACTB[:, j, :], in0=pbu, in1=SIG[:, sg, :], op=ALU.mult),
                             reads=[rbu, rg("SIG%d" % sg)], writes=[rg("ACT%d" % j)])
                for f in range(4):
                    bk = [bank() for _ in range(4)]
                    for j in range(nch):
                        ws = (wslot_d[0]) % 6
                        wslot_d[0] += 1
                        wreg = rg("WD%d" % ws)
                        S.dma("pool", lambda e, ws=ws, j=j, f=f, c0=c0: e.dma_start(out=WD[:, ws, :], in_=w_down[(c0 + j) * 128:(c0 + j + 1) * 128, f * 512:(f + 1) * 512]),
                              "wd%d" % ws, writes=[wreg])
                        for sub in range(4):
                            S.op("pe", lambda e, sub=sub, j=j, ws=ws, pb_=bk[sub][1]: e.matmul(pb_, ACTB[:, j, sub * 128:(sub + 1) * 128], WD[:, ws, :],
                                                                                              start=(j == 0), stop=(j == nch - 1)),
                                 reads=[rg("ACT%d" % j), wreg], writes=[bk[sub][0]], sig=(j == nch - 1 or sub == 3))
                    for sub in range(4):
                        S.op("dve", lambda e, sub=sub, f=f, pb_=bk[sub][1]: e.tensor_tensor(
                            out=X[:, sub, f * 512:(f + 1) * 512], in0=pb_, in1=X[:, sub, f * 512:(f + 1) * 512], op=ALU.add),
                            reads=[bk[sub][0], sub_regs[sub]], writes=[sub_regs[sub]])
                c0 += nch
            for sub in range(4):
                out_toks.append(S.dma("sp", lambda e, sub=sub: e.dma_start(out=out[tt_i * NT + sub * 128: tt_i * NT + (sub + 1) * 128, :], in_=X[:, sub, :]),
                                      "ot%d" % sub, reads=[sub_regs[sub]]))

        wslot_d = [0]
        out_toks = []
        ping = 0
        if do_pred:
            for t in range(n_tt):
                process_tt(x_pred, t, True, t == n_tt - 1, ping)
            ping = 1 - ping
        for t in range(n_tt):
            process_tt(x_main, t, False, False, ping)
            ping = 1 - ping
        for tk in out_toks:
            S.wait_tok("sp", tk)
        with nc.Block() as block:
            S.replay(block)
    return nc


def _t5_bucket(dist):
    n = np.maximum(dist, 0)
    max_exact = 16
    nf = np.maximum(n, 1).astype(np.float32)
    large = max_exact + (np.log(nf / max_exact) / math.log(128 / max_exact) * (32 - max_exact)).astype(np.int32)
    large = np.minimum(large, 31)
    return np.where(n < max_exact, n, large).astype(np.int32)


def _consts():
    ident = np.eye(128, dtype=np.float32)
    s_idx = np.arange(128) // 16
    mask = (s_idx[:, None] <= s_idx[None, :]).astype(np.float32)
    mv = np.broadcast_to(np.asarray(MS, np.float32)[None, :, None], (128, NM, 32)).reshape(128, NM * 32).copy()
    bones = (s_idx[:, None] // 4 == s_idx[None, :] // 4).astype(np.float32)
    bucket = _t5_bucket(np.arange(128))
    oh = np.zeros((33, 512), np.float32)
    for e in range(255):
        if e < 127:
            oh[bucket[e + 1], e] = 1.0
            oh[32, 256 + e] = NEG
        else:
            oh[32, e] = NEG
            oh[bucket[e - 127], 256 + e] = 1.0
    oh[32, 255] = NEG
    oh[32, 511] = NEG
    dup = np.zeros((128, 2, 128), np.float32)
    for c in range(2):
        for d in range(64):
            dup[c * 64 + d, c, d] = 1.0
            dup[c * 64 + d, c, 64 + d] = 1.0
    mv2 = np.broadcast_to((8.0 * np.arange(1, 65, dtype=np.float32))[None, :, None], (128, 64, 32)).reshape(128, 64 * 32).copy()
    return {"c_mv2": mv2, "c_dup": dup.reshape(128, 256), "c_ident": ident, "c_mask": mask, "c_mv": mv, "c_oh": oh, "c_bones": bones, "c_anti": np.ascontiguousarray(ident[::-1])}


_NC_CACHE = {}


def kernel(**inputs):
    n_tt = int(os.environ.get("MK_NTT", "4"))
    do_pred = os.environ.get("MK_PRED", "1") == "1"
    stage = int(os.environ.get("MK_STAGE", "99"))
    do_setup = os.environ.get("MK_SETUP", "1") == "1"
    key = (n_tt, do_pred, stage, do_setup)
    if key not in _NC_CACHE:
        _NC_CACHE[key] = build(n_tt, do_pred, stage, do_setup)
    nc = _NC_CACHE[key]
    x = np.asarray(inputs["x"], np.float32)
    TOK = n_tt * NT
    shared = {k: np.ascontiguousarray(np.asarray(inputs[k], np.float32)[0]) for k in
              ["ln1_g", "w_in", "q_norm_g", "k_norm_g", "attn_sinks", "ssm_a_re", "ssm_a_im", "ssm_log_dt", "ssm_b_re", "ssm_b_im",
               "ssm_c_re", "ssm_c_im", "w_glu", "attn_out_g", "ssm_out_g", "w_out", "ln2_g", "w_ff_gate", "w_ff_up", "w_ff_down"]}
    shared["ssm_d"] = np.ascontiguousarray(np.asarray(inputs["ssm_d"], np.float32)[0].reshape(-1))
    shared["rel_bias"] = np.ascontiguousarray(np.asarray(inputs["rel_bias"], np.float32))
    shared.update(_consts())
    in_maps = []
    ncores = int(os.environ.get("MK_CORES", "8"))
    for c in range(ncores):
        b, half = c // 2, c % 2
        m = dict(shared)
        m["x_main"] = np.ascontiguousarray(x[b, half * 2048: half * 2048 + TOK])
        if half == 1:
            m["x_pred"] = np.ascontiguousarray(x[b, 2048 - TOK:2048])
            m["hm8"] = np.full((128, 1), -SHIFT, np.float32)
        else:
            m["x_pred"] = np.zeros((TOK, D), np.float32)
            m["hm8"] = np.full((128, 1), NEG - SHIFT, np.float32)
        in_maps.append(m)
    if os.environ.get("MK_TRACE", "0") == "1":
        res = run_bass_kernel_spmd(nc, in_maps, core_ids=list(range(ncores)), trace=True)
        print("EXEC_NS", res.exec_time_ns)
    else:
        res = run_bass_kernel_spmd(nc, in_maps, core_ids=list(range(ncores)))
    outp = np.zeros((4, 4096, D), np.float32)
    for c in range(ncores):
        b, half = c // 2, c % 2
        outp[b, half * 2048: half * 2048 + TOK] = res.results[c]["out"]
    return outp
```

```python
import os
import math
import numpy as np
import ml_dtypes
import concourse.bass as bass
import concourse.mybir as mybir
from concourse.bass_utils import run_bass_kernel_spmd

F32 = mybir.dt.float32
BF16 = mybir.dt.bfloat16
I32 = mybir.dt.int32
ALU = mybir.AluOpType
AF = mybir.ActivationFunctionType

D = 2048
NT = 512
FF = 5632
EPS = 1e-6
NEG = -30000.0
SHIFT = 8.0
MS = [-(s + 1) for s in range(8)] + [t + 1 for t in range(8)] + [7 - s for s in range(8)] + [8, 16, 32, 64, 128, 256]
NM = len(MS)
GROUPS_FF = [12, 10, 12, 10]


class Reg:
    __slots__ = ("w", "r")

    def __init__(self):
        self.w = None
        self.r = {}


class Sched:
    def __init__(self, nc, semh):
        self.nc = nc
        self.semh = semh
        self.names = ["pe", "act", "dve", "pool", "sp"]
        self.ops = {k: [] for k in self.names}
        self.cnt = {k: 0 for k in self.names}
        self.waited = {k: {} for k in self.names}
        self.dcnt = {}

    def _deps(self, e, reads, writes):
        deps = {}

        def add(tok):
            if tok is None:
                return
            s, v = tok
            if e == "pe" and s == "pe":
                return
            if deps.get(s, 0) < v:
                deps[s] = v
        for r in reads:
            add(r.w)
        for w in writes:
            add(w.w)
            for s, v in w.r.items():
                add((s, v))
        out = []
        for s, v in deps.items():
            if self.waited[e].get(s, 0) < v:
                self.waited[e][s] = v
                out.append((s, v))
        return out

    def _upd(self, tok, reads, writes):
        s, v = tok
        for r in reads:
            if r.r.get(s, 0) < v:
                r.r[s] = v
        for w in writes:
            w.w = tok
            w.r = {}

    def op(self, e, fn, reads=(), writes=(), sig=True):
        waits = self._deps(e, reads, writes)
        if sig:
            self.cnt[e] += 1
            v = self.cnt[e]
        else:
            v = self.cnt[e] + 1
        self.ops[e].append((waits, fn, (e, 1) if sig else None))
        self._upd((e, v), reads, writes)

    def dma(self, q, fn, sem, reads=(), writes=()):
        waits = self._deps(q, reads, writes)
        self.dcnt[sem] = self.dcnt.get(sem, 0) + 16
        tok = (sem, self.dcnt[sem])
        self.ops[q].append((waits, fn, (sem, 16)))
        self._upd(tok, reads, writes)
        return tok

    def wait_tok(self, e, tok):
        s, v = tok
        if self.waited[e].get(s, 0) < v:
            self.waited[e][s] = v
            self.ops[e].append(([(s, v)], None, None))

    def replay(self, block):
        decs = {"pe": block.tensor, "act": block.scalar, "dve": block.vector, "pool": block.gpsimd, "sp": block.sync}
        for name in self.names:
            lst = self.ops[name]
            if not lst:
                continue

            def body(e, lst=lst):
                for waits, fn, inc in lst:
                    for s, v in waits:
                        e.wait_ge(self.semh[s], v)
                    if fn is None:
                        continue
                    ins = fn(e)
                    if inc is not None:
                        ins.then_inc(self.semh[inc[0]], inc[1])
            decs[name](body)
            self.ops[name] = []


def dap(t, offset, dims):
    return bass.AP(tensor=t.tensor, offset=offset, ap=[[s, c] for s, c in dims])


def build(n_tt=4, do_pred=True, stage=99, do_setup=True):
    nc = bass.Bass("TRN2", target_bir_lowering=False)
    TOK = n_tt * NT

    def din(name, shape, dt=F32):
        return nc.dram_tensor(name, list(shape), dt, kind="ExternalInput").ap()

    x_main = din("x_main", [TOK, D])
    x_pred = din("x_pred", [TOK, D])
    hm8 = din("hm8", [128, 1])
    rel_bias = din("rel_bias", [32, 16])
    ln1_g = din("ln1_g", [D])
    w_in = din("w_in", [D, 2560])
    q_norm_g = din("q_norm_g", [64])
    k_norm_g = din("k_norm_g", [64])
    sinks = din("attn_sinks", [16])
    a_re = din("ssm_a_re", [64, 64])
    a_im = din("ssm_a_im", [64, 64])
    log_dt = din("ssm_log_dt", [64])
    b_re = din("ssm_b_re", [64, 64, 16])
    b_im = din("ssm_b_im", [64, 64, 16])
    c_re = din("ssm_c_re", [64, 16, 64])
    c_im = din("ssm_c_im", [64, 16, 64])
    ssm_d = din("ssm_d", [64 * 16])
    w_glu = din("w_glu", [1024, 1024])
    attn_out_g = din("attn_out_g", [1024])
    ssm_out_g = din("ssm_out_g", [1024])
    w_out = din("w_out", [D, D])
    ln2_g = din("ln2_g", [D])
    w_gate = din("w_ff_gate", [D, FF])
    w_up = din("w_ff_up", [D, FF])
    w_down = din("w_ff_down", [FF, D])
    c_ident = din("c_ident", [128, 128])
    c_mask = din("c_mask", [128, 128])
    c_mv = din("c_mv", [128, NM * 32])
    c_mv2 = din("c_mv2", [128, 64 * 32])
    c_oh = din("c_oh", [33, 512])
    c_bones = din("c_bones", [128, 128])
    c_anti = din("c_anti", [128, 128])
    c_dup = din("c_dup", [128, 256])
    out = nc.dram_tensor("out", [TOK, D], F32, kind="ExternalOutput").ap()
    wt_d = nc.dram_tensor("wt_d", [128, 64 * 128], BF16, kind="Internal").ap()
    kt_d = nc.dram_tensor("kt_d", [128, 64 * 128], BF16, kind="Internal").ap()
    et_d = nc.dram_tensor("et_d", [128, 64 * 256], BF16, kind="Internal").ap()
    ext_d = nc.dram_tensor("ext_d", [16, 512], F32, kind="Internal").ap()

    sem_names = ["pe", "act", "dve", "pool", "sp", "xl0", "xl1", "xl2", "xl3", "wb0", "wb1", "wb2", "wd", "tb0", "tb1",
                 "st", "misc", "ot0", "ot1", "ot2", "ot3"] + ["wd%d" % i for i in range(8)]
    import contextlib
    with contextlib.ExitStack() as es:
        semh = {n: es.enter_context(nc.semaphore(n)) for n in sem_names}
        S = Sched(nc, semh)

        def sb(name, shape, dt):
            return es.enter_context(nc.sbuf_tensor(name, list(shape), dt))

        ROTC = sb("ROTC", [128, 32, 64], BF16)
        ROTS = sb("ROTS", [128, 32, 64], BF16)
        RHO = sb("RHO", [128, 32], F32)
        BIAS = sb("BIAS", [128, 16, 2, 128], BF16)
        G1 = sb("G1", [128, 16], F32)
        G2 = sb("G2", [128, 16], F32)
        GA = sb("GA", [128, 8], F32)
        GS = sb("GS", [128, 8], F32)
        QG = sb("QG", [128, 1], F32)
        KG = sb("KG", [128, 1], F32)
        ESK = sb("ESK", [128, 16], F32)
        HM8 = sb("HM8", [128, 1], F32)
        DUPB = sb("DUPB", [128, 2, 128], BF16)
        NEG8 = sb("NEG8", [128, 1], F32)
        IDB = sb("IDB", [128, 128], BF16)
        BONES = sb("BONES", [128, 128], BF16)
        ONESB = sb("ONESB", [128, 128], BF16)
        r_const = Reg()
        PS = es.enter_context(nc.psum_tensor("PS", [128, 8, 512], F32))
        r_ps = [Reg() for _ in range(8)]
        bank_i = [0]

        reserved = set()

        def bank(reserve=False):
            while True:
                i = bank_i[0] % 8
                bank_i[0] += 1
                if i not in reserved:
                    break
            if reserve:
                reserved.add(i)
            return r_ps[i], PS[:, i, :]

        with contextlib.ExitStack() as es2:
            def sb2(name, shape, dt=F32):
                return es2.enter_context(nc.sbuf_tensor(name, list(shape), dt))
            ARE = sb2("ARE", [128, 32]); AIM = sb2("AIM", [128, 32]); LDT = sb2("LDT", [128, 32])
            DT = sb2("DT", [128, 32]); ARD = sb2("ARD", [128, 32]); AID = sb2("AID", [128, 32])
            MV = sb2("MV", [128, NM, 32])
            MAG = sb2("MAG", [128, NM, 32])
            ANG = sb2("ANG", [128, 2, NM, 32])
            SC = sb2("SC", [128, 2, NM, 32])
            PWR = sb2("PWR", [128, NM, 32]); PWI = sb2("PWI", [128, NM, 32])
            SM = [sb2("SM%d" % i, [128, 32]) for i in range(10)]
            BRE = sb2("BRE", [128, 32, 16]); BIM = sb2("BIM", [128, 32, 16])
            BBR = sb2("BBR", [128, 32, 16]); BBI = sb2("BBI", [128, 32, 16])
            CR = sb2("CR", [128, 32, 16]); CI = sb2("CI", [128, 32, 16])
            T1 = sb2("T1", [128, 32, 8, 16]); T2 = T1
            VR = sb2("VR", [128, 32, 8, 16]); VI = sb2("VI", [128, 32, 8, 16])
            WR = VR; WI = VI
            ER = sb2("ER", [128, 32, 8, 16]); EI = sb2("EI", [128, 32, 8, 16])
            KIv = ER[:].bitcast(I32).rearrange("p a b c -> p (a b c)")[:, 0:2 * NM * 32].rearrange("p (s m r) -> p s m r", s=2, m=NM)
            KFv = VR[:].rearrange("p a b c -> p (a b c)")[:, 0:2 * NM * 32].rearrange("p (s m r) -> p s m r", s=2, m=NM)
            IDF = sb2("IDF", [128, 128]); MASK = sb2("MASK", [128, 128]); BONF = sb2("BONF", [128, 128])
            DROW = sb2("DROW", [128, 64, 16])
            KTB = sb2("KTB", [128, 64, 128], BF16)
            MV2 = KTB[:].bitcast(F32).rearrange("p g c -> p (g c)")[:, 0:2048].rearrange("p (m r) -> p m r", m=64)
            WTB = KTB
            ETB = sb2("ETB", [128, 64, 128], BF16)
            TMPK = sb2("TMPK", [128, 4, 8, 16])
            RBA = sb2("RBA", [33, 16]); OH = sb2("OH", [33, 512])
            EXTV = SC[:].rearrange("p a m r -> p (a m r)")[0:16, 0:512]
            SKV = sb2("SKV", [128, 16])
            ANTF = sb2("ANTF", [128, 128]); ANTB = sb2("ANTB", [128, 128], BF16)
            DUPF = sb2("DUPF", [128, 2, 128])
            r = {n: Reg() for n in ["in", "dt", "mag", "ang", "ki", "kf", "cm", "sc", "pw", "sm", "bb", "t1", "t2", "v", "w", "e",
                                    "ktb", "wtb", "etb", "tmpk", "ext", "biasf", "extd", "wtd", "ktd", "etd", "pers", "cin"]}

            ld = []

            def L(out_ap, in_ap):
                ld.append(S.dma("sp", lambda e, o=out_ap, i=in_ap: e.dma_start(out=o, in_=i, allow_slow_non_contiguous=True),
                                "misc", writes=[r["in"]]))
            for gi in range(2):
                hs = slice(gi * 64, (gi + 1) * 64)
                L(ARE[hs, :], dap(a_re, gi * 64, [(1, 64), (128, 32)]))
                L(AIM[hs, :], dap(a_im, gi * 64, [(1, 64), (128, 32)]))
                L(LDT[hs, :], dap(log_dt, gi, [(0, 64), (2, 32)]))
                L(BRE[hs, :, :], dap(b_re, gi * 1024, [(16, 64), (2048, 32), (1, 16)]))
                L(BIM[hs, :, :], dap(b_im, gi * 1024, [(16, 64), (2048, 32), (1, 16)]))
            L(MV[:], c_mv.rearrange("p (m r) -> p m r", m=NM))
            L(MV2, c_mv2.rearrange("p (m r) -> p m r", m=64))
            L(IDF[:], c_ident)
            L(MASK[:], c_mask)
            L(BONF[:], c_bones)
            L(ANTF[:], c_anti)
            L(DUPF[:], c_dup.rearrange("p (c m) -> p c m", c=2))
            L(DROW[:], dap(ssm_d, 0, [(0, 128), (16, 64), (1, 16)]))
            L(RBA[0:32, :], rel_bias)
            L(OH[:], c_oh)
            L(G1[:], dap(ln1_g, 0, [(1, 128), (128, 16)]))
            L(G2[:], dap(ln2_g, 0, [(1, 128), (128, 16)]))
            L(GA[:], dap(attn_out_g, 0, [(1, 128), (128, 8)]))
            L(GS[:], dap(ssm_out_g, 0, [(1, 128), (128, 8)]))
            for h2 in range(2):
                L(QG[h2 * 64:(h2 + 1) * 64, :], dap(q_norm_g, 0, [(1, 64), (1, 1)]))
                L(KG[h2 * 64:(h2 + 1) * 64, :], dap(k_norm_g, 0, [(1, 64), (1, 1)]))
            L(SKV[:], dap(sinks, 0, [(0, 128), (1, 16)]))
            L(HM8[:], hm8)

            V_ = "dve"
            rin = [r["in"]]
            CZ = T1[0:32].rearrange("p a b c -> p (a b c)")[:, 0:1024].rearrange("p (i c) -> p i c", i=8)
            S.op("dve", lambda e: e.memset(CZ, 0.0), writes=[r["t1"]])
            for ci, (csrc, CX) in enumerate(((c_re, CR), (c_im, CI))):
                for ch in range(4):
                    pr0 = ch * 8
                    for gi in range(2):
                        S.dma("sp", lambda e, gi=gi, pr0=pr0, csrc=csrc: e.dma_start(
                            out=CZ[gi * 16:(gi + 1) * 16, :, gi * 64:(gi + 1) * 64],
                            in_=dap(csrc, (2 * pr0 + gi) * 1024, [(64, 16), (2048, 8), (1, 64)])), "xl0", writes=[r["t1"]])
                    rb_, pb_ = bank()
                    for i in range(8):
                        S.op("pe", lambda e, pb_=pb_, i=i: e.transpose(pb_[:, i * 32:(i + 1) * 32], CZ[:, i, :], IDF[0:32, 0:32]),
                             reads=[r["t1"]] + rin, writes=[rb_], sig=(i == 7))
                    for gi in range(2):
                        hs = slice(gi * 64, (gi + 1) * 64)
                        S.op("act", lambda e, pb_=pb_, hs=hs, gi=gi, pr0=pr0, CX=CX: e.activation(
                            out=CX[hs, pr0:pr0 + 8, :], in_=pb_[hs, 0:256].rearrange("p (i g q) -> p i g q", i=8, g=2)[:, :, gi, :], func=AF.Copy),
                            reads=[rb_], writes=[r["cin"]])
            S.op(V_, lambda e: e.memset(NEG8[:], -SHIFT), writes=[r_const])
            S.op(V_, lambda e: e.memset(ONESB[:], 1.0), writes=[r_const])
            S.op(V_, lambda e: e.tensor_copy(out=IDB[:], in_=IDF[:]), reads=rin, writes=[r_const])
            S.op(V_, lambda e: e.tensor_copy(out=BONES[:], in_=BONF[:]), reads=rin, writes=[r_const])
            S.op(V_, lambda e: e.tensor_copy(out=DUPB[:], in_=DUPF[:]), reads=rin, writes=[r_const])
            S.op(V_, lambda e: e.memset(RBA[32:33, :], 1.0), reads=rin, writes=[r["in"]])
            S.op("act", lambda e: e.activation(out=ESK[:], in_=SKV[:], func=AF.Exp, bias=NEG8[:, 0:1], scale=1.0),
                 reads=rin + [r_const], writes=[r["pers"]])
            S.op("act", lambda e: e.activation(out=DT[:], in_=LDT[:], func=AF.Exp), reads=rin, writes=[r["dt"]])
            S.op(V_, lambda e: e.tensor_tensor(out=ARD[:], in0=ARE[:], in1=DT[:], op=ALU.mult), reads=rin + [r["dt"]], writes=[r["sm"]])
            S.op(V_, lambda e: e.tensor_tensor(out=AID[:], in0=AIM[:], in1=DT[:], op=ALU.mult), reads=rin + [r["dt"]], writes=[r["sm"]])

            def bc_m(t):
                return t[:].unsqueeze(1).broadcast_to([128, NM, 32])
            S.op(V_, lambda e: e.tensor_tensor(out=MAG[:], in0=MV[:], in1=bc_m(ARD), op=ALU.mult), reads=rin + [r["sm"]], writes=[r["mag"]])
            S.op("act", lambda e: e.activation(out=MAG[:], in_=MAG[:], func=AF.Exp), reads=[r["mag"]], writes=[r["mag"]])
            S.op(V_, lambda e: e.tensor_tensor(out=ANG[:, 0], in0=MV[:], in1=bc_m(AID), op=ALU.mult), reads=rin + [r["sm"]], writes=[r["ang"]])
            S.op(V_, lambda e: e.tensor_scalar(out=ANG[:, 0], in0=ANG[:, 0], scalar1=1.0 / (2 * math.pi), scalar2=None, op0=ALU.mult),
                 reads=[r["ang"]], writes=[r["ang"]])
            S.op(V_, lambda e: e.tensor_scalar(out=ANG[:, 1], in0=ANG[:, 0], scalar1=0.25, scalar2=None, op0=ALU.add),
                 reads=[r["ang"]], writes=[r["ang"]])
            S.op(V_, lambda e: e.tensor_copy(out=KIv, in_=ANG[:]), reads=[r["ang"]], writes=[r["e"]])
            S.op(V_, lambda e: e.tensor_copy(out=KFv, in_=KIv), reads=[r["e"]], writes=[r["v"]])
            S.op(V_, lambda e: e.tensor_tensor(out=ANG[:], in0=ANG[:], in1=KFv, op=ALU.subtract), reads=[r["ang"], r["v"]], writes=[r["ang"]])
            S.op(V_, lambda e: e.tensor_scalar(out=KFv, in0=ANG[:], scalar1=0.5, scalar2=None, op0=ALU.is_gt), reads=[r["ang"]], writes=[r["v"]])
            S.op(V_, lambda e: e.tensor_tensor(out=ANG[:], in0=ANG[:], in1=KFv, op=ALU.subtract), reads=[r["ang"], r["v"]], writes=[r["ang"]])
            S.op(V_, lambda e: e.tensor_scalar(out=KFv, in0=ANG[:], scalar1=-0.5, scalar2=None, op0=ALU.is_lt), reads=[r["ang"]], writes=[r["v"]])
            S.op(V_, lambda e: e.tensor_tensor(out=ANG[:], in0=ANG[:], in1=KFv, op=ALU.add), reads=[r["ang"], r["v"]], writes=[r["ang"]])
            S.op("act", lambda e: e.activation(out=SC[:], in_=ANG[:], func=AF.Sin, scale=6.283185), reads=[r["ang"]], writes=[r["sc"]])
            S.op(V_, lambda e: e.tensor_tensor(out=PWR[:], in0=MAG[:], in1=SC[:, 1], op=ALU.mult), reads=[r["mag"], r["sc"]], writes=[r["pw"]])
            S.op(V_, lambda e: e.tensor_tensor(out=PWI[:], in0=MAG[:], in1=SC[:, 0], op=ALU.mult), reads=[r["mag"], r["sc"]], writes=[r["pw"]])
            S.op(V_, lambda e: e.tensor_copy(out=RHO[:], in_=MAG[:, 15, :]), reads=[r["mag"]], writes=[r["pers"]])
            A2 = T1[:].rearrange("p a b c -> p (a b c)").rearrange("p (s j r) -> p s j r", s=2, j=64)
            K2 = ER[:].bitcast(I32).rearrange("p a b c -> p (a b c)").rearrange("p (s j r) -> p s j r", s=2, j=64)
            F2 = VR[:].rearrange("p a b c -> p (a b c)").rearrange("p (s j r) -> p s j r", s=2, j=64)
            S2 = VI[:].rearrange("p a b c -> p (a b c)").rearrange("p (s j r) -> p s j r", s=2, j=64)
            ra, rk, rf = [r["t1"]], [r["e"]], [r["v"]]
            S.op(V_, lambda e: e.tensor_tensor(out=A2[:, 0], in0=MV2, in1=AID[:].unsqueeze(1).broadcast_to([128, 64, 32]), op=ALU.mult),
                 reads=rin + [r["sm"], r["ktb"]], writes=ra)
            S.op(V_, lambda e: e.tensor_scalar(out=A2[:, 0], in0=A2[:, 0], scalar1=1.0 / (2 * math.pi), scalar2=None, op0=ALU.mult), reads=ra, writes=ra)
            S.op(V_, lambda e: e.tensor_scalar(out=A2[:, 1], in0=A2[:, 0], scalar1=0.25, scalar2=None, op0=ALU.add), reads=ra, writes=ra)
            S.op(V_, lambda e: e.tensor_copy(out=K2, in_=A2), reads=ra, writes=rk)
            S.op(V_, lambda e: e.tensor_copy(out=F2, in_=K2), reads=rk, writes=rf)
            S.op(V_, lambda e: e.tensor_tensor(out=A2, in0=A2, in1=F2, op=ALU.subtract), reads=ra + rf, writes=ra)
            S.op(V_, lambda e: e.tensor_scalar(out=F2, in0=A2, scalar1=0.5, scalar2=None, op0=ALU.is_gt), reads=ra, writes=rf)
            S.op(V_, lambda e: e.tensor_tensor(out=A2, in0=A2, in1=F2, op=ALU.subtract), reads=ra + rf, writes=ra)
            S.op(V_, lambda e: e.tensor_scalar(out=F2, in0=A2, scalar1=-0.5, scalar2=None, op0=ALU.is_lt), reads=ra, writes=rf)
            S.op(V_, lambda e: e.tensor_tensor(out=A2, in0=A2, in1=F2, op=ALU.add), reads=ra + rf, writes=ra)
            S.op("act", lambda e: e.activation(out=S2, in_=A2, func=AF.Sin, scale=6.283185), reads=ra, writes=rf)
            S.op(V_, lambda e: e.tensor_copy(out=ROTS[:].rearrange("p r j -> p j r"), in_=S2[:, 0]), reads=rf, writes=[r["pers"]])
            S.op(V_, lambda e: e.tensor_copy(out=ROTC[:].rearrange("p r j -> p j r"), in_=S2[:, 1]), reads=rf, writes=[r["pers"]])
            nr, ni, den, rden, t0, t1_, fr, fi = SM[0], SM[1], SM[2], SM[3], SM[4], SM[5], SM[6], SM[7]
            rs = [r["sm"]]
            rp = [r["pw"]]

            def tt(o, a, b, op, reads, writes):
                S.op(V_, lambda e: e.tensor_tensor(out=o, in0=a, in1=b, op=op), reads=reads, writes=writes)
            S.op(V_, lambda e: e.tensor_scalar(out=nr[:], in0=PWR[:, 8, :], scalar1=-1.0, scalar2=None, op0=ALU.add), reads=rp, writes=rs)
            tt(den[:], ARE[:], ARE[:], ALU.mult, rin, rs)
            tt(t0[:], AIM[:], AIM[:], ALU.mult, rin, rs)
            tt(den[:], den[:], t0[:], ALU.add, rs, rs)
            S.op(V_, lambda e: e.reciprocal(out=rden[:], in_=den[:]), reads=rs, writes=rs)
            tt(t0[:], nr[:], ARE[:], ALU.mult, rs + rin, rs)
            tt(t1_[:], PWI[:, 8, :], AIM[:], ALU.mult, rp + rin, rs)
            tt(t0[:], t0[:], t1_[:], ALU.add, rs, rs)
            tt(fr[:], t0[:], rden[:], ALU.mult, rs, rs)
            tt(t0[:], PWI[:, 8, :], ARE[:], ALU.mult, rp + rin, rs)
            tt(t1_[:], nr[:], AIM[:], ALU.mult, rs + rin, rs)
            tt(t0[:], t0[:], t1_[:], ALU.subtract, rs, rs)
            tt(fi[:], t0[:], rden[:], ALU.mult, rs, rs)

            def bq(t):
                return t[:].unsqueeze(2).broadcast_to([128, 32, 16])
            rb = [r["bb"]]
            tt(BBR[:], BRE[:], bq(fr), ALU.mult, rin + rs, rb)
            tt(T1[:, :, 0, :], BIM[:], bq(fi), ALU.mult, rin + rs, [r["t1"]])
            tt(BBR[:], BBR[:], T1[:, :, 0, :], ALU.subtract, rb + [r["t1"]], rb)
            tt(BBI[:], BIM[:], bq(fr), ALU.mult, rin + rs, rb)
            tt(T1[:, :, 0, :], BRE[:], bq(fi), ALU.mult, rin + rs, [r["t1"]])
            tt(BBI[:], BBI[:], T1[:, :, 0, :], ALU.add, rb + [r["t1"]], rb)

            def pw8(t, i0):
                return t[:, i0:i0 + 8, :].rearrange("p s r -> p r s").unsqueeze(3).broadcast_to([128, 32, 8, 16])

            def x8(t):
                return t[:].unsqueeze(2).broadcast_to([128, 32, 8, 16])

            def cmul(OR, OI, i0, XR, XI, rx, ro, neg_im=False):
                tt(OR[:], pw8(PWR, i0), x8(XR), ALU.mult, rp + rx, ro)
                tt(T1[:], pw8(PWI, i0), x8(XI), ALU.mult, rp + rx, [r["t1"]])
                tt(OR[:], OR[:], T1[:], ALU.subtract, ro + [r["t1"]], ro)
                tt(OI[:], pw8(PWR, i0), x8(XI), ALU.mult, rp + rx, ro)
                tt(T2[:], pw8(PWI, i0), x8(XR), ALU.mult, rp + rx, [r["t1"]])
                tt(OI[:], OI[:], T2[:], ALU.add, ro + [r["t1"]], ro)
                if neg_im:
                    S.op(V_, lambda e: e.tensor_scalar(out=OI[:], in0=OI[:], scalar1=-1.0, scalar2=None, op0=ALU.mult), reads=ro, writes=ro)
            cmul(VR, VI, 0, BBR, BBI, rb, [r["v"]])
            cmul(ER, EI, 8, CR, CI, rin + [r["cin"]], [r["e"]], neg_im=True)

            for pq in range(8):
                banks = [bank(), bank()]
                for gi in range(2):
                    hs = slice(gi * 64, (gi + 1) * 64)
                    rb_, pb_ = banks[gi]
                    for p4 in range(4):
                        pr = pq * 4 + p4
                        o = pb_[:, p4 * 128:(p4 + 1) * 128]
                        S.op("pe", lambda e, o=o, hs=hs, pr=pr: e.matmul(o, VR[hs, pr].rearrange("p s q -> p (s q)"),
                                                                         ER[hs, pr].rearrange("p s q -> p (s q)"), start=True, stop=False),
                             reads=[r["v"], r["e"]], writes=[rb_], sig=False)
                        S.op("pe", lambda e, o=o, hs=hs, pr=pr: e.matmul(o, VI[hs, pr].rearrange("p s q -> p (s q)"),
                                                                         EI[hs, pr].rearrange("p s q -> p (s q)"), start=False, stop=True),
                             reads=[r["v"], r["e"]], writes=[rb_], sig=(p4 == 3))
                    gsel = slice(2 * pq * 4 + gi, 2 * (pq * 4 + 4), 2)
                    S.op(V_, lambda e, gsel=gsel: e.tensor_tensor(
                        out=TMPK[:], in0=IDF[:].rearrange("p (t q) -> p t q", t=8).unsqueeze(1).broadcast_to([128, 4, 8, 16]),
                        in1=DROW[:, gsel, :].unsqueeze(2).broadcast_to([128, 4, 8, 16]), op=ALU.mult),
                        reads=rin, writes=[r["tmpk"]])
                    S.op(V_, lambda e, pb_=pb_: e.tensor_tensor(
                        out=pb_.rearrange("p (g c) -> p g c", g=4), in0=pb_.rearrange("p (g c) -> p g c", g=4),
                        in1=MASK[:].unsqueeze(1).broadcast_to([128, 4, 128]), op=ALU.mult), reads=[rb_] + rin, writes=[rb_])
                    S.op(V_, lambda e, pb_=pb_, gsel=gsel: e.tensor_tensor(
                        out=KTB[:, gsel, :], in0=pb_.rearrange("p (g c) -> p g c", g=4),
                        in1=TMPK[:].rearrange("p g t q -> p g (t q)"), op=ALU.add), reads=[rb_, r["tmpk"]], writes=[r["ktb"]])
            tk1 = S.dma("sp", lambda e: e.dma_start(out=kt_d, in_=KTB[:].rearrange("p g c -> p (g c)")), "st", reads=[r["ktb"]], writes=[r["ktd"]])
            cmul(WR, WI, 16, BBR, BBI, rb, [r["v"]])
            for c, WX in enumerate((WR, WI)):
                for pq in range(8):
                    rb_, pb_ = bank()
                    for p4 in range(4):
                        pr = pq * 4 + p4
                        S.op("pe", lambda e, pb_=pb_, p4=p4, pr=pr, WX=WX: e.transpose(
                            pb_[:, p4 * 128:(p4 + 1) * 128], WX[:, pr].rearrange("p s q -> p (s q)"), IDF[:]),
                            reads=[r["v"]] + rin, writes=[rb_], sig=(p4 == 3))
                    S.op("act", lambda e, pb_=pb_, pq=pq, c=c: e.activation(
                        out=WTB[:, pq * 8:(pq + 1) * 8, c * 64:(c + 1) * 64],
                        in_=pb_.rearrange("p (g n) -> p g n", g=8), func=AF.Copy), reads=[rb_], writes=[r["ktb"]])
            tk2 = S.dma("sp", lambda e: e.dma_start(out=wt_d, in_=WTB[:].rearrange("p g c -> p (g c)")), "st", reads=[r["ktb"]], writes=[r["wtd"]])
            tk3s = []
            for c, EX in enumerate((ER, EI)):
                S.op("pool", lambda e: e.memset(ETB[:], 0.0), writes=[r["etb"]])
                for gi in range(2):
                    hs = slice(gi * 64, (gi + 1) * 64)
                    S.op("act", lambda e, hs=hs, gi=gi, EX=EX: e.activation(
                        out=ETB[hs, gi::2, :], in_=EX[hs].rearrange("p r t q -> p r (t q)"), func=AF.Copy),
                        reads=[r["e"]], writes=[r["etb"]])
                tk3s.append(S.dma("sp", lambda e, c=c: e.dma_start(out=et_d.rearrange("p (g c x) -> p g c x", g=64, c=2)[:, :, c, :], in_=ETB[:]),
                                  "st", reads=[r["etb"]], writes=[r["etd"]]))
            rb_, pb_ = bank()
            S.op("pe", lambda e: e.matmul(pb_[0:16, :], RBA[:], OH[:], start=True, stop=True), reads=rin, writes=[rb_])
            S.op("act", lambda e: e.activation(out=EXTV, in_=pb_[0:16, :], func=AF.Copy), reads=[rb_, r["sc"]], writes=[r["sc"]])
            tk4 = S.dma("sp", lambda e: e.dma_start(out=ext_d, in_=EXTV), "st", reads=[r["sc"]], writes=[r["extd"]])
            BREV = T1[:].bitcast(BF16).rearrange("p a b c -> p (a b c)")[:, 0:4096].rearrange("p (h a i) -> p h a i", h=16, a=2)
            for h in range(16):
                S.dma("pool", lambda e, h=h: e.dma_start(out=BREV[:, h, :, :], in_=dap(ext_d, h * 512, [(1, 128), (256, 2), (1, 128)])),
                      "wd", reads=[r["extd"]], writes=[r["biasf"], r["t1"]])
            S.op(V_, lambda e: e.tensor_copy(out=ANTB[:], in_=ANTF[:]), reads=rin, writes=[r["tmpk"]])
            for h4 in range(8):
                rb_, pb_ = bank()
                S.op("pe", lambda e, pb_=pb_, h4=h4: e.matmul(pb_, ANTB[:], BREV[:, h4 * 2:(h4 + 1) * 2].rearrange("p h a i -> p (h a i)"), start=True, stop=True),
                     reads=[r["biasf"], r["tmpk"]], writes=[rb_])
                S.op("act", lambda e, pb_=pb_, h4=h4: e.activation(out=BIAS[:, h4 * 2:(h4 + 1) * 2].rearrange("p h a i -> p (h a i)"), in_=pb_, func=AF.Copy),
                     reads=[rb_], writes=[r["pers"]])
            for tk in (tk1, tk2, tk4) + tuple(tk3s) + tuple(ld):
                S.wait_tok("sp", tk)
            if not do_setup:
                for k_ in S.ops:
                    S.ops[k_] = []
            else:
                with nc.Block() as block:
                    S.replay(block)

        X = sb("X", [128, 4, D], F32)
        HT = sb("HT", [128, 16, NT], BF16)
        ACTB = sb("ACTB", [128, 12, NT], BF16)
        WB = sb("WB", [128, 3, 16, 256], BF16)
        WD = sb("WD", [128, 6, 512], BF16)
        TB = sb("TB", [128, 2, 8, 4, 128], BF16)
        QT = sb("QT", [128, 8, NT], BF16)
        ST = QT
        KT2 = sb("KT2", [128, 2, 4, NT], BF16)
        KN = sb("KN", [128, 2, NT], BF16)
        VA = sb("VA", [128, 2, 4, 4, 66], BF16)
        YA = sb("YA", [128, 1, 1024], F32)
        HN = sb("HN", [128, 2, D], BF16)
        SS = sb("SS", [128, 8], F32)
        RSTD = sb("RSTD", [128, 8], F32)
        UQ = sb("UQ", [64, 16, 8, 16], BF16)
        UT = sb("UT", [128, 16, 64], BF16)
        SA = sb("SA", [128, 2, 8, 64], F32)
        SB_ = sb("SB", [128, 2, 8, 64], F32)
        HIN = sb("HIN", [128, 2, 32], F32)
        HB = sb("HB", [128, 2, 8, 64], BF16)
        TA = sb("TA", [128, 2, 8, 64], F32)
        YS = sb("YS", [64, 2, 8, 128], BF16)
        GX = sb("GX", [64, 1, 3, 512], F32)
        YT = sb("YT", [128, 8, 8, 64], BF16)
        SQ = sb("SQ", [128, 2, NT], BF16)
        SIG = sb("SIG", [128, 2, NT], BF16)
        RB = sb("RB", [128, 1, NT], F32)
        LG = sb("LG", [128, 2, 256], F32)
        PT = sb("PT", [128, 2, 256], BF16)
        DEN = sb("DEN", [128, 2, 4], F32)

        R = {}

        def rg(name):
            if name not in R:
                R[name] = Reg()
            return R[name]
        rc = [r_const]
        sub_regs = [rg("X%d" % i) for i in range(4)]

        S.op("dve", lambda e: e.memset(VA[:], 1.0), writes=[rg("VA0"), rg("VA1")])
        S.op("dve", lambda e: e.memset(KT2[:], 0.0), writes=[rg("KT0"), rg("KT1")])
        S.op("dve", lambda e: e.memset(HIN[:], 0.0), writes=[rg("HIN")])

        wslot = [0]

        def load_wblock(src_ap_fn_list):
            s_ = wslot[0] % 3
            wslot[0] += 1
            reg = rg("WB%d" % s_)
            for ov, ia in src_ap_fn_list:
                S.dma("pool", lambda e, ov=ov, ia=ia, s_=s_: e.dma_start(out=ov(WB[:, s_]), in_=ia, allow_slow_non_contiguous=True),
                      "wb%d" % s_, writes=[reg])
            return s_, reg

        def wcols(W, ncols_total, c0, ncol, krows=16):
            return dap(W, c0, [(ncols_total, 128), (128 * ncols_total, krows), (1, ncol)])

        def norm_to_HT(G, xsrc_regs):
            for sub in range(4):
                S.op("act", lambda e, sub=sub: e.activation(out=ACTB[:, 8:12, :].rearrange("p a b -> p (a b)"), in_=X[:, sub, :], func=AF.Square, accum_out=SS[:, sub:sub + 1]),
                     reads=[xsrc_regs[sub]], writes=[rg("ACT8"), rg("ACT9"), rg("ACT10"), rg("ACT11"), rg("SS")])
            S.op("dve", lambda e: e.tensor_scalar(out=RSTD[:, 0:4], in0=SS[:, 0:4], scalar1=1.0 / D, scalar2=EPS, op0=ALU.mult, op1=ALU.add),
                 reads=[rg("SS")], writes=[rg("RSTD")])
            S.op("act", lambda e: e.activation(out=RSTD[:, 0:4], in_=RSTD[:, 0:4], func=AF.Sqrt), reads=[rg("RSTD")], writes=[rg("RSTD")])
            S.op("dve", lambda e: e.reciprocal(out=RSTD[:, 0:4], in_=RSTD[:, 0:4]), reads=[rg("RSTD")], writes=[rg("RSTD")])
            for sub in range(4):
                hb = sub % 2
                S.op("act", lambda e, sub=sub, hb=hb: e.activation(out=HN[:, hb, :], in_=X[:, sub, :], func=AF.Copy, scale=RSTD[:, sub:sub + 1]),
                     reads=[xsrc_regs[sub], rg("RSTD")], writes=[rg("HN%d" % hb)])
                for kh in range(2):
                    rb_, pb_ = bank()
                    pbb = pb_.bitcast(BF16)
                    for k8 in range(8):
                        k = kh * 8 + k8
                        S.op("pe", lambda e, pbb=pbb, k8=k8, k=k, hb=hb: e.transpose(pbb[:, k8 * 128:(k8 + 1) * 128],
                                                                                     HN[:, hb, k * 128:(k + 1) * 128], IDB[:]),
                             reads=[rg("HN%d" % hb)] + rc, writes=[rb_], sig=(k8 == 7))
                    S.op("dve", lambda e, pbb=pbb, kh=kh, sub=sub: e.tensor_tensor(
                        out=HT[:, kh * 8:(kh + 1) * 8, sub * 128:(sub + 1) * 128], in0=pbb.rearrange("p (k t) -> p k t", k=8),
                        in1=G[:, kh * 8:(kh + 1) * 8].unsqueeze(2).broadcast_to([128, 8, 128]), op=ALU.mult),
                        reads=[rb_] + rc, writes=[rg("HT")])

        def qk_norm(pb_, rb_, GV, out_ap, out_reg):
            S.op("act", lambda e: e.activation(out=SQ[:, 0, :], in_=pb_, func=AF.Square), reads=[rb_], writes=[rg("SQ0")])
            rb2, pb2 = bank()
            S.op("pe", lambda e: e.matmul(pb2, BONES[:], SQ[:, 0, :], start=True, stop=True), reads=[rg("SQ0")] + rc, writes=[rb2])
            S.op("act", lambda e: e.activation(out=RB[:, 0, :], in_=pb2, func=AF.Sqrt, scale=1.0 / 64, bias=EPS), reads=[rb2], writes=[rg("RB0")])
            S.op("dve", lambda e: e.reciprocal(out=RB[:, 0, :], in_=RB[:, 0, :]), reads=[rg("RB0")], writes=[rg("RB0")])
            S.op("dve", lambda e: e.scalar_tensor_tensor(out=out_ap, in0=pb_, scalar=GV[:, 0:1], in1=RB[:, 0, :], op0=ALU.mult, op1=ALU.mult),
                 reads=[rb_, rg("RB0")] + rc, writes=[out_reg])

        ssm_batch = [0]

        ssm_ctx = {}

        def ssm_part1(ub):
            s_, wreg = load_wblock([(lambda sl: sl, wcols(w_in, 2560, 1536 + ub * 256, 256))])
            tsl = []
            for ch in range(2):
                g0 = ub * 16 + ch * 8
                ts_ = ssm_batch[0] % 2
                ssm_batch[0] += 1
                treg = rg("TB%d" % ts_)
                S.dma("sp", lambda e, ts_=ts_, g0=g0: e.dma_start(out=TB[:, ts_, :, 0, :], in_=wt_d[:, g0 * 128:(g0 + 8) * 128].rearrange("p (g c) -> p g c", g=8)),
                      "tb%d" % ts_, writes=[treg])
                S.dma("sp", lambda e, ts_=ts_, g0=g0: e.dma_start(out=TB[:, ts_, :, 1, :], in_=kt_d[:, g0 * 128:(g0 + 8) * 128].rearrange("p (g c) -> p g c", g=8)),
                      "tb%d" % ts_, writes=[treg])
                S.dma("sp", lambda e, ts_=ts_, g0=g0: e.dma_start(out=TB[:, ts_, :, 2:4, :], in_=et_d[:, g0 * 256:(g0 + 8) * 256].rearrange("p (g c x) -> p g c x", g=8, c=2)),
                      "tb%d" % ts_, writes=[treg])
                tsl.append((ts_, treg))
            for sp_ in range(4):
                rb_, pb_ = bank()
                for s2 in range(2):
                    s = sp_ * 2 + s2
                    for k in range(16):
                        S.op("pe", lambda e, pb_=pb_, s2=s2, s=s, k=k, s_=s_: e.matmul(
                            pb_[0:64, s2 * 256:(s2 + 1) * 256], HT[:, k, s:NT:8], WB[:, s_, k, :], start=(k == 0), stop=(k == 15)),
                            reads=[rg("HT"), wreg], writes=[rb_], sig=(k == 15 and s2 == 1))
                S.op("act", lambda e, pb_=pb_, sp_=sp_: e.activation(
                    out=UQ[:, :, sp_ * 2:(sp_ + 1) * 2, :].rearrange("p g s q -> p s g q"),
                    in_=pb_[0:64, :].rearrange("p (s g q) -> p s g q", s=2, g=16), func=AF.Copy),
                    reads=[rb_], writes=[rg("UQ")])
            for gh in range(2):
                rb_, pb_ = bank()
                pbb = pb_.bitcast(BF16)
                for g8 in range(8):
                    g = gh * 8 + g8
                    S.op("pe", lambda e, pbb=pbb, g8=g8, g=g: e.transpose(pbb[:, g8 * 64:(g8 + 1) * 64],
                                                                          UQ[:, g].rearrange("p s q -> p (s q)"), IDB[0:64, 0:64]),
                         reads=[rg("UQ")] + rc, writes=[rb_], sig=(g8 == 7))
                S.op("dve", lambda e, pbb=pbb, gh=gh: e.tensor_copy(out=UT[:, gh * 8:(gh + 1) * 8, :],
                                                                    in_=pbb[:, 0:512].rearrange("p (g j) -> p g j", g=8)),
                     reads=[rb_], writes=[rg("UT")])
            rbr, pbr = bank()
            rbi, pbi = bank()
            for pr in range(8):
                for gi in range(2):
                    g = 2 * pr + gi
                    ts_, treg = tsl[g // 8]
                    hs = slice(gi * 64, (gi + 1) * 64)
                    last = (pr == 7 and gi == 1)
                    S.op("pe", lambda e, hs=hs, pr=pr, g=g, ts_=ts_: e.matmul(pbr[hs, pr * 64:(pr + 1) * 64], TB[:, ts_, g % 8, 0, 0:64], UT[:, g, :],
                                                                               start=True, stop=True),
                         reads=[rg("UT"), treg], writes=[rbr], sig=False)
                    S.op("pe", lambda e, hs=hs, pr=pr, g=g, ts_=ts_: e.matmul(pbi[hs, pr * 64:(pr + 1) * 64], TB[:, ts_, g % 8, 0, 64:128], UT[:, g, :],
                                                                               start=True, stop=True),
                         reads=[rg("UT"), treg], writes=[rbi], sig=last)
            S.op("act", lambda e: e.activation(out=SA[:, 0], in_=pbr.rearrange("p (r j) -> p r j", r=8), func=AF.Copy), reads=[rbr], writes=[rg("SA"), rg("SC0")])
            S.op("act", lambda e: e.activation(out=SA[:, 1], in_=pbi.rearrange("p (r j) -> p r j", r=8), func=AF.Copy), reads=[rbi], writes=[rg("SA"), rg("SC1")])
            prs = slice(ub * 8, (ub + 1) * 8)
            rH, rSA, rSB = rg("HIN"), rg("SA"), rg("SB")
            Cc, Sn = ROTC[:, prs, :], ROTS[:, prs, :]

            def vtt(o, a_, b_, op, reads, writes, eng="dve"):
                S.op(eng, lambda e: e.tensor_tensor(out=o, in0=a_, in1=b_, op=op), reads=reads, writes=writes)
            vtt(TA[:, 0], Cc, SA[:, 0], ALU.mult, [rSA] + rc, [rg("TA0")])
            vtt(TA[:, 1], Sn, SA[:, 1], ALU.mult, [rSA] + rc, [rg("TA1")])
            vtt(SB_[:, 0], TA[:, 0], TA[:, 1], ALU.add, [rg("TA0"), rg("TA1")], [rg("SB0")])
            vtt(TA[:, 0], Cc, SA[:, 1], ALU.mult, [rSA] + rc, [rg("TA0")])
            vtt(TA[:, 1], Sn, SA[:, 0], ALU.mult, [rSA] + rc, [rg("TA1")])
            vtt(SB_[:, 1], TA[:, 0], TA[:, 1], ALU.subtract, [rg("TA0"), rg("TA1")], [rg("SB1")])
            S.op("dve", lambda e: e.tensor_copy(out=HB[:, :, :, 0], in_=HIN[:, :, prs]), reads=[rH], writes=[rg("HB")])
            for c in range(2):
                for pr in range(8):
                    gp = ub * 8 + pr
                    S.op("dve", lambda e, c=c, pr=pr, gp=gp: e.tensor_tensor_scan(
                        out=SA[:, c, pr, :], data0=RHO[:, gp:gp + 1].broadcast_to([128, 64]), data1=SB_[:, c, pr, :],
                        initial=HIN[:, c, gp:gp + 1], op0=ALU.mult, op1=ALU.add),
                        reads=[rg("SB%d" % c), rH] + rc, writes=[rg("SC%d" % c)])
            vtt(TA[:, 0], Cc, SA[:, 0], ALU.mult, [rg("SC0")] + rc, [rg("TA0")])
            vtt(TA[:, 1], Sn, SA[:, 1], ALU.mult, [rg("SC1")] + rc, [rg("TA1")])
            vtt(SB_[:, 0], TA[:, 0], TA[:, 1], ALU.subtract, [rg("TA0"), rg("TA1")], [rg("SB0")])
            vtt(TA[:, 0], Cc, SA[:, 1], ALU.mult, [rg("SC1")] + rc, [rg("TA0")])
            vtt(TA[:, 1], Sn, SA[:, 0], ALU.mult, [rg("SC0")] + rc, [rg("TA1")])
            vtt(SB_[:, 1], TA[:, 0], TA[:, 1], ALU.add, [rg("TA0"), rg("TA1")], [rg("SB1")])
            S.op("dve", lambda e: e.tensor_copy(out=HIN[:, :, prs], in_=SB_[:, :, :, 63]), reads=[rg("SB0"), rg("SB1")], writes=[rH])
            ssm_ctx[ub] = tsl

        def ssm_part2(ub):
            tsl = ssm_ctx[ub]
            S.op("dve", lambda e: e.tensor_copy(out=HB[:, :, :, 1:64], in_=SB_[:, :, :, 0:63]), reads=[rg("SB0"), rg("SB1")], writes=[rg("HB")])
            for gq in range(4):
                rb_, pb_ = bank()
                for g4 in range(4):
                    g = gq * 4 + g4
                    ts_, treg = tsl[g // 8]
                    o = pb_[0:64, g4 * 128:(g4 + 1) * 128]
                    pr = g // 2
                    S.op("pe", lambda e, o=o, g=g, ts_=ts_: e.matmul(o, UT[:, g, :], TB[:, ts_, g % 8, 1, :], start=True, stop=False),
                         reads=[rg("UT"), treg], writes=[rb_], sig=False)
                    S.op("pe", lambda e, o=o, g=g, ts_=ts_, pr=pr: e.matmul(o, HB[:, 0, pr, :], TB[:, ts_, g % 8, 2, :], start=False, stop=False),
                         reads=[rg("HB"), treg], writes=[rb_], sig=False)
                    S.op("pe", lambda e, o=o, g=g, ts_=ts_, pr=pr: e.matmul(o, HB[:, 1, pr, :], TB[:, ts_, g % 8, 3, :], start=False, stop=True),
                         reads=[rg("HB"), treg], writes=[rb_], sig=(g4 == 3))
                hb = (gq // 2) % 2
                gb = gq % 2
                src = pb_[0:64, :]
                gx = [GX[:, 0, i, :] for i in range(3)]
                rgx = rg("GX0")
                S.op("act", lambda e, src=src, gx=gx: e.activation(out=gx[0], in_=src, func=AF.Square), reads=[rb_], writes=[rgx])
                S.op("dve", lambda e, gx=gx: e.tensor_scalar(out=gx[0], in0=gx[0], scalar1=0.044715, scalar2=1.0, op0=ALU.mult, op1=ALU.add), reads=[rgx], writes=[rgx])
                S.op("dve", lambda e, src=src, gx=gx: e.tensor_tensor(out=gx[1], in0=src, in1=gx[0], op=ALU.mult), reads=[rb_, rgx], writes=[rgx])
                S.op("act", lambda e, gx=gx: e.activation(out=gx[2], in_=gx[1], func=AF.Sigmoid, scale=2.0 * math.sqrt(2.0 / math.pi)), reads=[rgx], writes=[rgx])
                S.op("dve", lambda e, src=src, gx=gx, hb=hb, gb=gb: e.tensor_tensor(
                    out=YS[:, hb, :, gb * 64:(gb + 1) * 64].rearrange("p t (g q) -> p g t q", g=4),
                    in0=src.rearrange("p (g t q) -> p g t q", g=4, t=8), in1=gx[2].rearrange("p (g t q) -> p g t q", g=4, t=8), op=ALU.mult),
                    reads=[rb_, rgx], writes=[rg("YS%d" % hb)])
                if gb == 1:
                    ct = ub * 2 + gq // 2
                    rb2, pb2 = bank()
                    pbb = pb2.bitcast(BF16)
                    for t in range(8):
                        S.op("pe", lambda e, pbb=pbb, t=t, hb=hb: e.transpose(pbb[:, t * 64:(t + 1) * 64], YS[:, hb, t, :], IDB[0:64, 0:64]),
                             reads=[rg("YS%d" % hb)] + rc, writes=[rb2], sig=(t == 7))
                    S.op("act", lambda e, pbb=pbb, ct=ct: e.activation(out=YT[:, ct].rearrange("p t j -> p (t j)"), in_=pbb[:, 0:512], func=AF.Copy),
                         reads=[rb2], writes=[rg("YT")])

        def process_tt(xsrc, tt_i, pred, last_pred, ping):
            for sub in range(4):
                S.dma("sp", lambda e, sub=sub: e.dma_start(out=X[:, sub, :], in_=xsrc[tt_i * NT + sub * 128: tt_i * NT + (sub + 1) * 128, :]),
                      "xl%d" % sub, writes=[sub_regs[sub]])
            main = not pred
            if stage >= 1:
                norm_to_HT(G1, sub_regs)
            if main and stage >= 2:
                for qb in range(4):
                    s_, wreg = load_wblock([(lambda sl: sl, wcols(w_in, 2560, qb * 256, 256))])
                    for m in range(2):
                        rb_, pb_ = bank()
                        for k in range(16):
                            S.op("pe", lambda e, pb_=pb_, k=k, m=m, s_=s_: e.matmul(pb_, WB[:, s_, k, m * 128:(m + 1) * 128], HT[:, k, :],
                                                                                   start=(k == 0), stop=(k == 15)),
                                 reads=[rg("HT"), wreg], writes=[rb_], sig=(k == 15))
                        qk_norm(pb_, rb_, QG, QT[:, qb * 2 + m, :], rg("QT"))
            if (main or last_pred) and stage >= 2:
                s_, wreg = load_wblock([(lambda sl: sl, wcols(w_in, 2560, 1024, 256))])
                for kt in range(2):
                    rb_, pb_ = bank()
                    for k in range(16):
                        S.op("pe", lambda e, pb_=pb_, k=k, kt=kt, s_=s_: e.matmul(pb_, WB[:, s_, k, kt * 128:(kt + 1) * 128], HT[:, k, :],
                                                                               start=(k == 0), stop=(k == 15)),
                             reads=[rg("HT"), wreg], writes=[rb_], sig=(k == 15))
                    qk_norm(pb_, rb_, KG, KN[:, kt, :], rg("KN%d" % kt))
                    for c in range(2):
                        kh = kt * 2 + c
                        rb2, pb2 = bank()
                        S.op("pe", lambda e, pb2=pb2, c=c, kt=kt: e.matmul(pb2, DUPB[:, c, :], KN[:, kt, :], start=True, stop=True),
                             reads=[rg("KN%d" % kt)] + rc, writes=[rb2])
                        S.op("act", lambda e, pb2=pb2, kh=kh: e.activation(out=KT2[:, ping, kh, :], in_=pb2, func=AF.Copy),
                             reads=[rb2], writes=[rg("KT%d" % ping)])
                s_, wreg = load_wblock([(lambda sl: sl, wcols(w_in, 2560, 1280, 256))])
                for sub in range(4):
                    rb_, pb_ = bank()
                    for k in range(16):
                        S.op("pe", lambda e, pb_=pb_, k=k, sub=sub, s_=s_: e.matmul(pb_[:, 0:256], HT[:, k, sub * 128:(sub + 1) * 128], WB[:, s_, k, :],
                                                                                   start=(k == 0), stop=(k == 15)),
                             reads=[rg("HT"), wreg], writes=[rb_], sig=(k == 15))
                    S.op("act", lambda e, pb_=pb_, sub=sub: e.activation(out=VA[:, ping, sub, :, 0:64], in_=pb_[:, 0:256].rearrange("p (h d) -> p h d", h=4),
                                                                         func=AF.Copy), reads=[rb_], writes=[rg("VA%d" % ping)])
            if pred:
                for ub in range(4):
                    ssm_part1(ub)
                return

            def attn_block(b):
                yb = 0
                hnb = b % 2
                for kh in range(4):
                    rbo, pbo = bank()
                    for g4 in range(4):
                        h = kh * 4 + g4
                        hp = slice((h % 2) * 64, (h % 2) * 64 + 64)
                        qv = QT[hp, h // 2, b * 128:(b + 1) * 128]
                        if b > 0:
                            kprev = KT2[hp, ping, kh, (b - 1) * 128:b * 128]
                            vprev = VA[:, ping, b - 1, kh, 0:65]
                            rkp, rvp = rg("KT%d" % ping), rg("VA%d" % ping)
                        else:
                            kprev = KT2[hp, 1 - ping, kh, 384:512]
                            vprev = VA[:, 1 - ping, 3, kh, 0:65]
                            rkp, rvp = rg("KT%d" % (1 - ping)), rg("VA%d" % (1 - ping))
                        kcur = KT2[hp, ping, kh, b * 128:(b + 1) * 128]
                        vcur = VA[:, ping, b, kh, 0:65]
                        rbs, pbs = bank()
                        S.op("pe", lambda e, pbs=pbs, kprev=kprev, qv=qv: e.matmul(pbs[:, 0:128], kprev, qv, start=True, stop=True),
                             reads=[rkp, rg("QT")], writes=[rbs], sig=False)
                        S.op("pe", lambda e, pbs=pbs, kcur=kcur, qv=qv: e.matmul(pbs[:, 128:256], kcur, qv, start=True, stop=True),
                             reads=[rg("KT%d" % ping), rg("QT")], writes=[rbs])
                        lb = h % 2
                        S.op("dve", lambda e, pbs=pbs, h=h, lb=lb: e.scalar_tensor_tensor(
                            out=LG[:, lb, :], in0=pbs[:, 0:256], scalar=0.125, in1=BIAS[:, h].rearrange("p a i -> p (a i)"), op0=ALU.mult, op1=ALU.add),
                            reads=[rbs] + rc, writes=[rg("LG%d" % lb)])
                        if b == 0 and tt_i == 0:
                            S.op("act", lambda e, lb=lb: e.activation(out=PT[:, lb, 0:128], in_=LG[:, lb, 0:128], func=AF.Exp, bias=HM8[:, 0:1], scale=1.0),
                                 reads=[rg("LG%d" % lb)] + rc, writes=[rg("PT%d" % lb)])
                            S.op("act", lambda e, lb=lb: e.activation(out=PT[:, lb, 128:256], in_=LG[:, lb, 128:256], func=AF.Exp, bias=NEG8[:, 0:1], scale=1.0),
                                 reads=[rg("LG%d" % lb)] + rc, writes=[rg("PT%d" % lb)])
                        else:
                            S.op("act", lambda e, lb=lb: e.activation(out=PT[:, lb, :], in_=LG[:, lb, :], func=AF.Exp, bias=NEG8[:, 0:1], scale=1.0),
                                 reads=[rg("LG%d" % lb)] + rc, writes=[rg("PT%d" % lb)])
                        o = pbo[:, g4 * 65:(g4 + 1) * 65]
                        S.op("pe", lambda e, o=o, lb=lb, vprev=vprev: e.matmul(o, PT[:, lb, 0:128], vprev, start=True, stop=False),
                             reads=[rg("PT%d" % lb), rvp], writes=[rbo], sig=False)
                        S.op("pe", lambda e, o=o, lb=lb, vcur=vcur: e.matmul(o, PT[:, lb, 128:256], vcur, start=False, stop=True),
                             reads=[rg("PT%d" % lb), rg("VA%d" % ping)], writes=[rbo], sig=(g4 == 3))
                    dn = kh % 2
                    ov = pbo[:, 0:260].rearrange("p (g c) -> p g c", g=4)
                    S.op("dve", lambda e, ov=ov, dn=dn, kh=kh: e.tensor_tensor(out=DEN[:, dn, :], in0=ov[:, :, 64], in1=ESK[:, kh * 4:(kh + 1) * 4], op=ALU.add),
                         reads=[rbo] + rc, writes=[rg("DEN%d" % dn)])
                    S.op("dve", lambda e, dn=dn: e.reciprocal(out=DEN[:, dn, :], in_=DEN[:, dn, :]), reads=[rg("DEN%d" % dn)], writes=[rg("DEN%d" % dn)])
                    S.op("dve", lambda e, ov=ov, dn=dn, kh=kh, yb=yb: e.tensor_tensor(
                        out=YA[:, yb, kh * 256:(kh + 1) * 256].rearrange("p (g d) -> p g d", g=4), in0=ov[:, :, 0:64],
                        in1=DEN[:, dn, :].unsqueeze(2).broadcast_to([128, 4, 64]), op=ALU.mult),
                        reads=[rbo, rg("DEN%d" % dn)], writes=[rg("YA%d" % yb)])
                S.op("act", lambda e, yb=yb, b=b: e.activation(out=ACTB[:, 8:10, :].rearrange("p a b -> p (a b)"), in_=YA[:, yb, :], func=AF.Square, accum_out=SS[:, 4 + b:5 + b]),
                     reads=[rg("YA%d" % yb)], writes=[rg("ACT8"), rg("ACT9"), rg("SSA%d" % b)])
                S.op("dve", lambda e, b=b: e.tensor_scalar(out=RSTD[:, 4 + b:5 + b], in0=SS[:, 4 + b:5 + b], scalar1=1.0 / 1024, scalar2=EPS, op0=ALU.mult, op1=ALU.add),
                     reads=[rg("SSA%d" % b)], writes=[rg("RSA%d" % b)])
                S.op("act", lambda e, b=b: e.activation(out=RSTD[:, 4 + b:5 + b], in_=RSTD[:, 4 + b:5 + b], func=AF.Sqrt), reads=[rg("RSA%d" % b)], writes=[rg("RSA%d" % b)])
                S.op("dve", lambda e, b=b: e.reciprocal(out=RSTD[:, 4 + b:5 + b], in_=RSTD[:, 4 + b:5 + b]), reads=[rg("RSA%d" % b)], writes=[rg("RSA%d" % b)])
                S.op("act", lambda e, yb=yb, b=b, hnb=hnb: e.activation(out=HN[:, hnb, 0:1024], in_=YA[:, yb, :], func=AF.Copy, scale=RSTD[:, 4 + b:5 + b]),
                     reads=[rg("YA%d" % yb), rg("RSA%d" % b)], writes=[rg("HN%d" % hnb)])
                rb_, pb_ = bank()
                pbb = pb_.bitcast(BF16)
                for k8 in range(8):
                    S.op("pe", lambda e, pbb=pbb, k8=k8, hnb=hnb: e.transpose(pbb[:, k8 * 128:(k8 + 1) * 128], HN[:, hnb, k8 * 128:(k8 + 1) * 128], IDB[:]),
                         reads=[rg("HN%d" % hnb)] + rc, writes=[rb_], sig=(k8 == 7))
                S.op("dve", lambda e, pbb=pbb, b=b: e.tensor_tensor(
                    out=ACTB[:, 0:8, b * 128:(b + 1) * 128], in0=pbb.rearrange("p (k t) -> p k t", k=8),
                    in1=GA[:].unsqueeze(2).broadcast_to([128, 8, 128]), op=ALU.mult), reads=[rb_] + rc, writes=[rg("ACT%d" % j_) for j_ in range(8)])

            for i_ in range(4):
                ssm_part1(i_)
                attn_block(i_)
                ssm_part2(i_)
            rbss, pbss = bank(reserve=True)
            for nb in range(4):
                s_, wreg = load_wblock([(lambda sl: sl[:, 0:8, :], wcols(w_glu, 1024, nb * 256, 256, krows=8))])
                for m in range(2):
                    ct = nb * 2 + m
                    rb_, pb_ = bank()
                    for c in range(8):
                        S.op("pe", lambda e, pb_=pb_, c=c, m=m, s_=s_: e.matmul(pb_, WB[:, s_, c, m * 128:(m + 1) * 128], YT[:, c].rearrange("p t j -> p (t j)"),
                                                                               start=(c == 0), stop=(c == 7)),
                             reads=[rg("YT"), wreg], writes=[rb_], sig=(c == 7))
                    sg = ct % 2
                    S.op("act", lambda e, pb_=pb_, sg=sg: e.activation(out=SIG[:, sg, :], in_=pb_, func=AF.Sigmoid), reads=[rb_], writes=[rg("SIG%d" % sg)])
                    S.op("dve", lambda e, ct=ct, sg=sg: e.tensor_tensor(out=ST[:, ct, :], in0=YT[:, ct].rearrange("p t j -> p (t j)"), in1=SIG[:, sg, :], op=ALU.mult),
                         reads=[rg("YT"), rg("SIG%d" % sg)], writes=[rg("QT")])
                    S.op("act", lambda e, ct=ct, sg=sg: e.activation(out=SQ[:, sg, :], in_=ST[:, ct, :], func=AF.Square), reads=[rg("QT")], writes=[rg("SQ%d" % sg)])
                    S.op("pe", lambda e, ct=ct, sg=sg: e.matmul(pbss, ONESB[:], SQ[:, sg, :], start=(ct == 0), stop=(ct == 7)),
                         reads=[rg("SQ%d" % sg)] + rc, writes=[rbss])
            reserved.clear()
            if True:
                S.op("act", lambda e: e.activation(out=RB[:, 0, :], in_=pbss, func=AF.Sqrt, scale=1.0 / 1024, bias=EPS), reads=[rbss], writes=[rg("RB0")])
                S.op("dve", lambda e: e.reciprocal(out=RB[:, 0, :], in_=RB[:, 0, :]), reads=[rg("RB0")], writes=[rg("RB0")])
            for ct in range(8):
                S.op("dve", lambda e, ct=ct: e.scalar_tensor_tensor(
                    out=HT[:, 8 + ct, :].rearrange("p (j t) -> p t j", t=8), in0=ST[:, ct, :].rearrange("p (t j) -> p t j", t=8),
                    scalar=GS[:, ct:ct + 1], in1=RB[:, 0, :].rearrange("p (t j) -> p t j", t=8), op0=ALU.mult, op1=ALU.mult),
                    reads=[rg("QT"), rg("RB0")] + rc, writes=[rg("HT")])
            for fb in range(8):
                s_, wreg = load_wblock([(lambda sl: sl, wcols(w_out, D, fb * 256, 256))])
                for sp2 in range(2):
                    rb_, pb_ = bank()
                    for s2 in range(2):
                        sub = sp2 * 2 + s2
                        for k in range(16):
                            src_ = ACTB if k < 8 else HT
                            S.op("pe", lambda e, pb_=pb_, s2=s2, sub=sub, k=k, s_=s_, src_=src_: e.matmul(
                                pb_[:, s2 * 256:(s2 + 1) * 256], src_[:, k, sub * 128:(sub + 1) * 128], WB[:, s_, k, :], start=(k == 0), stop=(k == 15)),
                                reads=[rg("HT") if k >= 8 else rg("ACT%d" % k), wreg], writes=[rb_], sig=(k == 15))
                        S.op("dve", lambda e, pb_=pb_, s2=s2, sub=sub, fb=fb: e.tensor_tensor(
                            out=X[:, sub, fb * 256:(fb + 1) * 256], in0=pb_[:, s2 * 256:(s2 + 1) * 256], in1=X[:, sub, fb * 256:(fb + 1) * 256], op=ALU.add),
                            reads=[rb_, sub_regs[sub]], writes=[sub_regs[sub]])
            if stage >= 7:
                norm_to_HT(G2, sub_regs)
            c0 = 0
            for grp, nch in enumerate(GROUPS_FF if stage >= 7 else []):
                for bl in range(nch // 2):
                    col = (c0 + bl * 2) * 128
                    sg_, wg = load_wblock([(lambda sl: sl, wcols(w_gate, FF, col, 256))])
                    su_, wu = load_wblock([(lambda sl: sl, wcols(w_up, FF, col, 256))])
                    for m in range(2):
                        j = bl * 2 + m
                        rbg, pbg = bank()
                        for k in range(16):
                            S.op("pe", lambda e, pbg=pbg, k=k, m=m, sg_=sg_: e.matmul(pbg, WB[:, sg_, k, m * 128:(m + 1) * 128], HT[:, k, :], start=(k == 0), stop=(k == 15)),
                                 reads=[rg("HT"), wg], writes=[rbg], sig=(k == 15))
                        rbu, pbu = bank()
                        for k in range(16):
                            S.op("pe", lambda e, pbu=pbu, k=k, m=m, su_=su_: e.matmul(pbu, WB[:, su_, k, m * 128:(m + 1) * 128], HT[:, k, :], start=(k == 0), stop=(k == 15)),
                                 reads=[rg("HT"), wu], writes=[rbu], sig=(k == 15))
                        sg = j % 2
                        S.op("act", lambda e, pbg=pbg, sg=sg: e.activation(out=SIG[:, sg, :], in_=pbg, func=AF.Silu), reads=[rbg], writes=[rg("SIG%d" % sg)])
                        S.op("dve", lambda e, pbu=pbu, sg=sg, j=j: e.tensor_tensor(out=ACTB[:, j, :], in0=pbu, in1=SIG[:, sg, :], op=ALU.mult),
                             reads=[rbu, rg("SIG%d" % sg)], writes=[rg("ACT%d" % j)])
                for f in range(4):
                    bk = [bank() for _ in range(4)]
                    for j in range(nch):
                        ws = (wslot_d[0]) % 6
                        wslot_d[0] += 1
                        wreg = rg("WD%d" % ws)
                        S.dma("pool", lambda e, ws=ws, j=j, f=f, c0=c0: e.dma_start(out=WD[:, ws, :], in_=w_down[(c0 + j) * 128:(c0 + j + 1) * 128, f * 512:(f + 1) * 512]),
                              "wd%d" % ws, writes=[wreg])
                        for sub in range(4):
                            S.op("pe", lambda e, sub=sub, j=j, ws=ws, pb_=bk[sub][1]: e.matmul(pb_, ACTB[:, j, sub * 128:(sub + 1) * 128], WD[:, ws, :],
                                                                                              start=(j == 0), stop=(j == nch - 1)),
                                 reads=[rg("ACT%d" % j), wreg], writes=[bk[sub][0]], sig=(j == nch - 1 or sub == 3))
                    for sub in range(4):
                        S.op("dve", lambda e, sub=sub, f=f, pb_=bk[sub][1]: e.tensor_tensor(
                            out=X[:, sub, f * 512:(f + 1) * 512], in0=pb_, in1=X[:, sub, f * 512:(f + 1) * 512], op=ALU.add),
                            reads=[bk[sub][0], sub_regs[sub]], writes=[sub_regs[sub]])
                c0 += nch
            for sub in range(4):
                out_toks.append(S.dma("sp", lambda e, sub=sub: e.dma_start(out=out[tt_i * NT + sub * 128: tt_i * NT + (sub + 1) * 128, :], in_=X[:, sub, :]),
                                      "ot%d" % sub, reads=[sub_regs[sub]]))

        wslot_d = [0]
        out_toks = []
        ping = 0
        if do_pred:
            for t in range(n_tt):
                process_tt(x_pred, t, True, t == n_tt - 1, ping)
            ping = 1 - ping
        for t in range(n_tt):
            process_tt(x_main, t, False, False, ping)
            ping = 1 - ping
        for tk in out_toks:
            S.wait_tok("sp", tk)
        with nc.Block() as block:
            S.replay(block)
    return nc


def _t5_bucket(dist):
    n = np.maximum(dist, 0)
    max_exact = 16
    nf = np.maximum(n, 1).astype(np.float32)
    large = max_exact + (np.log(nf / max_exact) / math.log(128 / max_exact) * (32 - max_exact)).astype(np.int32)
    large = np.minimum(large, 31)
    return np.where(n < max_exact, n, large).astype(np.int32)


def _consts():
    ident = np.eye(128, dtype=np.float32)
    s_idx = np.arange(128) // 16
    mask = (s_idx[:, None] <= s_idx[None, :]).astype(np.float32)
    mv = np.broadcast_to(np.asarray(MS, np.float32)[None, :, None], (128, NM, 32)).reshape(128, NM * 32).copy()
    bones = (s_idx[:, None] // 4 == s_idx[None, :] // 4).astype(np.float32)
    bucket = _t5_bucket(np.arange(128))
    oh = np.zeros((33, 512), np.float32)
    for e in range(255):
        if e < 127:
            oh[bucket[e + 1], e] = 1.0
            oh[32, 256 + e] = NEG
        else:
            oh[32, e] = NEG
            oh[bucket[e - 127], 256 + e] = 1.0
    oh[32, 255] = NEG
    oh[32, 511] = NEG
    dup = np.zeros((128, 2, 128), np.float32)
    for c in range(2):
        for d in range(64):
            dup[c * 64 + d, c, d] = 1.0
            dup[c * 64 + d, c, 64 + d] = 1.0
    mv2 = np.broadcast_to((8.0 * np.arange(1, 65, dtype=np.float32))[None, :, None], (128, 64, 32)).reshape(128, 64 * 32).copy()
    return {"c_mv2": mv2, "c_dup": dup.reshape(128, 256), "c_ident": ident, "c_mask": mask, "c_mv": mv, "c_oh": oh, "c_bones": bones, "c_anti": np.ascontiguousarray(ident[::-1])}


_NC_CACHE = {}


def kernel(**inputs):
    n_tt = int(os.environ.get("MK_NTT", "4"))
    do_pred = os.environ.get("MK_PRED", "1") == "1"
    stage = int(os.environ.get("MK_STAGE", "99"))
    do_setup = os.environ.get("MK_SETUP", "1") == "1"
    key = (n_tt, do_pred, stage, do_setup)
    if key not in _NC_CACHE:
        _NC_CACHE[key] = build(n_tt, do_pred, stage, do_setup)
    nc = _NC_CACHE[key]
    x = np.asarray(inputs["x"], np.float32)
    TOK = n_tt * NT
    shared = {k: np.ascontiguousarray(np.asarray(inputs[k], np.float32)[0]) for k in
              ["ln1_g", "w_in", "q_norm_g", "k_norm_g", "attn_sinks", "ssm_a_re", "ssm_a_im", "ssm_log_dt", "ssm_b_re", "ssm_b_im",
               "ssm_c_re", "ssm_c_im", "w_glu", "attn_out_g", "ssm_out_g", "w_out", "ln2_g", "w_ff_gate", "w_ff_up", "w_ff_down"]}
    shared["ssm_d"] = np.ascontiguousarray(np.asarray(inputs["ssm_d"], np.float32)[0].reshape(-1))
    shared["rel_bias"] = np.ascontiguousarray(np.asarray(inputs["rel_bias"], np.float32))
    shared.update(_consts())
    in_maps = []
    ncores = int(os.environ.get("MK_CORES", "8"))
    for c in range(ncores):
        b, half = c // 2, c % 2
        m = dict(shared)
        m["x_main"] = np.ascontiguousarray(x[b, half * 2048: half * 2048 + TOK])
        if half == 1:
            m["x_pred"] = np.ascontiguousarray(x[b, 2048 - TOK:2048])
            m["hm8"] = np.full((128, 1), -SHIFT, np.float32)
        else:
            m["x_pred"] = np.zeros((TOK, D), np.float32)
            m["hm8"] = np.full((128, 1), NEG - SHIFT, np.float32)
        in_maps.append(m)
    if os.environ.get("MK_TRACE", "0") == "1":
        res = run_bass_kernel_spmd(nc, in_maps, core_ids=list(range(ncores)), trace=True)
        print("EXEC_NS", res.exec_time_ns)
    else:
        res = run_bass_kernel_spmd(nc, in_maps, core_ids=list(range(ncores)))
    outp = np.zeros((4, 4096, D), np.float32)
    for c in range(ncores):
        b, half = c // 2, c % 2
        outp[b, half * 2048: half * 2048 + TOK] = res.results[c]["out"]
    return outp
```

```python
import os
import math
import numpy as np
import ml_dtypes
import concourse.bass as bass
import concourse.mybir as mybir
from concourse.bass_utils import run_bass_kernel_spmd

F32 = mybir.dt.float32
BF16 = mybir.dt.bfloat16
I32 = mybir.dt.int32
ALU = mybir.AluOpType
AF = mybir.ActivationFunctionType

D = 2048
NT = 512
FF = 5632
EPS = 1e-6
NEG = -30000.0
SHIFT = 8.0
MS = [-(s + 1) for s in range(8)] + [t + 1 for t in range(8)] + [7 - s for s in range(8)] + [8, 16, 32, 64, 128, 256]
NM = len(MS)
GROUPS_FF = [12, 10, 12, 10]


class Reg:
    __slots__ = ("w", "r")

    def __init__(self):
        self.w = None
        self.r = {}


class Sched:
    def __init__(self, nc, semh):
        self.nc = nc
        self.semh = semh
        self.names = ["pe", "act", "dve", "pool", "sp"]
        self.ops = {k: [] for k in self.names}
        self.cnt = {k: 0 for k in self.names}
        self.waited = {k: {} for k in self.names}
        self.dcnt = {}

    def _deps(self, e, reads, writes):
        deps = {}

        def add(tok):
            if tok is None:
                return
            s, v = tok
            if e == "pe" and s == "pe":
                return
            if deps.get(s, 0) < v:
                deps[s] = v
        for r in reads:
            add(r.w)
        for w in writes:
            add(w.w)
            for s, v in w.r.items():
                add((s, v))
        out = []
        for s, v in deps.items():
            if self.waited[e].get(s, 0) < v:
                self.waited[e][s] = v
                out.append((s, v))
        return out

    def _upd(self, tok, reads, writes):
        s, v = tok
        for r in reads:
            if r.r.get(s, 0) < v:
                r.r[s] = v
        for w in writes:
            w.w = tok
            w.r = {}

    def op(self, e, fn, reads=(), writes=(), sig=True):
        waits = self._deps(e, reads, writes)
        if sig:
            self.cnt[e] += 1
            v = self.cnt[e]
        else:
            v = self.cnt[e] + 1
        self.ops[e].append((waits, fn, (e, 1) if sig else None))
        self._upd((e, v), reads, writes)

    def dma(self, q, fn, sem, reads=(), writes=()):
        waits = self._deps(q, reads, writes)
        self.dcnt[sem] = self.dcnt.get(sem, 0) + 16
        tok = (sem, self.dcnt[sem])
        self.ops[q].append((waits, fn, (sem, 16)))
        self._upd(tok, reads, writes)
        return tok

    def wait_tok(self, e, tok):
        s, v = tok
        if self.waited[e].get(s, 0) < v:
            self.waited[e][s] = v
            self.ops[e].append(([(s, v)], None, None))

    def replay(self, block):
        decs = {"pe": block.tensor, "act": block.scalar, "dve": block.vector, "pool": block.gpsimd, "sp": block.sync}
        for name in self.names:
            lst = self.ops[name]
            if not lst:
                continue

            def body(e, lst=lst):
                for waits, fn, inc in lst:
                    for s, v in waits:
                        e.wait_ge(self.semh[s], v)
                    if fn is None:
                        continue
                    ins = fn(e)
                    if inc is not None:
                        ins.then_inc(self.semh[inc[0]], inc[1])
            decs[name](body)
            self.ops[name] = []


def dap(t, offset, dims):
    return bass.AP(tensor=t.tensor, offset=offset, ap=[[s, c] for s, c in dims])


def build(n_tt=4, do_pred=True, stage=99, do_setup=True):
    nc = bass.Bass("TRN2", target_bir_lowering=False)
    TOK = n_tt * NT

    def din(name, shape, dt=F32):
        return nc.dram_tensor(name, list(shape), dt, kind="ExternalInput").ap()

    x_main = din("x_main", [TOK, D])
    x_pred = din("x_pred", [TOK, D])
    hm8 = din("hm8", [128, 1])
    rel_bias = din("rel_bias", [32, 16])
    ln1_g = din("ln1_g", [D])
    w_in = din("w_in", [D, 2560])
    q_norm_g = din("q_norm_g", [64])
    k_norm_g = din("k_norm_g", [64])
    sinks = din("attn_sinks", [16])
    a_re = din("ssm_a_re", [64, 64])
    a_im = din("ssm_a_im", [64, 64])
    log_dt = din("ssm_log_dt", [64])
    b_re = din("ssm_b_re", [64, 64, 16])
    b_im = din("ssm_b_im", [64, 64, 16])
    c_re = din("ssm_c_re", [64, 16, 64])
    c_im = din("ssm_c_im", [64, 16, 64])
    ssm_d = din("ssm_d", [64 * 16])
    w_glu = din("w_glu", [1024, 1024])
    attn_out_g = din("attn_out_g", [1024])
    ssm_out_g = din("ssm_out_g", [1024])
    w_out = din("w_out", [D, D])
    ln2_g = din("ln2_g", [D])
    w_gate = din("w_ff_gate", [D, FF])
    w_up = din("w_ff_up", [D, FF])
    w_down = din("w_ff_down", [FF, D])
    c_ident = din("c_ident", [128, 128])
    c_mask = din("c_mask", [128, 128])
    c_mv = din("c_mv", [128, NM * 32])
    c_mv2 = din("c_mv2", [128, 64 * 32])
    c_oh = din("c_oh", [33, 512])
    c_bones = din("c_bones", [128, 128])
    c_anti = din("c_anti", [128, 128])
    c_dup = din("c_dup", [128, 256])
    out = nc.dram_tensor("out", [TOK, D], F32, kind="ExternalOutput").ap()
    wt_d = nc.dram_tensor("wt_d", [128, 64 * 128], BF16, kind="Internal").ap()
    kt_d = nc.dram_tensor("kt_d", [128, 64 * 128], BF16, kind="Internal").ap()
    et_d = nc.dram_tensor("et_d", [128, 64 * 256], BF16, kind="Internal").ap()
    ext_d = nc.dram_tensor("ext_d", [16, 512], F32, kind="Internal").ap()

    sem_names = ["pe", "act", "dve", "pool", "sp", "xl0", "xl1", "xl2", "xl3", "wb0", "wb1", "wb2", "wd", "tb0", "tb1",
                 "st", "misc", "ot0", "ot1", "ot2", "ot3"] + ["wd%d" % i for i in range(10)] + ["wb3", "wb4"]
    import contextlib
    with contextlib.ExitStack() as es:
        semh = {n: es.enter_context(nc.semaphore(n)) for n in sem_names}
        S = Sched(nc, semh)

        def sb(name, shape, dt):
            return es.enter_context(nc.sbuf_tensor(name, list(shape), dt))

        ROTC = sb("ROTC", [128, 32, 64], BF16)
        ROTS = sb("ROTS", [128, 32, 64], BF16)
        RHO = sb("RHO", [128, 32], F32)
        BIAS = sb("BIAS", [128, 16, 2, 128], BF16)
        G1 = sb("G1", [128, 16], F32)
        G2 = sb("G2", [128, 16], F32)
        GA = sb("GA", [128, 8], F32)
        GS = sb("GS", [128, 8], F32)
        QG = sb("QG", [128, 1], F32)
        KG = sb("KG", [128, 1], F32)
        ESK = sb("ESK", [128, 16], F32)
        HM8 = sb("HM8", [128, 1], F32)
        DUPB = sb("DUPB", [128, 2, 128], BF16)
        NEG8 = sb("NEG8", [128, 1], F32)
        IDB = sb("IDB", [128, 128], BF16)
        BONES = sb("BONES", [128, 128], BF16)
        ONESB = sb("ONESB", [128, 128], BF16)
        r_const = Reg()
        PS = es.enter_context(nc.psum_tensor("PS", [128, 8, 512], F32))
        r_ps = [Reg() for _ in range(8)]
        bank_i = [0]

        reserved = set()

        def bank(reserve=False):
            while True:
                i = bank_i[0] % 8
                bank_i[0] += 1
                if i not in reserved:
                    break
            if reserve:
                reserved.add(i)
            return r_ps[i], PS[:, i, :]

        with contextlib.ExitStack() as es2:
            def sb2(name, shape, dt=F32):
                return es2.enter_context(nc.sbuf_tensor(name, list(shape), dt))
            ARE = sb2("ARE", [128, 32]); AIM = sb2("AIM", [128, 32]); LDT = sb2("LDT", [128, 32])
            DT = sb2("DT", [128, 32]); ARD = sb2("ARD", [128, 32]); AID = sb2("AID", [128, 32])
            MV = sb2("MV", [128, NM, 32])
            MAG = sb2("MAG", [128, NM, 32])
            ANG = sb2("ANG", [128, 2, NM, 32])
            SC = sb2("SC", [128, 2, NM, 32])
            PWR = sb2("PWR", [128, NM, 32]); PWI = sb2("PWI", [128, NM, 32])
            SM = [sb2("SM%d" % i, [128, 32]) for i in range(10)]
            BRE = sb2("BRE", [128, 32, 16]); BIM = sb2("BIM", [128, 32, 16])
            BBR = sb2("BBR", [128, 32, 16]); BBI = sb2("BBI", [128, 32, 16])
            CR = sb2("CR", [128, 32, 16]); CI = sb2("CI", [128, 32, 16])
            T1 = sb2("T1", [128, 32, 8, 16]); T2 = T1
            VR = sb2("VR", [128, 32, 8, 16]); VI = sb2("VI", [128, 32, 8, 16])
            WR = VR; WI = VI
            ER = sb2("ER", [128, 32, 8, 16]); EI = sb2("EI", [128, 32, 8, 16])
            KIv = ER[:].bitcast(I32).rearrange("p a b c -> p (a b c)")[:, 0:2 * NM * 32].rearrange("p (s m r) -> p s m r", s=2, m=NM)
            KFv = VR[:].rearrange("p a b c -> p (a b c)")[:, 0:2 * NM * 32].rearrange("p (s m r) -> p s m r", s=2, m=NM)
            IDF = sb2("IDF", [128, 128]); MASK = sb2("MASK", [128, 128]); BONF = sb2("BONF", [128, 128])
            DROW = sb2("DROW", [128, 64, 16])
            KTB = sb2("KTB", [128, 64, 128], BF16)
            MV2 = KTB[:].bitcast(F32).rearrange("p g c -> p (g c)")[:, 0:2048].rearrange("p (m r) -> p m r", m=64)
            WTB = KTB
            ETB = sb2("ETB", [128, 64, 128], BF16)
            TMPK = sb2("TMPK", [128, 4, 8, 16])
            RBA = sb2("RBA", [33, 16]); OH = sb2("OH", [33, 512])
            EXTV = SC[:].rearrange("p a m r -> p (a m r)")[0:16, 0:512]
            SKV = sb2("SKV", [128, 16])
            ANTF = sb2("ANTF", [128, 128]); ANTB = sb2("ANTB", [128, 128], BF16)
            DUPF = sb2("DUPF", [128, 2, 128])
            r = {n: Reg() for n in ["in", "dt", "mag", "ang", "ki", "kf", "cm", "sc", "pw", "sm", "bb", "t1", "t2", "v", "w", "e",
                                    "ktb", "wtb", "etb", "tmpk", "ext", "biasf", "extd", "wtd", "ktd", "etd", "pers", "cin"]}

            ld = []

            def L(out_ap, in_ap):
                ld.append(S.dma("sp", lambda e, o=out_ap, i=in_ap: e.dma_start(out=o, in_=i, allow_slow_non_contiguous=True),
                                "misc", writes=[r["in"]]))
            for gi in range(2):
                hs = slice(gi * 64, (gi + 1) * 64)
                L(ARE[hs, :], dap(a_re, gi * 64, [(1, 64), (128, 32)]))
                L(AIM[hs, :], dap(a_im, gi * 64, [(1, 64), (128, 32)]))
                L(LDT[hs, :], dap(log_dt, gi, [(0, 64), (2, 32)]))
                L(BRE[hs, :, :], dap(b_re, gi * 1024, [(16, 64), (2048, 32), (1, 16)]))
                L(BIM[hs, :, :], dap(b_im, gi * 1024, [(16, 64), (2048, 32), (1, 16)]))
            L(MV[:], c_mv.rearrange("p (m r) -> p m r", m=NM))
            L(MV2, c_mv2.rearrange("p (m r) -> p m r", m=64))
            L(IDF[:], c_ident)
            L(MASK[:], c_mask)
            L(BONF[:], c_bones)
            L(ANTF[:], c_anti)
            L(DUPF[:], c_dup.rearrange("p (c m) -> p c m", c=2))
            L(DROW[:], dap(ssm_d, 0, [(0, 128), (16, 64), (1, 16)]))
            L(RBA[0:32, :], rel_bias)
            L(OH[:], c_oh)
            L(G1[:], dap(ln1_g, 0, [(1, 128), (128, 16)]))
            L(G2[:], dap(ln2_g, 0, [(1, 128), (128, 16)]))
            L(GA[:], dap(attn_out_g, 0, [(1, 128), (128, 8)]))
            L(GS[:], dap(ssm_out_g, 0, [(1, 128), (128, 8)]))
            for h2 in range(2):
                L(QG[h2 * 64:(h2 + 1) * 64, :], dap(q_norm_g, 0, [(1, 64), (1, 1)]))
                L(KG[h2 * 64:(h2 + 1) * 64, :], dap(k_norm_g, 0, [(1, 64), (1, 1)]))
            L(SKV[:], dap(sinks, 0, [(0, 128), (1, 16)]))
            L(HM8[:], hm8)

            V_ = "dve"
            rin = [r["in"]]
            CZ = T1[0:32].rearrange("p a b c -> p (a b c)")[:, 0:1024].rearrange("p (i c) -> p i c", i=8)
            S.op("dve", lambda e: e.memset(CZ, 0.0), writes=[r["t1"]])
            for ci, (csrc, CX) in enumerate(((c_re, CR), (c_im, CI))):
                for ch in range(4):
                    pr0 = ch * 8
                    for gi in range(2):
                        S.dma("sp", lambda e, gi=gi, pr0=pr0, csrc=csrc: e.dma_start(
                            out=CZ[gi * 16:(gi + 1) * 16, :, gi * 64:(gi + 1) * 64],
                            in_=dap(csrc, (2 * pr0 + gi) * 1024, [(64, 16), (2048, 8), (1, 64)])), "xl0", writes=[r["t1"]])
                    rb_, pb_ = bank()
                    for i in range(8):
                        S.op("pe", lambda e, pb_=pb_, i=i: e.transpose(pb_[:, i * 32:(i + 1) * 32], CZ[:, i, :], IDF[0:32, 0:32]),
                             reads=[r["t1"]] + rin, writes=[rb_], sig=(i == 7))
                    for gi in range(2):
                        hs = slice(gi * 64, (gi + 1) * 64)
                        S.op("act", lambda e, pb_=pb_, hs=hs, gi=gi, pr0=pr0, CX=CX: e.activation(
                            out=CX[hs, pr0:pr0 + 8, :], in_=pb_[hs, 0:256].rearrange("p (i g q) -> p i g q", i=8, g=2)[:, :, gi, :], func=AF.Copy),
                            reads=[rb_], writes=[r["cin"]])
            S.op(V_, lambda e: e.memset(NEG8[:], -SHIFT), writes=[r_const])
            S.op(V_, lambda e: e.memset(ONESB[:], 1.0), writes=[r_const])
            S.op(V_, lambda e: e.tensor_copy(out=IDB[:], in_=IDF[:]), reads=rin, writes=[r_const])
            S.op(V_, lambda e: e.tensor_copy(out=BONES[:], in_=BONF[:]), reads=rin, writes=[r_const])
            S.op(V_, lambda e: e.tensor_copy(out=DUPB[:], in_=DUPF[:]), reads=rin, writes=[r_const])
            S.op(V_, lambda e: e.memset(RBA[32:33, :], 1.0), reads=rin, writes=[r["in"]])
            S.op("act", lambda e: e.activation(out=ESK[:], in_=SKV[:], func=AF.Exp, bias=NEG8[:, 0:1], scale=1.0),
                 reads=rin + [r_const], writes=[r["pers"]])
            S.op("act", lambda e: e.activation(out=DT[:], in_=LDT[:], func=AF.Exp), reads=rin, writes=[r["dt"]])
            S.op(V_, lambda e: e.tensor_tensor(out=ARD[:], in0=ARE[:], in1=DT[:], op=ALU.mult), reads=rin + [r["dt"]], writes=[r["sm"]])
            S.op(V_, lambda e: e.tensor_tensor(out=AID[:], in0=AIM[:], in1=DT[:], op=ALU.mult), reads=rin + [r["dt"]], writes=[r["sm"]])

            def bc_m(t):
                return t[:].unsqueeze(1).broadcast_to([128, NM, 32])
            S.op(V_, lambda e: e.tensor_tensor(out=MAG[:], in0=MV[:], in1=bc_m(ARD), op=ALU.mult), reads=rin + [r["sm"]], writes=[r["mag"]])
            S.op("act", lambda e: e.activation(out=MAG[:], in_=MAG[:], func=AF.Exp), reads=[r["mag"]], writes=[r["mag"]])
            S.op(V_, lambda e: e.tensor_tensor(out=ANG[:, 0], in0=MV[:], in1=bc_m(AID), op=ALU.mult), reads=rin + [r["sm"]], writes=[r["ang"]])
            S.op(V_, lambda e: e.tensor_scalar(out=ANG[:, 0], in0=ANG[:, 0], scalar1=1.0 / (2 * math.pi), scalar2=None, op0=ALU.mult),
                 reads=[r["ang"]], writes=[r["ang"]])
            S.op(V_, lambda e: e.tensor_scalar(out=ANG[:, 1], in0=ANG[:, 0], scalar1=0.25, scalar2=None, op0=ALU.add),
                 reads=[r["ang"]], writes=[r["ang"]])
            S.op(V_, lambda e: e.tensor_copy(out=KIv, in_=ANG[:]), reads=[r["ang"]], writes=[r["e"]])
            S.op(V_, lambda e: e.tensor_copy(out=KFv, in_=KIv), reads=[r["e"]], writes=[r["v"]])
            S.op(V_, lambda e: e.tensor_tensor(out=ANG[:], in0=ANG[:], in1=KFv, op=ALU.subtract), reads=[r["ang"], r["v"]], writes=[r["ang"]])
            S.op(V_, lambda e: e.tensor_scalar(out=KFv, in0=ANG[:], scalar1=0.5, scalar2=None, op0=ALU.is_gt), reads=[r["ang"]], writes=[r["v"]])
            S.op(V_, lambda e: e.tensor_tensor(out=ANG[:], in0=ANG[:], in1=KFv, op=ALU.subtract), reads=[r["ang"], r["v"]], writes=[r["ang"]])
            S.op(V_, lambda e: e.tensor_scalar(out=KFv, in0=ANG[:], scalar1=-0.5, scalar2=None, op0=ALU.is_lt), reads=[r["ang"]], writes=[r["v"]])
            S.op(V_, lambda e: e.tensor_tensor(out=ANG[:], in0=ANG[:], in1=KFv, op=ALU.add), reads=[r["ang"], r["v"]], writes=[r["ang"]])
            S.op("act", lambda e: e.activation(out=SC[:], in_=ANG[:], func=AF.Sin, scale=6.283185), reads=[r["ang"]], writes=[r["sc"]])
            S.op(V_, lambda e: e.tensor_tensor(out=PWR[:], in0=MAG[:], in1=SC[:, 1], op=ALU.mult), reads=[r["mag"], r["sc"]], writes=[r["pw"]])
            S.op(V_, lambda e: e.tensor_tensor(out=PWI[:], in0=MAG[:], in1=SC[:, 0], op=ALU.mult), reads=[r["mag"], r["sc"]], writes=[r["pw"]])
            S.op(V_, lambda e: e.tensor_copy(out=RHO[:], in_=MAG[:, 15, :]), reads=[r["mag"]], writes=[r["pers"]])
            A2 = T1[:].rearrange("p a b c -> p (a b c)").rearrange("p (s j r) -> p s j r", s=2, j=64)
            K2 = ER[:].bitcast(I32).rearrange("p a b c -> p (a b c)").rearrange("p (s j r) -> p s j r", s=2, j=64)
            F2 = VR[:].rearrange("p a b c -> p (a b c)").rearrange("p (s j r) -> p s j r", s=2, j=64)
            S2 = VI[:].rearrange("p a b c -> p (a b c)").rearrange("p (s j r) -> p s j r", s=2, j=64)
            ra, rk, rf = [r["t1"]], [r["e"]], [r["v"]]
            S.op(V_, lambda e: e.tensor_tensor(out=A2[:, 0], in0=MV2, in1=AID[:].unsqueeze(1).broadcast_to([128, 64, 32]), op=ALU.mult),
                 reads=rin + [r["sm"], r["ktb"]], writes=ra)
            S.op(V_, lambda e: e.tensor_scalar(out=A2[:, 0], in0=A2[:, 0], scalar1=1.0 / (2 * math.pi), scalar2=None, op0=ALU.mult), reads=ra, writes=ra)
            S.op(V_, lambda e: e.tensor_scalar(out=A2[:, 1], in0=A2[:, 0], scalar1=0.25, scalar2=None, op0=ALU.add), reads=ra, writes=ra)
            S.op(V_, lambda e: e.tensor_copy(out=K2, in_=A2), reads=ra, writes=rk)
            S.op(V_, lambda e: e.tensor_copy(out=F2, in_=K2), reads=rk, writes=rf)
            S.op(V_, lambda e: e.tensor_tensor(out=A2, in0=A2, in1=F2, op=ALU.subtract), reads=ra + rf, writes=ra)
            S.op(V_, lambda e: e.tensor_scalar(out=F2, in0=A2, scalar1=0.5, scalar2=None, op0=ALU.is_gt), reads=ra, writes=rf)
            S.op(V_, lambda e: e.tensor_tensor(out=A2, in0=A2, in1=F2, op=ALU.subtract), reads=ra + rf, writes=ra)
            S.op(V_, lambda e: e.tensor_scalar(out=F2, in0=A2, scalar1=-0.5, scalar2=None, op0=ALU.is_lt), reads=ra, writes=rf)
            S.op(V_, lambda e: e.tensor_tensor(out=A2, in0=A2, in1=F2, op=ALU.add), reads=ra + rf, writes=ra)
            S.op("act", lambda e: e.activation(out=S2, in_=A2, func=AF.Sin, scale=6.283185), reads=ra, writes=rf)
            S.op(V_, lambda e: e.tensor_copy(out=ROTS[:].rearrange("p r j -> p j r"), in_=S2[:, 0]), reads=rf, writes=[r["pers"]])
            S.op(V_, lambda e: e.tensor_copy(out=ROTC[:].rearrange("p r j -> p j r"), in_=S2[:, 1]), reads=rf, writes=[r["pers"]])
            nr, ni, den, rden, t0, t1_, fr, fi = SM[0], SM[1], SM[2], SM[3], SM[4], SM[5], SM[6], SM[7]
            rs = [r["sm"]]
            rp = [r["pw"]]

            def tt(o, a, b, op, reads, writes):
                S.op(V_, lambda e: e.tensor_tensor(out=o, in0=a, in1=b, op=op), reads=reads, writes=writes)
            S.op(V_, lambda e: e.tensor_scalar(out=nr[:], in0=PWR[:, 8, :], scalar1=-1.0, scalar2=None, op0=ALU.add), reads=rp, writes=rs)
            tt(den[:], ARE[:], ARE[:], ALU.mult, rin, rs)
            tt(t0[:], AIM[:], AIM[:], ALU.mult, rin, rs)
            tt(den[:], den[:], t0[:], ALU.add, rs, rs)
            S.op(V_, lambda e: e.reciprocal(out=rden[:], in_=den[:]), reads=rs, writes=rs)
            tt(t0[:], nr[:], ARE[:], ALU.mult, rs + rin, rs)
            tt(t1_[:], PWI[:, 8, :], AIM[:], ALU.mult, rp + rin, rs)
            tt(t0[:], t0[:], t1_[:], ALU.add, rs, rs)
            tt(fr[:], t0[:], rden[:], ALU.mult, rs, rs)
            tt(t0[:], PWI[:, 8, :], ARE[:], ALU.mult, rp + rin, rs)
            tt(t1_[:], nr[:], AIM[:], ALU.mult, rs + rin, rs)
            tt(t0[:], t0[:], t1_[:], ALU.subtract, rs, rs)
            tt(fi[:], t0[:], rden[:], ALU.mult, rs, rs)

            def bq(t):
                return t[:].unsqueeze(2).broadcast_to([128, 32, 16])
            rb = [r["bb"]]
            tt(BBR[:], BRE[:], bq(fr), ALU.mult, rin + rs, rb)
            tt(T1[:, :, 0, :], BIM[:], bq(fi), ALU.mult, rin + rs, [r["t1"]])
            tt(BBR[:], BBR[:], T1[:, :, 0, :], ALU.subtract, rb + [r["t1"]], rb)
            tt(BBI[:], BIM[:], bq(fr), ALU.mult, rin + rs, rb)
            tt(T1[:, :, 0, :], BRE[:], bq(fi), ALU.mult, rin + rs, [r["t1"]])
            tt(BBI[:], BBI[:], T1[:, :, 0, :], ALU.add, rb + [r["t1"]], rb)

            def pw8(t, i0):
                return t[:, i0:i0 + 8, :].rearrange("p s r -> p r s").unsqueeze(3).broadcast_to([128, 32, 8, 16])

            def x8(t):
                return t[:].unsqueeze(2).broadcast_to([128, 32, 8, 16])

            def cmul(OR, OI, i0, XR, XI, rx, ro, neg_im=False):
                tt(OR[:], pw8(PWR, i0), x8(XR), ALU.mult, rp + rx, ro)
                tt(T1[:], pw8(PWI, i0), x8(XI), ALU.mult, rp + rx, [r["t1"]])
                tt(OR[:], OR[:], T1[:], ALU.subtract, ro + [r["t1"]], ro)
                tt(OI[:], pw8(PWR, i0), x8(XI), ALU.mult, rp + rx, ro)
                tt(T2[:], pw8(PWI, i0), x8(XR), ALU.mult, rp + rx, [r["t1"]])
                tt(OI[:], OI[:], T2[:], ALU.add, ro + [r["t1"]], ro)
                if neg_im:
                    S.op(V_, lambda e: e.tensor_scalar(out=OI[:], in0=OI[:], scalar1=-1.0, scalar2=None, op0=ALU.mult), reads=ro, writes=ro)
            cmul(VR, VI, 0, BBR, BBI, rb, [r["v"]])
            cmul(ER, EI, 8, CR, CI, rin + [r["cin"]], [r["e"]], neg_im=True)

            for pq in range(8):
                banks = [bank(), bank()]
                for gi in range(2):
                    hs = slice(gi * 64, (gi + 1) * 64)
                    rb_, pb_ = banks[gi]
                    for p4 in range(4):
                        pr = pq * 4 + p4
                        o = pb_[:, p4 * 128:(p4 + 1) * 128]
                        S.op("pe", lambda e, o=o, hs=hs, pr=pr: e.matmul(o, VR[hs, pr].rearrange("p s q -> p (s q)"),
                                                                         ER[hs, pr].rearrange("p s q -> p (s q)"), start=True, stop=False),
                             reads=[r["v"], r["e"]], writes=[rb_], sig=False)
                        S.op("pe", lambda e, o=o, hs=hs, pr=pr: e.matmul(o, VI[hs, pr].rearrange("p s q -> p (s q)"),
                                                                         EI[hs, pr].rearrange("p s q -> p (s q)"), start=False, stop=True),
                             reads=[r["v"], r["e"]], writes=[rb_], sig=(p4 == 3))
                    gsel = slice(2 * pq * 4 + gi, 2 * (pq * 4 + 4), 2)
                    S.op(V_, lambda e, gsel=gsel: e.tensor_tensor(
                        out=TMPK[:], in0=IDF[:].rearrange("p (t q) -> p t q", t=8).unsqueeze(1).broadcast_to([128, 4, 8, 16]),
                        in1=DROW[:, gsel, :].unsqueeze(2).broadcast_to([128, 4, 8, 16]), op=ALU.mult),
                        reads=rin, writes=[r["tmpk"]])
                    S.op(V_, lambda e, pb_=pb_: e.tensor_tensor(
                        out=pb_.rearrange("p (g c) -> p g c", g=4), in0=pb_.rearrange("p (g c) -> p g c", g=4),
                        in1=MASK[:].unsqueeze(1).broadcast_to([128, 4, 128]), op=ALU.mult), reads=[rb_] + rin, writes=[rb_])
                    S.op(V_, lambda e, pb_=pb_, gsel=gsel: e.tensor_tensor(
                        out=KTB[:, gsel, :], in0=pb_.rearrange("p (g c) -> p g c", g=4),
                        in1=TMPK[:].rearrange("p g t q -> p g (t q)"), op=ALU.add), reads=[rb_, r["tmpk"]], writes=[r["ktb"]])
            tk1 = S.dma("sp", lambda e: e.dma_start(out=kt_d, in_=KTB[:].rearrange("p g c -> p (g c)")), "st", reads=[r["ktb"]], writes=[r["ktd"]])
            cmul(WR, WI, 16, BBR, BBI, rb, [r["v"]])
            for c, WX in enumerate((WR, WI)):
                for pq in range(8):
                    rb_, pb_ = bank()
                    for p4 in range(4):
                        pr = pq * 4 + p4
                        S.op("pe", lambda e, pb_=pb_, p4=p4, pr=pr, WX=WX: e.transpose(
                            pb_[:, p4 * 128:(p4 + 1) * 128], WX[:, pr].rearrange("p s q -> p (s q)"), IDF[:]),
                            reads=[r["v"]] + rin, writes=[rb_], sig=(p4 == 3))
                    S.op("act", lambda e, pb_=pb_, pq=pq, c=c: e.activation(
                        out=WTB[:, pq * 8:(pq + 1) * 8, c * 64:(c + 1) * 64],
                        in_=pb_.rearrange("p (g n) -> p g n", g=8), func=AF.Copy), reads=[rb_], writes=[r["ktb"]])
            tk2 = S.dma("sp", lambda e: e.dma_start(out=wt_d, in_=WTB[:].rearrange("p g c -> p (g c)")), "st", reads=[r["ktb"]], writes=[r["wtd"]])
            tk3s = []
            for c, EX in enumerate((ER, EI)):
                S.op("pool", lambda e: e.memset(ETB[:], 0.0), writes=[r["etb"]])
                for gi in range(2):
                    hs = slice(gi * 64, (gi + 1) * 64)
                    S.op("act", lambda e, hs=hs, gi=gi, EX=EX: e.activation(
                        out=ETB[hs, gi::2, :], in_=EX[hs].rearrange("p r t q -> p r (t q)"), func=AF.Copy),
                        reads=[r["e"]], writes=[r["etb"]])
                tk3s.append(S.dma("sp", lambda e, c=c: e.dma_start(out=et_d.rearrange("p (g c x) -> p g c x", g=64, c=2)[:, :, c, :], in_=ETB[:]),
                                  "st", reads=[r["etb"]], writes=[r["etd"]]))
            rb_, pb_ = bank()
            S.op("pe", lambda e: e.matmul(pb_[0:16, :], RBA[:], OH[:], start=True, stop=True), reads=rin, writes=[rb_])
            S.op("act", lambda e: e.activation(out=EXTV, in_=pb_[0:16, :], func=AF.Copy), reads=[rb_, r["sc"]], writes=[r["sc"]])
            tk4 = S.dma("sp", lambda e: e.dma_start(out=ext_d, in_=EXTV), "st", reads=[r["sc"]], writes=[r["extd"]])
            BREV = T1[:].bitcast(BF16).rearrange("p a b c -> p (a b c)")[:, 0:4096].rearrange("p (h a i) -> p h a i", h=16, a=2)
            for h in range(16):
                S.dma("pool", lambda e, h=h: e.dma_start(out=BREV[:, h, :, :], in_=dap(ext_d, h * 512, [(1, 128), (256, 2), (1, 128)])),
                      "wd", reads=[r["extd"]], writes=[r["biasf"], r["t1"]])
            S.op(V_, lambda e: e.tensor_copy(out=ANTB[:], in_=ANTF[:]), reads=rin, writes=[r["tmpk"]])
            for h4 in range(8):
                rb_, pb_ = bank()
                S.op("pe", lambda e, pb_=pb_, h4=h4: e.matmul(pb_, ANTB[:], BREV[:, h4 * 2:(h4 + 1) * 2].rearrange("p h a i -> p (h a i)"), start=True, stop=True),
                     reads=[r["biasf"], r["tmpk"]], writes=[rb_])
                S.op("act", lambda e, pb_=pb_, h4=h4: e.activation(out=BIAS[:, h4 * 2:(h4 + 1) * 2].rearrange("p h a i -> p (h a i)"), in_=pb_, func=AF.Copy),
                     reads=[rb_], writes=[r["pers"]])
            for tk in (tk1, tk2, tk4) + tuple(tk3s) + tuple(ld):
                S.wait_tok("sp", tk)
            if not do_setup:
                for k_ in S.ops:
                    S.ops[k_] = []
            else:
                with nc.Block() as block:
                    S.replay(block)

        X = sb("X", [128, 4, D], F32)
        HT = sb("HT", [128, 16, NT], BF16)
        ACTB = sb("ACTB", [128, 12, NT], BF16)
        WB = sb("WB", [128, 3, 16, 256], BF16)
        WD = sb("WD", [128, 6, 512], BF16)
        TB = sb("TB", [128, 2, 8, 4, 128], BF16)
        QT = sb("QT", [128, 8, NT], BF16)
        ST = QT
        KT2 = sb("KT2", [128, 2, 4, NT], BF16)
        KN = sb("KN", [128, 2, NT], BF16)
        VA = sb("VA", [128, 2, 4, 4, 66], BF16)
        YA = sb("YA", [128, 1, 1024], F32)
        HN = sb("HN", [128, 2, D], BF16)
        SS = sb("SS", [128, 8], F32)
        RSTD = sb("RSTD", [128, 8], F32)
        UQ = sb("UQ", [64, 16, 8, 16], BF16)
        UT = sb("UT", [128, 16, 64], BF16)
        SA = sb("SA", [128, 2, 8, 64], F32)
        SB_ = sb("SB", [128, 2, 8, 64], F32)
        HIN = sb("HIN", [128, 2, 32], F32)
        HB = sb("HB", [128, 2, 8, 64], BF16)
        TA = sb("TA", [128, 2, 8, 64], F32)
        YS = sb("YS", [64, 2, 8, 128], BF16)
        GX = sb("GX", [64, 1, 3, 512], F32)
        YT = sb("YT", [128, 8, 8, 64], BF16)
        SQ = sb("SQ", [128, 2, NT], BF16)
        SIG = sb("SIG", [128, 2, NT], BF16)
        RB = sb("RB", [128, 1, NT], F32)
        LG = sb("LG", [128, 2, 256], F32)
        PT = sb("PT", [128, 2, 256], BF16)
        DEN = sb("DEN", [128, 2, 4], F32)

        R = {}

        def rg(name):
            if name not in R:
                R[name] = Reg()
            return R[name]
        rc = [r_const]
        sub_regs = [rg("X%d" % i) for i in range(4)]

        S.op("dve", lambda e: e.memset(VA[:], 1.0), writes=[rg("VA0"), rg("VA1")])
        S.op("dve", lambda e: e.memset(KT2[:], 0.0), writes=[rg("KT0"), rg("KT1")])
        S.op("dve", lambda e: e.memset(HIN[:], 0.0), writes=[rg("HIN")])

        wslot = [0]
        wb_ring3 = [(WB[:, i], "WB%d" % i, "wb%d" % i) for i in range(3)]
        wb_ring5 = wb_ring3 + [
            (QT[:].rearrange("p a b -> p (a b)").rearrange("p (k c) -> p k c", k=16), "QT", "wb3"),
            (YT[:].rearrange("p a b c -> p (a b c)").rearrange("p (k c) -> p k c", k=16), "YT", "wb4")]
        wd_ring = [(WD[:, i, :], "WD%d" % i, "wd%d" % i) for i in range(6)] + [
            (KN[:, 0, :], "KN0", "wd6"), (KN[:, 1, :], "KN1", "wd7"), (SQ[:, 0, :], "SQ0", "wd8"), (SQ[:, 1, :], "SQ1", "wd9")]

        def load_wblock(src_ap_fn_list, ffn=False):
            ring = wb_ring5 if ffn else wb_ring3
            ap_, rname, sname = ring[wslot[0] % len(ring)]
            wslot[0] += 1
            reg = rg(rname)
            for ov, ia in src_ap_fn_list:
                S.dma("pool", lambda e, ov=ov, ia=ia, ap_=ap_: e.dma_start(out=ov(ap_), in_=ia, allow_slow_non_contiguous=True),
                      sname, writes=[reg])
            return ap_, reg

        def wcols(W, ncols_total, c0, ncol, krows=16):
            return dap(W, c0, [(ncols_total, 128), (128 * ncols_total, krows), (1, ncol)])

        def norm_to_HT(G, xsrc_regs):
            for sub in range(4):
                S.op("act", lambda e, sub=sub: e.activation(out=ACTB[:, 8:12, :].rearrange("p a b -> p (a b)"), in_=X[:, sub, :], func=AF.Square, accum_out=SS[:, sub:sub + 1]),
                     reads=[xsrc_regs[sub]], writes=[rg("ACT8"), rg("ACT9"), rg("ACT10"), rg("ACT11"), rg("SS")])
            S.op("dve", lambda e: e.tensor_scalar(out=RSTD[:, 0:4], in0=SS[:, 0:4], scalar1=1.0 / D, scalar2=EPS, op0=ALU.mult, op1=ALU.add),
                 reads=[rg("SS")], writes=[rg("RSTD")])
            S.op("act", lambda e: e.activation(out=RSTD[:, 0:4], in_=RSTD[:, 0:4], func=AF.Sqrt), reads=[rg("RSTD")], writes=[rg("RSTD")])
            S.op("dve", lambda e: e.reciprocal(out=RSTD[:, 0:4], in_=RSTD[:, 0:4]), reads=[rg("RSTD")], writes=[rg("RSTD")])
            for sub in range(4):
                hb = sub % 2
                S.op("act", lambda e, sub=sub, hb=hb: e.activation(out=HN[:, hb, :], in_=X[:, sub, :], func=AF.Copy, scale=RSTD[:, sub:sub + 1]),
                     reads=[xsrc_regs[sub], rg("RSTD")], writes=[rg("HN%d" % hb)])
                for kh in range(2):
                    rb_, pb_ = bank()
                    pbb = pb_.bitcast(BF16)
                    for k8 in range(8):
                        k = kh * 8 + k8
                        S.op("pe", lambda e, pbb=pbb, k8=k8, k=k, hb=hb: e.transpose(pbb[:, k8 * 128:(k8 + 1) * 128],
                                                                                     HN[:, hb, k * 128:(k + 1) * 128], IDB[:]),
                             reads=[rg("HN%d" % hb)] + rc, writes=[rb_], sig=(k8 == 7))
                    S.op("dve", lambda e, pbb=pbb, kh=kh, sub=sub: e.tensor_tensor(
                        out=HT[:, kh * 8:(kh + 1) * 8, sub * 128:(sub + 1) * 128], in0=pbb.rearrange("p (k t) -> p k t", k=8),
                        in1=G[:, kh * 8:(kh + 1) * 8].unsqueeze(2).broadcast_to([128, 8, 128]), op=ALU.mult),
                        reads=[rb_] + rc, writes=[rg("HT")])

        def qk_norm(pb_, rb_, GV, out_ap, out_reg):
            S.op("act", lambda e: e.activation(out=SQ[:, 0, :], in_=pb_, func=AF.Square), reads=[rb_], writes=[rg("SQ0")])
            rb2, pb2 = bank()
            S.op("pe", lambda e: e.matmul(pb2, BONES[:], SQ[:, 0, :], start=True, stop=True), reads=[rg("SQ0")] + rc, writes=[rb2])
            S.op("act", lambda e: e.activation(out=RB[:, 0, :], in_=pb2, func=AF.Sqrt, scale=1.0 / 64, bias=EPS), reads=[rb2], writes=[rg("RB0")])
            S.op("dve", lambda e: e.reciprocal(out=RB[:, 0, :], in_=RB[:, 0, :]), reads=[rg("RB0")], writes=[rg("RB0")])
            S.op("dve", lambda e: e.scalar_tensor_tensor(out=out_ap, in0=pb_, scalar=GV[:, 0:1], in1=RB[:, 0, :], op0=ALU.mult, op1=ALU.mult),
                 reads=[rb_, rg("RB0")] + rc, writes=[out_reg])

        ssm_batch = [0]

        ssm_ctx = {}

        def ssm_part1(ub):
            s_, wreg = load_wblock([(lambda sl: sl, wcols(w_in, 2560, 1536 + ub * 256, 256))])
            tsl = []
            for ch in range(2):
                g0 = ub * 16 + ch * 8
                ts_ = ssm_batch[0] % 2
                ssm_batch[0] += 1
                treg = rg("TB%d" % ts_)
                S.dma("sp", lambda e, ts_=ts_, g0=g0: e.dma_start(out=TB[:, ts_, :, 0, :], in_=wt_d[:, g0 * 128:(g0 + 8) * 128].rearrange("p (g c) -> p g c", g=8)),
                      "tb%d" % ts_, writes=[treg])
                S.dma("sp", lambda e, ts_=ts_, g0=g0: e.dma_start(out=TB[:, ts_, :, 1, :], in_=kt_d[:, g0 * 128:(g0 + 8) * 128].rearrange("p (g c) -> p g c", g=8)),
                      "tb%d" % ts_, writes=[treg])
                S.dma("sp", lambda e, ts_=ts_, g0=g0: e.dma_start(out=TB[:, ts_, :, 2:4, :], in_=et_d[:, g0 * 256:(g0 + 8) * 256].rearrange("p (g c x) -> p g c x", g=8, c=2)),
                      "tb%d" % ts_, writes=[treg])
                tsl.append((ts_, treg))
            for sp_ in range(4):
                rb_, pb_ = bank()
                for s2 in range(2):
                    s = sp_ * 2 + s2
                    for k in range(16):
                        S.op("pe", lambda e, pb_=pb_, s2=s2, s=s, k=k, s_=s_: e.matmul(
                            pb_[0:64, s2 * 256:(s2 + 1) * 256], HT[:, k, s:NT:8], s_[:, k, :], start=(k == 0), stop=(k == 15)),
                            reads=[rg("HT"), wreg], writes=[rb_], sig=(k == 15 and s2 == 1))
                S.op("act", lambda e, pb_=pb_, sp_=sp_: e.activation(
                    out=UQ[:, :, sp_ * 2:(sp_ + 1) * 2, :].rearrange("p g s q -> p s g q"),
                    in_=pb_[0:64, :].rearrange("p (s g q) -> p s g q", s=2, g=16), func=AF.Copy),
                    reads=[rb_], writes=[rg("UQ")])
            for gh in range(2):
                rb_, pb_ = bank()
                pbb = pb_.bitcast(BF16)
                for g8 in range(8):
                    g = gh * 8 + g8
                    S.op("pe", lambda e, pbb=pbb, g8=g8, g=g: e.transpose(pbb[:, g8 * 64:(g8 + 1) * 64],
                                                                          UQ[:, g].rearrange("p s q -> p (s q)"), IDB[0:64, 0:64]),
                         reads=[rg("UQ")] + rc, writes=[rb_], sig=(g8 == 7))
                S.op("dve", lambda e, pbb=pbb, gh=gh: e.tensor_copy(out=UT[:, gh * 8:(gh + 1) * 8, :],
                                                                    in_=pbb[:, 0:512].rearrange("p (g j) -> p g j", g=8)),
                     reads=[rb_], writes=[rg("UT")])
            rbr, pbr = bank()
            rbi, pbi = bank()
            for pr in range(8):
                for gi in range(2):
                    g = 2 * pr + gi
                    ts_, treg = tsl[g // 8]
                    hs = slice(gi * 64, (gi + 1) * 64)
                    last = (pr == 7 and gi == 1)
                    S.op("pe", lambda e, hs=hs, pr=pr, g=g, ts_=ts_: e.matmul(pbr[hs, pr * 64:(pr + 1) * 64], TB[:, ts_, g % 8, 0, 0:64], UT[:, g, :],
                                                                               start=True, stop=True),
                         reads=[rg("UT"), treg], writes=[rbr], sig=False)
                    S.op("pe", lambda e, hs=hs, pr=pr, g=g, ts_=ts_: e.matmul(pbi[hs, pr * 64:(pr + 1) * 64], TB[:, ts_, g % 8, 0, 64:128], UT[:, g, :],
                                                                               start=True, stop=True),
                         reads=[rg("UT"), treg], writes=[rbi], sig=last)
            S.op("act", lambda e: e.activation(out=SA[:, 0], in_=pbr.rearrange("p (r j) -> p r j", r=8), func=AF.Copy), reads=[rbr], writes=[rg("SA"), rg("SC0")])
            S.op("act", lambda e: e.activation(out=SA[:, 1], in_=pbi.rearrange("p (r j) -> p r j", r=8), func=AF.Copy), reads=[rbi], writes=[rg("SA"), rg("SC1")])
            prs = slice(ub * 8, (ub + 1) * 8)
            rH, rSA, rSB = rg("HIN"), rg("SA"), rg("SB")
            Cc, Sn = ROTC[:, prs, :], ROTS[:, prs, :]

            def vtt(o, a_, b_, op, reads, writes, eng="dve"):
                S.op(eng, lambda e: e.tensor_tensor(out=o, in0=a_, in1=b_, op=op), reads=reads, writes=writes)
            vtt(TA[:, 0], Cc, SA[:, 0], ALU.mult, [rSA] + rc, [rg("TA0")])
            vtt(TA[:, 1], Sn, SA[:, 1], ALU.mult, [rSA] + rc, [rg("TA1")])
            vtt(SB_[:, 0], TA[:, 0], TA[:, 1], ALU.add, [rg("TA0"), rg("TA1")], [rg("SB0")])
            vtt(TA[:, 0], Cc, SA[:, 1], ALU.mult, [rSA] + rc, [rg("TA0")])
            vtt(TA[:, 1], Sn, SA[:, 0], ALU.mult, [rSA] + rc, [rg("TA1")])
            vtt(SB_[:, 1], TA[:, 0], TA[:, 1], ALU.subtract, [rg("TA0"), rg("TA1")], [rg("SB1")])
            S.op("dve", lambda e: e.tensor_copy(out=HB[:, :, :, 0], in_=HIN[:, :, prs]), reads=[rH], writes=[rg("HB")])
            for c in range(2):
                for pr in range(8):
                    gp = ub * 8 + pr
                    S.op("dve", lambda e, c=c, pr=pr, gp=gp: e.tensor_tensor_scan(
                        out=SA[:, c, pr, :], data0=RHO[:, gp:gp + 1].broadcast_to([128, 64]), data1=SB_[:, c, pr, :],
                        initial=HIN[:, c, gp:gp + 1], op0=ALU.mult, op1=ALU.add),
                        reads=[rg("SB%d" % c), rH] + rc, writes=[rg("SC%d" % c)])
            vtt(TA[:, 0], Cc, SA[:, 0], ALU.mult, [rg("SC0")] + rc, [rg("TA0")])
            vtt(TA[:, 1], Sn, SA[:, 1], ALU.mult, [rg("SC1")] + rc, [rg("TA1")])
            vtt(SB_[:, 0], TA[:, 0], TA[:, 1], ALU.subtract, [rg("TA0"), rg("TA1")], [rg("SB0")])
            vtt(TA[:, 0], Cc, SA[:, 1], ALU.mult, [rg("SC1")] + rc, [rg("TA0")])
            vtt(TA[:, 1], Sn, SA[:, 0], ALU.mult, [rg("SC0")] + rc, [rg("TA1")])
            vtt(SB_[:, 1], TA[:, 0], TA[:, 1], ALU.add, [rg("TA0"), rg("TA1")], [rg("SB1")])
            S.op("dve", lambda e: e.tensor_copy(out=HIN[:, :, prs], in_=SB_[:, :, :, 63]), reads=[rg("SB0"), rg("SB1")], writes=[rH])
            ssm_ctx[ub] = tsl

        def ssm_part2(ub):
            tsl = ssm_ctx[ub]
            S.op("dve", lambda e: e.tensor_copy(out=HB[:, :, :, 1:64], in_=SB_[:, :, :, 0:63]), reads=[rg("SB0"), rg("SB1")], writes=[rg("HB")])
            for gq in range(4):
                rb_, pb_ = bank()
                for g4 in range(4):
                    g = gq * 4 + g4
                    ts_, treg = tsl[g // 8]
                    o = pb_[0:64, g4 * 128:(g4 + 1) * 128]
                    pr = g // 2
                    S.op("pe", lambda e, o=o, g=g, ts_=ts_: e.matmul(o, UT[:, g, :], TB[:, ts_, g % 8, 1, :], start=True, stop=False),
                         reads=[rg("UT"), treg], writes=[rb_], sig=False)
                    S.op("pe", lambda e, o=o, g=g, ts_=ts_, pr=pr: e.matmul(o, HB[:, 0, pr, :], TB[:, ts_, g % 8, 2, :], start=False, stop=False),
                         reads=[rg("HB"), treg], writes=[rb_], sig=False)
                    S.op("pe", lambda e, o=o, g=g, ts_=ts_, pr=pr: e.matmul(o, HB[:, 1, pr, :], TB[:, ts_, g % 8, 3, :], start=False, stop=True),
                         reads=[rg("HB"), treg], writes=[rb_], sig=(g4 == 3))
                hb = (gq // 2) % 2
                gb = gq % 2
                src = pb_[0:64, :]
                gx = [GX[:, 0, i, :] for i in range(3)]
                rgx = rg("GX0")
                S.op("act", lambda e, src=src, gx=gx: e.activation(out=gx[0], in_=src, func=AF.Square), reads=[rb_], writes=[rgx])
                S.op("dve", lambda e, gx=gx: e.tensor_scalar(out=gx[0], in0=gx[0], scalar1=0.044715, scalar2=1.0, op0=ALU.mult, op1=ALU.add), reads=[rgx], writes=[rgx])
                S.op("dve", lambda e, src=src, gx=gx: e.tensor_tensor(out=gx[1], in0=src, in1=gx[0], op=ALU.mult), reads=[rb_, rgx], writes=[rgx])
                S.op("act", lambda e, gx=gx: e.activation(out=gx[2], in_=gx[1], func=AF.Sigmoid, scale=2.0 * math.sqrt(2.0 / math.pi)), reads=[rgx], writes=[rgx])
                S.op("dve", lambda e, src=src, gx=gx, hb=hb, gb=gb: e.tensor_tensor(
                    out=YS[:, hb, :, gb * 64:(gb + 1) * 64].rearrange("p t (g q) -> p g t q", g=4),
                    in0=src.rearrange("p (g t q) -> p g t q", g=4, t=8), in1=gx[2].rearrange("p (g t q) -> p g t q", g=4, t=8), op=ALU.mult),
                    reads=[rb_, rgx], writes=[rg("YS%d" % hb)])
                if gb == 1:
                    ct = ub * 2 + gq // 2
                    rb2, pb2 = bank()
                    pbb = pb2.bitcast(BF16)
                    for t in range(8):
                        S.op("pe", lambda e, pbb=pbb, t=t, hb=hb: e.transpose(pbb[:, t * 64:(t + 1) * 64], YS[:, hb, t, :], IDB[0:64, 0:64]),
                             reads=[rg("YS%d" % hb)] + rc, writes=[rb2], sig=(t == 7))
                    S.op("act", lambda e, pbb=pbb, ct=ct: e.activation(out=YT[:, ct].rearrange("p t j -> p (t j)"), in_=pbb[:, 0:512], func=AF.Copy),
                         reads=[rb2], writes=[rg("YT")])

        def process_tt(xsrc, tt_i, pred, last_pred, ping):
            for sub in range(4):
                S.dma("sp", lambda e, sub=sub: e.dma_start(out=X[:, sub, :], in_=xsrc[tt_i * NT + sub * 128: tt_i * NT + (sub + 1) * 128, :]),
                      "xl%d" % sub, writes=[sub_regs[sub]])
            main = not pred
            if stage >= 1:
                norm_to_HT(G1, sub_regs)
            if main and stage >= 2:
                for qb in range(4):
                    s_, wreg = load_wblock([(lambda sl: sl, wcols(w_in, 2560, qb * 256, 256))])
                    for m in range(2):
                        rb_, pb_ = bank()
                        for k in range(16):
                            S.op("pe", lambda e, pb_=pb_, k=k, m=m, s_=s_: e.matmul(pb_, s_[:, k, m * 128:(m + 1) * 128], HT[:, k, :],
                                                                                   start=(k == 0), stop=(k == 15)),
                                 reads=[rg("HT"), wreg], writes=[rb_], sig=(k == 15))
                        qk_norm(pb_, rb_, QG, QT[:, qb * 2 + m, :], rg("QT"))
            if (main or last_pred) and stage >= 2:
                s_, wreg = load_wblock([(lambda sl: sl, wcols(w_in, 2560, 1024, 256))])
                for kt in range(2):
                    rb_, pb_ = bank()
                    for k in range(16):
                        S.op("pe", lambda e, pb_=pb_, k=k, kt=kt, s_=s_: e.matmul(pb_, s_[:, k, kt * 128:(kt + 1) * 128], HT[:, k, :],
                                                                               start=(k == 0), stop=(k == 15)),
                             reads=[rg("HT"), wreg], writes=[rb_], sig=(k == 15))
                    qk_norm(pb_, rb_, KG, KN[:, kt, :], rg("KN%d" % kt))
                    for c in range(2):
                        kh = kt * 2 + c
                        rb2, pb2 = bank()
                        S.op("pe", lambda e, pb2=pb2, c=c, kt=kt: e.matmul(pb2, DUPB[:, c, :], KN[:, kt, :], start=True, stop=True),
                             reads=[rg("KN%d" % kt)] + rc, writes=[rb2])
                        S.op("act", lambda e, pb2=pb2, kh=kh: e.activation(out=KT2[:, ping, kh, :], in_=pb2, func=AF.Copy),
                             reads=[rb2], writes=[rg("KT%d" % ping)])
                s_, wreg = load_wblock([(lambda sl: sl, wcols(w_in, 2560, 1280, 256))])
                for sub in range(4):
                    rb_, pb_ = bank()
                    for k in range(16):
                        S.op("pe", lambda e, pb_=pb_, k=k, sub=sub, s_=s_: e.matmul(pb_[:, 0:256], HT[:, k, sub * 128:(sub + 1) * 128], s_[:, k, :],
                                                                                   start=(k == 0), stop=(k == 15)),
                             reads=[rg("HT"), wreg], writes=[rb_], sig=(k == 15))
                    S.op("act", lambda e, pb_=pb_, sub=sub: e.activation(out=VA[:, ping, sub, :, 0:64], in_=pb_[:, 0:256].rearrange("p (h d) -> p h d", h=4),
                                                                         func=AF.Copy), reads=[rb_], writes=[rg("VA%d" % ping)])
            if pred:
                for ub in range(4):
                    ssm_part1(ub)
                return

            def attn_block(b):
                yb = 0
                hnb = b % 2
                for kh in range(4):
                    rbo, pbo = bank()
                    for g4 in range(4):
                        h = kh * 4 + g4
                        hp = slice((h % 2) * 64, (h % 2) * 64 + 64)
                        qv = QT[hp, h // 2, b * 128:(b + 1) * 128]
                        if b > 0:
                            kprev = KT2[hp, ping, kh, (b - 1) * 128:b * 128]
                            vprev = VA[:, ping, b - 1, kh, 0:65]
                            rkp, rvp = rg("KT%d" % ping), rg("VA%d" % ping)
                        else:
                            kprev = KT2[hp, 1 - ping, kh, 384:512]
                            vprev = VA[:, 1 - ping, 3, kh, 0:65]
                            rkp, rvp = rg("KT%d" % (1 - ping)), rg("VA%d" % (1 - ping))
                        kcur = KT2[hp, ping, kh, b * 128:(b + 1) * 128]
                        vcur = VA[:, ping, b, kh, 0:65]
                        rbs, pbs = bank()
                        S.op("pe", lambda e, pbs=pbs, kprev=kprev, qv=qv: e.matmul(pbs[:, 0:128], kprev, qv, start=True, stop=True),
                             reads=[rkp, rg("QT")], writes=[rbs], sig=False)
                        S.op("pe", lambda e, pbs=pbs, kcur=kcur, qv=qv: e.matmul(pbs[:, 128:256], kcur, qv, start=True, stop=True),
                             reads=[rg("KT%d" % ping), rg("QT")], writes=[rbs])
                        lb = h % 2
                        S.op("dve", lambda e, pbs=pbs, h=h, lb=lb: e.scalar_tensor_tensor(
                            out=LG[:, lb, :], in0=pbs[:, 0:256], scalar=0.125, in1=BIAS[:, h].rearrange("p a i -> p (a i)"), op0=ALU.mult, op1=ALU.add),
                            reads=[rbs] + rc, writes=[rg("LG%d" % lb)])
                        if b == 0 and tt_i == 0:
                            S.op("act", lambda e, lb=lb: e.activation(out=PT[:, lb, 0:128], in_=LG[:, lb, 0:128], func=AF.Exp, bias=HM8[:, 0:1], scale=1.0),
                                 reads=[rg("LG%d" % lb)] + rc, writes=[rg("PT%d" % lb)])
                            S.op("act", lambda e, lb=lb: e.activation(out=PT[:, lb, 128:256], in_=LG[:, lb, 128:256], func=AF.Exp, bias=NEG8[:, 0:1], scale=1.0),
                                 reads=[rg("LG%d" % lb)] + rc, writes=[rg("PT%d" % lb)])
                        else:
                            S.op("act", lambda e, lb=lb: e.activation(out=PT[:, lb, :], in_=LG[:, lb, :], func=AF.Exp, bias=NEG8[:, 0:1], scale=1.0),
                                 reads=[rg("LG%d" % lb)] + rc, writes=[rg("PT%d" % lb)])
                        o = pbo[:, g4 * 65:(g4 + 1) * 65]
                        S.op("pe", lambda e, o=o, lb=lb, vprev=vprev: e.matmul(o, PT[:, lb, 0:128], vprev, start=True, stop=False),
                             reads=[rg("PT%d" % lb), rvp], writes=[rbo], sig=False)
                        S.op("pe", lambda e, o=o, lb=lb, vcur=vcur: e.matmul(o, PT[:, lb, 128:256], vcur, start=False, stop=True),
                             reads=[rg("PT%d" % lb), rg("VA%d" % ping)], writes=[rbo], sig=(g4 == 3))
                    dn = kh % 2
                    ov = pbo[:, 0:260].rearrange("p (g c) -> p g c", g=4)
                    S.op("dve", lambda e, ov=ov, dn=dn, kh=kh: e.tensor_tensor(out=DEN[:, dn, :], in0=ov[:, :, 64], in1=ESK[:, kh * 4:(kh + 1) * 4], op=ALU.add),
                         reads=[rbo] + rc, writes=[rg("DEN%d" % dn)])
                    S.op("dve", lambda e, dn=dn: e.reciprocal(out=DEN[:, dn, :], in_=DEN[:, dn, :]), reads=[rg("DEN%d" % dn)], writes=[rg("DEN%d" % dn)])
                    S.op("dve", lambda e, ov=ov, dn=dn, kh=kh, yb=yb: e.tensor_tensor(
                        out=YA[:, yb, kh * 256:(kh + 1) * 256].rearrange("p (g d) -> p g d", g=4), in0=ov[:, :, 0:64],
                        in1=DEN[:, dn, :].unsqueeze(2).broadcast_to([128, 4, 64]), op=ALU.mult),
                        reads=[rbo, rg("DEN%d" % dn)], writes=[rg("YA%d" % yb)])
                S.op("act", lambda e, yb=yb, b=b: e.activation(out=ACTB[:, 8:10, :].rearrange("p a b -> p (a b)"), in_=YA[:, yb, :], func=AF.Square, accum_out=SS[:, 4 + b:5 + b]),
                     reads=[rg("YA%d" % yb)], writes=[rg("ACT8"), rg("ACT9"), rg("SSA%d" % b)])
                S.op("dve", lambda e, b=b: e.tensor_scalar(out=RSTD[:, 4 + b:5 + b], in0=SS[:, 4 + b:5 + b], scalar1=1.0 / 1024, scalar2=EPS, op0=ALU.mult, op1=ALU.add),
                     reads=[rg("SSA%d" % b)], writes=[rg("RSA%d" % b)])
                S.op("act", lambda e, b=b: e.activation(out=RSTD[:, 4 + b:5 + b], in_=RSTD[:, 4 + b:5 + b], func=AF.Sqrt), reads=[rg("RSA%d" % b)], writes=[rg("RSA%d" % b)])
                S.op("dve", lambda e, b=b: e.reciprocal(out=RSTD[:, 4 + b:5 + b], in_=RSTD[:, 4 + b:5 + b]), reads=[rg("RSA%d" % b)], writes=[rg("RSA%d" % b)])
                S.op("act", lambda e, yb=yb, b=b, hnb=hnb: e.activation(out=HN[:, hnb, 0:1024], in_=YA[:, yb, :], func=AF.Copy, scale=RSTD[:, 4 + b:5 + b]),
                     reads=[rg("YA%d" % yb), rg("RSA%d" % b)], writes=[rg("HN%d" % hnb)])
                rb_, pb_ = bank()
                pbb = pb_.bitcast(BF16)
                for k8 in range(8):
                    S.op("pe", lambda e, pbb=pbb, k8=k8, hnb=hnb: e.transpose(pbb[:, k8 * 128:(k8 + 1) * 128], HN[:, hnb, k8 * 128:(k8 + 1) * 128], IDB[:]),
                         reads=[rg("HN%d" % hnb)] + rc, writes=[rb_], sig=(k8 == 7))
                S.op("dve", lambda e, pbb=pbb, b=b: e.tensor_tensor(
                    out=ACTB[:, 0:8, b * 128:(b + 1) * 128], in0=pbb.rearrange("p (k t) -> p k t", k=8),
                    in1=GA[:].unsqueeze(2).broadcast_to([128, 8, 128]), op=ALU.mult), reads=[rb_] + rc, writes=[rg("ACT%d" % j_) for j_ in range(8)])

            for i_ in range(4):
                ssm_part1(i_)
                attn_block(i_)
                ssm_part2(i_)
            rbss, pbss = bank(reserve=True)
            for nb in range(4):
                s_, wreg = load_wblock([(lambda sl: sl[:, 0:8, :], wcols(w_glu, 1024, nb * 256, 256, krows=8))])
                for m in range(2):
                    ct = nb * 2 + m
                    rb_, pb_ = bank()
                    for c in range(8):
                        S.op("pe", lambda e, pb_=pb_, c=c, m=m, s_=s_: e.matmul(pb_, s_[:, c, m * 128:(m + 1) * 128], YT[:, c].rearrange("p t j -> p (t j)"),
                                                                               start=(c == 0), stop=(c == 7)),
                             reads=[rg("YT"), wreg], writes=[rb_], sig=(c == 7))
                    sg = ct % 2
                    S.op("act", lambda e, pb_=pb_, sg=sg: e.activation(out=SIG[:, sg, :], in_=pb_, func=AF.Sigmoid), reads=[rb_], writes=[rg("SIG%d" % sg)])
                    S.op("dve", lambda e, ct=ct, sg=sg: e.tensor_tensor(out=ST[:, ct, :], in0=YT[:, ct].rearrange("p t j -> p (t j)"), in1=SIG[:, sg, :], op=ALU.mult),
                         reads=[rg("YT"), rg("SIG%d" % sg)], writes=[rg("QT")])
                    S.op("act", lambda e, ct=ct, sg=sg: e.activation(out=SQ[:, sg, :], in_=ST[:, ct, :], func=AF.Square), reads=[rg("QT")], writes=[rg("SQ%d" % sg)])
                    S.op("pe", lambda e, ct=ct, sg=sg: e.matmul(pbss, ONESB[:], SQ[:, sg, :], start=(ct == 0), stop=(ct == 7)),
                         reads=[rg("SQ%d" % sg)] + rc, writes=[rbss])
            reserved.clear()
            if True:
                S.op("act", lambda e: e.activation(out=RB[:, 0, :], in_=pbss, func=AF.Sqrt, scale=1.0 / 1024, bias=EPS), reads=[rbss], writes=[rg("RB0")])
                S.op("dve", lambda e: e.reciprocal(out=RB[:, 0, :], in_=RB[:, 0, :]), reads=[rg("RB0")], writes=[rg("RB0")])
            for ct in range(8):
                S.op("dve", lambda e, ct=ct: e.scalar_tensor_tensor(
                    out=HT[:, 8 + ct, :].rearrange("p (j t) -> p t j", t=8), in0=ST[:, ct, :].rearrange("p (t j) -> p t j", t=8),
                    scalar=GS[:, ct:ct + 1], in1=RB[:, 0, :].rearrange("p (t j) -> p t j", t=8), op0=ALU.mult, op1=ALU.mult),
                    reads=[rg("QT"), rg("RB0")] + rc, writes=[rg("HT")])
            for fb in range(8):
                s_, wreg = load_wblock([(lambda sl: sl, wcols(w_out, D, fb * 256, 256))])
                for sp2 in range(2):
                    rb_, pb_ = bank()
                    for s2 in range(2):
                        sub = sp2 * 2 + s2
                        for k in range(16):
                            src_ = ACTB if k < 8 else HT
                            S.op("pe", lambda e, pb_=pb_, s2=s2, sub=sub, k=k, s_=s_, src_=src_: e.matmul(
                                pb_[:, s2 * 256:(s2 + 1) * 256], src_[:, k, sub * 128:(sub + 1) * 128], s_[:, k, :], start=(k == 0), stop=(k == 15)),
                                reads=[rg("HT") if k >= 8 else rg("ACT%d" % k), wreg], writes=[rb_], sig=(k == 15))
                        S.op("dve", lambda e, pb_=pb_, s2=s2, sub=sub, fb=fb: e.tensor_tensor(
                            out=X[:, sub, fb * 256:(fb + 1) * 256], in0=pb_[:, s2 * 256:(s2 + 1) * 256], in1=X[:, sub, fb * 256:(fb + 1) * 256], op=ALU.add),
                            reads=[rb_, sub_regs[sub]], writes=[sub_regs[sub]])
            if stage >= 7:
                norm_to_HT(G2, sub_regs)
            c0 = 0
            for grp, nch in enumerate(GROUPS_FF if stage >= 7 else []):
                for bl in range(nch // 2):
                    col = (c0 + bl * 2) * 128
                    sg_, wg = load_wblock([(lambda sl: sl, wcols(w_gate, FF, col, 256))], ffn=True)
                    su_, wu = load_wblock([(lambda sl: sl, wcols(w_up, FF, col, 256))], ffn=True)
                    for m in range(2):
                        j = bl * 2 + m
                        rbg, pbg = bank()
                        for k in range(16):
                            S.op("pe", lambda e, pbg=pbg, k=k, m=m, sg_=sg_: e.matmul(pbg, sg_[:, k, m * 128:(m + 1) * 128], HT[:, k, :], start=(k == 0), stop=(k == 15)),
                                 reads=[rg("HT"), wg], writes=[rbg], sig=(k == 15))
                        rbu, pbu = bank()
                        for k in range(16):
                            S.op("pe", lambda e, pbu=pbu, k=k, m=m, su_=su_: e.matmul(pbu, su_[:, k, m * 128:(m + 1) * 128], HT[:, k, :], start=(k == 0), stop=(k == 15)),
                                 reads=[rg("HT"), wu], writes=[rbu], sig=(k == 15))
                        sg = j % 2
                        S.op("act", lambda e, pbg=pbg, sg=sg: e.activation(out=SIG[:, sg, :], in_=pbg, func=AF.Silu), reads=[rbg], writes=[rg("SIG%d" % sg)])
                        S.op("dve", lambda e, pbu=pbu, sg=sg, j=j: e.tensor_tensor(out=ACTB[:, j, :], in0=pbu, in1=SIG[:, sg, :], op=ALU.mult),
                             reads=[rbu, rg("SIG%d" % sg)], writes=[rg("ACT%d" % j)])
                for f in range(4):
                    bk = [bank() for _ in range(4)]
                    for j in range(nch):
                        wap, wrn, wsn = wd_ring[wslot_d[0] % len(wd_ring)]
                        wslot_d[0] += 1
                        wreg = rg(wrn)
                        S.dma("pool", lambda e, wap=wap, j=j, f=f, c0=c0: e.dma_start(out=wap, in_=w_down[(c0 + j) * 128:(c0 + j + 1) * 128, f * 512:(f + 1) * 512]),
                              wsn, writes=[wreg])
                        for sub in range(4):
                            S.op("pe", lambda e, sub=sub, j=j, wap=wap, pb_=bk[sub][1]: e.matmul(pb_, ACTB[:, j, sub * 128:(sub + 1) * 128], wap,
                                                                                              start=(j == 0), stop=(j == nch - 1)),
                                 reads=[rg("ACT%d" % j), wreg], writes=[bk[sub][0]], sig=(j == nch - 1 or sub == 3))
                    for sub in range(4):
                        S.op("dve", lambda e, sub=sub, f=f, pb_=bk[sub][1]: e.tensor_tensor(
                            out=X[:, sub, f * 512:(f + 1) * 512], in0=pb_, in1=X[:, sub, f * 512:(f + 1) * 512], op=ALU.add),
                            reads=[bk[sub][0], sub_regs[sub]], writes=[sub_regs[sub]])
                c0 += nch
            for sub in range(4):
                out_toks.append(S.dma("sp", lambda e, sub=sub: e.dma_start(out=out[tt_i * NT + sub * 128: tt_i * NT + (sub + 1) * 128, :], in_=X[:, sub, :]),
                                      "ot%d" % sub, reads=[sub_regs[sub]]))

        wslot_d = [0]
        out_toks = []
        ping = 0
        if do_pred:
            for t in range(n_tt):
                process_tt(x_pred, t, True, t == n_tt - 1, ping)
            ping = 1 - ping
        for t in range(n_tt):
            process_tt(x_main, t, False, False, ping)
            ping = 1 - ping
        for tk in out_toks:
            S.wait_tok("sp", tk)
        with nc.Block() as block:
            S.replay(block)
    return nc


def _t5_bucket(dist):
    n = np.maximum(dist, 0)
    max_exact = 16
    nf = np.maximum(n, 1).astype(np.float32)
    large = max_exact + (np.log(nf / max_exact) / math.log(128 / max_exact) * (32 - max_exact)).astype(np.int32)
    large = np.minimum(large, 31)
    return np.where(n < max_exact, n, large).astype(np.int32)


def _consts():
    ident = np.eye(128, dtype=np.float32)
    s_idx = np.arange(128) // 16
    mask = (s_idx[:, None] <= s_idx[None, :]).astype(np.float32)
    mv = np.broadcast_to(np.asarray(MS, np.float32)[None, :, None], (128, NM, 32)).reshape(128, NM * 32).copy()
    bones = (s_idx[:, None] // 4 == s_idx[None, :] // 4).astype(np.float32)
    bucket = _t5_bucket(np.arange(128))
    oh = np.zeros((33, 512), np.float32)
    for e in range(255):
        if e < 127:
            oh[bucket[e + 1], e] = 1.0
            oh[32, 256 + e] = NEG
        else:
            oh[32, e] = NEG
            oh[bucket[e - 127], 256 + e] = 1.0
    oh[32, 255] = NEG
    oh[32, 511] = NEG
    dup = np.zeros((128, 2, 128), np.float32)
    for c in range(2):
        for d in range(64):
            dup[c * 64 + d, c, d] = 1.0
            dup[c * 64 + d, c, 64 + d] = 1.0
    mv2 = np.broadcast_to((8.0 * np.arange(1, 65, dtype=np.float32))[None, :, None], (128, 64, 32)).reshape(128, 64 * 32).copy()
    return {"c_mv2": mv2, "c_dup": dup.reshape(128, 256), "c_ident": ident, "c_mask": mask, "c_mv": mv, "c_oh": oh, "c_bones": bones, "c_anti": np.ascontiguousarray(ident[::-1])}


_NC_CACHE = {}


def kernel(**inputs):
    n_tt = int(os.environ.get("MK_NTT", "4"))
    do_pred = os.environ.get("MK_PRED", "1") == "1"
    stage = int(os.environ.get("MK_STAGE", "99"))
    do_setup = os.environ.get("MK_SETUP", "1") == "1"
    key = (n_tt, do_pred, stage, do_setup)
    if key not in _NC_CACHE:
        _NC_CACHE[key] = build(n_tt, do_pred, stage, do_setup)
    nc = _NC_CACHE[key]
    x = np.asarray(inputs["x"], np.float32)
    TOK = n_tt * NT
    shared = {k: np.ascontiguousarray(np.asarray(inputs[k], np.float32)[0]) for k in
              ["ln1_g", "w_in", "q_norm_g", "k_norm_g", "attn_sinks", "ssm_a_re", "ssm_a_im", "ssm_log_dt", "ssm_b_re", "ssm_b_im",
               "ssm_c_re", "ssm_c_im", "w_glu", "attn_out_g", "ssm_out_g", "w_out", "ln2_g", "w_ff_gate", "w_ff_up", "w_ff_down"]}
    shared["ssm_d"] = np.ascontiguousarray(np.asarray(inputs["ssm_d"], np.float32)[0].reshape(-1))
    shared["rel_bias"] = np.ascontiguousarray(np.asarray(inputs["rel_bias"], np.float32))
    shared.update(_consts())
    in_maps = []
    ncores = int(os.environ.get("MK_CORES", "8"))
    for c in range(ncores):
        b, half = c // 2, c % 2
        m = dict(shared)
        m["x_main"] = np.ascontiguousarray(x[b, half * 2048: half * 2048 + TOK])
        if half == 1:
            m["x_pred"] = np.ascontiguousarray(x[b, 2048 - TOK:2048])
            m["hm8"] = np.full((128, 1), -SHIFT, np.float32)
        else:
            m["x_pred"] = np.zeros((TOK, D), np.float32)
            m["hm8"] = np.full((128, 1), NEG - SHIFT, np.float32)
        in_maps.append(m)
    if os.environ.get("MK_TRACE", "0") == "1":
        res = run_bass_kernel_spmd(nc, in_maps, core_ids=list(range(ncores)), trace=True)
        print("EXEC_NS", res.exec_time_ns)
    else:
        res = run_bass_kernel_spmd(nc, in_maps, core_ids=list(range(ncores)))
    outp = np.zeros((4, 4096, D), np.float32)
    for c in range(ncores):
        b, half = c // 2, c % 2
        outp[b, half * 2048: half * 2048 + TOK] = res.results[c]["out"]
    return outp
```

```python
import os
import math
import numpy as np
import ml_dtypes
import concourse.bass as bass
import concourse.mybir as mybir
from concourse.bass_utils import run_bass_kernel_spmd

F32 = mybir.dt.float32
BF16 = mybir.dt.bfloat16
I32 = mybir.dt.int32
ALU = mybir.AluOpType
AF = mybir.ActivationFunctionType

D = 2048
NT = 512
FF = 5632
EPS = 1e-6
NEG = -30000.0
SHIFT = 8.0
MS = [-(s + 1) for s in range(8)] + [t + 1 for t in range(8)] + [7 - s for s in range(8)] + [8, 16, 32, 64, 128, 256]
NM = len(MS)
GROUPS_FF = [12, 10, 12, 10]


class Reg:
    __slots__ = ("w", "r")

    def __init__(self):
        self.w = None
        self.r = {}


class Sched:
    def __init__(self, nc, semh):
        self.nc = nc
        self.semh = semh
        self.names = ["pe", "act", "dve", "pool", "sp"]
        self.ops = {k: [] for k in self.names}
        self.cnt = {k: 0 for k in self.names}
        self.waited = {k: {} for k in self.names}
        self.dcnt = {}

    def _deps(self, e, reads, writes):
        deps = {}

        def add(tok):
            if tok is None:
                return
            s, v = tok
            if e == "pe" and s == "pe":
                return
            if deps.get(s, 0) < v:
                deps[s] = v
        for r in reads:
            add(r.w)
        for w in writes:
            add(w.w)
            for s, v in w.r.items():
                add((s, v))
        out = []
        for s, v in deps.items():
            if self.waited[e].get(s, 0) < v:
                self.waited[e][s] = v
                out.append((s, v))
        return out

    def _upd(self, tok, reads, writes):
        s, v = tok
        for r in reads:
            if r.r.get(s, 0) < v:
                r.r[s] = v
        for w in writes:
            w.w = tok
            w.r = {}

    def op(self, e, fn, reads=(), writes=(), sig=True):
        waits = self._deps(e, reads, writes)
        if sig:
            self.cnt[e] += 1
            v = self.cnt[e]
        else:
            v = self.cnt[e] + 1
        self.ops[e].append((waits, fn, (e, 1) if sig else None))
        self._upd((e, v), reads, writes)

    def dma(self, q, fn, sem, reads=(), writes=()):
        waits = self._deps(q, reads, writes)
        self.dcnt[sem] = self.dcnt.get(sem, 0) + 16
        tok = (sem, self.dcnt[sem])
        self.ops[q].append((waits, fn, (sem, 16)))
        self._upd(tok, reads, writes)
        return tok

    def wait_tok(self, e, tok):
        s, v = tok
        if self.waited[e].get(s, 0) < v:
            self.waited[e][s] = v
            self.ops[e].append(([(s, v)], None, None))

    def replay(self, block):
        decs = {"pe": block.tensor, "act": block.scalar, "dve": block.vector, "pool": block.gpsimd, "sp": block.sync}
        for name in self.names:
            lst = self.ops[name]
            if not lst:
                continue

            def body(e, lst=lst):
                for waits, fn, inc in lst:
                    for s, v in waits:
                        e.wait_ge(self.semh[s], v)
                    if fn is None:
                        continue
                    ins = fn(e)
                    if inc is not None:
                        ins.then_inc(self.semh[inc[0]], inc[1])
            decs[name](body)
            self.ops[name] = []


def dap(t, offset, dims):
    return bass.AP(tensor=t.tensor, offset=offset, ap=[[s, c] for s, c in dims])


def build(n_tt=4, do_pred=True, stage=99, do_setup=True):
    nc = bass.Bass("TRN2", target_bir_lowering=False)
    TOK = n_tt * NT

    def din(name, shape, dt=F32):
        return nc.dram_tensor(name, list(shape), dt, kind="ExternalInput").ap()

    x_main = din("x_main", [TOK, D])
    x_pred = din("x_pred", [TOK, D])
    hm8 = din("hm8", [128, 1])
    rel_bias = din("rel_bias", [32, 16])
    ln1_g = din("ln1_g", [D])
    w_in = din("w_in", [D, 2560])
    q_norm_g = din("q_norm_g", [64])
    k_norm_g = din("k_norm_g", [64])
    sinks = din("attn_sinks", [16])
    a_re = din("ssm_a_re", [64, 64])
    a_im = din("ssm_a_im", [64, 64])
    log_dt = din("ssm_log_dt", [64])
    b_re = din("ssm_b_re", [64, 64, 16])
    b_im = din("ssm_b_im", [64, 64, 16])
    c_re = din("ssm_c_re", [64, 16, 64])
    c_im = din("ssm_c_im", [64, 16, 64])
    ssm_d = din("ssm_d", [64 * 16])
    w_glu = din("w_glu", [1024, 1024])
    attn_out_g = din("attn_out_g", [1024])
    ssm_out_g = din("ssm_out_g", [1024])
    w_out = din("w_out", [D, D])
    ln2_g = din("ln2_g", [D])
    w_gate = din("w_ff_gate", [D, FF])
    w_up = din("w_ff_up", [D, FF])
    w_down = din("w_ff_down", [FF, D])
    c_ident = din("c_ident", [128, 128])
    c_mask = din("c_mask", [128, 128])
    c_mv = din("c_mv", [128, NM * 32])
    c_mv2 = din("c_mv2", [128, 64 * 32])
    c_oh = din("c_oh", [33, 512])
    c_bones = din("c_bones", [128, 128])
    c_anti = din("c_anti", [128, 128])
    c_dup = din("c_dup", [128, 256])
    out = nc.dram_tensor("out", [TOK, D], F32, kind="ExternalOutput").ap()
    wt_d = nc.dram_tensor("wt_d", [128, 64 * 128], BF16, kind="Internal").ap()
    kt_d = nc.dram_tensor("kt_d", [128, 64 * 128], BF16, kind="Internal").ap()
    et_d = nc.dram_tensor("et_d", [128, 64 * 256], BF16, kind="Internal").ap()
    ext_d = nc.dram_tensor("ext_d", [16, 512], F32, kind="Internal").ap()

    sem_names = ["pe", "act", "dve", "pool", "sp", "xl0", "xl1", "xl2", "xl3", "wb0", "wb1", "wb2", "wd", "tb0", "tb1",
                 "st", "misc", "ot0", "ot1", "ot2", "ot3"] + ["wd%d" % i for i in range(10)] + ["wb3", "wb4"]
    import contextlib
    with contextlib.ExitStack() as es:
        semh = {n: es.enter_context(nc.semaphore(n)) for n in sem_names}
        S = Sched(nc, semh)

        def sb(name, shape, dt):
            return es.enter_context(nc.sbuf_tensor(name, list(shape), dt))

        ROTC = sb("ROTC", [128, 32, 64], BF16)
        ROTS = sb("ROTS", [128, 32, 64], BF16)
        RHO = sb("RHO", [128, 32], F32)
        BIAS = sb("BIAS", [128, 16, 2, 128], BF16)
        G1 = sb("G1", [128, 16], F32)
        G2 = sb("G2", [128, 16], F32)
        GA = sb("GA", [128, 8], F32)
        GS = sb("GS", [128, 8], F32)
        QG = sb("QG", [128, 1], F32)
        KG = sb("KG", [128, 1], F32)
        ESK = sb("ESK", [128, 16], F32)
        HM8 = sb("HM8", [128, 1], F32)
        DUPB = sb("DUPB", [128, 2, 128], BF16)
        NEG8 = sb("NEG8", [128, 1], F32)
        IDB = sb("IDB", [128, 128], BF16)
        BONES = sb("BONES", [128, 128], BF16)
        ONESB = sb("ONESB", [128, 128], BF16)
        r_const = Reg()
        PS = es.enter_context(nc.psum_tensor("PS", [128, 8, 512], F32))
        r_ps = [Reg() for _ in range(8)]
        bank_i = [0]

        reserved = set()

        def bank(reserve=False):
            while True:
                i = bank_i[0] % 8
                bank_i[0] += 1
                if i not in reserved:
                    break
            if reserve:
                reserved.add(i)
            return r_ps[i], PS[:, i, :]

        with contextlib.ExitStack() as es2:
            def sb2(name, shape, dt=F32):
                return es2.enter_context(nc.sbuf_tensor(name, list(shape), dt))
            ARE = sb2("ARE", [128, 32]); AIM = sb2("AIM", [128, 32]); LDT = sb2("LDT", [128, 32])
            DT = sb2("DT", [128, 32]); ARD = sb2("ARD", [128, 32]); AID = sb2("AID", [128, 32])
            MV = sb2("MV", [128, NM, 32])
            MAG = sb2("MAG", [128, NM, 32])
            ANG = sb2("ANG", [128, 2, NM, 32])
            SC = sb2("SC", [128, 2, NM, 32])
            PWR = sb2("PWR", [128, NM, 32]); PWI = sb2("PWI", [128, NM, 32])
            SM = [sb2("SM%d" % i, [128, 32]) for i in range(10)]
            BRE = sb2("BRE", [128, 32, 16]); BIM = sb2("BIM", [128, 32, 16])
            BBR = sb2("BBR", [128, 32, 16]); BBI = sb2("BBI", [128, 32, 16])
            CR = sb2("CR", [128, 32, 16]); CI = sb2("CI", [128, 32, 16])
            T1 = sb2("T1", [128, 32, 8, 16]); T2 = T1
            VR = sb2("VR", [128, 32, 8, 16]); VI = sb2("VI", [128, 32, 8, 16])
            WR = VR; WI = VI
            ER = sb2("ER", [128, 32, 8, 16]); EI = sb2("EI", [128, 32, 8, 16])
            KIv = ER[:].bitcast(I32).rearrange("p a b c -> p (a b c)")[:, 0:2 * NM * 32].rearrange("p (s m r) -> p s m r", s=2, m=NM)
            KFv = VR[:].rearrange("p a b c -> p (a b c)")[:, 0:2 * NM * 32].rearrange("p (s m r) -> p s m r", s=2, m=NM)
            IDF = sb2("IDF", [128, 128]); MASK = sb2("MASK", [128, 128]); BONF = sb2("BONF", [128, 128])
            DROW = sb2("DROW", [128, 64, 16])
            KTB = sb2("KTB", [128, 64, 128], BF16)
            MV2 = KTB[:].bitcast(F32).rearrange("p g c -> p (g c)")[:, 0:2048].rearrange("p (m r) -> p m r", m=64)
            WTB = KTB
            ETB = sb2("ETB", [128, 64, 128], BF16)
            TMPK = sb2("TMPK", [128, 4, 8, 16])
            RBA = sb2("RBA", [33, 16]); OH = sb2("OH", [33, 512])
            EXTV = SC[:].rearrange("p a m r -> p (a m r)")[0:16, 0:512]
            SKV = sb2("SKV", [128, 16])
            ANTF = sb2("ANTF", [128, 128]); ANTB = sb2("ANTB", [128, 128], BF16)
            DUPF = sb2("DUPF", [128, 2, 128])
            r = {n: Reg() for n in ["in", "dt", "mag", "ang", "ki", "kf", "cm", "sc", "pw", "sm", "bb", "t1", "t2", "v", "w", "e",
                                    "ktb", "wtb", "etb", "tmpk", "ext", "biasf", "extd", "wtd", "ktd", "etd", "pers", "cin"]}

            ld = []

            def L(out_ap, in_ap):
                ld.append(S.dma("sp", lambda e, o=out_ap, i=in_ap: e.dma_start(out=o, in_=i, allow_slow_non_contiguous=True),
                                "misc", writes=[r["in"]]))
            for gi in range(2):
                hs = slice(gi * 64, (gi + 1) * 64)
                L(ARE[hs, :], dap(a_re, gi * 64, [(1, 64), (128, 32)]))
                L(AIM[hs, :], dap(a_im, gi * 64, [(1, 64), (128, 32)]))
                L(LDT[hs, :], dap(log_dt, gi, [(0, 64), (2, 32)]))
                L(BRE[hs, :, :], dap(b_re, gi * 1024, [(16, 64), (2048, 32), (1, 16)]))
                L(BIM[hs, :, :], dap(b_im, gi * 1024, [(16, 64), (2048, 32), (1, 16)]))
            L(MV[:], c_mv.rearrange("p (m r) -> p m r", m=NM))
            L(MV2, c_mv2.rearrange("p (m r) -> p m r", m=64))
            L(IDF[:], c_ident)
            L(MASK[:], c_mask)
            L(BONF[:], c_bones)
            L(ANTF[:], c_anti)
            L(DUPF[:], c_dup.rearrange("p (c m) -> p c m", c=2))
            L(DROW[:], dap(ssm_d, 0, [(0, 128), (16, 64), (1, 16)]))
            L(RBA[0:32, :], rel_bias)
            L(OH[:], c_oh)
            L(G1[:], dap(ln1_g, 0, [(1, 128), (128, 16)]))
            L(G2[:], dap(ln2_g, 0, [(1, 128), (128, 16)]))
            L(GA[:], dap(attn_out_g, 0, [(1, 128), (128, 8)]))
            L(GS[:], dap(ssm_out_g, 0, [(1, 128), (128, 8)]))
            for h2 in range(2):
                L(QG[h2 * 64:(h2 + 1) * 64, :], dap(q_norm_g, 0, [(1, 64), (1, 1)]))
                L(KG[h2 * 64:(h2 + 1) * 64, :], dap(k_norm_g, 0, [(1, 64), (1, 1)]))
            L(SKV[:], dap(sinks, 0, [(0, 128), (1, 16)]))
            L(HM8[:], hm8)

            V_ = "dve"
            rin = [r["in"]]
            CZ = T1[0:32].rearrange("p a b c -> p (a b c)")[:, 0:1024].rearrange("p (i c) -> p i c", i=8)
            S.op("dve", lambda e: e.memset(CZ, 0.0), writes=[r["t1"]])
            for ci, (csrc, CX) in enumerate(((c_re, CR), (c_im, CI))):
                for ch in range(4):
                    pr0 = ch * 8
                    for gi in range(2):
                        S.dma("sp", lambda e, gi=gi, pr0=pr0, csrc=csrc: e.dma_start(
                            out=CZ[gi * 16:(gi + 1) * 16, :, gi * 64:(gi + 1) * 64],
                            in_=dap(csrc, (2 * pr0 + gi) * 1024, [(64, 16), (2048, 8), (1, 64)])), "xl0", writes=[r["t1"]])
                    rb_, pb_ = bank()
                    for i in range(8):
                        S.op("pe", lambda e, pb_=pb_, i=i: e.transpose(pb_[:, i * 32:(i + 1) * 32], CZ[:, i, :], IDF[0:32, 0:32]),
                             reads=[r["t1"]] + rin, writes=[rb_], sig=(i == 7))
                    for gi in range(2):
                        hs = slice(gi * 64, (gi + 1) * 64)
                        S.op("act", lambda e, pb_=pb_, hs=hs, gi=gi, pr0=pr0, CX=CX: e.activation(
                            out=CX[hs, pr0:pr0 + 8, :], in_=pb_[hs, 0:256].rearrange("p (i g q) -> p i g q", i=8, g=2)[:, :, gi, :], func=AF.Copy),
                            reads=[rb_], writes=[r["cin"]])
            S.op(V_, lambda e: e.memset(NEG8[:], -SHIFT), writes=[r_const])
            S.op(V_, lambda e: e.memset(ONESB[:], 1.0), writes=[r_const])
            S.op(V_, lambda e: e.tensor_copy(out=IDB[:], in_=IDF[:]), reads=rin, writes=[r_const])
            S.op(V_, lambda e: e.tensor_copy(out=BONES[:], in_=BONF[:]), reads=rin, writes=[r_const])
            S.op(V_, lambda e: e.tensor_copy(out=DUPB[:], in_=DUPF[:]), reads=rin, writes=[r_const])
            S.op(V_, lambda e: e.memset(RBA[32:33, :], 1.0), reads=rin, writes=[r["in"]])
            S.op("act", lambda e: e.activation(out=ESK[:], in_=SKV[:], func=AF.Exp, bias=NEG8[:, 0:1], scale=1.0),
                 reads=rin + [r_const], writes=[r["pers"]])
            S.op("act", lambda e: e.activation(out=DT[:], in_=LDT[:], func=AF.Exp), reads=rin, writes=[r["dt"]])
            S.op(V_, lambda e: e.tensor_tensor(out=ARD[:], in0=ARE[:], in1=DT[:], op=ALU.mult), reads=rin + [r["dt"]], writes=[r["sm"]])
            S.op(V_, lambda e: e.tensor_tensor(out=AID[:], in0=AIM[:], in1=DT[:], op=ALU.mult), reads=rin + [r["dt"]], writes=[r["sm"]])

            def bc_m(t):
                return t[:].unsqueeze(1).broadcast_to([128, NM, 32])
            S.op(V_, lambda e: e.tensor_tensor(out=MAG[:], in0=MV[:], in1=bc_m(ARD), op=ALU.mult), reads=rin + [r["sm"]], writes=[r["mag"]])
            S.op("act", lambda e: e.activation(out=MAG[:], in_=MAG[:], func=AF.Exp), reads=[r["mag"]], writes=[r["mag"]])
            S.op(V_, lambda e: e.tensor_tensor(out=ANG[:, 0], in0=MV[:], in1=bc_m(AID), op=ALU.mult), reads=rin + [r["sm"]], writes=[r["ang"]])
            S.op(V_, lambda e: e.tensor_scalar(out=ANG[:, 0], in0=ANG[:, 0], scalar1=1.0 / (2 * math.pi), scalar2=None, op0=ALU.mult),
                 reads=[r["ang"]], writes=[r["ang"]])
            S.op(V_, lambda e: e.tensor_scalar(out=ANG[:, 1], in0=ANG[:, 0], scalar1=0.25, scalar2=None, op0=ALU.add),
                 reads=[r["ang"]], writes=[r["ang"]])
            S.op(V_, lambda e: e.tensor_copy(out=KIv, in_=ANG[:]), reads=[r["ang"]], writes=[r["e"]])
            S.op(V_, lambda e: e.tensor_copy(out=KFv, in_=KIv), reads=[r["e"]], writes=[r["v"]])
            S.op(V_, lambda e: e.tensor_tensor(out=ANG[:], in0=ANG[:], in1=KFv, op=ALU.subtract), reads=[r["ang"], r["v"]], writes=[r["ang"]])
            S.op(V_, lambda e: e.tensor_scalar(out=KFv, in0=ANG[:], scalar1=0.5, scalar2=None, op0=ALU.is_gt), reads=[r["ang"]], writes=[r["v"]])
            S.op(V_, lambda e: e.tensor_tensor(out=ANG[:], in0=ANG[:], in1=KFv, op=ALU.subtract), reads=[r["ang"], r["v"]], writes=[r["ang"]])
            S.op(V_, lambda e: e.tensor_scalar(out=KFv, in0=ANG[:], scalar1=-0.5, scalar2=None, op0=ALU.is_lt), reads=[r["ang"]], writes=[r["v"]])
            S.op(V_, lambda e: e.tensor_tensor(out=ANG[:], in0=ANG[:], in1=KFv, op=ALU.add), reads=[r["ang"], r["v"]], writes=[r["ang"]])
            S.op("act", lambda e: e.activation(out=SC[:], in_=ANG[:], func=AF.Sin, scale=6.283185), reads=[r["ang"]], writes=[r["sc"]])
            S.op(V_, lambda e: e.tensor_tensor(out=PWR[:], in0=MAG[:], in1=SC[:, 1], op=ALU.mult), reads=[r["mag"], r["sc"]], writes=[r["pw"]])
            S.op(V_, lambda e: e.tensor_tensor(out=PWI[:], in0=MAG[:], in1=SC[:, 0], op=ALU.mult), reads=[r["mag"], r["sc"]], writes=[r["pw"]])
            S.op(V_, lambda e: e.tensor_copy(out=RHO[:], in_=MAG[:, 15, :]), reads=[r["mag"]], writes=[r["pers"]])
            A2 = T1[:].rearrange("p a b c -> p (a b c)").rearrange("p (s j r) -> p s j r", s=2, j=64)
            K2 = ER[:].bitcast(I32).rearrange("p a b c -> p (a b c)").rearrange("p (s j r) -> p s j r", s=2, j=64)
            F2 = VR[:].rearrange("p a b c -> p (a b c)").rearrange("p (s j r) -> p s j r", s=2, j=64)
            S2 = VI[:].rearrange("p a b c -> p (a b c)").rearrange("p (s j r) -> p s j r", s=2, j=64)
            ra, rk, rf = [r["t1"]], [r["e"]], [r["v"]]
            S.op(V_, lambda e: e.tensor_tensor(out=A2[:, 0], in0=MV2, in1=AID[:].unsqueeze(1).broadcast_to([128, 64, 32]), op=ALU.mult),
                 reads=rin + [r["sm"], r["ktb"]], writes=ra)
            S.op(V_, lambda e: e.tensor_scalar(out=A2[:, 0], in0=A2[:, 0], scalar1=1.0 / (2 * math.pi), scalar2=None, op0=ALU.mult), reads=ra, writes=ra)
            S.op(V_, lambda e: e.tensor_scalar(out=A2[:, 1], in0=A2[:, 0], scalar1=0.25, scalar2=None, op0=ALU.add), reads=ra, writes=ra)
            S.op(V_, lambda e: e.tensor_copy(out=K2, in_=A2), reads=ra, writes=rk)
            S.op(V_, lambda e: e.tensor_copy(out=F2, in_=K2), reads=rk, writes=rf)
            S.op(V_, lambda e: e.tensor_tensor(out=A2, in0=A2, in1=F2, op=ALU.subtract), reads=ra + rf, writes=ra)
            S.op(V_, lambda e: e.tensor_scalar(out=F2, in0=A2, scalar1=0.5, scalar2=None, op0=ALU.is_gt), reads=ra, writes=rf)
            S.op(V_, lambda e: e.tensor_tensor(out=A2, in0=A2, in1=F2, op=ALU.subtract), reads=ra + rf, writes=ra)
            S.op(V_, lambda e: e.tensor_scalar(out=F2, in0=A2, scalar1=-0.5, scalar2=None, op0=ALU.is_lt), reads=ra, writes=rf)
            S.op(V_, lambda e: e.tensor_tensor(out=A2, in0=A2, in1=F2, op=ALU.add), reads=ra + rf, writes=ra)
            S.op("act", lambda e: e.activation(out=S2, in_=A2, func=AF.Sin, scale=6.283185), reads=ra, writes=rf)
            S.op(V_, lambda e: e.tensor_copy(out=ROTS[:].rearrange("p r j -> p j r"), in_=S2[:, 0]), reads=rf, writes=[r["pers"]])
            S.op(V_, lambda e: e.tensor_copy(out=ROTC[:].rearrange("p r j -> p j r"), in_=S2[:, 1]), reads=rf, writes=[r["pers"]])
            nr, ni, den, rden, t0, t1_, fr, fi = SM[0], SM[1], SM[2], SM[3], SM[4], SM[5], SM[6], SM[7]
            rs = [r["sm"]]
            rp = [r["pw"]]

            def tt(o, a, b, op, reads, writes):
                S.op(V_, lambda e: e.tensor_tensor(out=o, in0=a, in1=b, op=op), reads=reads, writes=writes)
            S.op(V_, lambda e: e.tensor_scalar(out=nr[:], in0=PWR[:, 8, :], scalar1=-1.0, scalar2=None, op0=ALU.add), reads=rp, writes=rs)
            tt(den[:], ARE[:], ARE[:], ALU.mult, rin, rs)
            tt(t0[:], AIM[:], AIM[:], ALU.mult, rin, rs)
            tt(den[:], den[:], t0[:], ALU.add, rs, rs)
            S.op(V_, lambda e: e.reciprocal(out=rden[:], in_=den[:]), reads=rs, writes=rs)
            tt(t0[:], nr[:], ARE[:], ALU.mult, rs + rin, rs)
            tt(t1_[:], PWI[:, 8, :], AIM[:], ALU.mult, rp + rin, rs)
            tt(t0[:], t0[:], t1_[:], ALU.add, rs, rs)
            tt(fr[:], t0[:], rden[:], ALU.mult, rs, rs)
            tt(t0[:], PWI[:, 8, :], ARE[:], ALU.mult, rp + rin, rs)
            tt(t1_[:], nr[:], AIM[:], ALU.mult, rs + rin, rs)
            tt(t0[:], t0[:], t1_[:], ALU.subtract, rs, rs)
            tt(fi[:], t0[:], rden[:], ALU.mult, rs, rs)

            def bq(t):
                return t[:].unsqueeze(2).broadcast_to([128, 32, 16])
            rb = [r["bb"]]
            tt(BBR[:], BRE[:], bq(fr), ALU.mult, rin + rs, rb)
            tt(T1[:, :, 0, :], BIM[:], bq(fi), ALU.mult, rin + rs, [r["t1"]])
            tt(BBR[:], BBR[:], T1[:, :, 0, :], ALU.subtract, rb + [r["t1"]], rb)
            tt(BBI[:], BIM[:], bq(fr), ALU.mult, rin + rs, rb)
            tt(T1[:, :, 0, :], BRE[:], bq(fi), ALU.mult, rin + rs, [r["t1"]])
            tt(BBI[:], BBI[:], T1[:, :, 0, :], ALU.add, rb + [r["t1"]], rb)

            def pw8(t, i0):
                return t[:, i0:i0 + 8, :].rearrange("p s r -> p r s").unsqueeze(3).broadcast_to([128, 32, 8, 16])

            def x8(t):
                return t[:].unsqueeze(2).broadcast_to([128, 32, 8, 16])

            def cmul(OR, OI, i0, XR, XI, rx, ro, neg_im=False):
                tt(OR[:], pw8(PWR, i0), x8(XR), ALU.mult, rp + rx, ro)
                tt(T1[:], pw8(PWI, i0), x8(XI), ALU.mult, rp + rx, [r["t1"]])
                tt(OR[:], OR[:], T1[:], ALU.subtract, ro + [r["t1"]], ro)
                tt(OI[:], pw8(PWR, i0), x8(XI), ALU.mult, rp + rx, ro)
                tt(T2[:], pw8(PWI, i0), x8(XR), ALU.mult, rp + rx, [r["t1"]])
                tt(OI[:], OI[:], T2[:], ALU.add, ro + [r["t1"]], ro)
                if neg_im:
                    S.op(V_, lambda e: e.tensor_scalar(out=OI[:], in0=OI[:], scalar1=-1.0, scalar2=None, op0=ALU.mult), reads=ro, writes=ro)
            cmul(VR, VI, 0, BBR, BBI, rb, [r["v"]])
            cmul(ER, EI, 8, CR, CI, rin + [r["cin"]], [r["e"]], neg_im=True)

            for pq in range(8):
                banks = [bank(), bank()]
                for gi in range(2):
                    hs = slice(gi * 64, (gi + 1) * 64)
                    rb_, pb_ = banks[gi]
                    for p4 in range(4):
                        pr = pq * 4 + p4
                        o = pb_[:, p4 * 128:(p4 + 1) * 128]
                        S.op("pe", lambda e, o=o, hs=hs, pr=pr: e.matmul(o, VR[hs, pr].rearrange("p s q -> p (s q)"),
                                                                         ER[hs, pr].rearrange("p s q -> p (s q)"), start=True, stop=False),
                             reads=[r["v"], r["e"]], writes=[rb_], sig=False)
                        S.op("pe", lambda e, o=o, hs=hs, pr=pr: e.matmul(o, VI[hs, pr].rearrange("p s q -> p (s q)"),
                                                                         EI[hs, pr].rearrange("p s q -> p (s q)"), start=False, stop=True),
                             reads=[r["v"], r["e"]], writes=[rb_], sig=(p4 == 3))
                    gsel = slice(2 * pq * 4 + gi, 2 * (pq * 4 + 4), 2)
                    S.op(V_, lambda e, gsel=gsel: e.tensor_tensor(
                        out=TMPK[:], in0=IDF[:].rearrange("p (t q) -> p t q", t=8).unsqueeze(1).broadcast_to([128, 4, 8, 16]),
                        in1=DROW[:, gsel, :].unsqueeze(2).broadcast_to([128, 4, 8, 16]), op=ALU.mult),
                        reads=rin, writes=[r["tmpk"]])
                    S.op(V_, lambda e, pb_=pb_: e.tensor_tensor(
                        out=pb_.rearrange("p (g c) -> p g c", g=4), in0=pb_.rearrange("p (g c) -> p g c", g=4),
                        in1=MASK[:].unsqueeze(1).broadcast_to([128, 4, 128]), op=ALU.mult), reads=[rb_] + rin, writes=[rb_])
                    S.op(V_, lambda e, pb_=pb_, gsel=gsel: e.tensor_tensor(
                        out=KTB[:, gsel, :], in0=pb_.rearrange("p (g c) -> p g c", g=4),
                        in1=TMPK[:].rearrange("p g t q -> p g (t q)"), op=ALU.add), reads=[rb_, r["tmpk"]], writes=[r["ktb"]])
            tk1 = S.dma("sp", lambda e: e.dma_start(out=kt_d, in_=KTB[:].rearrange("p g c -> p (g c)")), "st", reads=[r["ktb"]], writes=[r["ktd"]])
            cmul(WR, WI, 16, BBR, BBI, rb, [r["v"]])
            for c, WX in enumerate((WR, WI)):
                for pq in range(8):
                    rb_, pb_ = bank()
                    for p4 in range(4):
                        pr = pq * 4 + p4
                        S.op("pe", lambda e, pb_=pb_, p4=p4, pr=pr, WX=WX: e.transpose(
                            pb_[:, p4 * 128:(p4 + 1) * 128], WX[:, pr].rearrange("p s q -> p (s q)"), IDF[:]),
                            reads=[r["v"]] + rin, writes=[rb_], sig=(p4 == 3))
                    S.op("act", lambda e, pb_=pb_, pq=pq, c=c: e.activation(
                        out=WTB[:, pq * 8:(pq + 1) * 8, c * 64:(c + 1) * 64],
                        in_=pb_.rearrange("p (g n) -> p g n", g=8), func=AF.Copy), reads=[rb_], writes=[r["ktb"]])
            tk2 = S.dma("sp", lambda e: e.dma_start(out=wt_d, in_=WTB[:].rearrange("p g c -> p (g c)")), "st", reads=[r["ktb"]], writes=[r["wtd"]])
            tk3s = []
            for c, EX in enumerate((ER, EI)):
                S.op("pool", lambda e: e.memset(ETB[:], 0.0), writes=[r["etb"]])
                for gi in range(2):
                    hs = slice(gi * 64, (gi + 1) * 64)
                    S.op("act", lambda e, hs=hs, gi=gi, EX=EX: e.activation(
                        out=ETB[hs, gi::2, :], in_=EX[hs].rearrange("p r t q -> p r (t q)"), func=AF.Copy),
                        reads=[r["e"]], writes=[r["etb"]])
                tk3s.append(S.dma("sp", lambda e, c=c: e.dma_start(out=et_d.rearrange("p (g c x) -> p g c x", g=64, c=2)[:, :, c, :], in_=ETB[:]),
                                  "st", reads=[r["etb"]], writes=[r["etd"]]))
            rb_, pb_ = bank()
            S.op("pe", lambda e: e.matmul(pb_[0:16, :], RBA[:], OH[:], start=True, stop=True), reads=rin, writes=[rb_])
            S.op("act", lambda e: e.activation(out=EXTV, in_=pb_[0:16, :], func=AF.Copy), reads=[rb_, r["sc"]], writes=[r["sc"]])
            tk4 = S.dma("sp", lambda e: e.dma_start(out=ext_d, in_=EXTV), "st", reads=[r["sc"]], writes=[r["extd"]])
            BREV = T1[:].bitcast(BF16).rearrange("p a b c -> p (a b c)")[:, 0:4096].rearrange("p (h a i) -> p h a i", h=16, a=2)
            for h in range(16):
                S.dma("pool", lambda e, h=h: e.dma_start(out=BREV[:, h, :, :], in_=dap(ext_d, h * 512, [(1, 128), (256, 2), (1, 128)])),
                      "wd", reads=[r["extd"]], writes=[r["biasf"], r["t1"]])
            S.op(V_, lambda e: e.tensor_copy(out=ANTB[:], in_=ANTF[:]), reads=rin, writes=[r["tmpk"]])
            for h4 in range(8):
                rb_, pb_ = bank()
                S.op("pe", lambda e, pb_=pb_, h4=h4: e.matmul(pb_, ANTB[:], BREV[:, h4 * 2:(h4 + 1) * 2].rearrange("p h a i -> p (h a i)"), start=True, stop=True),
                     reads=[r["biasf"], r["tmpk"]], writes=[rb_])
                S.op("act", lambda e, pb_=pb_, h4=h4: e.activation(out=BIAS[:, h4 * 2:(h4 + 1) * 2].rearrange("p h a i -> p (h a i)"), in_=pb_, func=AF.Copy),
                     reads=[rb_], writes=[r["pers"]])
            for tk in (tk1, tk2, tk4) + tuple(tk3s) + tuple(ld):
                S.wait_tok("sp", tk)
            if not do_setup:
                for k_ in S.ops:
                    S.ops[k_] = []
            else:
                with nc.Block() as block:
                    S.replay(block)

        X = sb("X", [128, 4, D], F32)
        HT = sb("HT", [128, 16, NT], BF16)
        ACTB = sb("ACTB", [128, 12, NT], BF16)
        WB = sb("WB", [128, 3, 16, 256], BF16)
        WD = sb("WD", [128, 6, 512], BF16)
        TB = sb("TB", [128, 2, 8, 4, 128], BF16)
        QT = sb("QT", [128, 8, NT], BF16)
        ST = QT
        KT2 = sb("KT2", [128, 2, 4, NT], BF16)
        KN = sb("KN", [128, 2, NT], BF16)
        VA = sb("VA", [128, 2, 4, 4, 66], BF16)
        YA = sb("YA", [128, 1, 1024], F32)
        HN = sb("HN", [128, 2, D], BF16)
        SS = sb("SS", [128, 8], F32)
        RSTD = sb("RSTD", [128, 8], F32)
        UQ = sb("UQ", [64, 16, 8, 16], BF16)
        UT = sb("UT", [128, 16, 64], BF16)
        SA = sb("SA", [128, 2, 8, 64], F32)
        SB_ = sb("SB", [128, 2, 8, 64], F32)
        HIN = sb("HIN", [128, 2, 32], F32)
        HB = sb("HB", [128, 2, 8, 64], BF16)
        TA = sb("TA", [128, 2, 8, 64], F32)
        YS = sb("YS", [64, 2, 8, 128], BF16)
        GX = sb("GX", [64, 1, 3, 512], F32)
        YT = sb("YT", [128, 8, 8, 64], BF16)
        SQ = sb("SQ", [128, 2, NT], BF16)
        SIG = sb("SIG", [128, 2, NT], BF16)
        RB = sb("RB", [128, 1, NT], F32)
        LG = sb("LG", [128, 3, 256], F32)
        PT = sb("PT", [128, 3, 256], BF16)
        DEN = sb("DEN", [128, 2, 4], F32)

        R = {}

        def rg(name):
            if name not in R:
                R[name] = Reg()
            return R[name]
        rc = [r_const]
        sub_regs = [rg("X%d" % i) for i in range(4)]

        S.op("dve", lambda e: e.memset(VA[:], 1.0), writes=[rg("VA0"), rg("VA1")])
        S.op("dve", lambda e: e.memset(KT2[:], 0.0), writes=[rg("KT0"), rg("KT1")])
        S.op("dve", lambda e: e.memset(HIN[:], 0.0), writes=[rg("HIN")])

        wslot = [0]
        wb_ring3 = [(WB[:, i], "WB%d" % i, "wb%d" % i) for i in range(3)]
        wb_ring5 = wb_ring3 + [
            (QT[:].rearrange("p a b -> p (a b)").rearrange("p (k c) -> p k c", k=16), "QT", "wb3"),
            (YT[:].rearrange("p a b c -> p (a b c)").rearrange("p (k c) -> p k c", k=16), "YT", "wb4")]
        wd_ring = [(WD[:, i, :], "WD%d" % i, "wd%d" % i) for i in range(6)] + [
            (KN[:, 0, :], "KN0", "wd6"), (KN[:, 1, :], "KN1", "wd7"), (SQ[:, 0, :], "SQ0", "wd8"), (SQ[:, 1, :], "SQ1", "wd9")]

        def load_wblock(src_ap_fn_list, ffn=False):
            ring = wb_ring5 if ffn else wb_ring3
            ap_, rname, sname = ring[wslot[0] % len(ring)]
            wslot[0] += 1
            reg = rg(rname)
            for ov, ia in src_ap_fn_list:
                S.dma("pool", lambda e, ov=ov, ia=ia, ap_=ap_: e.dma_start(out=ov(ap_), in_=ia, allow_slow_non_contiguous=True),
                      sname, writes=[reg])
            return ap_, reg

        def wcols(W, ncols_total, c0, ncol, krows=16):
            return dap(W, c0, [(ncols_total, 128), (128 * ncols_total, krows), (1, ncol)])

        def norm_to_HT(G, xsrc_regs):
            for sub in range(4):
                S.op("act", lambda e, sub=sub: e.activation(out=ACTB[:, 8:12, :].rearrange("p a b -> p (a b)"), in_=X[:, sub, :], func=AF.Square, accum_out=SS[:, sub:sub + 1]),
                     reads=[xsrc_regs[sub]], writes=[rg("ACT8"), rg("ACT9"), rg("ACT10"), rg("ACT11"), rg("SS")])
            S.op("dve", lambda e: e.tensor_scalar(out=RSTD[:, 0:4], in0=SS[:, 0:4], scalar1=1.0 / D, scalar2=EPS, op0=ALU.mult, op1=ALU.add),
                 reads=[rg("SS")], writes=[rg("RSTD")])
            S.op("act", lambda e: e.activation(out=RSTD[:, 0:4], in_=RSTD[:, 0:4], func=AF.Sqrt), reads=[rg("RSTD")], writes=[rg("RSTD")])
            S.op("dve", lambda e: e.reciprocal(out=RSTD[:, 0:4], in_=RSTD[:, 0:4]), reads=[rg("RSTD")], writes=[rg("RSTD")])
            for sub in range(4):
                hb = sub % 2
                S.op("act", lambda e, sub=sub, hb=hb: e.activation(out=HN[:, hb, :], in_=X[:, sub, :], func=AF.Copy, scale=RSTD[:, sub:sub + 1]),
                     reads=[xsrc_regs[sub], rg("RSTD")], writes=[rg("HN%d" % hb)])
                for kh in range(2):
                    rb_, pb_ = bank()
                    pbb = pb_.bitcast(BF16)
                    for k8 in range(8):
                        k = kh * 8 + k8
                        S.op("pe", lambda e, pbb=pbb, k8=k8, k=k, hb=hb: e.transpose(pbb[:, k8 * 128:(k8 + 1) * 128],
                                                                                     HN[:, hb, k * 128:(k + 1) * 128], IDB[:]),
                             reads=[rg("HN%d" % hb)] + rc, writes=[rb_], sig=(k8 == 7))
                    S.op("dve", lambda e, pbb=pbb, kh=kh, sub=sub: e.tensor_tensor(
                        out=HT[:, kh * 8:(kh + 1) * 8, sub * 128:(sub + 1) * 128], in0=pbb.rearrange("p (k t) -> p k t", k=8),
                        in1=G[:, kh * 8:(kh + 1) * 8].unsqueeze(2).broadcast_to([128, 8, 128]), op=ALU.mult),
                        reads=[rb_] + rc, writes=[rg("HT")])

        def qk_norm(pb_, rb_, GV, out_ap, out_reg):
            S.op("act", lambda e: e.activation(out=SQ[:, 0, :], in_=pb_, func=AF.Square), reads=[rb_], writes=[rg("SQ0")])
            rb2, pb2 = bank()
            S.op("pe", lambda e: e.matmul(pb2, BONES[:], SQ[:, 0, :], start=True, stop=True), reads=[rg("SQ0")] + rc, writes=[rb2])
            S.op("act", lambda e: e.activation(out=RB[:, 0, :], in_=pb2, func=AF.Sqrt, scale=1.0 / 64, bias=EPS), reads=[rb2], writes=[rg("RB0")])
            S.op("dve", lambda e: e.reciprocal(out=RB[:, 0, :], in_=RB[:, 0, :]), reads=[rg("RB0")], writes=[rg("RB0")])
            S.op("dve", lambda e: e.scalar_tensor_tensor(out=out_ap, in0=pb_, scalar=GV[:, 0:1], in1=RB[:, 0, :], op0=ALU.mult, op1=ALU.mult),
                 reads=[rb_, rg("RB0")] + rc, writes=[out_reg])

        ssm_batch = [0]

        ssm_ctx = {}

        def ssm_part1(ub):
            s_, wreg = load_wblock([(lambda sl: sl, wcols(w_in, 2560, 1536 + ub * 256, 256))])
            tsl = []
            for ch in range(2):
                g0 = ub * 16 + ch * 8
                ts_ = ssm_batch[0] % 2
                ssm_batch[0] += 1
                treg = rg("TB%d" % ts_)
                S.dma("sp", lambda e, ts_=ts_, g0=g0: e.dma_start(out=TB[:, ts_, :, 0, :], in_=wt_d[:, g0 * 128:(g0 + 8) * 128].rearrange("p (g c) -> p g c", g=8)),
                      "tb%d" % ts_, writes=[treg])
                S.dma("sp", lambda e, ts_=ts_, g0=g0: e.dma_start(out=TB[:, ts_, :, 1, :], in_=kt_d[:, g0 * 128:(g0 + 8) * 128].rearrange("p (g c) -> p g c", g=8)),
                      "tb%d" % ts_, writes=[treg])
                S.dma("sp", lambda e, ts_=ts_, g0=g0: e.dma_start(out=TB[:, ts_, :, 2:4, :], in_=et_d[:, g0 * 256:(g0 + 8) * 256].rearrange("p (g c x) -> p g c x", g=8, c=2)),
                      "tb%d" % ts_, writes=[treg])
                tsl.append((ts_, treg))
            for sp_ in range(4):
                rb_, pb_ = bank()
                for s2 in range(2):
                    s = sp_ * 2 + s2
                    for k in range(16):
                        S.op("pe", lambda e, pb_=pb_, s2=s2, s=s, k=k, s_=s_: e.matmul(
                            pb_[0:64, s2 * 256:(s2 + 1) * 256], HT[:, k, s:NT:8], s_[:, k, :], start=(k == 0), stop=(k == 15)),
                            reads=[rg("HT"), wreg], writes=[rb_], sig=(k == 15 and s2 == 1))
                S.op("act", lambda e, pb_=pb_, sp_=sp_: e.activation(
                    out=UQ[:, :, sp_ * 2:(sp_ + 1) * 2, :].rearrange("p g s q -> p s g q"),
                    in_=pb_[0:64, :].rearrange("p (s g q) -> p s g q", s=2, g=16), func=AF.Copy),
                    reads=[rb_], writes=[rg("UQ")])
            for gh in range(2):
                rb_, pb_ = bank()
                pbb = pb_.bitcast(BF16)
                for g8 in range(8):
                    g = gh * 8 + g8
                    S.op("pe", lambda e, pbb=pbb, g8=g8, g=g: e.transpose(pbb[:, g8 * 64:(g8 + 1) * 64],
                                                                          UQ[:, g].rearrange("p s q -> p (s q)"), IDB[0:64, 0:64]),
                         reads=[rg("UQ")] + rc, writes=[rb_], sig=(g8 == 7))
                S.op("dve", lambda e, pbb=pbb, gh=gh: e.tensor_copy(out=UT[:, gh * 8:(gh + 1) * 8, :],
                                                                    in_=pbb[:, 0:512].rearrange("p (g j) -> p g j", g=8)),
                     reads=[rb_], writes=[rg("UT")])
            rbr, pbr = bank()
            rbi, pbi = bank()
            for pr in range(8):
                for gi in range(2):
                    g = 2 * pr + gi
                    ts_, treg = tsl[g // 8]
                    hs = slice(gi * 64, (gi + 1) * 64)
                    last = (pr == 7 and gi == 1)
                    S.op("pe", lambda e, hs=hs, pr=pr, g=g, ts_=ts_: e.matmul(pbr[hs, pr * 64:(pr + 1) * 64], TB[:, ts_, g % 8, 0, 0:64], UT[:, g, :],
                                                                               start=True, stop=True),
                         reads=[rg("UT"), treg], writes=[rbr], sig=False)
                    S.op("pe", lambda e, hs=hs, pr=pr, g=g, ts_=ts_: e.matmul(pbi[hs, pr * 64:(pr + 1) * 64], TB[:, ts_, g % 8, 0, 64:128], UT[:, g, :],
                                                                               start=True, stop=True),
                         reads=[rg("UT"), treg], writes=[rbi], sig=last)
            S.op("act", lambda e: e.activation(out=SA[:, 0], in_=pbr.rearrange("p (r j) -> p r j", r=8), func=AF.Copy), reads=[rbr], writes=[rg("SA"), rg("SC0")])
            S.op("act", lambda e: e.activation(out=SA[:, 1], in_=pbi.rearrange("p (r j) -> p r j", r=8), func=AF.Copy), reads=[rbi], writes=[rg("SA"), rg("SC1")])
            prs = slice(ub * 8, (ub + 1) * 8)
            rH, rSA, rSB = rg("HIN"), rg("SA"), rg("SB")
            Cc, Sn = ROTC[:, prs, :], ROTS[:, prs, :]

            def vtt(o, a_, b_, op, reads, writes, eng="dve"):
                S.op(eng, lambda e: e.tensor_tensor(out=o, in0=a_, in1=b_, op=op), reads=reads, writes=writes)
            vtt(TA[:, 0], Cc, SA[:, 0], ALU.mult, [rSA] + rc, [rg("TA0")])
            vtt(TA[:, 1], Sn, SA[:, 1], ALU.mult, [rSA] + rc, [rg("TA1")])
            vtt(SB_[:, 0], TA[:, 0], TA[:, 1], ALU.add, [rg("TA0"), rg("TA1")], [rg("SB0")])
            vtt(TA[:, 0], Cc, SA[:, 1], ALU.mult, [rSA] + rc, [rg("TA0")])
            vtt(TA[:, 1], Sn, SA[:, 0], ALU.mult, [rSA] + rc, [rg("TA1")])
            vtt(SB_[:, 1], TA[:, 0], TA[:, 1], ALU.subtract, [rg("TA0"), rg("TA1")], [rg("SB1")])
            S.op("dve", lambda e: e.tensor_copy(out=HB[:, :, :, 0], in_=HIN[:, :, prs]), reads=[rH], writes=[rg("HB")])
            for c in range(2):
                for pr in range(8):
                    gp = ub * 8 + pr
                    S.op("dve", lambda e, c=c, pr=pr, gp=gp: e.tensor_tensor_scan(
                        out=SA[:, c, pr, :], data0=RHO[:, gp:gp + 1].broadcast_to([128, 64]), data1=SB_[:, c, pr, :],
                        initial=HIN[:, c, gp:gp + 1], op0=ALU.mult, op1=ALU.add),
                        reads=[rg("SB%d" % c), rH] + rc, writes=[rg("SC%d" % c)])
            vtt(TA[:, 0], Cc, SA[:, 0], ALU.mult, [rg("SC0")] + rc, [rg("TA0")])
            vtt(TA[:, 1], Sn, SA[:, 1], ALU.mult, [rg("SC1")] + rc, [rg("TA1")])
            vtt(SB_[:, 0], TA[:, 0], TA[:, 1], ALU.subtract, [rg("TA0"), rg("TA1")], [rg("SB0")])
            vtt(TA[:, 0], Cc, SA[:, 1], ALU.mult, [rg("SC1")] + rc, [rg("TA0")])
            vtt(TA[:, 1], Sn, SA[:, 0], ALU.mult, [rg("SC0")] + rc, [rg("TA1")])
            vtt(SB_[:, 1], TA[:, 0], TA[:, 1], ALU.add, [rg("TA0"), rg("TA1")], [rg("SB1")])
            S.op("dve", lambda e: e.tensor_copy(out=HIN[:, :, prs], in_=SB_[:, :, :, 63]), reads=[rg("SB0"), rg("SB1")], writes=[rH])
            ssm_ctx[ub] = tsl

        def ssm_part2(ub):
            tsl = ssm_ctx[ub]
            S.op("dve", lambda e: e.tensor_copy(out=HB[:, :, :, 1:64], in_=SB_[:, :, :, 0:63]), reads=[rg("SB0"), rg("SB1")], writes=[rg("HB")])
            for gq in range(4):
                rb_, pb_ = bank()
                for g4 in range(4):
                    g = gq * 4 + g4
                    ts_, treg = tsl[g // 8]
                    o = pb_[0:64, g4 * 128:(g4 + 1) * 128]
                    pr = g // 2
                    S.op("pe", lambda e, o=o, g=g, ts_=ts_: e.matmul(o, UT[:, g, :], TB[:, ts_, g % 8, 1, :], start=True, stop=False),
                         reads=[rg("UT"), treg], writes=[rb_], sig=False)
                    S.op("pe", lambda e, o=o, g=g, ts_=ts_, pr=pr: e.matmul(o, HB[:, 0, pr, :], TB[:, ts_, g % 8, 2, :], start=False, stop=False),
                         reads=[rg("HB"), treg], writes=[rb_], sig=False)
                    S.op("pe", lambda e, o=o, g=g, ts_=ts_, pr=pr: e.matmul(o, HB[:, 1, pr, :], TB[:, ts_, g % 8, 3, :], start=False, stop=True),
                         reads=[rg("HB"), treg], writes=[rb_], sig=(g4 == 3))
                hb = (gq // 2) % 2
                gb = gq % 2
                src = pb_[0:64, :]
                gx = [GX[:, 0, i, :] for i in range(3)]
                rgx = rg("GX0")
                S.op("act", lambda e, src=src, gx=gx: e.activation(out=gx[0], in_=src, func=AF.Square), reads=[rb_], writes=[rgx])
                S.op("dve", lambda e, gx=gx: e.tensor_scalar(out=gx[0], in0=gx[0], scalar1=0.044715, scalar2=1.0, op0=ALU.mult, op1=ALU.add), reads=[rgx], writes=[rgx])
                S.op("dve", lambda e, src=src, gx=gx: e.tensor_tensor(out=gx[1], in0=src, in1=gx[0], op=ALU.mult), reads=[rb_, rgx], writes=[rgx])
                S.op("act", lambda e, gx=gx: e.activation(out=gx[2], in_=gx[1], func=AF.Sigmoid, scale=2.0 * math.sqrt(2.0 / math.pi)), reads=[rgx], writes=[rgx])
                S.op("dve", lambda e, src=src, gx=gx, hb=hb, gb=gb: e.tensor_tensor(
                    out=YS[:, hb, :, gb * 64:(gb + 1) * 64].rearrange("p t (g q) -> p g t q", g=4),
                    in0=src.rearrange("p (g t q) -> p g t q", g=4, t=8), in1=gx[2].rearrange("p (g t q) -> p g t q", g=4, t=8), op=ALU.mult),
                    reads=[rb_, rgx], writes=[rg("YS%d" % hb)])
                if gb == 1:
                    ct = ub * 2 + gq // 2
                    rb2, pb2 = bank()
                    pbb = pb2.bitcast(BF16)
                    for t in range(8):
                        S.op("pe", lambda e, pbb=pbb, t=t, hb=hb: e.transpose(pbb[:, t * 64:(t + 1) * 64], YS[:, hb, t, :], IDB[0:64, 0:64]),
                             reads=[rg("YS%d" % hb)] + rc, writes=[rb2], sig=(t == 7))
                    S.op("act", lambda e, pbb=pbb, ct=ct: e.activation(out=YT[:, ct].rearrange("p t j -> p (t j)"), in_=pbb[:, 0:512], func=AF.Copy),
                         reads=[rb2], writes=[rg("YT")])

        def process_tt(xsrc, tt_i, pred, last_pred, ping):
            for sub in range(4):
                S.dma("sp", lambda e, sub=sub: e.dma_start(out=X[:, sub, :], in_=xsrc[tt_i * NT + sub * 128: tt_i * NT + (sub + 1) * 128, :]),
                      "xl%d" % sub, writes=[sub_regs[sub]])
            main = not pred
            if stage >= 1:
                norm_to_HT(G1, sub_regs)
            if main and stage >= 2:
                for qb in range(4):
                    s_, wreg = load_wblock([(lambda sl: sl, wcols(w_in, 2560, qb * 256, 256))])
                    for m in range(2):
                        rb_, pb_ = bank()
                        for k in range(16):
                            S.op("pe", lambda e, pb_=pb_, k=k, m=m, s_=s_: e.matmul(pb_, s_[:, k, m * 128:(m + 1) * 128], HT[:, k, :],
                                                                                   start=(k == 0), stop=(k == 15)),
                                 reads=[rg("HT"), wreg], writes=[rb_], sig=(k == 15))
                        qk_norm(pb_, rb_, QG, QT[:, qb * 2 + m, :], rg("QT"))
            if (main or last_pred) and stage >= 2:
                s_, wreg = load_wblock([(lambda sl: sl, wcols(w_in, 2560, 1024, 256))])
                for kt in range(2):
                    rb_, pb_ = bank()
                    for k in range(16):
                        S.op("pe", lambda e, pb_=pb_, k=k, kt=kt, s_=s_: e.matmul(pb_, s_[:, k, kt * 128:(kt + 1) * 128], HT[:, k, :],
                                                                               start=(k == 0), stop=(k == 15)),
                             reads=[rg("HT"), wreg], writes=[rb_], sig=(k == 15))
                    qk_norm(pb_, rb_, KG, KN[:, kt, :], rg("KN%d" % kt))
                    for c in range(2):
                        kh = kt * 2 + c
                        rb2, pb2 = bank()
                        S.op("pe", lambda e, pb2=pb2, c=c, kt=kt: e.matmul(pb2, DUPB[:, c, :], KN[:, kt, :], start=True, stop=True),
                             reads=[rg("KN%d" % kt)] + rc, writes=[rb2])
                        S.op("act", lambda e, pb2=pb2, kh=kh: e.activation(out=KT2[:, ping, kh, :], in_=pb2, func=AF.Copy),
                             reads=[rb2], writes=[rg("KT%d" % ping)])
                s_, wreg = load_wblock([(lambda sl: sl, wcols(w_in, 2560, 1280, 256))])
                for sub in range(4):
                    rb_, pb_ = bank()
                    for k in range(16):
                        S.op("pe", lambda e, pb_=pb_, k=k, sub=sub, s_=s_: e.matmul(pb_[:, 0:256], HT[:, k, sub * 128:(sub + 1) * 128], s_[:, k, :],
                                                                                   start=(k == 0), stop=(k == 15)),
                             reads=[rg("HT"), wreg], writes=[rb_], sig=(k == 15))
                    S.op("act", lambda e, pb_=pb_, sub=sub: e.activation(out=VA[:, ping, sub, :, 0:64], in_=pb_[:, 0:256].rearrange("p (h d) -> p h d", h=4),
                                                                         func=AF.Copy), reads=[rb_], writes=[rg("VA%d" % ping)])
            if pred:
                for ub in range(4):
                    ssm_part1(ub)
                return

            def attn_block(b):
                yb = 0
                hnb = b % 2
                def stage1(h):
                    kh = h // 4
                    hp = slice((h % 2) * 64, (h % 2) * 64 + 64)
                    qv = QT[hp, h // 2, b * 128:(b + 1) * 128]
                    if b > 0:
                        kprev = KT2[hp, ping, kh, (b - 1) * 128:b * 128]
                        vprev = VA[:, ping, b - 1, kh, 0:65]
                        rkp, rvp = rg("KT%d" % ping), rg("VA%d" % ping)
                    else:
                        kprev = KT2[hp, 1 - ping, kh, 384:512]
                        vprev = VA[:, 1 - ping, 3, kh, 0:65]
                        rkp, rvp = rg("KT%d" % (1 - ping)), rg("VA%d" % (1 - ping))
                    kcur = KT2[hp, ping, kh, b * 128:(b + 1) * 128]
                    vcur = VA[:, ping, b, kh, 0:65]
                    rbs, pbs = bank()
                    S.op("pe", lambda e, pbs=pbs, kprev=kprev, qv=qv: e.matmul(pbs[:, 0:128], kprev, qv, start=True, stop=True),
                         reads=[rkp, rg("QT")], writes=[rbs], sig=False)
                    S.op("pe", lambda e, pbs=pbs, kcur=kcur, qv=qv: e.matmul(pbs[:, 128:256], kcur, qv, start=True, stop=True),
                         reads=[rg("KT%d" % ping), rg("QT")], writes=[rbs])
                    lb = h % 3
                    S.op("dve", lambda e, pbs=pbs, h=h, lb=lb: e.scalar_tensor_tensor(
                        out=LG[:, lb, :], in0=pbs[:, 0:256], scalar=0.125, in1=BIAS[:, h].rearrange("p a i -> p (a i)"), op0=ALU.mult, op1=ALU.add),
                        reads=[rbs] + rc, writes=[rg("LG%d" % lb)])
                    if b == 0 and tt_i == 0:
                        S.op("act", lambda e, lb=lb: e.activation(out=PT[:, lb, 0:128], in_=LG[:, lb, 0:128], func=AF.Exp, bias=HM8[:, 0:1], scale=1.0),
                             reads=[rg("LG%d" % lb)] + rc, writes=[rg("PT%d" % lb)])
                        S.op("act", lambda e, lb=lb: e.activation(out=PT[:, lb, 128:256], in_=LG[:, lb, 128:256], func=AF.Exp, bias=NEG8[:, 0:1], scale=1.0),
                             reads=[rg("LG%d" % lb)] + rc, writes=[rg("PT%d" % lb)])
                    else:
                        S.op("act", lambda e, lb=lb: e.activation(out=PT[:, lb, :], in_=LG[:, lb, :], func=AF.Exp, bias=NEG8[:, 0:1], scale=1.0),
                             reads=[rg("LG%d" % lb)] + rc, writes=[rg("PT%d" % lb)])
                    return lb, vprev, rvp, vcur

                pbo_cur = [None]

                def stage2(h, ctx):
                    lb, vprev, rvp, vcur = ctx
                    kh, g4 = divmod(h, 4)
                    if g4 == 0:
                        pbo_cur[0] = bank()
                    rbo, pbo = pbo_cur[0]
                    o = pbo[:, g4 * 65:(g4 + 1) * 65]
                    S.op("pe", lambda e, o=o, lb=lb, vprev=vprev: e.matmul(o, PT[:, lb, 0:128], vprev, start=True, stop=False),
                         reads=[rg("PT%d" % lb), rvp], writes=[rbo], sig=False)
                    S.op("pe", lambda e, o=o, lb=lb, vcur=vcur: e.matmul(o, PT[:, lb, 128:256], vcur, start=False, stop=True),
                         reads=[rg("PT%d" % lb), rg("VA%d" % ping)], writes=[rbo], sig=True)
                    if g4 != 3:
                        return
                    dn = kh % 2
                    ov = pbo[:, 0:260].rearrange("p (g c) -> p g c", g=4)
                    S.op("dve", lambda e, ov=ov, dn=dn, kh=kh: e.tensor_tensor(out=DEN[:, dn, :], in0=ov[:, :, 64], in1=ESK[:, kh * 4:(kh + 1) * 4], op=ALU.add),
                         reads=[rbo] + rc, writes=[rg("DEN%d" % dn)])
                    S.op("dve", lambda e, dn=dn: e.reciprocal(out=DEN[:, dn, :], in_=DEN[:, dn, :]), reads=[rg("DEN%d" % dn)], writes=[rg("DEN%d" % dn)])
                    S.op("dve", lambda e, ov=ov, dn=dn, kh=kh, yb=yb: e.tensor_tensor(
                        out=YA[:, yb, kh * 256:(kh + 1) * 256].rearrange("p (g d) -> p g d", g=4), in0=ov[:, :, 0:64],
                        in1=DEN[:, dn, :].unsqueeze(2).broadcast_to([128, 4, 64]), op=ALU.mult),
                        reads=[rbo, rg("DEN%d" % dn)], writes=[rg("YA%d" % yb)])

                ctxs = {0: stage1(0), 1: stage1(1)}
                for h in range(16):
                    if h + 2 < 16:
                        ctxs[h + 2] = stage1(h + 2)
                    stage2(h, ctxs.pop(h))
                S.op("act", lambda e, yb=yb, b=b: e.activation(out=ACTB[:, 8:10, :].rearrange("p a b -> p (a b)"), in_=YA[:, yb, :], func=AF.Square, accum_out=SS[:, 4 + b:5 + b]),
                     reads=[rg("YA%d" % yb)], writes=[rg("ACT8"), rg("ACT9"), rg("SSA%d" % b)])
                S.op("dve", lambda e, b=b: e.tensor_scalar(out=RSTD[:, 4 + b:5 + b], in0=SS[:, 4 + b:5 + b], scalar1=1.0 / 1024, scalar2=EPS, op0=ALU.mult, op1=ALU.add),
                     reads=[rg("SSA%d" % b)], writes=[rg("RSA%d" % b)])
                S.op("act", lambda e, b=b: e.activation(out=RSTD[:, 4 + b:5 + b], in_=RSTD[:, 4 + b:5 + b], func=AF.Sqrt), reads=[rg("RSA%d" % b)], writes=[rg("RSA%d" % b)])
                S.op("dve", lambda e, b=b: e.reciprocal(out=RSTD[:, 4 + b:5 + b], in_=RSTD[:, 4 + b:5 + b]), reads=[rg("RSA%d" % b)], writes=[rg("RSA%d" % b)])
                S.op("act", lambda e, yb=yb, b=b, hnb=hnb: e.activation(out=HN[:, hnb, 0:1024], in_=YA[:, yb, :], func=AF.Copy, scale=RSTD[:, 4 + b:5 + b]),
                     reads=[rg("YA%d" % yb), rg("RSA%d" % b)], writes=[rg("HN%d" % hnb)])
                rb_, pb_ = bank()
                pbb = pb_.bitcast(BF16)
                for k8 in range(8):
                    S.op("pe", lambda e, pbb=pbb, k8=k8, hnb=hnb: e.transpose(pbb[:, k8 * 128:(k8 + 1) * 128], HN[:, hnb, k8 * 128:(k8 + 1) * 128], IDB[:]),
                         reads=[rg("HN%d" % hnb)] + rc, writes=[rb_], sig=(k8 == 7))
                S.op("dve", lambda e, pbb=pbb, b=b: e.tensor_tensor(
                    out=ACTB[:, 0:8, b * 128:(b + 1) * 128], in0=pbb.rearrange("p (k t) -> p k t", k=8),
                    in1=GA[:].unsqueeze(2).broadcast_to([128, 8, 128]), op=ALU.mult), reads=[rb_] + rc, writes=[rg("ACT%d" % j_) for j_ in range(8)])

            for i_ in range(4):
                ssm_part1(i_)
                attn_block(i_)
                ssm_part2(i_)
            rbss, pbss = bank(reserve=True)
            for nb in range(4):
                s_, wreg = load_wblock([(lambda sl: sl[:, 0:8, :], wcols(w_glu, 1024, nb * 256, 256, krows=8))])
                for m in range(2):
                    ct = nb * 2 + m
                    rb_, pb_ = bank()
                    for c in range(8):
                        S.op("pe", lambda e, pb_=pb_, c=c, m=m, s_=s_: e.matmul(pb_, s_[:, c, m * 128:(m + 1) * 128], YT[:, c].rearrange("p t j -> p (t j)"),
                                                                               start=(c == 0), stop=(c == 7)),
                             reads=[rg("YT"), wreg], writes=[rb_], sig=(c == 7))
                    sg = ct % 2
                    S.op("act", lambda e, pb_=pb_, sg=sg: e.activation(out=SIG[:, sg, :], in_=pb_, func=AF.Sigmoid), reads=[rb_], writes=[rg("SIG%d" % sg)])
                    S.op("dve", lambda e, ct=ct, sg=sg: e.tensor_tensor(out=ST[:, ct, :], in0=YT[:, ct].rearrange("p t j -> p (t j)"), in1=SIG[:, sg, :], op=ALU.mult),
                         reads=[rg("YT"), rg("SIG%d" % sg)], writes=[rg("QT")])
                    S.op("act", lambda e, ct=ct, sg=sg: e.activation(out=SQ[:, sg, :], in_=ST[:, ct, :], func=AF.Square), reads=[rg("QT")], writes=[rg("SQ%d" % sg)])
                    S.op("pe", lambda e, ct=ct, sg=sg: e.matmul(pbss, ONESB[:], SQ[:, sg, :], start=(ct == 0), stop=(ct == 7)),
                         reads=[rg("SQ%d" % sg)] + rc, writes=[rbss])
            reserved.clear()
            if True:
                S.op("act", lambda e: e.activation(out=RB[:, 0, :], in_=pbss, func=AF.Sqrt, scale=1.0 / 1024, bias=EPS), reads=[rbss], writes=[rg("RB0")])
                S.op("dve", lambda e: e.reciprocal(out=RB[:, 0, :], in_=RB[:, 0, :]), reads=[rg("RB0")], writes=[rg("RB0")])
            for ct in range(8):
                S.op("dve", lambda e, ct=ct: e.scalar_tensor_tensor(
                    out=HT[:, 8 + ct, :].rearrange("p (j t) -> p t j", t=8), in0=ST[:, ct, :].rearrange("p (t j) -> p t j", t=8),
                    scalar=GS[:, ct:ct + 1], in1=RB[:, 0, :].rearrange("p (t j) -> p t j", t=8), op0=ALU.mult, op1=ALU.mult),
                    reads=[rg("QT"), rg("RB0")] + rc, writes=[rg("HT")])
            for fb in range(8):
                s_, wreg = load_wblock([(lambda sl: sl, wcols(w_out, D, fb * 256, 256))])
                for sp2 in range(2):
                    rb_, pb_ = bank()
                    for s2 in range(2):
                        sub = sp2 * 2 + s2
                        for k in range(16):
                            src_ = ACTB if k < 8 else HT
                            S.op("pe", lambda e, pb_=pb_, s2=s2, sub=sub, k=k, s_=s_, src_=src_: e.matmul(
                                pb_[:, s2 * 256:(s2 + 1) * 256], src_[:, k, sub * 128:(sub + 1) * 128], s_[:, k, :], start=(k == 0), stop=(k == 15)),
                                reads=[rg("HT") if k >= 8 else rg("ACT%d" % k), wreg], writes=[rb_], sig=(k == 15))
                        S.op("dve", lambda e, pb_=pb_, s2=s2, sub=sub, fb=fb: e.tensor_tensor(
                            out=X[:, sub, fb * 256:(fb + 1) * 256], in0=pb_[:, s2 * 256:(s2 + 1) * 256], in1=X[:, sub, fb * 256:(fb + 1) * 256], op=ALU.add),
                            reads=[rb_, sub_regs[sub]], writes=[sub_regs[sub]])
            if stage >= 7:
                norm_to_HT(G2, sub_regs)
            c0 = 0
            for grp, nch in enumerate(GROUPS_FF if stage >= 7 else []):
                for bl in range(nch // 2):
                    col = (c0 + bl * 2) * 128
                    sg_, wg = load_wblock([(lambda sl: sl, wcols(w_gate, FF, col, 256))], ffn=True)
                    su_, wu = load_wblock([(lambda sl: sl, wcols(w_up, FF, col, 256))], ffn=True)
                    for m in range(2):
                        j = bl * 2 + m
                        rbg, pbg = bank()
                        for k in range(16):
                            S.op("pe", lambda e, pbg=pbg, k=k, m=m, sg_=sg_: e.matmul(pbg, sg_[:, k, m * 128:(m + 1) * 128], HT[:, k, :], start=(k == 0), stop=(k == 15)),
                                 reads=[rg("HT"), wg], writes=[rbg], sig=(k == 15))
                        rbu, pbu = bank()
                        for k in range(16):
                            S.op("pe", lambda e, pbu=pbu, k=k, m=m, su_=su_: e.matmul(pbu, su_[:, k, m * 128:(m + 1) * 128], HT[:, k, :], start=(k == 0), stop=(k == 15)),
                                 reads=[rg("HT"), wu], writes=[rbu], sig=(k == 15))
                        sg = j % 2
                        S.op("act", lambda e, pbg=pbg, sg=sg: e.activation(out=SIG[:, sg, :], in_=pbg, func=AF.Silu), reads=[rbg], writes=[rg("SIG%d" % sg)])
                        S.op("dve", lambda e, pbu=pbu, sg=sg, j=j: e.tensor_tensor(out=ACTB[:, j, :], in0=pbu, in1=SIG[:, sg, :], op=ALU.mult),
                             reads=[rbu, rg("SIG%d" % sg)], writes=[rg("ACT%d" % j)])
                for f in range(4):
                    bk = [bank() for _ in range(4)]
                    for j in range(nch):
                        wap, wrn, wsn = wd_ring[wslot_d[0] % len(wd_ring)]
                        wslot_d[0] += 1
                        wreg = rg(wrn)
                        S.dma("pool", lambda e, wap=wap, j=j, f=f, c0=c0: e.dma_start(out=wap, in_=w_down[(c0 + j) * 128:(c0 + j + 1) * 128, f * 512:(f + 1) * 512]),
                              wsn, writes=[wreg])
                        for sub in range(4):
                            S.op("pe", lambda e, sub=sub, j=j, wap=wap, pb_=bk[sub][1]: e.matmul(pb_, ACTB[:, j, sub * 128:(sub + 1) * 128], wap,
                                                                                              start=(j == 0), stop=(j == nch - 1)),
                                 reads=[rg("ACT%d" % j), wreg], writes=[bk[sub][0]], sig=(j == nch - 1 or sub == 3))
                    for sub in range(4):
                        S.op("dve", lambda e, sub=sub, f=f, pb_=bk[sub][1]: e.tensor_tensor(
                            out=X[:, sub, f * 512:(f + 1) * 512], in0=pb_, in1=X[:, sub, f * 512:(f + 1) * 512], op=ALU.add),
                            reads=[bk[sub][0], sub_regs[sub]], writes=[sub_regs[sub]])
                c0 += nch
            for sub in range(4):
                out_toks.append(S.dma("sp", lambda e, sub=sub: e.dma_start(out=out[tt_i * NT + sub * 128: tt_i * NT + (sub + 1) * 128, :], in_=X[:, sub, :]),
                                      "ot%d" % sub, reads=[sub_regs[sub]]))

        wslot_d = [0]
        out_toks = []
        ping = 0
        if do_pred:
            for t in range(n_tt):
                process_tt(x_pred, t, True, t == n_tt - 1, ping)
            ping = 1 - ping
        for t in range(n_tt):
            process_tt(x_main, t, False, False, ping)
            ping = 1 - ping
        for tk in out_toks:
            S.wait_tok("sp", tk)
        with nc.Block() as block:
            S.replay(block)
    return nc


def _t5_bucket(dist):
    n = np.maximum(dist, 0)
    max_exact = 16
    nf = np.maximum(n, 1).astype(np.float32)
    large = max_exact + (np.log(nf / max_exact) / math.log(128 / max_exact) * (32 - max_exact)).astype(np.int32)
    large = np.minimum(large, 31)
    return np.where(n < max_exact, n, large).astype(np.int32)


def _consts():
    ident = np.eye(128, dtype=np.float32)
    s_idx = np.arange(128) // 16
    mask = (s_idx[:, None] <= s_idx[None, :]).astype(np.float32)
    mv = np.broadcast_to(np.asarray(MS, np.float32)[None, :, None], (128, NM, 32)).reshape(128, NM * 32).copy()
    bones = (s_idx[:, None] // 4 == s_idx[None, :] // 4).astype(np.float32)
    bucket = _t5_bucket(np.arange(128))
    oh = np.zeros((33, 512), np.float32)
    for e in range(255):
        if e < 127:
            oh[bucket[e + 1], e] = 1.0
            oh[32, 256 + e] = NEG
        else:
            oh[32, e] = NEG
            oh[bucket[e - 127], 256 + e] = 1.0
    oh[32, 255] = NEG
    oh[32, 511] = NEG
    dup = np.zeros((128, 2, 128), np.float32)
    for c in range(2):
        for d in range(64):
            dup[c * 64 + d, c, d] = 1.0
            dup[c * 64 + d, c, 64 + d] = 1.0
    mv2 = np.broadcast_to((8.0 * np.arange(1, 65, dtype=np.float32))[None, :, None], (128, 64, 32)).reshape(128, 64 * 32).copy()
    return {"c_mv2": mv2, "c_dup": dup.reshape(128, 256), "c_ident": ident, "c_mask": mask, "c_mv": mv, "c_oh": oh, "c_bones": bones, "c_anti": np.ascontiguousarray(ident[::-1])}


_NC_CACHE = {}


def kernel(**inputs):
    n_tt = int(os.environ.get("MK_NTT", "4"))
    do_pred = os.environ.get("MK_PRED", "1") == "1"
    stage = int(os.environ.get("MK_STAGE", "99"))
    do_setup = os.environ.get("MK_SETUP", "1") == "1"
    key = (n_tt, do_pred, stage, do_setup)
    if key not in _NC_CACHE:
        _NC_CACHE[key] = build(n_tt, do_pred, stage, do_setup)
    nc = _NC_CACHE[key]
    x = np.asarray(inputs["x"], np.float32)
    TOK = n_tt * NT
    shared = {k: np.ascontiguousarray(np.asarray(inputs[k], np.float32)[0]) for k in
              ["ln1_g", "w_in", "q_norm_g", "k_norm_g", "attn_sinks", "ssm_a_re", "ssm_a_im", "ssm_log_dt", "ssm_b_re", "ssm_b_im",
               "ssm_c_re", "ssm_c_im", "w_glu", "attn_out_g", "ssm_out_g", "w_out", "ln2_g", "w_ff_gate", "w_ff_up", "w_ff_down"]}
    shared["ssm_d"] = np.ascontiguousarray(np.asarray(inputs["ssm_d"], np.float32)[0].reshape(-1))
    shared["rel_bias"] = np.ascontiguousarray(np.asarray(inputs["rel_bias"], np.float32))
    shared.update(_consts())
    in_maps = []
    ncores = int(os.environ.get("MK_CORES", "8"))
    for c in range(ncores):
        b, half = c // 2, c % 2
        m = dict(shared)
        m["x_main"] = np.ascontiguousarray(x[b, half * 2048: half * 2048 + TOK])
        if half == 1:
            m["x_pred"] = np.ascontiguousarray(x[b, 2048 - TOK:2048])
            m["hm8"] = np.full((128, 1), -SHIFT, np.float32)
        else:
            m["x_pred"] = np.zeros((TOK, D), np.float32)
            m["hm8"] = np.full((128, 1), NEG - SHIFT, np.float32)
        in_maps.append(m)
    if os.environ.get("MK_TRACE", "0") == "1":
        res = run_bass_kernel_spmd(nc, in_maps, core_ids=list(range(ncores)), trace=True)
        print("EXEC_NS", res.exec_time_ns)
    else:
        res = run_bass_kernel_spmd(nc, in_maps, core_ids=list(range(ncores)))
    outp = np.zeros((4, 4096, D), np.float32)
    for c in range(ncores):
        b, half = c // 2, c % 2
        outp[b, half * 2048: half * 2048 + TOK] = res.results[c]["out"]
    return outp
```

```python
import os
import math
import numpy as np
import ml_dtypes
import concourse.bass as bass
import concourse.mybir as mybir
from concourse.bass_utils import run_bass_kernel_spmd

F32 = mybir.dt.float32
BF16 = mybir.dt.bfloat16
I32 = mybir.dt.int32
ALU = mybir.AluOpType
AF = mybir.ActivationFunctionType

D = 2048
NT = 512
FF = 5632
EPS = 1e-6
NEG = -30000.0
SHIFT = 8.0
MS = [-(s + 1) for s in range(8)] + [t + 1 for t in range(8)] + [7 - s for s in range(8)] + [8, 16, 32, 64, 128, 256]
NM = len(MS)
GROUPS_FF = [12, 10, 12, 10]


class Reg:
    __slots__ = ("w", "r")

    def __init__(self):
        self.w = None
        self.r = {}


class Sched:
    def __init__(self, nc, semh):
        self.nc = nc
        self.semh = semh
        self.names = ["pe", "act", "dve", "pool", "sp"]
        self.ops = {k: [] for k in self.names}
        self.cnt = {k: 0 for k in self.names}
        self.waited = {k: {} for k in self.names}
        self.dcnt = {}

    def _deps(self, e, reads, writes):
        deps = {}

        def add(tok):
            if tok is None:
                return
            s, v = tok
            if e == "pe" and s == "pe":
                return
            if deps.get(s, 0) < v:
                deps[s] = v
        for r in reads:
            add(r.w)
        for w in writes:
            add(w.w)
            for s, v in w.r.items():
                add((s, v))
        out = []
        for s, v in deps.items():
            if self.waited[e].get(s, 0) < v:
                self.waited[e][s] = v
                out.append((s, v))
        return out

    def _upd(self, tok, reads, writes):
        s, v = tok
        for r in reads:
            if r.r.get(s, 0) < v:
                r.r[s] = v
        for w in writes:
            w.w = tok
            w.r = {}

    def op(self, e, fn, reads=(), writes=(), sig=True):
        waits = self._deps(e, reads, writes)
        if sig:
            self.cnt[e] += 1
            v = self.cnt[e]
        else:
            v = self.cnt[e] + 1
        self.ops[e].append((waits, fn, (e, 1) if sig else None))
        self._upd((e, v), reads, writes)

    def dma(self, q, fn, sem, reads=(), writes=()):
        waits = self._deps(q, reads, writes)
        self.dcnt[sem] = self.dcnt.get(sem, 0) + 16
        tok = (sem, self.dcnt[sem])
        self.ops[q].append((waits, fn, (sem, 16)))
        self._upd(tok, reads, writes)
        return tok

    def wait_tok(self, e, tok):
        s, v = tok
        if self.waited[e].get(s, 0) < v:
            self.waited[e][s] = v
            self.ops[e].append(([(s, v)], None, None))

    def replay(self, block):
        decs = {"pe": block.tensor, "act": block.scalar, "dve": block.vector, "pool": block.gpsimd, "sp": block.sync}
        for name in self.names:
            lst = self.ops[name]
            if not lst:
                continue

            def body(e, lst=lst):
                for waits, fn, inc in lst:
                    for s, v in waits:
                        e.wait_ge(self.semh[s], v)
                    if fn is None:
                        continue
                    ins = fn(e)
                    if inc is not None:
                        ins.then_inc(self.semh[inc[0]], inc[1])
            decs[name](body)
            self.ops[name] = []


def dap(t, offset, dims):
    return bass.AP(tensor=t.tensor, offset=offset, ap=[[s, c] for s, c in dims])


def build(n_tt=4, do_pred=True, stage=99, do_setup=True):
    nc = bass.Bass("TRN2", target_bir_lowering=False)
    TOK = n_tt * NT

    def din(name, shape, dt=F32):
        return nc.dram_tensor(name, list(shape), dt, kind="ExternalInput").ap()

    x_main = din("x_main", [TOK, D])
    x_pred = din("x_pred", [TOK, D])
    hm8 = din("hm8", [128, 1])
    rel_bias = din("rel_bias", [32, 16])
    ln1_g = din("ln1_g", [D])
    w_in = din("w_in", [D, 2560])
    q_norm_g = din("q_norm_g", [64])
    k_norm_g = din("k_norm_g", [64])
    sinks = din("attn_sinks", [16])
    a_re = din("ssm_a_re", [64, 64])
    a_im = din("ssm_a_im", [64, 64])
    log_dt = din("ssm_log_dt", [64])
    b_re = din("ssm_b_re", [64, 64, 16])
    b_im = din("ssm_b_im", [64, 64, 16])
    c_re = din("ssm_c_re", [64, 16, 64])
    c_im = din("ssm_c_im", [64, 16, 64])
    ssm_d = din("ssm_d", [64 * 16])
    w_glu = din("w_glu", [1024, 1024])
    attn_out_g = din("attn_out_g", [1024])
    ssm_out_g = din("ssm_out_g", [1024])
    w_out = din("w_out", [D, D])
    ln2_g = din("ln2_g", [D])
    w_gate = din("w_ff_gate", [D, FF])
    w_up = din("w_ff_up", [D, FF])
    w_down = din("w_ff_down", [FF, D])
    c_ident = din("c_ident", [128, 128])
    c_mask = din("c_mask", [128, 128])
    c_mv = din("c_mv", [128, NM * 32])
    c_mv2 = din("c_mv2", [128, 64 * 32])
    c_oh = din("c_oh", [33, 512])
    c_bones = din("c_bones", [128, 128])
    c_anti = din("c_anti", [128, 128])
    c_dup = din("c_dup", [128, 256])
    out = nc.dram_tensor("out", [TOK, D], F32, kind="ExternalOutput").ap()
    wt_d = nc.dram_tensor("wt_d", [128, 64 * 128], BF16, kind="Internal").ap()
    kt_d = nc.dram_tensor("kt_d", [128, 64 * 128], BF16, kind="Internal").ap()
    et_d = nc.dram_tensor("et_d", [128, 64 * 256], BF16, kind="Internal").ap()
    ext_d = nc.dram_tensor("ext_d", [16, 512], F32, kind="Internal").ap()

    sem_names = ["pe", "act", "dve", "pool", "sp", "xl0", "xl1", "xl2", "xl3", "wb0", "wb1", "wb2", "wd", "tb0", "tb1",
                 "st", "misc", "ot0", "ot1", "ot2", "ot3"] + ["wd%d" % i for i in range(10)] + ["wb3", "wb4"]
    import contextlib
    with contextlib.ExitStack() as es:
        semh = {n: es.enter_context(nc.semaphore(n)) for n in sem_names}
        S = Sched(nc, semh)

        def sb(name, shape, dt):
            return es.enter_context(nc.sbuf_tensor(name, list(shape), dt))

        ROTC = sb("ROTC", [128, 32, 64], BF16)
        ROTS = sb("ROTS", [128, 32, 64], BF16)
        RHO = sb("RHO", [128, 32], F32)
        BIAS = sb("BIAS", [128, 16, 2, 128], BF16)
        G1 = sb("G1", [128, 16], F32)
        G2 = sb("G2", [128, 16], F32)
        GA = sb("GA", [128, 8], F32)
        GS = sb("GS", [128, 8], F32)
        QG = sb("QG", [128, 1], F32)
        KG = sb("KG", [128, 1], F32)
        ESK = sb("ESK", [128, 16], F32)
        HM8 = sb("HM8", [128, 1], F32)
        DUPB = sb("DUPB", [128, 2, 128], BF16)
        NEG8 = sb("NEG8", [128, 1], F32)
        IDB = sb("IDB", [128, 128], BF16)
        BONES = sb("BONES", [128, 128], BF16)
        ONESB = sb("ONESB", [128, 128], BF16)
        r_const = Reg()
        PS = es.enter_context(nc.psum_tensor("PS", [128, 8, 512], F32))
        r_ps = [Reg() for _ in range(8)]
        bank_i = [0]

        reserved = set()

        def bank(reserve=False):
            while True:
                i = bank_i[0] % 8
                bank_i[0] += 1
                if i not in reserved:
                    break
            if reserve:
                reserved.add(i)
            return r_ps[i], PS[:, i, :]

        with contextlib.ExitStack() as es2:
            def sb2(name, shape, dt=F32):
                return es2.enter_context(nc.sbuf_tensor(name, list(shape), dt))
            ARE = sb2("ARE", [128, 32]); AIM = sb2("AIM", [128, 32]); LDT = sb2("LDT", [128, 32])
            DT = sb2("DT", [128, 32]); ARD = sb2("ARD", [128, 32]); AID = sb2("AID", [128, 32])
            MV = sb2("MV", [128, NM, 32])
            MAG = sb2("MAG", [128, NM, 32])
            ANG = sb2("ANG", [128, 2, NM, 32])
            SC = sb2("SC", [128, 2, NM, 32])
            PWR = sb2("PWR", [128, NM, 32]); PWI = sb2("PWI", [128, NM, 32])
            SM = [sb2("SM%d" % i, [128, 32]) for i in range(10)]
            BRE = sb2("BRE", [128, 32, 16]); BIM = sb2("BIM", [128, 32, 16])
            BBR = sb2("BBR", [128, 32, 16]); BBI = sb2("BBI", [128, 32, 16])
            CR = sb2("CR", [128, 32, 16]); CI = sb2("CI", [128, 32, 16])
            T1 = sb2("T1", [128, 32, 8, 16]); T2 = T1
            VR = sb2("VR", [128, 32, 8, 16]); VI = sb2("VI", [128, 32, 8, 16])
            WR = VR; WI = VI
            ER = sb2("ER", [128, 32, 8, 16]); EI = sb2("EI", [128, 32, 8, 16])
            KIv = ER[:].bitcast(I32).rearrange("p a b c -> p (a b c)")[:, 0:2 * NM * 32].rearrange("p (s m r) -> p s m r", s=2, m=NM)
            KFv = VR[:].rearrange("p a b c -> p (a b c)")[:, 0:2 * NM * 32].rearrange("p (s m r) -> p s m r", s=2, m=NM)
            IDF = sb2("IDF", [128, 128]); MASK = sb2("MASK", [128, 128]); BONF = sb2("BONF", [128, 128])
            DROW = sb2("DROW", [128, 64, 16])
            KTB = sb2("KTB", [128, 64, 128], BF16)
            MV2 = KTB[:].bitcast(F32).rearrange("p g c -> p (g c)")[:, 0:2048].rearrange("p (m r) -> p m r", m=64)
            WTB = KTB
            ETB = sb2("ETB", [128, 64, 128], BF16)
            TMPK = sb2("TMPK", [128, 4, 8, 16])
            RBA = sb2("RBA", [33, 16]); OH = sb2("OH", [33, 512])
            EXTV = SC[:].rearrange("p a m r -> p (a m r)")[0:16, 0:512]
            SKV = sb2("SKV", [128, 16])
            ANTF = sb2("ANTF", [128, 128]); ANTB = sb2("ANTB", [128, 128], BF16)
            DUPF = sb2("DUPF", [128, 2, 128])
            r = {n: Reg() for n in ["in", "dt", "mag", "ang", "ki", "kf", "cm", "sc", "pw", "sm", "bb", "t1", "t2", "v", "w", "e",
                                    "ktb", "wtb", "etb", "tmpk", "ext", "biasf", "extd", "wtd", "ktd", "etd", "pers", "cin"]}

            ld = []

            def L(out_ap, in_ap):
                ld.append(S.dma("sp", lambda e, o=out_ap, i=in_ap: e.dma_start(out=o, in_=i, allow_slow_non_contiguous=True),
                                "misc", writes=[r["in"]]))
            for gi in range(2):
                hs = slice(gi * 64, (gi + 1) * 64)
                L(ARE[hs, :], dap(a_re, gi * 64, [(1, 64), (128, 32)]))
                L(AIM[hs, :], dap(a_im, gi * 64, [(1, 64), (128, 32)]))
                L(LDT[hs, :], dap(log_dt, gi, [(0, 64), (2, 32)]))
                L(BRE[hs, :, :], dap(b_re, gi * 1024, [(16, 64), (2048, 32), (1, 16)]))
                L(BIM[hs, :, :], dap(b_im, gi * 1024, [(16, 64), (2048, 32), (1, 16)]))
            L(MV[:], c_mv.rearrange("p (m r) -> p m r", m=NM))
            L(MV2, c_mv2.rearrange("p (m r) -> p m r", m=64))
            L(IDF[:], c_ident)
            L(MASK[:], c_mask)
            L(BONF[:], c_bones)
            L(ANTF[:], c_anti)
            L(DUPF[:], c_dup.rearrange("p (c m) -> p c m", c=2))
            L(DROW[:], dap(ssm_d, 0, [(0, 128), (16, 64), (1, 16)]))
            L(RBA[0:32, :], rel_bias)
            L(OH[:], c_oh)
            L(G1[:], dap(ln1_g, 0, [(1, 128), (128, 16)]))
            L(G2[:], dap(ln2_g, 0, [(1, 128), (128, 16)]))
            L(GA[:], dap(attn_out_g, 0, [(1, 128), (128, 8)]))
            L(GS[:], dap(ssm_out_g, 0, [(1, 128), (128, 8)]))
            for h2 in range(2):
                L(QG[h2 * 64:(h2 + 1) * 64, :], dap(q_norm_g, 0, [(1, 64), (1, 1)]))
                L(KG[h2 * 64:(h2 + 1) * 64, :], dap(k_norm_g, 0, [(1, 64), (1, 1)]))
            L(SKV[:], dap(sinks, 0, [(0, 128), (1, 16)]))
            L(HM8[:], hm8)

            V_ = "dve"
            rin = [r["in"]]
            CZ = T1[0:32].rearrange("p a b c -> p (a b c)")[:, 0:1024].rearrange("p (i c) -> p i c", i=8)
            S.op("dve", lambda e: e.memset(CZ, 0.0), writes=[r["t1"]])
            for ci, (csrc, CX) in enumerate(((c_re, CR), (c_im, CI))):
                for ch in range(4):
                    pr0 = ch * 8
                    for gi in range(2):
                        S.dma("sp", lambda e, gi=gi, pr0=pr0, csrc=csrc: e.dma_start(
                            out=CZ[gi * 16:(gi + 1) * 16, :, gi * 64:(gi + 1) * 64],
                            in_=dap(csrc, (2 * pr0 + gi) * 1024, [(64, 16), (2048, 8), (1, 64)])), "xl0", writes=[r["t1"]])
                    rb_, pb_ = bank()
                    for i in range(8):
                        S.op("pe", lambda e, pb_=pb_, i=i: e.transpose(pb_[:, i * 32:(i + 1) * 32], CZ[:, i, :], IDF[0:32, 0:32]),
                             reads=[r["t1"]] + rin, writes=[rb_], sig=(i == 7))
                    for gi in range(2):
                        hs = slice(gi * 64, (gi + 1) * 64)
                        S.op("act", lambda e, pb_=pb_, hs=hs, gi=gi, pr0=pr0, CX=CX: e.activation(
                            out=CX[hs, pr0:pr0 + 8, :], in_=pb_[hs, 0:256].rearrange("p (i g q) -> p i g q", i=8, g=2)[:, :, gi, :], func=AF.Copy),
                            reads=[rb_], writes=[r["cin"]])
            S.op(V_, lambda e: e.memset(NEG8[:], -SHIFT), writes=[r_const])
            S.op(V_, lambda e: e.memset(ONESB[:], 1.0), writes=[r_const])
            S.op(V_, lambda e: e.tensor_copy(out=IDB[:], in_=IDF[:]), reads=rin, writes=[r_const])
            S.op(V_, lambda e: e.tensor_copy(out=BONES[:], in_=BONF[:]), reads=rin, writes=[r_const])
            S.op(V_, lambda e: e.tensor_copy(out=DUPB[:], in_=DUPF[:]), reads=rin, writes=[r_const])
            S.op(V_, lambda e: e.memset(RBA[32:33, :], 1.0), reads=rin, writes=[r["in"]])
            S.op("act", lambda e: e.activation(out=ESK[:], in_=SKV[:], func=AF.Exp, bias=NEG8[:, 0:1], scale=1.0),
                 reads=rin + [r_const], writes=[r["pers"]])
            S.op("act", lambda e: e.activation(out=DT[:], in_=LDT[:], func=AF.Exp), reads=rin, writes=[r["dt"]])
            S.op(V_, lambda e: e.tensor_tensor(out=ARD[:], in0=ARE[:], in1=DT[:], op=ALU.mult), reads=rin + [r["dt"]], writes=[r["sm"]])
            S.op(V_, lambda e: e.tensor_tensor(out=AID[:], in0=AIM[:], in1=DT[:], op=ALU.mult), reads=rin + [r["dt"]], writes=[r["sm"]])

            def bc_m(t):
                return t[:].unsqueeze(1).broadcast_to([128, NM, 32])
            S.op(V_, lambda e: e.tensor_tensor(out=MAG[:], in0=MV[:], in1=bc_m(ARD), op=ALU.mult), reads=rin + [r["sm"]], writes=[r["mag"]])
            S.op("act", lambda e: e.activation(out=MAG[:], in_=MAG[:], func=AF.Exp), reads=[r["mag"]], writes=[r["mag"]])
            S.op(V_, lambda e: e.tensor_tensor(out=ANG[:, 0], in0=MV[:], in1=bc_m(AID), op=ALU.mult), reads=rin + [r["sm"]], writes=[r["ang"]])
            S.op(V_, lambda e: e.tensor_scalar(out=ANG[:, 0], in0=ANG[:, 0], scalar1=1.0 / (2 * math.pi), scalar2=None, op0=ALU.mult),
                 reads=[r["ang"]], writes=[r["ang"]])
            S.op(V_, lambda e: e.tensor_scalar(out=ANG[:, 1], in0=ANG[:, 0], scalar1=0.25, scalar2=None, op0=ALU.add),
                 reads=[r["ang"]], writes=[r["ang"]])
            S.op(V_, lambda e: e.tensor_copy(out=KIv, in_=ANG[:]), reads=[r["ang"]], writes=[r["e"]])
            S.op(V_, lambda e: e.tensor_copy(out=KFv, in_=KIv), reads=[r["e"]], writes=[r["v"]])
            S.op(V_, lambda e: e.tensor_tensor(out=ANG[:], in0=ANG[:], in1=KFv, op=ALU.subtract), reads=[r["ang"], r["v"]], writes=[r["ang"]])
            S.op(V_, lambda e: e.tensor_scalar(out=KFv, in0=ANG[:], scalar1=0.5, scalar2=None, op0=ALU.is_gt), reads=[r["ang"]], writes=[r["v"]])
            S.op(V_, lambda e: e.tensor_tensor(out=ANG[:], in0=ANG[:], in1=KFv, op=ALU.subtract), reads=[r["ang"], r["v"]], writes=[r["ang"]])
            S.op(V_, lambda e: e.tensor_scalar(out=KFv, in0=ANG[:], scalar1=-0.5, scalar2=None, op0=ALU.is_lt), reads=[r["ang"]], writes=[r["v"]])
            S.op(V_, lambda e: e.tensor_tensor(out=ANG[:], in0=ANG[:], in1=KFv, op=ALU.add), reads=[r["ang"], r["v"]], writes=[r["ang"]])
            S.op("act", lambda e: e.activation(out=SC[:], in_=ANG[:], func=AF.Sin, scale=6.283185), reads=[r["ang"]], writes=[r["sc"]])
            S.op(V_, lambda e: e.tensor_tensor(out=PWR[:], in0=MAG[:], in1=SC[:, 1], op=ALU.mult), reads=[r["mag"], r["sc"]], writes=[r["pw"]])
            S.op(V_, lambda e: e.tensor_tensor(out=PWI[:], in0=MAG[:], in1=SC[:, 0], op=ALU.mult), reads=[r["mag"], r["sc"]], writes=[r["pw"]])
            S.op(V_, lambda e: e.tensor_copy(out=RHO[:], in_=MAG[:, 15, :]), reads=[r["mag"]], writes=[r["pers"]])
            A2 = T1[:].rearrange("p a b c -> p (a b c)").rearrange("p (s j r) -> p s j r", s=2, j=64)
            K2 = ER[:].bitcast(I32).rearrange("p a b c -> p (a b c)").rearrange("p (s j r) -> p s j r", s=2, j=64)
            F2 = VR[:].rearrange("p a b c -> p (a b c)").rearrange("p (s j r) -> p s j r", s=2, j=64)
            S2 = VI[:].rearrange("p a b c -> p (a b c)").rearrange("p (s j r) -> p s j r", s=2, j=64)
            ra, rk, rf = [r["t1"]], [r["e"]], [r["v"]]
            S.op(V_, lambda e: e.tensor_tensor(out=A2[:, 0], in0=MV2, in1=AID[:].unsqueeze(1).broadcast_to([128, 64, 32]), op=ALU.mult),
                 reads=rin + [r["sm"], r["ktb"]], writes=ra)
            S.op(V_, lambda e: e.tensor_scalar(out=A2[:, 0], in0=A2[:, 0], scalar1=1.0 / (2 * math.pi), scalar2=None, op0=ALU.mult), reads=ra, writes=ra)
            S.op(V_, lambda e: e.tensor_scalar(out=A2[:, 1], in0=A2[:, 0], scalar1=0.25, scalar2=None, op0=ALU.add), reads=ra, writes=ra)
            S.op(V_, lambda e: e.tensor_copy(out=K2, in_=A2), reads=ra, writes=rk)
            S.op(V_, lambda e: e.tensor_copy(out=F2, in_=K2), reads=rk, writes=rf)
            S.op(V_, lambda e: e.tensor_tensor(out=A2, in0=A2, in1=F2, op=ALU.subtract), reads=ra + rf, writes=ra)
            S.op(V_, lambda e: e.tensor_scalar(out=F2, in0=A2, scalar1=0.5, scalar2=None, op0=ALU.is_gt), reads=ra, writes=rf)
            S.op(V_, lambda e: e.tensor_tensor(out=A2, in0=A2, in1=F2, op=ALU.subtract), reads=ra + rf, writes=ra)
            S.op(V_, lambda e: e.tensor_scalar(out=F2, in0=A2, scalar1=-0.5, scalar2=None, op0=ALU.is_lt), reads=ra, writes=rf)
            S.op(V_, lambda e: e.tensor_tensor(out=A2, in0=A2, in1=F2, op=ALU.add), reads=ra + rf, writes=ra)
            S.op("act", lambda e: e.activation(out=S2, in_=A2, func=AF.Sin, scale=6.283185), reads=ra, writes=rf)
            S.op(V_, lambda e: e.tensor_copy(out=ROTS[:].rearrange("p r j -> p j r"), in_=S2[:, 0]), reads=rf, writes=[r["pers"]])
            S.op(V_, lambda e: e.tensor_copy(out=ROTC[:].rearrange("p r j -> p j r"), in_=S2[:, 1]), reads=rf, writes=[r["pers"]])
            nr, ni, den, rden, t0, t1_, fr, fi = SM[0], SM[1], SM[2], SM[3], SM[4], SM[5], SM[6], SM[7]
            rs = [r["sm"]]
            rp = [r["pw"]]

            def tt(o, a, b, op, reads, writes):
                S.op(V_, lambda e: e.tensor_tensor(out=o, in0=a, in1=b, op=op), reads=reads, writes=writes)
            S.op(V_, lambda e: e.tensor_scalar(out=nr[:], in0=PWR[:, 8, :], scalar1=-1.0, scalar2=None, op0=ALU.add), reads=rp, writes=rs)
            tt(den[:], ARE[:], ARE[:], ALU.mult, rin, rs)
            tt(t0[:], AIM[:], AIM[:], ALU.mult, rin, rs)
            tt(den[:], den[:], t0[:], ALU.add, rs, rs)
            S.op(V_, lambda e: e.reciprocal(out=rden[:], in_=den[:]), reads=rs, writes=rs)
            tt(t0[:], nr[:], ARE[:], ALU.mult, rs + rin, rs)
            tt(t1_[:], PWI[:, 8, :], AIM[:], ALU.mult, rp + rin, rs)
            tt(t0[:], t0[:], t1_[:], ALU.add, rs, rs)
            tt(fr[:], t0[:], rden[:], ALU.mult, rs, rs)
            tt(t0[:], PWI[:, 8, :], ARE[:], ALU.mult, rp + rin, rs)
            tt(t1_[:], nr[:], AIM[:], ALU.mult, rs + rin, rs)
            tt(t0[:], t0[:], t1_[:], ALU.subtract, rs, rs)
            tt(fi[:], t0[:], rden[:], ALU.mult, rs, rs)

            def bq(t):
                return t[:].unsqueeze(2).broadcast_to([128, 32, 16])
            rb = [r["bb"]]
            tt(BBR[:], BRE[:], bq(fr), ALU.mult, rin + rs, rb)
            tt(T1[:, :, 0, :], BIM[:], bq(fi), ALU.mult, rin + rs, [r["t1"]])
            tt(BBR[:], BBR[:], T1[:, :, 0, :], ALU.subtract, rb + [r["t1"]], rb)
            tt(BBI[:], BIM[:], bq(fr), ALU.mult, rin + rs, rb)
            tt(T1[:, :, 0, :], BRE[:], bq(fi), ALU.mult, rin + rs, [r["t1"]])
            tt(BBI[:], BBI[:], T1[:, :, 0, :], ALU.add, rb + [r["t1"]], rb)

            def pw8(t, i0):
                return t[:, i0:i0 + 8, :].rearrange("p s r -> p r s").unsqueeze(3).broadcast_to([128, 32, 8, 16])

            def x8(t):
                return t[:].unsqueeze(2).broadcast_to([128, 32, 8, 16])

            def cmul(OR, OI, i0, XR, XI, rx, ro, neg_im=False):
                tt(OR[:], pw8(PWR, i0), x8(XR), ALU.mult, rp + rx, ro)
                tt(T1[:], pw8(PWI, i0), x8(XI), ALU.mult, rp + rx, [r["t1"]])
                tt(OR[:], OR[:], T1[:], ALU.subtract, ro + [r["t1"]], ro)
                tt(OI[:], pw8(PWR, i0), x8(XI), ALU.mult, rp + rx, ro)
                tt(T2[:], pw8(PWI, i0), x8(XR), ALU.mult, rp + rx, [r["t1"]])
                tt(OI[:], OI[:], T2[:], ALU.add, ro + [r["t1"]], ro)
                if neg_im:
                    S.op(V_, lambda e: e.tensor_scalar(out=OI[:], in0=OI[:], scalar1=-1.0, scalar2=None, op0=ALU.mult), reads=ro, writes=ro)
            cmul(VR, VI, 0, BBR, BBI, rb, [r["v"]])
            cmul(ER, EI, 8, CR, CI, rin + [r["cin"]], [r["e"]], neg_im=True)

            for pq in range(8):
                banks = [bank(), bank()]
                for gi in range(2):
                    hs = slice(gi * 64, (gi + 1) * 64)
                    rb_, pb_ = banks[gi]
                    for p4 in range(4):
                        pr = pq * 4 + p4
                        o = pb_[:, p4 * 128:(p4 + 1) * 128]
                        S.op("pe", lambda e, o=o, hs=hs, pr=pr: e.matmul(o, VR[hs, pr].rearrange("p s q -> p (s q)"),
                                                                         ER[hs, pr].rearrange("p s q -> p (s q)"), start=True, stop=False),
                             reads=[r["v"], r["e"]], writes=[rb_], sig=False)
                        S.op("pe", lambda e, o=o, hs=hs, pr=pr: e.matmul(o, VI[hs, pr].rearrange("p s q -> p (s q)"),
                                                                         EI[hs, pr].rearrange("p s q -> p (s q)"), start=False, stop=True),
                             reads=[r["v"], r["e"]], writes=[rb_], sig=(p4 == 3))
                    gsel = slice(2 * pq * 4 + gi, 2 * (pq * 4 + 4), 2)
                    S.op(V_, lambda e, gsel=gsel: e.tensor_tensor(
                        out=TMPK[:], in0=IDF[:].rearrange("p (t q) -> p t q", t=8).unsqueeze(1).broadcast_to([128, 4, 8, 16]),
                        in1=DROW[:, gsel, :].unsqueeze(2).broadcast_to([128, 4, 8, 16]), op=ALU.mult),
                        reads=rin, writes=[r["tmpk"]])
                    S.op(V_, lambda e, pb_=pb_: e.tensor_tensor(
                        out=pb_.rearrange("p (g c) -> p g c", g=4), in0=pb_.rearrange("p (g c) -> p g c", g=4),
                        in1=MASK[:].unsqueeze(1).broadcast_to([128, 4, 128]), op=ALU.mult), reads=[rb_] + rin, writes=[rb_])
                    S.op(V_, lambda e, pb_=pb_, gsel=gsel: e.tensor_tensor(
                        out=KTB[:, gsel, :], in0=pb_.rearrange("p (g c) -> p g c", g=4),
                        in1=TMPK[:].rearrange("p g t q -> p g (t q)"), op=ALU.add), reads=[rb_, r["tmpk"]], writes=[r["ktb"]])
            tk1 = S.dma("sp", lambda e: e.dma_start(out=kt_d, in_=KTB[:].rearrange("p g c -> p (g c)")), "st", reads=[r["ktb"]], writes=[r["ktd"]])
            cmul(WR, WI, 16, BBR, BBI, rb, [r["v"]])
            for c, WX in enumerate((WR, WI)):
                for pq in range(8):
                    rb_, pb_ = bank()
                    for p4 in range(4):
                        pr = pq * 4 + p4
                        S.op("pe", lambda e, pb_=pb_, p4=p4, pr=pr, WX=WX: e.transpose(
                            pb_[:, p4 * 128:(p4 + 1) * 128], WX[:, pr].rearrange("p s q -> p (s q)"), IDF[:]),
                            reads=[r["v"]] + rin, writes=[rb_], sig=(p4 == 3))
                    S.op("act", lambda e, pb_=pb_, pq=pq, c=c: e.activation(
                        out=WTB[:, pq * 8:(pq + 1) * 8, c * 64:(c + 1) * 64],
                        in_=pb_.rearrange("p (g n) -> p g n", g=8), func=AF.Copy), reads=[rb_], writes=[r["ktb"]])
            tk2 = S.dma("sp", lambda e: e.dma_start(out=wt_d, in_=WTB[:].rearrange("p g c -> p (g c)")), "st", reads=[r["ktb"]], writes=[r["wtd"]])
            tk3s = []
            for c, EX in enumerate((ER, EI)):
                S.op("pool", lambda e: e.memset(ETB[:], 0.0), writes=[r["etb"]])
                for gi in range(2):
                    hs = slice(gi * 64, (gi + 1) * 64)
                    S.op("act", lambda e, hs=hs, gi=gi, EX=EX: e.activation(
                        out=ETB[hs, gi::2, :], in_=EX[hs].rearrange("p r t q -> p r (t q)"), func=AF.Copy),
                        reads=[r["e"]], writes=[r["etb"]])
                tk3s.append(S.dma("sp", lambda e, c=c: e.dma_start(out=et_d.rearrange("p (g c x) -> p g c x", g=64, c=2)[:, :, c, :], in_=ETB[:]),
                                  "st", reads=[r["etb"]], writes=[r["etd"]]))
            rb_, pb_ = bank()
            S.op("pe", lambda e: e.matmul(pb_[0:16, :], RBA[:], OH[:], start=True, stop=True), reads=rin, writes=[rb_])
            S.op("act", lambda e: e.activation(out=EXTV, in_=pb_[0:16, :], func=AF.Copy), reads=[rb_, r["sc"]], writes=[r["sc"]])
            tk4 = S.dma("sp", lambda e: e.dma_start(out=ext_d, in_=EXTV), "st", reads=[r["sc"]], writes=[r["extd"]])
            BREV = T1[:].bitcast(BF16).rearrange("p a b c -> p (a b c)")[:, 0:4096].rearrange("p (h a i) -> p h a i", h=16, a=2)
            for h in range(16):
                S.dma("pool", lambda e, h=h: e.dma_start(out=BREV[:, h, :, :], in_=dap(ext_d, h * 512, [(1, 128), (256, 2), (1, 128)])),
                      "wd", reads=[r["extd"]], writes=[r["biasf"], r["t1"]])
            S.op(V_, lambda e: e.tensor_copy(out=ANTB[:], in_=ANTF[:]), reads=rin, writes=[r["tmpk"]])
            for h4 in range(8):
                rb_, pb_ = bank()
                S.op("pe", lambda e, pb_=pb_, h4=h4: e.matmul(pb_, ANTB[:], BREV[:, h4 * 2:(h4 + 1) * 2].rearrange("p h a i -> p (h a i)"), start=True, stop=True),
                     reads=[r["biasf"], r["tmpk"]], writes=[rb_])
                S.op("act", lambda e, pb_=pb_, h4=h4: e.activation(out=BIAS[:, h4 * 2:(h4 + 1) * 2].rearrange("p h a i -> p (h a i)"), in_=pb_, func=AF.Exp),
                     reads=[rb_], writes=[r["pers"]])
            for tk in (tk1, tk2, tk4) + tuple(tk3s) + tuple(ld):
                S.wait_tok("sp", tk)
            if not do_setup:
                for k_ in S.ops:
                    S.ops[k_] = []
            else:
                with nc.Block() as block:
                    S.replay(block)

        X = sb("X", [128, 4, D], F32)
        HT = sb("HT", [128, 16, NT], BF16)
        ACTB = sb("ACTB", [128, 12, NT], BF16)
        WB = sb("WB", [128, 3, 16, 256], BF16)
        WD = sb("WD", [128, 6, 512], BF16)
        TB = sb("TB", [128, 2, 8, 4, 128], BF16)
        QT = sb("QT", [128, 8, NT], BF16)
        ST = QT
        KT2 = sb("KT2", [128, 2, 4, NT], BF16)
        KN = sb("KN", [128, 2, NT], BF16)
        VA = sb("VA", [128, 2, 4, 4, 66], BF16)
        YA = sb("YA", [128, 1, 1024], F32)
        HN = sb("HN", [128, 2, D], BF16)
        SS = sb("SS", [128, 8], F32)
        RSTD = sb("RSTD", [128, 8], F32)
        UQ = sb("UQ", [64, 16, 8, 16], BF16)
        UT = sb("UT", [128, 16, 64], BF16)
        SA = sb("SA", [128, 2, 8, 64], F32)
        SB_ = sb("SB", [128, 2, 8, 64], F32)
        HIN = sb("HIN", [128, 2, 32], F32)
        HB = sb("HB", [128, 2, 8, 64], BF16)
        TA = sb("TA", [128, 2, 8, 64], F32)
        YS = sb("YS", [64, 2, 8, 128], BF16)
        GX = sb("GX", [64, 1, 3, 512], F32)
        YT = sb("YT", [128, 8, 8, 64], BF16)
        SQ = sb("SQ", [128, 2, NT], BF16)
        SIG = sb("SIG", [128, 2, NT], BF16)
        RB = sb("RB", [128, 1, NT], F32)
        LG = sb("LG", [128, 3, 256], BF16)
        PT = sb("PT", [128, 3, 256], BF16)
        DEN = sb("DEN", [128, 2, 4], F32)

        R = {}

        def rg(name):
            if name not in R:
                R[name] = Reg()
            return R[name]
        rc = [r_const]
        sub_regs = [rg("X%d" % i) for i in range(4)]

        S.op("dve", lambda e: e.memset(VA[:], 1.0), writes=[rg("VA0"), rg("VA1")])
        S.op("dve", lambda e: e.memset(KT2[:], 0.0), writes=[rg("KT0"), rg("KT1")])
        S.op("dve", lambda e: e.memset(HIN[:], 0.0), writes=[rg("HIN")])

        wslot = [0]
        wb_ring3 = [(WB[:, i], "WB%d" % i, "wb%d" % i) for i in range(3)]
        wb_ring5 = wb_ring3 + [
            (QT[:].rearrange("p a b -> p (a b)").rearrange("p (k c) -> p k c", k=16), "QT", "wb3"),
            (YT[:].rearrange("p a b c -> p (a b c)").rearrange("p (k c) -> p k c", k=16), "YT", "wb4")]
        wd_ring = [(WD[:, i, :], "WD%d" % i, "wd%d" % i) for i in range(6)] + [
            (KN[:, 0, :], "KN0", "wd6"), (KN[:, 1, :], "KN1", "wd7"), (SQ[:, 0, :], "SQ0", "wd8"), (SQ[:, 1, :], "SQ1", "wd9")]

        def load_wblock(src_ap_fn_list, ffn=False):
            ring = wb_ring5 if ffn else wb_ring3
            ap_, rname, sname = ring[wslot[0] % len(ring)]
            wslot[0] += 1
            reg = rg(rname)
            for ov, ia in src_ap_fn_list:
                S.dma("pool", lambda e, ov=ov, ia=ia, ap_=ap_: e.dma_start(out=ov(ap_), in_=ia, allow_slow_non_contiguous=True),
                      sname, writes=[reg])
            return ap_, reg

        def wcols(W, ncols_total, c0, ncol, krows=16):
            return dap(W, c0, [(ncols_total, 128), (128 * ncols_total, krows), (1, ncol)])

        def norm_to_HT(G, xsrc_regs):
            for sub in range(4):
                S.op("act", lambda e, sub=sub: e.activation(out=ACTB[:, 8:12, :].rearrange("p a b -> p (a b)"), in_=X[:, sub, :], func=AF.Square, accum_out=SS[:, sub:sub + 1]),
                     reads=[xsrc_regs[sub]], writes=[rg("ACT8"), rg("ACT9"), rg("ACT10"), rg("ACT11"), rg("SS")])
            S.op("dve", lambda e: e.tensor_scalar(out=RSTD[:, 0:4], in0=SS[:, 0:4], scalar1=1.0 / D, scalar2=EPS, op0=ALU.mult, op1=ALU.add),
                 reads=[rg("SS")], writes=[rg("RSTD")])
            S.op("act", lambda e: e.activation(out=RSTD[:, 0:4], in_=RSTD[:, 0:4], func=AF.Sqrt), reads=[rg("RSTD")], writes=[rg("RSTD")])
            S.op("dve", lambda e: e.reciprocal(out=RSTD[:, 0:4], in_=RSTD[:, 0:4]), reads=[rg("RSTD")], writes=[rg("RSTD")])
            for sub in range(4):
                hb = sub % 2
                S.op("act", lambda e, sub=sub, hb=hb: e.activation(out=HN[:, hb, :], in_=X[:, sub, :], func=AF.Copy, scale=RSTD[:, sub:sub + 1]),
                     reads=[xsrc_regs[sub], rg("RSTD")], writes=[rg("HN%d" % hb)])
                for kh in range(2):
                    rb_, pb_ = bank()
                    pbb = pb_.bitcast(BF16)
                    for k8 in range(8):
                        k = kh * 8 + k8
                        S.op("pe", lambda e, pbb=pbb, k8=k8, k=k, hb=hb: e.transpose(pbb[:, k8 * 128:(k8 + 1) * 128],
                                                                                     HN[:, hb, k * 128:(k + 1) * 128], IDB[:]),
                             reads=[rg("HN%d" % hb)] + rc, writes=[rb_], sig=(k8 == 7))
                    S.op("dve", lambda e, pbb=pbb, kh=kh, sub=sub: e.tensor_tensor(
                        out=HT[:, kh * 8:(kh + 1) * 8, sub * 128:(sub + 1) * 128], in0=pbb.rearrange("p (k t) -> p k t", k=8),
                        in1=G[:, kh * 8:(kh + 1) * 8].unsqueeze(2).broadcast_to([128, 8, 128]), op=ALU.mult),
                        reads=[rb_] + rc, writes=[rg("HT")])

        def qk_norm(pb_, rb_, GV, out_ap, out_reg):
            S.op("act", lambda e: e.activation(out=SQ[:, 0, :], in_=pb_, func=AF.Square), reads=[rb_], writes=[rg("SQ0")])
            rb2, pb2 = bank()
            S.op("pe", lambda e: e.matmul(pb2, BONES[:], SQ[:, 0, :], start=True, stop=True), reads=[rg("SQ0")] + rc, writes=[rb2])
            S.op("act", lambda e: e.activation(out=RB[:, 0, :], in_=pb2, func=AF.Sqrt, scale=1.0 / 64, bias=EPS), reads=[rb2], writes=[rg("RB0")])
            S.op("dve", lambda e: e.reciprocal(out=RB[:, 0, :], in_=RB[:, 0, :]), reads=[rg("RB0")], writes=[rg("RB0")])
            S.op("dve", lambda e: e.scalar_tensor_tensor(out=out_ap, in0=pb_, scalar=GV[:, 0:1], in1=RB[:, 0, :], op0=ALU.mult, op1=ALU.mult),
                 reads=[rb_, rg("RB0")] + rc, writes=[out_reg])

        ssm_batch = [0]

        ssm_ctx = {}

        def ssm_part1(ub):
            s_, wreg = load_wblock([(lambda sl: sl, wcols(w_in, 2560, 1536 + ub * 256, 256))])
            tsl = []
            for ch in range(2):
                g0 = ub * 16 + ch * 8
                ts_ = ssm_batch[0] % 2
                ssm_batch[0] += 1
                treg = rg("TB%d" % ts_)
                S.dma("sp", lambda e, ts_=ts_, g0=g0: e.dma_start(out=TB[:, ts_, :, 0, :], in_=wt_d[:, g0 * 128:(g0 + 8) * 128].rearrange("p (g c) -> p g c", g=8)),
                      "tb%d" % ts_, writes=[treg])
                S.dma("sp", lambda e, ts_=ts_, g0=g0: e.dma_start(out=TB[:, ts_, :, 1, :], in_=kt_d[:, g0 * 128:(g0 + 8) * 128].rearrange("p (g c) -> p g c", g=8)),
                      "tb%d" % ts_, writes=[treg])
                S.dma("sp", lambda e, ts_=ts_, g0=g0: e.dma_start(out=TB[:, ts_, :, 2:4, :], in_=et_d[:, g0 * 256:(g0 + 8) * 256].rearrange("p (g c x) -> p g c x", g=8, c=2)),
                      "tb%d" % ts_, writes=[treg])
                tsl.append((ts_, treg))
            for sp_ in range(4):
                rb_, pb_ = bank()
                for s2 in range(2):
                    s = sp_ * 2 + s2
                    for k in range(16):
                        S.op("pe", lambda e, pb_=pb_, s2=s2, s=s, k=k, s_=s_: e.matmul(
                            pb_[0:64, s2 * 256:(s2 + 1) * 256], HT[:, k, s:NT:8], s_[:, k, :], start=(k == 0), stop=(k == 15)),
                            reads=[rg("HT"), wreg], writes=[rb_], sig=(k == 15 and s2 == 1))
                S.op("act", lambda e, pb_=pb_, sp_=sp_: e.activation(
                    out=UQ[:, :, sp_ * 2:(sp_ + 1) * 2, :].rearrange("p g s q -> p s g q"),
                    in_=pb_[0:64, :].rearrange("p (s g q) -> p s g q", s=2, g=16), func=AF.Copy),
                    reads=[rb_], writes=[rg("UQ")])
            for gh in range(2):
                rb_, pb_ = bank()
                pbb = pb_.bitcast(BF16)
                for g8 in range(8):
                    g = gh * 8 + g8
                    S.op("pe", lambda e, pbb=pbb, g8=g8, g=g: e.transpose(pbb[:, g8 * 64:(g8 + 1) * 64],
                                                                          UQ[:, g].rearrange("p s q -> p (s q)"), IDB[0:64, 0:64]),
                         reads=[rg("UQ")] + rc, writes=[rb_], sig=(g8 == 7))
                S.op("dve", lambda e, pbb=pbb, gh=gh: e.tensor_copy(out=UT[:, gh * 8:(gh + 1) * 8, :],
                                                                    in_=pbb[:, 0:512].rearrange("p (g j) -> p g j", g=8)),
                     reads=[rb_], writes=[rg("UT")])
            rbr, pbr = bank()
            rbi, pbi = bank()
            for pr in range(8):
                for gi in range(2):
                    g = 2 * pr + gi
                    ts_, treg = tsl[g // 8]
                    hs = slice(gi * 64, (gi + 1) * 64)
                    last = (pr == 7 and gi == 1)
                    S.op("pe", lambda e, hs=hs, pr=pr, g=g, ts_=ts_: e.matmul(pbr[hs, pr * 64:(pr + 1) * 64], TB[:, ts_, g % 8, 0, 0:64], UT[:, g, :],
                                                                               start=True, stop=True),
                         reads=[rg("UT"), treg], writes=[rbr], sig=False)
                    S.op("pe", lambda e, hs=hs, pr=pr, g=g, ts_=ts_: e.matmul(pbi[hs, pr * 64:(pr + 1) * 64], TB[:, ts_, g % 8, 0, 64:128], UT[:, g, :],
                                                                               start=True, stop=True),
                         reads=[rg("UT"), treg], writes=[rbi], sig=last)
            S.op("act", lambda e: e.activation(out=SA[:, 0], in_=pbr.rearrange("p (r j) -> p r j", r=8), func=AF.Copy), reads=[rbr], writes=[rg("SA"), rg("SC0")])
            S.op("act", lambda e: e.activation(out=SA[:, 1], in_=pbi.rearrange("p (r j) -> p r j", r=8), func=AF.Copy), reads=[rbi], writes=[rg("SA"), rg("SC1")])
            prs = slice(ub * 8, (ub + 1) * 8)
            rH, rSA, rSB = rg("HIN"), rg("SA"), rg("SB")
            Cc, Sn = ROTC[:, prs, :], ROTS[:, prs, :]

            def vtt(o, a_, b_, op, reads, writes, eng="dve"):
                S.op(eng, lambda e: e.tensor_tensor(out=o, in0=a_, in1=b_, op=op), reads=reads, writes=writes)
            vtt(TA[:, 0], Cc, SA[:, 0], ALU.mult, [rSA] + rc, [rg("TA0")])
            vtt(TA[:, 1], Sn, SA[:, 1], ALU.mult, [rSA] + rc, [rg("TA1")])
            vtt(SB_[:, 0], TA[:, 0], TA[:, 1], ALU.add, [rg("TA0"), rg("TA1")], [rg("SB0")])
            vtt(TA[:, 0], Cc, SA[:, 1], ALU.mult, [rSA] + rc, [rg("TA0")])
            vtt(TA[:, 1], Sn, SA[:, 0], ALU.mult, [rSA] + rc, [rg("TA1")])
            vtt(SB_[:, 1], TA[:, 0], TA[:, 1], ALU.subtract, [rg("TA0"), rg("TA1")], [rg("SB1")])
            S.op("dve", lambda e: e.tensor_copy(out=HB[:, :, :, 0], in_=HIN[:, :, prs]), reads=[rH], writes=[rg("HB")])
            for c in range(2):
                for pr in range(8):
                    gp = ub * 8 + pr
                    S.op("dve", lambda e, c=c, pr=pr, gp=gp: e.tensor_tensor_scan(
                        out=SA[:, c, pr, :], data0=RHO[:, gp:gp + 1].broadcast_to([128, 64]), data1=SB_[:, c, pr, :],
                        initial=HIN[:, c, gp:gp + 1], op0=ALU.mult, op1=ALU.add),
                        reads=[rg("SB%d" % c), rH] + rc, writes=[rg("SC%d" % c)])
            vtt(TA[:, 0], Cc, SA[:, 0], ALU.mult, [rg("SC0")] + rc, [rg("TA0")])
            vtt(TA[:, 1], Sn, SA[:, 1], ALU.mult, [rg("SC1")] + rc, [rg("TA1")])
            vtt(SB_[:, 0], TA[:, 0], TA[:, 1], ALU.subtract, [rg("TA0"), rg("TA1")], [rg("SB0")])
            vtt(TA[:, 0], Cc, SA[:, 1], ALU.mult, [rg("SC1")] + rc, [rg("TA0")])
            vtt(TA[:, 1], Sn, SA[:, 0], ALU.mult, [rg("SC0")] + rc, [rg("TA1")])
            vtt(SB_[:, 1], TA[:, 0], TA[:, 1], ALU.add, [rg("TA0"), rg("TA1")], [rg("SB1")])
            S.op("dve", lambda e: e.tensor_copy(out=HIN[:, :, prs], in_=SB_[:, :, :, 63]), reads=[rg("SB0"), rg("SB1")], writes=[rH])
            ssm_ctx[ub] = tsl

        def ssm_part2(ub):
            tsl = ssm_ctx[ub]
            S.op("dve", lambda e: e.tensor_copy(out=HB[:, :, :, 1:64], in_=SB_[:, :, :, 0:63]), reads=[rg("SB0"), rg("SB1")], writes=[rg("HB")])
            for gq in range(4):
                rb_, pb_ = bank()
                for g4 in range(4):
                    g = gq * 4 + g4
                    ts_, treg = tsl[g // 8]
                    o = pb_[0:64, g4 * 128:(g4 + 1) * 128]
                    pr = g // 2
                    S.op("pe", lambda e, o=o, g=g, ts_=ts_: e.matmul(o, UT[:, g, :], TB[:, ts_, g % 8, 1, :], start=True, stop=False),
                         reads=[rg("UT"), treg], writes=[rb_], sig=False)
                    S.op("pe", lambda e, o=o, g=g, ts_=ts_, pr=pr: e.matmul(o, HB[:, 0, pr, :], TB[:, ts_, g % 8, 2, :], start=False, stop=False),
                         reads=[rg("HB"), treg], writes=[rb_], sig=False)
                    S.op("pe", lambda e, o=o, g=g, ts_=ts_, pr=pr: e.matmul(o, HB[:, 1, pr, :], TB[:, ts_, g % 8, 3, :], start=False, stop=True),
                         reads=[rg("HB"), treg], writes=[rb_], sig=(g4 == 3))
                hb = (gq // 2) % 2
                gb = gq % 2
                src = pb_[0:64, :]
                gx = [GX[:, 0, i, :] for i in range(3)]
                rgx = rg("GX0")
                S.op("act", lambda e, src=src, gx=gx: e.activation(out=gx[0], in_=src, func=AF.Square), reads=[rb_], writes=[rgx])
                S.op("dve", lambda e, gx=gx: e.tensor_scalar(out=gx[0], in0=gx[0], scalar1=0.044715, scalar2=1.0, op0=ALU.mult, op1=ALU.add), reads=[rgx], writes=[rgx])
                S.op("dve", lambda e, src=src, gx=gx: e.tensor_tensor(out=gx[1], in0=src, in1=gx[0], op=ALU.mult), reads=[rb_, rgx], writes=[rgx])
                S.op("act", lambda e, gx=gx: e.activation(out=gx[2], in_=gx[1], func=AF.Sigmoid, scale=2.0 * math.sqrt(2.0 / math.pi)), reads=[rgx], writes=[rgx])
                S.op("dve", lambda e, src=src, gx=gx, hb=hb, gb=gb: e.tensor_tensor(
                    out=YS[:, hb, :, gb * 64:(gb + 1) * 64].rearrange("p t (g q) -> p g t q", g=4),
                    in0=src.rearrange("p (g t q) -> p g t q", g=4, t=8), in1=gx[2].rearrange("p (g t q) -> p g t q", g=4, t=8), op=ALU.mult),
                    reads=[rb_, rgx], writes=[rg("YS%d" % hb)])
                if gb == 1:
                    ct = ub * 2 + gq // 2
                    rb2, pb2 = bank()
                    pbb = pb2.bitcast(BF16)
                    for t in range(8):
                        S.op("pe", lambda e, pbb=pbb, t=t, hb=hb: e.transpose(pbb[:, t * 64:(t + 1) * 64], YS[:, hb, t, :], IDB[0:64, 0:64]),
                             reads=[rg("YS%d" % hb)] + rc, writes=[rb2], sig=(t == 7))
                    S.op("act", lambda e, pbb=pbb, ct=ct: e.activation(out=YT[:, ct].rearrange("p t j -> p (t j)"), in_=pbb[:, 0:512], func=AF.Copy),
                         reads=[rb2], writes=[rg("YT")])

        def process_tt(xsrc, tt_i, pred, last_pred, ping):
            for sub in range(4):
                S.dma("sp", lambda e, sub=sub: e.dma_start(out=X[:, sub, :], in_=xsrc[tt_i * NT + sub * 128: tt_i * NT + (sub + 1) * 128, :]),
                      "xl%d" % sub, writes=[sub_regs[sub]])
            main = not pred
            if stage >= 1:
                norm_to_HT(G1, sub_regs)
            if main and stage >= 2:
                for qb in range(4):
                    s_, wreg = load_wblock([(lambda sl: sl, wcols(w_in, 2560, qb * 256, 256))])
                    for m in range(2):
                        rb_, pb_ = bank()
                        for k in range(16):
                            S.op("pe", lambda e, pb_=pb_, k=k, m=m, s_=s_: e.matmul(pb_, s_[:, k, m * 128:(m + 1) * 128], HT[:, k, :],
                                                                                   start=(k == 0), stop=(k == 15)),
                                 reads=[rg("HT"), wreg], writes=[rb_], sig=(k == 15))
                        qk_norm(pb_, rb_, QG, QT[:, qb * 2 + m, :], rg("QT"))
            if (main or last_pred) and stage >= 2:
                s_, wreg = load_wblock([(lambda sl: sl, wcols(w_in, 2560, 1024, 256))])
                for kt in range(2):
                    rb_, pb_ = bank()
                    for k in range(16):
                        S.op("pe", lambda e, pb_=pb_, k=k, kt=kt, s_=s_: e.matmul(pb_, s_[:, k, kt * 128:(kt + 1) * 128], HT[:, k, :],
                                                                               start=(k == 0), stop=(k == 15)),
                             reads=[rg("HT"), wreg], writes=[rb_], sig=(k == 15))
                    qk_norm(pb_, rb_, KG, KN[:, kt, :], rg("KN%d" % kt))
                    for c in range(2):
                        kh = kt * 2 + c
                        rb2, pb2 = bank()
                        S.op("pe", lambda e, pb2=pb2, c=c, kt=kt: e.matmul(pb2, DUPB[:, c, :], KN[:, kt, :], start=True, stop=True),
                             reads=[rg("KN%d" % kt)] + rc, writes=[rb2])
                        S.op("act", lambda e, pb2=pb2, kh=kh: e.activation(out=KT2[:, ping, kh, :], in_=pb2, func=AF.Copy),
                             reads=[rb2], writes=[rg("KT%d" % ping)])
                s_, wreg = load_wblock([(lambda sl: sl, wcols(w_in, 2560, 1280, 256))])
                for sub in range(4):
                    rb_, pb_ = bank()
                    for k in range(16):
                        S.op("pe", lambda e, pb_=pb_, k=k, sub=sub, s_=s_: e.matmul(pb_[:, 0:256], HT[:, k, sub * 128:(sub + 1) * 128], s_[:, k, :],
                                                                                   start=(k == 0), stop=(k == 15)),
                             reads=[rg("HT"), wreg], writes=[rb_], sig=(k == 15))
                    S.op("act", lambda e, pb_=pb_, sub=sub: e.activation(out=VA[:, ping, sub, :, 0:64], in_=pb_[:, 0:256].rearrange("p (h d) -> p h d", h=4),
                                                                         func=AF.Copy), reads=[rb_], writes=[rg("VA%d" % ping)])
            if pred:
                for ub in range(4):
                    ssm_part1(ub)
                return

            def attn_block(b):
                yb = 0
                hnb = b % 2
                def stage1(h):
                    kh = h // 4
                    hp = slice((h % 2) * 64, (h % 2) * 64 + 64)
                    qv = QT[hp, h // 2, b * 128:(b + 1) * 128]
                    if b > 0:
                        kprev = KT2[hp, ping, kh, (b - 1) * 128:b * 128]
                        vprev = VA[:, ping, b - 1, kh, 0:65]
                        rkp, rvp = rg("KT%d" % ping), rg("VA%d" % ping)
                    else:
                        kprev = KT2[hp, 1 - ping, kh, 384:512]
                        vprev = VA[:, 1 - ping, 3, kh, 0:65]
                        rkp, rvp = rg("KT%d" % (1 - ping)), rg("VA%d" % (1 - ping))
                    kcur = KT2[hp, ping, kh, b * 128:(b + 1) * 128]
                    vcur = VA[:, ping, b, kh, 0:65]
                    rbs, pbs = bank()
                    S.op("pe", lambda e, pbs=pbs, kprev=kprev, qv=qv: e.matmul(pbs[:, 0:128], kprev, qv, start=True, stop=True),
                         reads=[rkp, rg("QT")], writes=[rbs], sig=False)
                    S.op("pe", lambda e, pbs=pbs, kcur=kcur, qv=qv: e.matmul(pbs[:, 128:256], kcur, qv, start=True, stop=True),
                         reads=[rg("KT%d" % ping), rg("QT")], writes=[rbs])
                    lb = h % 3
                    if b == 0 and tt_i == 0:
                        S.op("act", lambda e, pbs=pbs, lb=lb: e.activation(out=LG[:, lb, 0:128], in_=pbs[:, 0:128], func=AF.Exp, bias=HM8[:, 0:1], scale=0.125),
                             reads=[rbs] + rc, writes=[rg("LG%d" % lb)])
                        S.op("act", lambda e, pbs=pbs, lb=lb: e.activation(out=LG[:, lb, 128:256], in_=pbs[:, 128:256], func=AF.Exp, bias=NEG8[:, 0:1], scale=0.125),
                             reads=[rbs] + rc, writes=[rg("LG%d" % lb)])
                    else:
                        S.op("act", lambda e, pbs=pbs, lb=lb: e.activation(out=LG[:, lb, :], in_=pbs[:, 0:256], func=AF.Exp, bias=NEG8[:, 0:1], scale=0.125),
                             reads=[rbs] + rc, writes=[rg("LG%d" % lb)])
                    S.op("pool", lambda e, h=h, lb=lb: e.tensor_tensor(out=PT[:, lb, :], in0=LG[:, lb, :], in1=BIAS[:, h].rearrange("p a i -> p (a i)"), op=ALU.mult),
                         reads=[rg("LG%d" % lb)] + rc, writes=[rg("PT%d" % lb)])
                    return lb, vprev, rvp, vcur

                pbo_cur = [None]

                def stage2(h, ctx):
                    lb, vprev, rvp, vcur = ctx
                    kh, g4 = divmod(h, 4)
                    if g4 == 0:
                        pbo_cur[0] = bank()
                    rbo, pbo = pbo_cur[0]
                    o = pbo[:, g4 * 65:(g4 + 1) * 65]
                    S.op("pe", lambda e, o=o, lb=lb, vprev=vprev: e.matmul(o, PT[:, lb, 0:128], vprev, start=True, stop=False),
                         reads=[rg("PT%d" % lb), rvp], writes=[rbo], sig=False)
                    S.op("pe", lambda e, o=o, lb=lb, vcur=vcur: e.matmul(o, PT[:, lb, 128:256], vcur, start=False, stop=True),
                         reads=[rg("PT%d" % lb), rg("VA%d" % ping)], writes=[rbo], sig=True)
                    if g4 != 3:
                        return
                    dn = kh % 2
                    ov = pbo[:, 0:260].rearrange("p (g c) -> p g c", g=4)
                    S.op("dve", lambda e, ov=ov, dn=dn, kh=kh: e.tensor_tensor(out=DEN[:, dn, :], in0=ov[:, :, 64], in1=ESK[:, kh * 4:(kh + 1) * 4], op=ALU.add),
                         reads=[rbo] + rc, writes=[rg("DEN%d" % dn)])
                    S.op("dve", lambda e, dn=dn: e.reciprocal(out=DEN[:, dn, :], in_=DEN[:, dn, :]), reads=[rg("DEN%d" % dn)], writes=[rg("DEN%d" % dn)])
                    S.op("dve", lambda e, ov=ov, dn=dn, kh=kh, yb=yb: e.tensor_tensor(
                        out=YA[:, yb, kh * 256:(kh + 1) * 256].rearrange("p (g d) -> p g d", g=4), in0=ov[:, :, 0:64],
                        in1=DEN[:, dn, :].unsqueeze(2).broadcast_to([128, 4, 64]), op=ALU.mult),
                        reads=[rbo, rg("DEN%d" % dn)], writes=[rg("YA%d" % yb)])

                ctxs = {0: stage1(0), 1: stage1(1)}
                for h in range(16):
                    if h + 2 < 16:
                        ctxs[h + 2] = stage1(h + 2)
                    stage2(h, ctxs.pop(h))
                S.op("act", lambda e, yb=yb, b=b: e.activation(out=ACTB[:, 8:10, :].rearrange("p a b -> p (a b)"), in_=YA[:, yb, :], func=AF.Square, accum_out=SS[:, 4 + b:5 + b]),
                     reads=[rg("YA%d" % yb)], writes=[rg("ACT8"), rg("ACT9"), rg("SSA%d" % b)])
                S.op("dve", lambda e, b=b: e.tensor_scalar(out=RSTD[:, 4 + b:5 + b], in0=SS[:, 4 + b:5 + b], scalar1=1.0 / 1024, scalar2=EPS, op0=ALU.mult, op1=ALU.add),
                     reads=[rg("SSA%d" % b)], writes=[rg("RSA%d" % b)])
                S.op("act", lambda e, b=b: e.activation(out=RSTD[:, 4 + b:5 + b], in_=RSTD[:, 4 + b:5 + b], func=AF.Sqrt), reads=[rg("RSA%d" % b)], writes=[rg("RSA%d" % b)])
                S.op("dve", lambda e, b=b: e.reciprocal(out=RSTD[:, 4 + b:5 + b], in_=RSTD[:, 4 + b:5 + b]), reads=[rg("RSA%d" % b)], writes=[rg("RSA%d" % b)])
                S.op("act", lambda e, yb=yb, b=b, hnb=hnb: e.activation(out=HN[:, hnb, 0:1024], in_=YA[:, yb, :], func=AF.Copy, scale=RSTD[:, 4 + b:5 + b]),
                     reads=[rg("YA%d" % yb), rg("RSA%d" % b)], writes=[rg("HN%d" % hnb)])
                rb_, pb_ = bank()
                pbb = pb_.bitcast(BF16)
                for k8 in range(8):
                    S.op("pe", lambda e, pbb=pbb, k8=k8, hnb=hnb: e.transpose(pbb[:, k8 * 128:(k8 + 1) * 128], HN[:, hnb, k8 * 128:(k8 + 1) * 128], IDB[:]),
                         reads=[rg("HN%d" % hnb)] + rc, writes=[rb_], sig=(k8 == 7))
                S.op("dve", lambda e, pbb=pbb, b=b: e.tensor_tensor(
                    out=ACTB[:, 0:8, b * 128:(b + 1) * 128], in0=pbb.rearrange("p (k t) -> p k t", k=8),
                    in1=GA[:].unsqueeze(2).broadcast_to([128, 8, 128]), op=ALU.mult), reads=[rb_] + rc, writes=[rg("ACT%d" % j_) for j_ in range(8)])

            for i_ in range(4):
                ssm_part1(i_)
                attn_block(i_)
                ssm_part2(i_)
            rbss, pbss = bank(reserve=True)
            for nb in range(4):
                s_, wreg = load_wblock([(lambda sl: sl[:, 0:8, :], wcols(w_glu, 1024, nb * 256, 256, krows=8))])
                for m in range(2):
                    ct = nb * 2 + m
                    rb_, pb_ = bank()
                    for c in range(8):
                        S.op("pe", lambda e, pb_=pb_, c=c, m=m, s_=s_: e.matmul(pb_, s_[:, c, m * 128:(m + 1) * 128], YT[:, c].rearrange("p t j -> p (t j)"),
                                                                               start=(c == 0), stop=(c == 7)),
                             reads=[rg("YT"), wreg], writes=[rb_], sig=(c == 7))
                    sg = ct % 2
                    S.op("act", lambda e, pb_=pb_, sg=sg: e.activation(out=SIG[:, sg, :], in_=pb_, func=AF.Sigmoid), reads=[rb_], writes=[rg("SIG%d" % sg)])
                    S.op("dve", lambda e, ct=ct, sg=sg: e.tensor_tensor(out=ST[:, ct, :], in0=YT[:, ct].rearrange("p t j -> p (t j)"), in1=SIG[:, sg, :], op=ALU.mult),
                         reads=[rg("YT"), rg("SIG%d" % sg)], writes=[rg("QT")])
                    S.op("act", lambda e, ct=ct, sg=sg: e.activation(out=SQ[:, sg, :], in_=ST[:, ct, :], func=AF.Square), reads=[rg("QT")], writes=[rg("SQ%d" % sg)])
                    S.op("pe", lambda e, ct=ct, sg=sg: e.matmul(pbss, ONESB[:], SQ[:, sg, :], start=(ct == 0), stop=(ct == 7)),
                         reads=[rg("SQ%d" % sg)] + rc, writes=[rbss])
            reserved.clear()
            if True:
                S.op("act", lambda e: e.activation(out=RB[:, 0, :], in_=pbss, func=AF.Sqrt, scale=1.0 / 1024, bias=EPS), reads=[rbss], writes=[rg("RB0")])
                S.op("dve", lambda e: e.reciprocal(out=RB[:, 0, :], in_=RB[:, 0, :]), reads=[rg("RB0")], writes=[rg("RB0")])
            for ct in range(8):
                S.op("dve", lambda e, ct=ct: e.scalar_tensor_tensor(
                    out=HT[:, 8 + ct, :].rearrange("p (j t) -> p t j", t=8), in0=ST[:, ct, :].rearrange("p (t j) -> p t j", t=8),
                    scalar=GS[:, ct:ct + 1], in1=RB[:, 0, :].rearrange("p (t j) -> p t j", t=8), op0=ALU.mult, op1=ALU.mult),
                    reads=[rg("QT"), rg("RB0")] + rc, writes=[rg("HT")])
            for fb in range(8):
                s_, wreg = load_wblock([(lambda sl: sl, wcols(w_out, D, fb * 256, 256))])
                for sp2 in range(2):
                    rb_, pb_ = bank()
                    for s2 in range(2):
                        sub = sp2 * 2 + s2
                        for k in range(16):
                            src_ = ACTB if k < 8 else HT
                            S.op("pe", lambda e, pb_=pb_, s2=s2, sub=sub, k=k, s_=s_, src_=src_: e.matmul(
                                pb_[:, s2 * 256:(s2 + 1) * 256], src_[:, k, sub * 128:(sub + 1) * 128], s_[:, k, :], start=(k == 0), stop=(k == 15)),
                                reads=[rg("HT") if k >= 8 else rg("ACT%d" % k), wreg], writes=[rb_], sig=(k == 15))
                        S.op("dve", lambda e, pb_=pb_, s2=s2, sub=sub, fb=fb: e.tensor_tensor(
                            out=X[:, sub, fb * 256:(fb + 1) * 256], in0=pb_[:, s2 * 256:(s2 + 1) * 256], in1=X[:, sub, fb * 256:(fb + 1) * 256], op=ALU.add),
                            reads=[rb_, sub_regs[sub]], writes=[sub_regs[sub]])
            if stage >= 7:
                norm_to_HT(G2, sub_regs)
            c0 = 0
            for grp, nch in enumerate(GROUPS_FF if stage >= 7 else []):
                for bl in range(nch // 2):
                    col = (c0 + bl * 2) * 128
                    sg_, wg = load_wblock([(lambda sl: sl, wcols(w_gate, FF, col, 256))], ffn=True)
                    su_, wu = load_wblock([(lambda sl: sl, wcols(w_up, FF, col, 256))], ffn=True)
                    for m in range(2):
                        j = bl * 2 + m
                        rbg, pbg = bank()
                        for k in range(16):
                            S.op("pe", lambda e, pbg=pbg, k=k, m=m, sg_=sg_: e.matmul(pbg, sg_[:, k, m * 128:(m + 1) * 128], HT[:, k, :], start=(k == 0), stop=(k == 15)),
                                 reads=[rg("HT"), wg], writes=[rbg], sig=(k == 15))
                        rbu, pbu = bank()
                        for k in range(16):
                            S.op("pe", lambda e, pbu=pbu, k=k, m=m, su_=su_: e.matmul(pbu, su_[:, k, m * 128:(m + 1) * 128], HT[:, k, :], start=(k == 0), stop=(k == 15)),
                                 reads=[rg("HT"), wu], writes=[rbu], sig=(k == 15))
                        sg = j % 2
                        S.op("act", lambda e, pbg=pbg, sg=sg: e.activation(out=SIG[:, sg, :], in_=pbg, func=AF.Silu), reads=[rbg], writes=[rg("SIG%d" % sg)])
                        S.op("dve", lambda e, pbu=pbu, sg=sg, j=j: e.tensor_tensor(out=ACTB[:, j, :], in0=pbu, in1=SIG[:, sg, :], op=ALU.mult),
                             reads=[rbu, rg("SIG%d" % sg)], writes=[rg("ACT%d" % j)])
                for f in range(4):
                    bk = [bank() for _ in range(4)]
                    for j in range(nch):
                        wap, wrn, wsn = wd_ring[wslot_d[0] % len(wd_ring)]
                        wslot_d[0] += 1
                        wreg = rg(wrn)
                        S.dma("pool", lambda e, wap=wap, j=j, f=f, c0=c0: e.dma_start(out=wap, in_=w_down[(c0 + j) * 128:(c0 + j + 1) * 128, f * 512:(f + 1) * 512]),
                              wsn, writes=[wreg])
                        for sub in range(4):
                            S.op("pe", lambda e, sub=sub, j=j, wap=wap, pb_=bk[sub][1]: e.matmul(pb_, ACTB[:, j, sub * 128:(sub + 1) * 128], wap,
                                                                                              start=(j == 0), stop=(j == nch - 1)),
                                 reads=[rg("ACT%d" % j), wreg], writes=[bk[sub][0]], sig=(j == nch - 1 or sub == 3))
                    for sub in range(4):
                        S.op("dve", lambda e, sub=sub, f=f, pb_=bk[sub][1]: e.tensor_tensor(
                            out=X[:, sub, f * 512:(f + 1) * 512], in0=pb_, in1=X[:, sub, f * 512:(f + 1) * 512], op=ALU.add),
                            reads=[bk[sub][0], sub_regs[sub]], writes=[sub_regs[sub]])
                c0 += nch
            for sub in range(4):
                out_toks.append(S.dma("sp", lambda e, sub=sub: e.dma_start(out=out[tt_i * NT + sub * 128: tt_i * NT + (sub + 1) * 128, :], in_=X[:, sub, :]),
                                      "ot%d" % sub, reads=[sub_regs[sub]]))

        wslot_d = [0]
        out_toks = []
        ping = 0
        if do_pred:
            for t in range(n_tt):
                process_tt(x_pred, t, True, t == n_tt - 1, ping)
            ping = 1 - ping
        for t in range(n_tt):
            process_tt(x_main, t, False, False, ping)
            ping = 1 - ping
        for tk in out_toks:
            S.wait_tok("sp", tk)
        with nc.Block() as block:
            S.replay(block)
    return nc


def _t5_bucket(dist):
    n = np.maximum(dist, 0)
    max_exact = 16
    nf = np.maximum(n, 1).astype(np.float32)
    large = max_exact + (np.log(nf / max_exact) / math.log(128 / max_exact) * (32 - max_exact)).astype(np.int32)
    large = np.minimum(large, 31)
    return np.where(n < max_exact, n, large).astype(np.int32)


def _consts():
    ident = np.eye(128, dtype=np.float32)
    s_idx = np.arange(128) // 16
    mask = (s_idx[:, None] <= s_idx[None, :]).astype(np.float32)
    mv = np.broadcast_to(np.asarray(MS, np.float32)[None, :, None], (128, NM, 32)).reshape(128, NM * 32).copy()
    bones = (s_idx[:, None] // 4 == s_idx[None, :] // 4).astype(np.float32)
    bucket = _t5_bucket(np.arange(128))
    oh = np.zeros((33, 512), np.float32)
    for e in range(255):
        if e < 127:
            oh[bucket[e + 1], e] = 1.0
            oh[32, 256 + e] = NEG
        else:
            oh[32, e] = NEG
            oh[bucket[e - 127], 256 + e] = 1.0
    oh[32, 255] = NEG
    oh[32, 511] = NEG
    dup = np.zeros((128, 2, 128), np.float32)
    for c in range(2):
        for d in range(64):
            dup[c * 64 + d, c, d] = 1.0
            dup[c * 64 + d, c, 64 + d] = 1.0
    mv2 = np.broadcast_to((8.0 * np.arange(1, 65, dtype=np.float32))[None, :, None], (128, 64, 32)).reshape(128, 64 * 32).copy()
    return {"c_mv2": mv2, "c_dup": dup.reshape(128, 256), "c_ident": ident, "c_mask": mask, "c_mv": mv, "c_oh": oh, "c_bones": bones, "c_anti": np.ascontiguousarray(ident[::-1])}


_NC_CACHE = {}


def kernel(**inputs):
    n_tt = int(os.environ.get("MK_NTT", "4"))
    do_pred = os.environ.get("MK_PRED", "1") == "1"
    stage = int(os.environ.get("MK_STAGE", "99"))
    do_setup = os.environ.get("MK_SETUP", "1") == "1"
    key = (n_tt, do_pred, stage, do_setup)
    if key not in _NC_CACHE:
        _NC_CACHE[key] = build(n_tt, do_pred, stage, do_setup)
    nc = _NC_CACHE[key]
    x = np.asarray(inputs["x"], np.float32)
    TOK = n_tt * NT
    shared = {k: np.ascontiguousarray(np.asarray(inputs[k], np.float32)[0]) for k in
              ["ln1_g", "w_in", "q_norm_g", "k_norm_g", "attn_sinks", "ssm_a_re", "ssm_a_im", "ssm_log_dt", "ssm_b_re", "ssm_b_im",
               "ssm_c_re", "ssm_c_im", "w_glu", "attn_out_g", "ssm_out_g", "w_out", "ln2_g", "w_ff_gate", "w_ff_up", "w_ff_down"]}
    shared["ssm_d"] = np.ascontiguousarray(np.asarray(inputs["ssm_d"], np.float32)[0].reshape(-1))
    shared["rel_bias"] = np.ascontiguousarray(np.asarray(inputs["rel_bias"], np.float32))
    shared.update(_consts())
    in_maps = []
    ncores = int(os.environ.get("MK_CORES", "8"))
    for c in range(ncores):
        b, half = c // 2, c % 2
        m = dict(shared)
        m["x_main"] = np.ascontiguousarray(x[b, half * 2048: half * 2048 + TOK])
        if half == 1:
            m["x_pred"] = np.ascontiguousarray(x[b, 2048 - TOK:2048])
            m["hm8"] = np.full((128, 1), -SHIFT, np.float32)
        else:
            m["x_pred"] = np.zeros((TOK, D), np.float32)
            m["hm8"] = np.full((128, 1), NEG - SHIFT, np.float32)
        in_maps.append(m)
    if os.environ.get("MK_TRACE", "0") == "1":
        res = run_bass_kernel_spmd(nc, in_maps, core_ids=list(range(ncores)), trace=True)
        print("EXEC_NS", res.exec_time_ns)
    else:
        res = run_bass_kernel_spmd(nc, in_maps, core_ids=list(range(ncores)))
    outp = np.zeros((4, 4096, D), np.float32)
    for c in range(ncores):
        b, half = c // 2, c % 2
        outp[b, half * 2048: half * 2048 + TOK] = res.results[c]["out"]
    return outp
```
